# Optimizing a Trainium2 kernel written in Bass

```python
import math
import jax, jax.numpy as jnp
from jax import lax
import numpy as np


D_MODEL = 1024
BATCH = 4
SEQ = 4096
DEPTH = 1
DEC_BATCH = 32
DEC_SEQ = 1
PAST_LEN = 8192
PAGE_SIZE = 128

N_MEM = 256
SB_HEADS = 8
SB_DH = 64
SB_W = SB_HEADS * SB_DH
SB_BIAS_INIT = -7.5
HG_HEADS = 4
HG_DK = 64
HG_DV = 64
HG_WK = HG_HEADS * HG_DK
HG_WV = HG_HEADS * HG_DV
XA_HEADS = 4
XA_DH = 64
XA_W = XA_HEADS * XA_DH
D_MIX = SB_W + HG_WV + XA_W
D_IN = 4 * SB_W + 2 * HG_WK + 2 * HG_WV + 2 * XA_W
Q_BLOCK = 128
HG_CHUNK = 64
EPS = 1e-6
F32 = jnp.float32

kernel_name = 'hymba_stickbreak_hgrn2_memxattn_step'


def _rmsnorm(x, g):
    xf = x.astype(F32)
    y = xf * lax.rsqrt(jnp.mean(xf * xf, axis=-1, keepdims=True) + EPS)
    return (y * g.astype(F32)).astype(x.dtype)


def _mixer_inputs(x, g_norm, w_in, lb):
    b, t, _ = x.shape
    h = _rmsnorm(x, g_norm) @ w_in
    sizes = (SB_W, SB_W, SB_W, SB_W, HG_WK, HG_WK, HG_WV, HG_WV, XA_W, XA_W)
    idx = []
    acc = 0
    for s in sizes[:-1]:
        acc += s
        idx.append(acc)
    sb_q, sb_k, sb_v, sb_g, hg_q, hg_f, hg_i, hg_g, xa_q, xa_g = jnp.split(h, idx, axis=-1)
    heads = lambda a, n, d: a.reshape(b, t, n, d)
    f = lb + (1.0 - lb) * jax.nn.sigmoid(hg_f.astype(F32))
    hg_logf = heads(jnp.log(f), HG_HEADS, HG_DK)
    hg_k = heads(1.0 - f, HG_HEADS, HG_DK)
    return (heads(sb_q, SB_HEADS, SB_DH), heads(sb_k, SB_HEADS, SB_DH), heads(sb_v, SB_HEADS, SB_DH), sb_g,
            heads(hg_q.astype(F32), HG_HEADS, HG_DK), hg_logf, hg_k, heads(hg_i.astype(F32), HG_HEADS, HG_DV), hg_g,
            heads(xa_q, XA_HEADS, XA_DH), xa_g)


def _sb_attend(q, k, v, bias, q_start):
    tq, s = q.shape[1], k.shape[1]
    z = jnp.einsum('bqhd,bkhd->bhqk', q.astype(F32), k.astype(F32)) / math.sqrt(SB_DH)
    z = z + bias.astype(F32)[None, :, None, None]
    t_pos = q_start + jnp.arange(tq)
    causal = (jnp.arange(s)[None, :] < t_pos[:, None])[None, None]
    log_beta = jax.nn.log_sigmoid(z)
    log_1m = jnp.where(causal, jax.nn.log_sigmoid(-z), 0.0)
    suffix = lax.cumsum(log_1m, axis=3, reverse=True) - log_1m
    a = jnp.exp(jnp.where(causal, log_beta + suffix, -jnp.inf))
    return jnp.einsum('bhqk,bkhd->bqhd', a, v.astype(F32)).astype(q.dtype)


def _sb_prompt(q, k, v, bias):
    b, s, h, d = q.shape
    nb = s // Q_BLOCK
    qb = q.reshape(b, nb, Q_BLOCK, h, d).swapaxes(0, 1)
    out = lax.map(lambda a: _sb_attend(a[1], k, v, bias, a[0] * Q_BLOCK), (jnp.arange(nb), qb))
    return out.swapaxes(0, 1).reshape(b, s, h, d)


def _hgrn_chunk(s0, q, logf, k, v):
    c = q.shape[1]
    bc = jnp.cumsum(logf, axis=1)
    mask = jnp.tril(jnp.ones((c, c), bool))[None, :, :, None, None]
    diff = bc[:, :, None] - bc[:, None, :]
    decay = jnp.exp(jnp.where(mask, diff, -jnp.inf))
    attn = jnp.einsum('bthd,bshd,btshd->bhts', q, k, decay)
    o = jnp.einsum('bhts,bshv->bthv', attn, v) + jnp.einsum('bthd,bhdv->bthv', q * jnp.exp(bc), s0)
    last = bc[:, -1]
    k_dec = k * jnp.exp(last[:, None] - bc)
    s_new = jnp.exp(last)[..., None] * s0 + jnp.einsum('bshd,bshv->bhdv', k_dec, v)
    return o, s_new


def _hgrn_prompt(q, logf, k, v):
    b, s = q.shape[0], q.shape[1]
    nc = s // HG_CHUNK
    chunks = lambda a: a.reshape(b, nc, HG_CHUNK, a.shape[2], a.shape[3]).swapaxes(0, 1)
    s0 = jnp.zeros((b, HG_HEADS, HG_DK, HG_DV), F32)

    def step(state, xs):
        o, st = _hgrn_chunk(state, *xs)
        return st, o

    s_fin, o = lax.scan(step, s0, (chunks(q), chunks(logf), chunks(k), chunks(v)))
    return o.swapaxes(0, 1).reshape(b, s, HG_HEADS, HG_DV), s_fin


def _mem_kv(mem, g, w):
    b, m, _ = mem.shape
    mk, mv = jnp.split(_rmsnorm(mem, g) @ w, 2, axis=-1)
    return mk.reshape(b, m, XA_HEADS, XA_DH), mv.reshape(b, m, XA_HEADS, XA_DH)


def _cross_attend(q, mk, mv):
    s = jnp.einsum('bthd,bmhd->bhtm', q.astype(F32), mk.astype(F32)) / math.sqrt(XA_DH)
    p = jax.nn.softmax(s, axis=-1)
    return jnp.einsum('bhtm,bmhd->bthd', p, mv.astype(F32)).astype(q.dtype)


def _mixer_output(x, sb_o, sb_g, hg_o, hg_g, hg_norm_g, xa_o, xa_g, w_out):
    b, t = sb_o.shape[0], sb_o.shape[1]
    sb = sb_o.reshape(b, t, SB_W).astype(F32) * jax.nn.silu(sb_g.astype(F32))
    hg = _rmsnorm(hg_o, hg_norm_g.reshape(HG_HEADS, HG_DV)).reshape(b, t, HG_WV) * jax.nn.silu(hg_g.astype(F32))
    xa = xa_o.reshape(b, t, XA_W).astype(F32) * jax.nn.silu(xa_g.astype(F32))
    mix = jnp.concatenate([sb, hg, xa], axis=-1).astype(x.dtype)
    return x + mix @ w_out


def setup_inputs(seed: int = 0) -> dict:
    key = jax.random.key(seed)
    ks = jax.random.split(key, 20)
    n_pages = PAST_LEN // PAGE_SIZE
    n_pool = (DEC_BATCH * n_pages * 5) // 4
    nrm = lambda k, shp: jax.random.normal(k, shp, F32)
    page_table = jax.random.permutation(ks[5], n_pool)[:DEC_BATCH * n_pages].reshape(DEC_BATCH, n_pages).astype(jnp.int32)
    return {
        'x_prompt': nrm(ks[0], (BATCH, SEQ, D_MODEL)),
        'x_sample': nrm(ks[1], (DEC_BATCH, DEC_SEQ, D_MODEL)),
        'mem_prompt': nrm(ks[2], (BATCH, N_MEM, D_MODEL)),
        'cache_k': nrm(ks[3], (DEPTH, n_pool, PAGE_SIZE, SB_HEADS, SB_DH)),
        'cache_v': nrm(ks[4], (DEPTH, n_pool, PAGE_SIZE, SB_HEADS, SB_DH)),
        'page_table': page_table,
        'state_hgrn': 0.5 * nrm(ks[6], (DEPTH, DEC_BATCH, HG_HEADS, HG_DK, HG_DV)),
        'cache_mem_k': nrm(ks[7], (DEPTH, DEC_BATCH, N_MEM, XA_HEADS, XA_DH)),
        'cache_mem_v': nrm(ks[8], (DEPTH, DEC_BATCH, N_MEM, XA_HEADS, XA_DH)),
        'norm_gain': 1.0 + 0.02 * nrm(ks[9], (DEPTH, D_MODEL)),
        'w_in': nrm(ks[10], (DEPTH, D_MODEL, D_IN)) * D_MODEL ** -0.5,
        'sb_bias': SB_BIAS_INIT + 0.3 * nrm(ks[17], (DEPTH, SB_HEADS)),
        'hg_lb_logits': 0.5 * nrm(ks[11], (DEPTH + 1, HG_WK)),
        'hg_norm_gain': 1.0 + 0.02 * nrm(ks[12], (DEPTH, HG_WV)),
        'mem_norm_gain': 1.0 + 0.02 * nrm(ks[13], (DEPTH, D_MODEL)),
        'w_mem_kv': nrm(ks[14], (DEPTH, D_MODEL, 2 * XA_W)) * D_MODEL ** -0.5,
        'w_out': nrm(ks[15], (DEPTH, D_MIX, D_MODEL)) * D_MIX ** -0.5,
        'final_norm_gain': 1.0 + 0.02 * nrm(ks[16], (D_MODEL,)),
    }


def reference(x_prompt, x_sample, mem_prompt, cache_k, cache_v, page_table, state_hgrn, cache_mem_k, cache_mem_v,
              norm_gain, w_in, sb_bias, hg_lb_logits, hg_norm_gain, mem_norm_gain, w_mem_kv, w_out, final_norm_gain):
    lb_all = jnp.cumsum(jax.nn.softmax(hg_lb_logits.astype(F32), axis=0), axis=0)
    n_pages = page_table.shape[1]
    dec_b = x_sample.shape[0]
    hp, hs = x_prompt, x_sample
    kp_l, vp_l, hgp_l, mkp_l, mvp_l, ks_l, vs_l, hgs_l = [], [], [], [], [], [], [], []
    for l in range(DEPTH):
        lb = lb_all[l]
        sq, sk, sv, sg, hq, hlf, hk, hv, hg, xq, xg = _mixer_inputs(hp, norm_gain[l], w_in[l], lb)
        sb_o = _sb_prompt(sq, sk, sv, sb_bias[l])
        hg_o, hg_st_p = _hgrn_prompt(hq, hlf, hk, hv)
        mk, mv = _mem_kv(mem_prompt, mem_norm_gain[l], w_mem_kv[l])
        xa_o = _cross_attend(xq, mk, mv)
        hp = _mixer_output(hp, sb_o, sg, hg_o, hg, hg_norm_gain[l], xa_o, xg, w_out[l])
        kp_l.append(sk); vp_l.append(sv); hgp_l.append(hg_st_p); mkp_l.append(mk); mvp_l.append(mv)
        sq, sk, sv, sg, hq, hlf, hk, hv, hg, xq, xg = _mixer_inputs(hs, norm_gain[l], w_in[l], lb)
        past_k = cache_k[l][page_table].reshape(dec_b, n_pages * PAGE_SIZE, SB_HEADS, SB_DH)
        past_v = cache_v[l][page_table].reshape(dec_b, n_pages * PAGE_SIZE, SB_HEADS, SB_DH)
        k_all = jnp.concatenate([past_k.astype(F32), sk.astype(F32)], axis=1)
        v_all = jnp.concatenate([past_v.astype(F32), sv.astype(F32)], axis=1)
        sb_o = _sb_attend(sq, k_all, v_all, sb_bias[l], n_pages * PAGE_SIZE)
        hg_o, hg_st_s = _hgrn_chunk(state_hgrn[l].astype(F32), hq, hlf, hk, hv)
        xa_o = _cross_attend(xq, cache_mem_k[l], cache_mem_v[l])
        hs = _mixer_output(hs, sb_o, sg, hg_o, hg, hg_norm_gain[l], xa_o, xg, w_out[l])
        ks_l.append(sk); vs_l.append(sv); hgs_l.append(hg_st_s.astype(state_hgrn.dtype))
    y_prompt = _rmsnorm(hp, final_norm_gain)
    y_sample = _rmsnorm(hs, final_norm_gain)
    k_prompt = jnp.stack(kp_l); v_prompt = jnp.stack(vp_l); hgrn_prompt = jnp.stack(hgp_l)
    mem_k_prompt = jnp.stack(mkp_l); mem_v_prompt = jnp.stack(mvp_l)
    k_sample = jnp.stack(ks_l); v_sample = jnp.stack(vs_l); hgrn_sample = jnp.stack(hgs_l)
    return (y_prompt, y_sample, k_prompt, v_prompt, hgrn_prompt, mem_k_prompt, mem_v_prompt, k_sample, v_sample, hgrn_sample)
```

```python
import contextlib
import numpy as np
import ml_dtypes
import concourse.bass as bass
import concourse.mybir as mybir
from concourse.bass_utils import run_bass_kernel_spmd

F32 = mybir.dt.float32
BF16 = mybir.dt.bfloat16
I32 = mybir.dt.int32
AF = mybir.ActivationFunctionType
ALU = mybir.AluOpType
AX = mybir.AxisListType

D = 1024
S = 4096
NT = S // 512
NB = S // 128
DIN = 3584
EPS = 1e-6
NSAMP = 4
NPAGE = 64
N_DMA_SEM = 12
RING = 3 * N_DMA_SEM

ENABLE_SAMPLE = True
N_TILES = NT
STAGE = 99
NCORES = 8
POOL_PAGES = 2560
NO_SCRATCH_READ = False
DBG_PRE = False
DBG_KBS = None


class Tok:
    __slots__ = ("w", "r", "ps", "old")

    def __init__(self, ps=False, old=None):
        self.w = None
        self.r = []
        self.ps = ps
        self.old = old


class Prog:
    def __init__(self, nc, es):
        self.nc = nc
        self.es = es
        self.ops = []
        self.eng = {"pe": nc.tensor, "act": nc.scalar, "dve": nc.vector, "pool": nc.gpsimd, "sp": nc.sync}

    def op(self, eng, fn, reads=(), writes=(), dma=False):
        idx = len(self.ops)
        deps = set()
        for t in reads:
            if t.w is not None:
                deps.add(t.w)
            if t.ps:
                deps.update(r for r in t.r if self.ops[r][0] != eng)
        for t in writes:
            if t.w is not None:
                deps.add(t.w)
            deps.update(t.r)
            if t.old:
                for o in t.old:
                    if o.w is not None:
                        deps.add(o.w)
                    deps.update(o.r)
                t.old = None
        deps.discard(idx)
        self.ops.append([eng, fn, deps, dma])
        for t in reads:
            t.r.append(idx)
        for t in writes:
            t.w = idx
            t.r = []
        return idx

    def dma(self, out, in_, reads=(), writes=(), q="sp", **kw):
        e = self.eng[q]
        return self.op(q, lambda: e.dma_start(out=out, in_=in_, **kw), reads, writes, dma=True)

    def gather(self, out, in_, idx_ap, reads=(), writes=()):
        g = self.nc.gpsimd
        return self.op("pool", lambda: g.indirect_dma_start(
            out=out, out_offset=None, in_=in_, in_offset=bass.IndirectOffsetOnAxis(ap=idx_ap, axis=0)),
            reads, writes, dma=True)

    def act(self, out, in_, func, reads=(), writes=(), **kw):
        a = self.nc.scalar
        return self.op("act", lambda: a.activation(out=out, in_=in_, func=func, **kw), reads, writes)

    def tt(self, out, in0, in1, op, reads=(), writes=(), eng="dve"):
        e = self.eng[eng]
        return self.op(eng, lambda: e.tensor_tensor(out=out, in0=in0, in1=in1, op=op), reads, writes)

    def ts(self, out, in0, s1, s2, op0, op1=None, reads=(), writes=(), eng="dve"):
        e = self.eng[eng]
        if op1 is None:
            return self.op(eng, lambda: e.tensor_scalar(out=out, in0=in0, scalar1=s1, scalar2=None, op0=op0),
                           reads, writes)
        return self.op(eng, lambda: e.tensor_scalar(out=out, in0=in0, scalar1=s1, scalar2=s2, op0=op0, op1=op1),
                       reads, writes)

    def stt(self, out, in0, scalar, in1, op0, op1, reads=(), writes=()):
        e = self.nc.vector
        return self.op("dve", lambda: e.scalar_tensor_tensor(out=out, in0=in0, scalar=scalar, in1=in1,
                                                             op0=op0, op1=op1), reads, writes)

    def copy(self, out, in_, reads=(), writes=(), eng="dve"):
        e = self.eng[eng]
        if eng == "act":
            return self.op(eng, lambda: e.activation(out=out, in_=in_, func=AF.Copy), reads, writes)
        return self.op(eng, lambda: e.tensor_copy(out=out, in_=in_), reads, writes)

    def recip(self, out, in_, reads=(), writes=()):
        e = self.nc.vector
        return self.op("dve", lambda: e.reciprocal(out=out, in_=in_), reads, writes)

    def reduce(self, out, in_, op, reads=(), writes=()):
        e = self.nc.vector
        return self.op("dve", lambda: e.tensor_reduce(out=out, in_=in_, axis=AX.X, op=op), reads, writes)

    def scan(self, out, d0, d1, reads=(), writes=()):
        e = self.nc.vector
        return self.op("dve", lambda: e.tensor_tensor_scan(out=out, data0=d0, data1=d1, initial=0.0,
                                                           op0=ALU.mult, op1=ALU.add), reads, writes)

    def memset(self, ap, val, writes=(), eng="pool"):
        e = self.eng[eng]
        return self.op(eng, lambda: e.memset(ap, val), (), writes)

    def mm(self, out, lhsT, rhs, start=True, stop=True, reads=(), writes=(), tp=None):
        t = self.nc.tensor
        if tp is None:
            return self.op("pe", lambda: t.matmul(out, lhsT=lhsT, rhs=rhs, start=start, stop=stop), reads, writes)
        return self.op("pe", lambda: t.matmul(out, lhsT=lhsT, rhs=rhs, start=start, stop=stop, tile_position=tp),
                       reads, writes)

    def tr(self, out, in_, ident, reads=(), writes=()):
        t = self.nc.tensor
        return self.op("pe", lambda: t.transpose(out=out, in_=in_, identity=ident), reads, writes)

    def emit(self):
        nc, es = self.nc, self.es
        ops = self.ops
        n = len(ops)
        comp = ("pe", "act", "dve", "pool")
        pos = [0] * n
        cnt = {e: 0 for e in comp}
        qbase = {"sp": 0, "pool": N_DMA_SEM, "act": 2 * N_DMA_SEM}
        dq = {q: [] for q in qbase}
        for i, (eng, fn, deps, dma) in enumerate(ops):
            if dma:
                k = len(dq[eng])
                pos[i] = (k // N_DMA_SEM) * RING + qbase[eng] + (k % N_DMA_SEM)
                dq[eng].append(i)
            else:
                pos[i] = cnt[eng]
                cnt[eng] += 1
        dma_ops = [i for i in range(n) if ops[i][3]]
        for q, lst in dq.items():
            for k, i in enumerate(lst):
                if k >= N_DMA_SEM:
                    ops[i][2].add(lst[k - N_DMA_SEM])
        waited = {}
        marked = [False] * n
        waits = [None] * n
        for i, (eng, fn, deps, dma) in enumerate(ops):
            wl = []
            for d in sorted(deps):
                deng, _, _, ddma = ops[d]
                if ddma:
                    key = (eng, "dma", pos[d] % RING)
                    val = pos[d] // RING
                else:
                    if deng == eng:
                        if eng == "pe":
                            continue
                        if eng != "pool" and pos[i] - pos[d] > 2:
                            continue
                    key = (eng, deng)
                    val = pos[d]
                if waited.get(key, -1) >= val:
                    continue
                waited[key] = val
                wl.append(d)
                marked[d] = True
            waits[i] = wl
        sem = {e: es.enter_context(nc.semaphore("s_" + e)) for e in comp}
        dsem = [es.enter_context(nc.semaphore("s_dma%d" % k)) for k in range(RING)]
        val = [0] * n
        c2 = {e: 0 for e in comp}
        for i, (eng, fn, deps, dma) in enumerate(ops):
            if dma:
                val[i] = 16 * (pos[i] // RING + 1)
            elif marked[i]:
                c2[eng] += 1
                val[i] = c2[eng]
        for i, (eng, fn, deps, dma) in enumerate(ops):
            e = self.eng[eng]
            for d in waits[i]:
                deng, _, _, ddma = ops[d]
                if ddma:
                    e.wait_ge(dsem[pos[d] % RING], val[d])
                else:
                    e.wait_ge(sem[deng], val[d])
            ins = fn()
            if dma:
                ins.then_inc(dsem[pos[i] % RING], 16)
            elif marked[i]:
                ins.then_inc(sem[eng], 1)
        last = {}
        for i in dma_ops:
            last[pos[i] % RING] = val[i]
        for k, v in last.items():
            nc.sync.wait_ge(dsem[k], v)
        for e in comp:
            if c2[e] > 0:
                nc.sync.wait_ge(sem[e], c2[e])


def build_program():
    nc = bass.Bass("TRN2", target_bir_lowering=False)
    es = contextlib.ExitStack()
    P = Prog(nc, es)

    def din(name, shape, dt=F32):
        return nc.dram_tensor(name, list(shape), dt, kind="ExternalInput").ap()

    def dout(name, shape, dt=F32):
        return nc.dram_tensor(name, list(shape), dt, kind="ExternalOutput").ap()

    x_d = din("x", [S // 2, D])
    xo_d = din("xo", [S // 2, D])
    kb_d = din("kbias", [128, 1])
    mem_d = din("mem", [256, D])
    w_in_d = din("w_in", [D, DIN])
    w_out_d = din("w_out", [D, D])
    w_mem_d = din("w_mem", [D, 512])
    gin_d = din("gin", [128, 8])
    gmem_d = din("gmem", [128, 8])
    fg_d = din("fg", [1, D])
    hgn_d = din("hgn", [128, 2])
    sbb_d = din("sbb", [1, 8])
    lbl_d = din("lbl", [1, 512])
    cf_d = din("cf", [128, 514])
    cb_d = din("cb", [128, 384 + 2048], BF16)
    y_d = dout("y", [S // 2, D])
    k_d = dout("k", [S // 2, 512])
    v_d = dout("v", [S // 2, 512])
    hgs_d = dout("hgs", [128, 128])
    mk_d = dout("mk", [256, 256])
    mv_d = dout("mv", [256, 256])
    wsc_d = nc.dram_tensor("wsc", [9, 128, 4096], BF16, kind="Internal").ap()
    if ENABLE_SAMPLE:
        xs_d = din("xs", [NSAMP, D])
        pt_d = din("pt", [1, NSAMP * NPAGE], I32)
        ckv_d = din("ckv", [POOL_PAGES * 128, 1024])
        sst_d = din("sst", [NSAMP, 128, 128])
        cmk_d = din("cmk", [NSAMP, 256, 256])
        cmv_d = din("cmv", [NSAMP, 256, 256])
        ys_d = dout("ys", [NSAMP, D])
        ks_d = dout("ks", [NSAMP, 512])
        vs_d = dout("vs", [NSAMP, 512])
        hss_d = dout("hss", [NSAMP, 128, 128])
        hgr_d = din("hgr", [1, 256])
        scr_d = nc.dram_tensor("scr", [NSAMP, DIN], F32, kind="Internal").ap()

    cnt = [0]

    def sb(shape, dt=F32, name=None):
        cnt[0] += 1
        return es.enter_context(nc.sbuf_tensor("sb_" + (name or ("t%d" % cnt[0])), list(shape), dt))

    def psum(shape, dt=F32):
        cnt[0] += 1
        return es.enter_context(nc.psum_tensor("p%d" % cnt[0], list(shape), dt))

    def T():
        return Tok()

    KT = sb([128, 4, S], BF16, "KT")
    kt_t = [[T() for _ in range(NB)] for _ in range(4)]
    VA = sb([128, NB, 512], BF16, "VA")
    va_t = [T() for _ in range(NB)]
    MKT = sb([128, 2, 256], BF16, "MKT")
    MV = sb([128, 2, 256], BF16, "MV")
    mk_t = T()
    cf = sb([128, 514], F32, "cf")
    cb = sb([128, 384 + 2048], BF16, "cb")
    c_t = T()
    ident_f, TRI, SU, OB = cf[:, 0:128], cf[:, 128:256], cf[:, 256:384], cf[:, 384:512]
    IOTA, ONEC = cf[:, 512:513], cf[:, 513:514]
    ident_b, UIN, LST = cb[:, 0:128], cb[:, 128:256], cb[:, 256:384]
    MSK = [cb[:, 384 + 512 * d: 384 + 512 * (d + 1)] for d in range(4)]
    ONESB = sb([128, 128], BF16, "onesb")
    gin = sb([128, 8], F32, "gin")
    gmem = sb([128, 8], F32, "gmem")
    fgb = sb([128, D], F32, "fgb")
    hgn = sb([128, 2], F32, "hgn")
    sbb = sb([128, 8], F32, "sbb")
    sbbp = sb([128, 8], F32, "sbbp")
    kbias = sb([128, 1], F32, "kbias")
    lbl = sb([128, 512], F32, "lbl")
    lb = sb([128, 256], F32, "lb")
    oml = sb([128, 256], F32, "oml")
    Sst = sb([128, 128], F32, "Sst")
    s_t = T()

    PF = [psum([128, 512], F32) for _ in range(6)]
    pf_t = [Tok(ps=True) for _ in range(6)]
    PB = [psum([128, 1024], BF16) for _ in range(2)]
    pb_t = [Tok(ps=True) for _ in range(2)]

    P.dma(cf[:], cf_d, writes=[c_t])
    P.dma(cb[:], cb_d, writes=[c_t])
    P.dma(gin[:], gin_d, writes=[c_t])
    P.dma(gmem[:], gmem_d, writes=[c_t])
    P.dma(fgb[:], fg_d.partition_broadcast(128), writes=[c_t])
    P.dma(hgn[:], hgn_d, writes=[c_t])
    P.dma(sbb[:], sbb_d.partition_broadcast(128), writes=[c_t])
    P.dma(lbl[:], lbl_d.partition_broadcast(128), writes=[c_t])
    P.dma(kbias[:], kb_d, writes=[c_t])
    P.ts(sbbp[:], sbb[:], kbias[:, 0:1], None, ALU.add, reads=[c_t], writes=[c_t])
    P.memset(ONESB[:], 1.0, writes=[c_t], eng="dve")
    P.memset(Sst[:], 0.0, writes=[s_t], eng="dve")
    P.tt(lb[:], lbl[:, 256:512], lbl[:, 0:256], ALU.subtract, reads=[c_t], writes=[c_t])
    P.act(lb[:], lb[:], AF.Exp, reads=[c_t], writes=[c_t])
    P.ts(lb[:], lb[:], 1.0, None, ALU.add, reads=[c_t], writes=[c_t])
    P.recip(lb[:], lb[:], reads=[c_t], writes=[c_t])
    P.ts(oml[:], lb[:], -1.0, 1.0, ALU.mult, ALU.add, reads=[c_t], writes=[c_t])

    if STAGE == 0:
        P.emit()
        return nc
    xst = [sb([128, D], F32) for _ in range(2)]
    xst_t = [T() for _ in range(2)]
    xsb = [sb([128, D], BF16) for _ in range(2)]
    xsb_t = [T() for _ in range(2)]
    st4 = [sb([128, 4], F32) for _ in range(2)]
    st4_t = [T() for _ in range(2)]
    wst = [sb([128, 512], F32) for _ in range(2)]
    wst_t = [T() for _ in range(2)]
    WG = [sb([128, 8, 512], BF16) for _ in range(2)]
    wg_t = [T() for _ in range(2)]
    wsc_t = [T() for _ in range(9)]
    rr = {"x": 0, "w": 0, "wg": 0, "pf": 0, "pb": 0, "npf": 6}

    def rmsnorm_to_bf(src_ap, reads, which):
        st = st4[which]
        stt_ = st4_t[which]
        P.act(xsb[which][:], src_ap, AF.Square, reads=reads, writes=[xsb_t[which], stt_], accum_out=st[:, 0:1])
        P.ts(st[:, 1:2], st[:, 0:1], 1.0 / D, EPS, ALU.mult, ALU.add, reads=[stt_], writes=[stt_])
        P.act(st[:, 2:3], st[:, 1:2], AF.Ln, reads=[stt_], writes=[stt_])
        P.act(st[:, 3:4], st[:, 2:3], AF.Exp, reads=[stt_], writes=[stt_], scale=-0.5)
        P.act(xsb[which][:], src_ap, AF.Copy, reads=list(reads) + [stt_], writes=[xsb_t[which]], scale=st[:, 3:4])

    def transpose_to(dst_ap_fn, which, dst_toks):
        pb = rr["pb"] % 2
        rr["pb"] += 1
        for kb in range(8):
            P.tr(PB[pb][:, kb * 128:(kb + 1) * 128], xsb[which][:, kb * 128:(kb + 1) * 128], ident_b,
                 reads=[xsb_t[which], c_t], writes=[pb_t[pb]])
        P.copy(dst_ap_fn(), PB[pb][:, :].rearrange("p (k t) -> p k t", k=8), reads=[pb_t[pb]], writes=dst_toks)

    for g in range(9):
        wgi = rr["wg"] % 2
        rr["wg"] += 1
        for kb in range(8):
            wi = rr["w"] % 2
            rr["w"] += 1
            if g < 7:
                src = w_in_d[kb * 128:(kb + 1) * 128, g * 512:(g + 1) * 512]
            else:
                src = w_out_d[kb * 128:(kb + 1) * 128, (g - 7) * 512:(g - 6) * 512]
            P.dma(wst[wi][:], src, writes=[wst_t[wi]])
            if g < 7:
                if kb % 2 == 0:
                    P.ts(WG[wgi][:, kb, :], wst[wi][:], gin[:, kb:kb + 1], None, ALU.mult,
                         reads=[wst_t[wi], c_t], writes=[wg_t[wgi]])
                else:
                    P.act(WG[wgi][:, kb, :], wst[wi][:], AF.Copy, reads=[wst_t[wi], c_t], writes=[wg_t[wgi]],
                          scale=gin[:, kb:kb + 1])
            else:
                P.copy(WG[wgi][:, kb, :], wst[wi][:], reads=[wst_t[wi]], writes=[wg_t[wgi]],
                       eng="dve" if kb % 2 == 0 else "pool")
        P.dma(wsc_d[g], WG[wgi][:, :, :].rearrange("p k c -> p (k c)"), reads=[wg_t[wgi]], writes=[wsc_t[g]])

    if STAGE == 1:
        P.emit()
        return nc

    def load_wgroup(g):
        wgi = rr["wg"] % 2
        rr["wg"] += 1
        if not NO_SCRATCH_READ:
            P.dma(WG[wgi][:, :, :].rearrange("p k c -> p (k c)"), wsc_d[g], reads=[wsc_t[g]], writes=[wg_t[wgi]])
        return wgi

    xnT = sb([128, 8, 512], BF16, "xnT")
    xnT_t = [T() for _ in range(4)]
    memT = [xnT[:, :, mb_ * 128:(mb_ + 1) * 128] for mb_ in range(2)]
    memT_t = [xnT_t[0], xnT_t[1]]
    for mb in range(2):
        xi = rr["x"] % 2
        rr["x"] += 1
        P.dma(xst[xi][:], mem_d[mb * 128:(mb + 1) * 128, :], writes=[xst_t[xi]])
        rmsnorm_to_bf(xst[xi][:], [xst_t[xi]], xi)
        transpose_to(lambda mb=mb: memT[mb], xi, [memT_t[mb]])
    wmb = [sb([128, 512], BF16) for _ in range(2)]
    wmb_t = [T() for _ in range(2)]
    for kb in range(8):
        wi = rr["w"] % 2
        rr["w"] += 1
        P.dma(wst[wi][:], w_mem_d[kb * 128:(kb + 1) * 128, :], writes=[wst_t[wi]])
        P.ts(wmb[kb % 2][:], wst[wi][:], gmem[:, kb:kb + 1], None, ALU.mult, reads=[wst_t[wi], c_t],
             writes=[wmb_t[kb % 2]])
        for mb in range(2):
            P.mm(PF[mb][:, :], xnT[:, kb, mb * 128:(mb + 1) * 128], wmb[kb % 2][:], start=(kb == 0), stop=(kb == 7),
                 reads=[memT_t[mb], wmb_t[kb % 2]], writes=[pf_t[mb]])
    kvst = [sb([128, 512], F32) for _ in range(2)]
    kvst_t = [T() for _ in range(2)]
    kbf = [sb([128, 512], BF16) for _ in range(2)]
    kbf_t = [T() for _ in range(2)]
    for mb in range(2):
        P.copy(kvst[mb][:], PF[mb][:, :], reads=[pf_t[mb]], writes=[kvst_t[mb]], eng="act" if mb else "dve")
        P.dma(mk_d[mb * 128:(mb + 1) * 128, :], kvst[mb][:, 0:256], reads=[kvst_t[mb]])
        P.dma(mv_d[mb * 128:(mb + 1) * 128, :], kvst[mb][:, 256:512], reads=[kvst_t[mb]])
        P.copy(kbf[mb][:, 0:256], kvst[mb][:, 0:256], reads=[kvst_t[mb]], writes=[kbf_t[mb]])
        P.copy(MV[:, mb, :], kvst[mb][:, 256:512], reads=[kvst_t[mb]], writes=[mk_t])
        pb = rr["pb"] % 2
        rr["pb"] += 1
        for hp in range(2):
            P.tr(PB[pb][:, hp * 128:(hp + 1) * 128], kbf[mb][:, hp * 128:(hp + 1) * 128], ident_b,
                 reads=[kbf_t[mb], c_t], writes=[pb_t[pb]])
        P.copy(MKT[:, :, mb * 128:(mb + 1) * 128], PB[pb][:, 0:256].rearrange("p (k t) -> p k t", k=2),
               reads=[pb_t[pb]], writes=[mk_t])

    if STAGE == 2:
        P.emit()
        return nc
    QT = sb([128, 4, 512], BF16, "QT")
    qt_t = [T() for _ in range(4)]
    SG = sb([128, 8, 512], BF16, "SG")
    sg_t = [T() for _ in range(8)]
    HQT = sb([128, 2, 512], F32, "HQT")
    hq_t = [T() for _ in range(2)]
    XQT = sb([128, 2, 512], BF16, "XQT")
    xq_t = [T() for _ in range(2)]
    HF = sb([128, 4, 256], F32, "HF")
    HI = sb([128, 4, 256], F32, "HI")
    hf_t = [T() for _ in range(4)]
    hi_t = [T() for _ in range(4)]
    MIX = sb([128, 8, 512], BF16, "MIX")
    mix_t = [T() for _ in range(8)]
    gtmp = [sb([128, 512], F32) for _ in range(2)]
    gtmp_t = [T() for _ in range(2)]
    EB = [sb([128, 512], F32) for _ in range(4)]
    eb_t = [T() for _ in range(4)]
    SPB = wmb + [sb([128, 512], BF16) for _ in range(2)]
    spb_t = wmb_t + [T() for _ in range(2)]
    WB = [sb([128, 512], BF16) for _ in range(2)]
    wb_t = [T() for _ in range(2)]
    AB = [sb([128, 512], BF16) for _ in range(4)]
    ab_t = [T() for _ in range(4)]
    KTf = KT[:, 0, :].bitcast(F32)
    SPACC = [KTf[:, 0:512], KTf[:, 512:1024]]
    spacc_t = [Tok(old=[kt_t[0][kb_] for kb_ in range(NB)]) for _ in range(2)]
    hw = {n_: sb([128, 256], F32, "hw_" + n_) for n_ in ("a", "f", "lf", "k", "kd")}
    hw_t = {n_: T() for n_ in hw}
    hx = {n_: sb([128, 2, 128], F32, "hx_" + n_) for n_ in ("ep", "en", "qe", "ke")}
    hx_t = {n_: T() for n_ in hx}
    hattn = [sb([128, 128], F32) for _ in range(2)]
    hattn_t = [T() for _ in range(2)]
    hdec = sb([128, 4], F32, "hdec")
    hdec_t = T()
    hosb = sb([128, 256], F32, "hosb")
    hosb_t = T()
    hsq = sb([128, 128], F32, "hsq")
    hsq_t = T()
    hrs = sb([128, 128], F32, "hrs")
    hrs_t = T()
    yst = sb([128, D], F32, "yst")
    yst_t = T()

    def pf_next():
        i = rr["pf"] % rr["npf"]
        rr["pf"] += 1
        return i

    def silu_from_psum(pi, dst_ap, dst_tok):
        gi = rr["x"] % 2
        rr["x"] += 1
        P.act(gtmp[gi][:], PF[pi][:, :], AF.Exp, reads=[pf_t[pi]], writes=[gtmp_t[gi]], scale=-1.0)
        P.ts(gtmp[gi][:], gtmp[gi][:], 1.0, None, ALU.add, reads=[gtmp_t[gi]], writes=[gtmp_t[gi]])
        P.recip(gtmp[gi][:], gtmp[gi][:], reads=[gtmp_t[gi]], writes=[gtmp_t[gi]])
        P.tt(dst_ap, PF[pi][:, :], gtmp[gi][:], ALU.mult, reads=[pf_t[pi], gtmp_t[gi]], writes=[dst_tok])

    def feat_proj(wgi, cbs, evac):
        for cbi in cbs:
            pi = pf_next()
            for kb in range(8):
                P.mm(PF[pi][:, :], WG[wgi][:, kb, cbi * 128:(cbi + 1) * 128], xnT[:, kb, :], start=(kb == 0),
                     stop=(kb == 7), reads=[wg_t[wgi]] + xnT_t, writes=[pf_t[pi]])
            evac(cbi, pi)

    def tok_proj(wgi, c0, c1, blk):
        pi = pf_next()
        for kb in range(8):
            P.mm(PF[pi][:, 0:c1 - c0], xnT[:, kb, blk * 128:(blk + 1) * 128], WG[wgi][:, kb, c0:c1],
                 start=(kb == 0), stop=(kb == 7), reads=[wg_t[wgi], xnT_t[blk]], writes=[pf_t[pi]])
        return pi

    def hgrn_block(blk, state_only=False):
        c0 = blk * 128
        tmpb = [0, 1, 2] if state_only else [0, 1, 2, 5]

        def tmp_next():
            i = tmpb[rr["pf"] % len(tmpb)]
            rr["pf"] += 1
            return i
        a, f, lf, k, kd = hw["a"], hw["f"], hw["lf"], hw["k"], hw["kd"]
        P.act(a[:], HF[:, blk, :], AF.Exp, reads=[hf_t[blk]], writes=[hw_t["a"]], scale=-1.0)
        P.ts(a[:], a[:], 1.0, None, ALU.add, reads=[hw_t["a"]], writes=[hw_t["a"]])
        P.recip(a[:], a[:], reads=[hw_t["a"]], writes=[hw_t["a"]])
        P.tt(f[:], a[:], oml[:], ALU.mult, reads=[hw_t["a"], c_t], writes=[hw_t["f"]])
        P.tt(f[:], f[:], lb[:], ALU.add, reads=[hw_t["f"], c_t], writes=[hw_t["f"]])
        P.act(lf[:], f[:], AF.Ln, reads=[hw_t["f"]], writes=[hw_t["lf"]])
        P.ts(k[:], f[:], -1.0, 1.0, ALU.mult, ALU.add, reads=[hw_t["f"]], writes=[hw_t["k"]])
        if STAGE == 3.1:
            return
        p_rev = tmp_next()
        P.mm(PF[p_rev][:, 0:256], SU, lf[:], reads=[c_t, hw_t["lf"]], writes=[pf_t[p_rev]])
        P.act(kd[:], PF[p_rev][:, 0:256], AF.Exp, reads=[pf_t[p_rev]], writes=[hw_t["kd"]])
        P.tt(kd[:], kd[:], k[:], ALU.mult, reads=[hw_t["kd"], hw_t["k"]], writes=[hw_t["kd"]])
        if STAGE == 3.2:
            return
        p_bc = tmp_next()
        for hp in range(2):
            P.mm(PF[p_bc][:, hp * 128:(hp + 1) * 128], lf[:, hp * 128:(hp + 1) * 128], TRI,
                 reads=[hw_t["lf"], c_t], writes=[pf_t[p_bc]])
        P.act(hx["ep"][:, :, :], PF[p_bc][:, 0:256].rearrange("p (k t) -> p k t", k=2), AF.Exp,
              reads=[pf_t[p_bc]], writes=[hx_t["ep"]])
        if not state_only:
            P.act(hx["en"][:, :, :], PF[p_bc][:, 0:256].rearrange("p (k t) -> p k t", k=2), AF.Exp,
                  reads=[pf_t[p_bc]], writes=[hx_t["en"]], scale=-1.0)
            P.tt(hx["qe"][:, :, :], hx["ep"][:, :, :], HQT[:, :, c0:c0 + 128], ALU.mult,
                 reads=[hx_t["ep"]] + hq_t, writes=[hx_t["qe"]])
            if STAGE == 3.3:
                return
            p_kt = tmp_next()
            for hp in range(2):
                P.tr(PF[p_kt][:, hp * 128:(hp + 1) * 128], k[:, hp * 128:(hp + 1) * 128], ident_f,
                     reads=[hw_t["k"], c_t], writes=[pf_t[p_kt]])
            P.tt(hx["ke"][:, :, :], PF[p_kt][:, 0:256].rearrange("p (k t) -> p k t", k=2), hx["en"][:, :, :], ALU.mult,
                 reads=[pf_t[p_kt], hx_t["en"]], writes=[hx_t["ke"]])
        for c in range(2):
            for hp in range(2):
                P.copy(hdec[:, c * 2 + hp: c * 2 + hp + 1], hx["ep"][:, hp, c * 64 + 63: c * 64 + 64],
                       reads=[hx_t["ep"]], writes=[hdec_t])
        if STAGE == 3.4:
            return
        if not state_only:
            p_o = 3
            for h in range(4):
                hp, rb = h // 2, (h % 2) * 64
                p_at = tmp_next()
                P.mm(PF[p_at][:, 0:128], hx["ke"][rb:rb + 64, hp, :], hx["qe"][rb:rb + 64, hp, :],
                     reads=[hx_t["ke"], hx_t["qe"]], writes=[pf_t[p_at]], tp=(rb, 0))
                ai = h % 2
                P.tt(hattn[ai][:], PF[p_at][:, 0:128], TRI, ALU.mult, reads=[pf_t[p_at], c_t], writes=[hattn_t[ai]])
                P.mm(PF[p_o][rb:rb + 64, hp * 128:(hp + 1) * 128], HI[:, blk, h * 64:(h + 1) * 64], hattn[ai][:],
                     start=True, stop=True, reads=[hi_t[blk], hattn_t[ai]], writes=[pf_t[p_o]], tp=(0, rb))
            if STAGE == 3.5:
                return
        p_i = 4
        for c in range(2):
            for h in range(4):
                if STAGE == 3.55 or state_only:
                    break
                hp, rb = h // 2, (h % 2) * 64
                P.mm(PF[p_i][rb:rb + 64, hp * 128 + c * 64: hp * 128 + c * 64 + 64],
                     Sst[rb:rb + 64, hp * 64:(hp + 1) * 64], hx["qe"][rb:rb + 64, hp, c * 64:(c + 1) * 64],
                     start=True, stop=True, reads=[s_t, hx_t["qe"]], writes=[pf_t[p_i]], tp=(rb, rb))
            if STAGE == 3.57:
                continue
            p_s = tmp_next()
            for h in range(4):
                hp, rb = h // 2, (h % 2) * 64
                P.mm(PF[p_s][rb:rb + 64, hp * 64:(hp + 1) * 64], kd[c * 64:(c + 1) * 64, h * 64:(h + 1) * 64],
                     HI[c * 64:(c + 1) * 64, blk, h * 64:(h + 1) * 64], reads=[hw_t["kd"], hi_t[blk]],
                     writes=[pf_t[p_s]], tp=(c * 64, rb))
            for hp in range(2):
                P.stt(Sst[:, hp * 64:(hp + 1) * 64], Sst[:, hp * 64:(hp + 1) * 64],
                      hdec[:, c * 2 + hp: c * 2 + hp + 1], PF[p_s][:, hp * 64:(hp + 1) * 64], ALU.mult, ALU.add,
                      reads=[s_t, hdec_t, pf_t[p_s]], writes=[s_t])
        if STAGE in (3.55, 3.57, 3.6) or state_only:
            return
        P.copy(hosb[:], PF[p_i][:, 0:256], reads=[pf_t[p_i]], writes=[hosb_t], eng="act")
        P.tt(hosb[:], hosb[:], PF[p_o][:, 0:256], ALU.add, reads=[hosb_t, pf_t[p_o]], writes=[hosb_t])
        for hp in range(2):
            P.act(hsq[:], hosb[:, hp * 128:(hp + 1) * 128], AF.Square, reads=[hosb_t], writes=[hsq_t])
            p_n = tmp_next()
            P.mm(PF[p_n][:, 0:128], OB, hsq[:], reads=[c_t, hsq_t], writes=[pf_t[p_n]])
            P.ts(hrs[:], PF[p_n][:, 0:128], 1.0 / 64, EPS, ALU.mult, ALU.add, reads=[pf_t[p_n]], writes=[hrs_t])
            P.act(hrs[:], hrs[:], AF.Ln, reads=[hrs_t], writes=[hrs_t])
            P.act(hrs[:], hrs[:], AF.Exp, reads=[hrs_t], writes=[hrs_t], scale=-0.5)
            P.tt(hrs[:], hosb[:, hp * 128:(hp + 1) * 128], hrs[:], ALU.mult, reads=[hosb_t, hrs_t],
                 writes=[hrs_t])
            P.stt(MIX[:, 4 + hp, c0:c0 + 128], hrs[:], hgn[:, hp:hp + 1], SG[:, 4 + hp, c0:c0 + 128], ALU.mult,
                  ALU.mult, reads=[hrs_t, c_t, sg_t[4 + hp]], writes=[mix_t[4 + hp]])

    def xattn():
        for hp in range(2):
            p_o, p_d = pf_next(), pf_next()
            for hh in range(2):
                h, rb = hp * 2 + hh, hh * 64
                for mb in range(2):
                    p_s = pf_next()
                    P.mm(PF[p_s][:, :], MKT[rb:rb + 64, hp, mb * 128:(mb + 1) * 128], XQT[rb:rb + 64, hp, :],
                         reads=[mk_t, xq_t[hp]], writes=[pf_t[p_s]], tp=(rb, 0))
                    ai = rr["x"] % 2
                    rr["x"] += 1
                    P.act(AB[ai][:], PF[p_s][:, :], AF.Exp, reads=[pf_t[p_s]], writes=[ab_t[ai]])
                    P.mm(PF[p_o][rb:rb + 64, :], MV[:, mb, h * 64:(h + 1) * 64], AB[ai][:], start=(mb == 0),
                         stop=(mb == 1), reads=[mk_t, ab_t[ai]], writes=[pf_t[p_o]], tp=(0, rb))
                    P.mm(PF[p_d][rb:rb + 64, :], ONESB[:, 0:64], AB[ai][:], start=(mb == 0), stop=(mb == 1),
                         reads=[c_t, ab_t[ai]], writes=[pf_t[p_d]], tp=(0, rb))
            gi = rr["x"] % 2
            rr["x"] += 1
            P.recip(gtmp[gi][:], PF[p_d][:, :], reads=[pf_t[p_d]], writes=[gtmp_t[gi]])
            P.tt(gtmp[gi][:], PF[p_o][:, :], gtmp[gi][:], ALU.mult, reads=[pf_t[p_o], gtmp_t[gi]],
                 writes=[gtmp_t[gi]])
            P.tt(MIX[:, 6 + hp, :], gtmp[gi][:], SG[:, 6 + hp, :], ALU.mult, reads=[gtmp_t[gi], sg_t[6 + hp]],
                 writes=[mix_t[6 + hp]])

    def sb_attention(qi):
        nkb = 4 * qi + 4
        for hp in range(4):
            av = 4 + (hp % 2)
            kbs = list(range(nkb - 1, -1, -1))
            n = len(kbs)

            def stage_a1(i):
                kb = kbs[i]
                r = i % 2
                for hh in range(2):
                    rb = hh * 64
                    P.mm(PF[hh][:, :], KT[rb:rb + 64, hp, kb * 128:(kb + 1) * 128], QT[rb:rb + 64, hp, :],
                         reads=[kt_t[hp][kb], qt_t[hp]], writes=[pf_t[hh]], tp=(rb, 0))
                z = kb - 4 * qi
                bsrc = sbbp if kb < 4 * HALF else sbb
                for hh in range(2):
                    h = hp * 2 + hh
                    e = hh * 2 + r
                    P.act(EB[e][:], PF[hh][:, :], AF.Exp, reads=[pf_t[hh], c_t], writes=[eb_t[e]],
                          bias=bsrc[:, h:h + 1])
                    if z >= 0:
                        P.tt(EB[e][:], EB[e][:], MSK[z], ALU.mult, reads=[eb_t[e], c_t], writes=[eb_t[e]])

            def stage_a2(i):
                r = i % 2
                for hh in range(2):
                    e = hh * 2 + r
                    P.act(SPB[e][:], EB[e][:], AF.Ln, reads=[eb_t[e]], writes=[spb_t[e]], bias=1.0)

            def stage_b(i):
                kb = kbs[i]
                r = i % 2
                for hh in range(2):
                    e = hh * 2 + r
                    P.mm(PF[2 + hh][:, :], UIN, SPB[e][:], start=(kb == nkb - 1), stop=(kb == 0),
                         reads=[c_t, spb_t[e]], writes=[pf_t[2 + hh]])
                for hh in range(2):
                    e = hh * 2 + r
                    P.act(WB[hh][:], PF[2 + hh][:, :], AF.Exp, reads=[pf_t[2 + hh]], writes=[wb_t[hh]], scale=-1.0)
                    P.tt(AB[e][:], EB[e][:], WB[hh][:], ALU.mult, reads=[eb_t[e], wb_t[hh]], writes=[ab_t[e]])

            def stage_b2(i):
                kb = kbs[i]
                r = i % 2
                if kb > 0:
                    for hh in range(2):
                        e = hh * 2 + r
                        P.mm(PF[2 + hh][:, :], LST, SPB[e][:], start=False, stop=False, reads=[c_t, spb_t[e]],
                             writes=[pf_t[2 + hh]])

            def stage_c(i):
                kb = kbs[i]
                r = i % 2
                for hh in range(2):
                    rb = hh * 64
                    e = hh * 2 + r
                    h = hp * 2 + hh
                    P.mm(PF[av][rb:rb + 64, :], VA[:, kb, h * 64:(h + 1) * 64], AB[e][:], start=(kb == nkb - 1),
                         stop=(kb == 0), reads=[va_t[kb], ab_t[e]], writes=[pf_t[av]], tp=(0, rb))

            for i in range(n + 2):
                if i < n:
                    stage_a1(i)
                if 0 <= i - 2 < n:
                    stage_b2(i - 2)
                if 0 <= i - 1 < n:
                    stage_b(i - 1)
                if i < n:
                    stage_a2(i)
                if 0 <= i - 2 < n:
                    stage_c(i - 2)
            P.tt(MIX[:, hp, :], PF[av][:, :], SG[:, hp, :], ALU.mult, reads=[pf_t[av], sg_t[hp]],
                 writes=[mix_t[hp]])

    HALF = NT // 2

    def tile(ti):
        own = ti >= HALF
        src = xo_d if own else x_d
        r0 = (ti - HALF) * 512 if own else ti * 512
        for blk in range(4):
            xi = rr["x"] % 2
            rr["x"] += 1
            P.dma(xst[xi][:], src[r0 + blk * 128: r0 + (blk + 1) * 128, :], writes=[xst_t[xi]])
            rmsnorm_to_bf(xst[xi][:], [xst_t[xi]], xi)
            transpose_to(lambda blk=blk: xnT[:, :, blk * 128:(blk + 1) * 128], xi, [xnT_t[blk]])
        rr["pf"] = 0
        wgi = load_wgroup(1)
        for blk in range(4):
            gb = ti * 4 + blk
            pi = tok_proj(wgi, 0, 512, blk)
            si = gb % 2
            P.copy(kvst[si][:], PF[pi][:, :], reads=[pf_t[pi]], writes=[kvst_t[si]], eng="act")
            if own:
                P.dma(k_d[r0 + blk * 128: r0 + (blk + 1) * 128, :], kvst[si][:], reads=[kvst_t[si]])
            P.copy(kbf[si][:], kvst[si][:], reads=[kvst_t[si]], writes=[kbf_t[si]])
            pb = rr["pb"] % 2
            rr["pb"] += 1
            for hp in range(4):
                P.tr(PB[pb][:, hp * 128:(hp + 1) * 128], kbf[si][:, hp * 128:(hp + 1) * 128], ident_b,
                     reads=[kbf_t[si], c_t], writes=[pb_t[pb]])
            P.copy(KT[:, :, gb * 128:(gb + 1) * 128], PB[pb][:, 0:512].rearrange("p (k t) -> p k t", k=4),
                   reads=[pb_t[pb]], writes=[kt_t[hp_][gb] for hp_ in range(4)])
            if not own:
                pump(4)
        wgi = load_wgroup(2)
        for blk in range(4):
            gb = ti * 4 + blk
            pi = tok_proj(wgi, 0, 512, blk)
            si = gb % 2
            P.copy(kvst[si][:], PF[pi][:, :], reads=[pf_t[pi]], writes=[kvst_t[si]], eng="act")
            if own:
                P.dma(v_d[r0 + blk * 128: r0 + (blk + 1) * 128, :], kvst[si][:], reads=[kvst_t[si]])
            P.copy(VA[:, gb, :], kvst[si][:], reads=[kvst_t[si]], writes=[va_t[gb]])
            if not own:
                pump(4)
        if own:
            wgi = load_wgroup(0)
            feat_proj(wgi, range(4), lambda cbi, pi: P.act(QT[:, cbi, :], PF[pi][:, :], AF.Copy, reads=[pf_t[pi]],
                                                          writes=[qt_t[cbi]], scale=0.125))
            wgi = load_wgroup(3)
            feat_proj(wgi, range(4), lambda cbi, pi: silu_from_psum(pi, SG[:, cbi, :], sg_t[cbi]))
        wgi = load_wgroup(4)
        if own:
            feat_proj(wgi, range(2), lambda cbi, pi: P.copy(HQT[:, cbi, :], PF[pi][:, :], reads=[pf_t[pi]],
                                                            writes=[hq_t[cbi]], eng="act"))
        for blk in range(4):
            pi = tok_proj(wgi, 256, 512, blk)
            P.copy(HF[:, blk, :], PF[pi][:, 0:256], reads=[pf_t[pi]], writes=[hf_t[blk]])
        wgi = load_wgroup(5)
        for blk in range(4):
            pi = tok_proj(wgi, 0, 256, blk)
            P.copy(HI[:, blk, :], PF[pi][:, 0:256], reads=[pf_t[pi]], writes=[hi_t[blk]], eng="act")
        if own:
            feat_proj(wgi, range(2, 4), lambda cbi, pi: silu_from_psum(pi, SG[:, 2 + cbi, :], sg_t[2 + cbi]))
            wgi = load_wgroup(6)
            feat_proj(wgi, range(2), lambda cbi, pi: P.act(XQT[:, cbi, :], PF[pi][:, :], AF.Copy, reads=[pf_t[pi]],
                                                           writes=[xq_t[cbi]], scale=0.125))
            feat_proj(wgi, range(2, 4), lambda cbi, pi: silu_from_psum(pi, SG[:, 4 + cbi, :], sg_t[4 + cbi]))
        for blk in range(4):
            hgrn_block(blk, state_only=not own)
            if not own:
                pump(8)
        if not own:
            return
        xattn()
        sb_attention(ti)
        wg0 = load_wgroup(7)
        wg1 = load_wgroup(8)
        for blk in range(4):
            xi = rr["x"] % 2
            rr["x"] += 1
            P.dma(xst[xi][:], src[r0 + blk * 128: r0 + (blk + 1) * 128, :], writes=[xst_t[xi]])
            for half, wg in ((0, wg0), (1, wg1)):
                pi = pf_next()
                kbs = list(range(8)) if DBG_KBS is None else list(DBG_KBS)
                for kb in kbs:
                    P.mm(PF[pi][:, :], MIX[:, kb, blk * 128:(blk + 1) * 128], WG[wg][:, kb, :], start=(kb == kbs[0]),
                         stop=(kb == kbs[-1]), reads=[mix_t[kb], wg_t[wg]], writes=[pf_t[pi]])
                P.tt(yst[:, half * 512:(half + 1) * 512], PF[pi][:, :], xst[xi][:, half * 512:(half + 1) * 512],
                     ALU.add, reads=[pf_t[pi], xst_t[xi]], writes=[yst_t])
            st, stt_ = st4[xi], st4_t[xi]
            P.act(xsb[xi][:], yst[:], AF.Square, reads=[yst_t], writes=[xsb_t[xi], stt_], accum_out=st[:, 0:1])
            P.ts(st[:, 1:2], st[:, 0:1], 1.0 / D, EPS, ALU.mult, ALU.add, reads=[stt_], writes=[stt_])
            P.act(st[:, 2:3], st[:, 1:2], AF.Ln, reads=[stt_], writes=[stt_])
            P.act(st[:, 3:4], st[:, 2:3], AF.Exp, reads=[stt_], writes=[stt_], scale=-0.5)
            if DBG_PRE:
                P.copy(xst[xi][:], yst[:], reads=[yst_t], writes=[xst_t[xi]])
            else:
                P.stt(xst[xi][:], yst[:], st[:, 3:4], fgb[:], ALU.mult, ALU.mult, reads=[yst_t, stt_, c_t],
                      writes=[xst_t[xi]])
            P.dma(y_d[r0 + blk * 128: r0 + (blk + 1) * 128, :], xst[xi][:], reads=[xst_t[xi]])

    SS = {}

    def sample_front():
        G = [EB[0], EB[1], kvst[0], kvst[1], wst[0], wst[1], gtmp[0]]
        g_t = [eb_t[0], eb_t[1], kvst_t[0], kvst_t[1], wst_t[0], wst_t[1], gtmp_t[0]]
        TMP = [gtmp[1][:, :], HQT[:, 1, :]]
        tmp_t = [gtmp_t[1], Tok(old=hq_t)]
        QB, qb_t = HQT[:, 0, :], Tok(old=hq_t)
        HFv = HF[:, :, :].rearrange("p a b -> p (a b)")
        HIv = HI[:, :, :].rearrange("p a b -> p (a b)")
        HFa, HFb, HIa, HIb = HFv[:, 0:512], HFv[:, 512:1024], HIv[:, 0:512], HIv[:, 512:1024]
        hfa_t, hfb_t, hia_t, hib_t = Tok(old=hf_t), Tok(old=hf_t), Tok(old=hi_t), Tok(old=hi_t)
        ZALL, E_ = SPACC[0], SPACC[1]
        v3 = lambda ap: ap.rearrange("p (g h) -> p g h", h=8)
        otok = sb([128, D], F32, "otok")
        otok_t = T()
        ptb = sb([128, NSAMP * NPAGE], I32, "ptb")
        idxa = sb([128, NSAMP * NPAGE], I32, "idxa")
        idx_t = T()
        hgnb = lbl[:, 0:256]
        FKQ = sb([128, 24], F32, "fkq")
        fkq_t = T()
        SM = sb([128, 32], F32, "sm")
        sm_t = T()
        scr_t = T()

        P.memset(otok[:], 0.0, writes=[otok_t], eng="dve")
        P.dma(hgnb, hgr_d.partition_broadcast(128), writes=[c_t])
        P.dma(ptb[:], pt_d.partition_broadcast(128), writes=[idx_t])
        P.ts(idxa[:], ptb[:], 128.0, IOTA, ALU.mult, ALU.add, reads=[idx_t, c_t], writes=[idx_t])
        P.memset(yst[:], 0.0, writes=[yst_t], eng="dve")
        P.dma(yst[0:NSAMP, :], xs_d, writes=[yst_t])
        rmsnorm_to_bf(yst[:], [yst_t], 0)
        transpose_to(lambda: xnT[:, :, 0:128], 0, [xnT_t[0]])
        for g in range(7):
            wgi = load_wgroup(g)
            pi = tok_proj(wgi, 0, 512, 0)
            P.copy(G[g][:], PF[pi][:, :], reads=[pf_t[pi]], writes=[g_t[g]], eng="act" if g % 2 else "dve")
            P.dma(scr_d[:, g * 512:(g + 1) * 512], G[g][0:NSAMP, :], reads=[g_t[g]], writes=[scr_t])
        P.dma(ks_d, G[1][0:NSAMP, :], reads=[g_t[1]])
        P.dma(vs_d, G[2][0:NSAMP, :], reads=[g_t[2]])
        if STAGE == 10.1:
            return
        a, f, k = hw["a"], hw["f"], hw["k"]
        P.act(a[:], G[4][:, 256:512], AF.Exp, reads=[g_t[4]], writes=[hw_t["a"]], scale=-1.0)
        P.ts(a[:], a[:], 1.0, None, ALU.add, reads=[hw_t["a"]], writes=[hw_t["a"]])
        P.recip(a[:], a[:], reads=[hw_t["a"]], writes=[hw_t["a"]])
        P.tt(f[:], a[:], oml[:], ALU.mult, reads=[hw_t["a"], c_t], writes=[hw_t["f"]])
        P.tt(f[:], f[:], lb[:], ALU.add, reads=[hw_t["f"], c_t], writes=[hw_t["f"]])
        P.ts(k[:], f[:], -1.0, 1.0, ALU.mult, ALU.add, reads=[hw_t["f"]], writes=[hw_t["k"]])
        p_f = pf_next()
        for j, (src, st_) in enumerate(((f, hw_t["f"]), (k, hw_t["k"]), (G[4], g_t[4]))):
            for hp in range(2):
                c = (j * 2 + hp) * 4
                P.tr(PF[p_f][:, c:c + 4], src[0:4, hp * 128:(hp + 1) * 128], ident_f[0:4, 0:4],
                     reads=[st_, c_t], writes=[pf_t[p_f]])
        P.copy(FKQ[:], PF[p_f][:, 0:24], reads=[pf_t[p_f]], writes=[fkq_t])

        SS.update(dict(G=G, g_t=g_t, TMP=TMP, tmp_t=tmp_t, HFa=HFa, HFb=HFb, HIa=HIa, HIb=HIb, hfa_t=hfa_t, hfb_t=hfb_t,
                       hia_t=hia_t, hib_t=hib_t, otok=otok, otok_t=otok_t, idxa=idxa, idx_t=idx_t, hgnb=hgnb, FKQ=FKQ,
                       fkq_t=fkq_t, SM=SM, sm_t=sm_t, scr_t=scr_t))

    def sample_stream():
        otok, otok_t, idxa, idx_t, scr_t = SS["otok"], SS["otok_t"], SS["idxa"], SS["idx_t"], SS["scr_t"]
        PG = [KT[:, hp_, S // 2:S].bitcast(F32) for hp_ in range(4)]
        VAf = VA[:, NB // 2:NB, :].rearrange("p a b -> p (a b)").bitcast(F32)
        PG += [VAf[:, 2048:3072], VAf[:, 3072:4096]]
        pg_t = [T() for _ in range(6)]
        TMPs = [VAf[:, 0:512], VAf[:, 512:1024]]
        tmps_t = [T(), T()]
        QBs, qbs_t = VAf[:, 1024:1536], T()
        RES, res_t = VAf[:, 1536:2048], T()
        for hp_ in range(4):
            for kb_ in range(NB // 2, NB):
                kt_t[hp_][kb_].old = [pg_t[hp_]]
        for kb_ in range(NB // 2, NB):
            va_t[kb_].old = [pg_t[4], pg_t[5], tmps_t[0], tmps_t[1], qbs_t, res_t]
        S8 = sb([128, 4, 48], F32, "s8")
        S8b = sb([128, 4, 8], BF16, "s8b")
        s8_t = [T() for _ in range(4)]
        NPG = NPAGE
        CB, AVB = 5, 4
        npg = [0]
        for n in range(NSAMP):
            P.dma(QBs, scr_d[n:n + 1, 0:512].partition_broadcast(128), reads=[scr_t], writes=[qbs_t])
            pages = list(range(NPG - 1, -1, -1))

            def st1(ii):
                p = pages[ii]
                i = npg[0] + ii
                col = n * NPAGE + p
                b8 = i % 4
                P.gather(PG[i % 6], ckv_d[:, :], idxa[:, col:col + 1], reads=[idx_t], writes=[pg_t[i % 6]])
                P.tt(TMPs[i % 2], PG[i % 6][:, 0:512], QBs, ALU.mult, reads=[pg_t[i % 6], qbs_t], writes=[tmps_t[i % 2]])
                P.reduce(S8[:, b8, 0:8], TMPs[i % 2].rearrange("p (h d) -> p h d", h=8), ALU.add,
                         reads=[tmps_t[i % 2]], writes=[s8_t[b8]])
                P.stt(S8[:, b8, 0:8], S8[:, b8, 0:8], 0.125, sbb[:, 0:8], ALU.mult, ALU.add,
                      reads=[s8_t[b8], c_t], writes=[s8_t[b8]])
                P.act(S8[:, b8, 8:16], S8[:, b8, 0:8], AF.Exp, reads=[s8_t[b8]], writes=[s8_t[b8]])
                P.act(S8b[:, b8, :], S8[:, b8, 8:16], AF.Ln, reads=[s8_t[b8]], writes=[s8_t[b8]], bias=1.0)

            def st2(ii):
                i = npg[0] + ii
                b8 = i % 4
                P.mm(PF[CB][:, 0:8], UIN, S8b[:, b8, :], start=(ii == 0), stop=(ii == NPG - 1),
                     reads=[c_t, s8_t[b8]], writes=[pf_t[CB]])
                P.act(S8[:, b8, 24:32], PF[CB][:, 0:8], AF.Exp, reads=[pf_t[CB]], writes=[s8_t[b8]], scale=-1.0)
                P.tt(S8[:, b8, 16:24], S8[:, b8, 8:16], S8[:, b8, 24:32], ALU.mult, reads=[s8_t[b8]],
                     writes=[s8_t[b8]])
                if ii < NPG - 1:
                    P.mm(PF[CB][:, 0:8], LST, S8b[:, b8, :], start=False, stop=False, reads=[c_t, s8_t[b8]],
                         writes=[pf_t[CB]])
                P.mm(PF[AVB][0:8, :], S8[:, b8, 16:24], PG[i % 6][:, 512:1024], start=(ii == 0), stop=(ii == NPG - 1),
                     reads=[s8_t[b8], pg_t[i % 6]], writes=[pf_t[AVB]])

            for ii in range(NPG + 1):
                if ii < NPG:
                    st1(ii)
                if ii >= 1:
                    st2(ii - 1)
                yield
            npg[0] += NPG
            P.copy(RES[0:8, :], PF[AVB][0:8, :], reads=[pf_t[AVB]], writes=[res_t])
            for h in range(8):
                P.dma(otok[n:n + 1, h * 64:(h + 1) * 64], RES[h:h + 1, h * 64:(h + 1) * 64], reads=[res_t],
                      writes=[otok_t])
            yield

    def sample_back():
        G, g_t, TMP, tmp_t = SS["G"], SS["g_t"], SS["TMP"], SS["tmp_t"]
        HFa, HFb, HIa, HIb = SS["HFa"], SS["HFb"], SS["HIa"], SS["HIb"]
        hfa_t, hfb_t, hia_t, hib_t = SS["hfa_t"], SS["hfb_t"], SS["hia_t"], SS["hib_t"]
        otok, otok_t, hgnb, FKQ, fkq_t, SM, sm_t, scr_t = (SS["otok"], SS["otok_t"], SS["hgnb"], SS["FKQ"], SS["fkq_t"],
                                                          SS["SM"], SS["sm_t"], SS["scr_t"])
        for g in (3, 5, 6):
            P.dma(G[g][0:NSAMP, :], scr_d[:, g * 512:(g + 1) * 512], reads=[scr_t], writes=[g_t[g]])
        P.memset(yst[:], 0.0, writes=[yst_t], eng="dve")
        P.dma(yst[0:NSAMP, :], xs_d, writes=[yst_t])
        for n in range(NSAMP):
            VB, S0, SN, KV = hw["kd"], hsq, hrs, hattn[0]
            P.dma(VB[:], scr_d[n:n + 1, 5 * 512:5 * 512 + 256].partition_broadcast(128), reads=[scr_t],
                  writes=[hw_t["kd"]])
            P.dma(S0[:], sst_d[n], writes=[hsq_t])
            for half in range(2):
                r0 = half * 64
                for hp in range(2):
                    h = hp * 2 + half
                    ck_, cf_ = (1 * 2 + hp) * 4 + n, (0 * 2 + hp) * 4 + n
                    P.ts(KV[r0:r0 + 64, hp * 64:(hp + 1) * 64], VB[r0:r0 + 64, h * 64:(h + 1) * 64],
                         FKQ[r0:r0 + 64, ck_:ck_ + 1], None, ALU.mult, reads=[hw_t["kd"], fkq_t],
                         writes=[hattn_t[0]])
                    P.stt(SN[r0:r0 + 64, hp * 64:(hp + 1) * 64], S0[r0:r0 + 64, hp * 64:(hp + 1) * 64],
                          FKQ[r0:r0 + 64, cf_:cf_ + 1], KV[r0:r0 + 64, hp * 64:(hp + 1) * 64], ALU.mult, ALU.add,
                          reads=[hsq_t, fkq_t, hattn_t[0]], writes=[hrs_t])
            P.dma(hss_d[n], SN[:], reads=[hrs_t])
            if STAGE == 10.55:
                continue
            p_h = 5
            for hp in range(2):
                cq = (2 * 2 + hp) * 4 + n
                P.ts(KV[:, hp * 64:(hp + 1) * 64], SN[:, hp * 64:(hp + 1) * 64], FKQ[:, cq:cq + 1], None, ALU.mult,
                     reads=[hrs_t, fkq_t], writes=[hattn_t[0]])
            P.mm(PF[p_h][:, 0:128], OB, KV[:, :], reads=[c_t, hattn_t[0]], writes=[pf_t[p_h]])
            P.copy(hosb[:, 0:128], PF[p_h][:, 0:128], reads=[pf_t[p_h]], writes=[hosb_t])
            for half in range(2):
                for hp in range(2):
                    h = hp * 2 + half
                    P.dma(otok[n:n + 1, 512 + h * 64:512 + (h + 1) * 64],
                          hosb[half * 64:half * 64 + 1, hp * 64:(hp + 1) * 64], reads=[hosb_t], writes=[otok_t])
            if STAGE == 10.6:
                continue
            XQB, CMK, CMV, XT = hw["lf"], TMP[0], TMP[1], HFb
            P.dma(XQB[:], scr_d[n:n + 1, 6 * 512:6 * 512 + 256].partition_broadcast(128), reads=[scr_t],
                  writes=[hw_t["lf"]])
            P.dma(CMK.rearrange("p (b c) -> p b c", b=2), cmk_d[n].rearrange("(b m) c -> m b c", b=2),
                  writes=[tmp_t[0]])
            P.dma(CMV.rearrange("p (b c) -> p b c", b=2), cmv_d[n].rearrange("(b m) c -> m b c", b=2),
                  writes=[tmp_t[1]])
            for mb in range(2):
                P.tt(XT[:, mb * 256:(mb + 1) * 256], CMK[:, mb * 256:(mb + 1) * 256], XQB[:], ALU.mult,
                     reads=[tmp_t[0], hw_t["lf"]], writes=[hfb_t])
                P.reduce(SM[:, 8 + mb * 4:12 + mb * 4],
                         XT[:, mb * 256:(mb + 1) * 256].rearrange("p (h d) -> p h d", h=4), ALU.add,
                         reads=[hfb_t], writes=[sm_t])
            P.act(SM[:, 16:24], SM[:, 8:16], AF.Exp, reads=[sm_t], writes=[sm_t], scale=0.125)
            p_x, p_d = 0, 1
            for mb in range(2):
                P.mm(PF[p_x][0:4, 0:256], SM[:, 16 + mb * 4:20 + mb * 4], CMV[:, mb * 256:(mb + 1) * 256],
                     start=(mb == 0), stop=(mb == 1), reads=[sm_t, tmp_t[1]], writes=[pf_t[p_x]])
            for mb in range(2):
                P.mm(PF[p_d][0:4, 0:1], SM[:, 16 + mb * 4:20 + mb * 4], ONEC, start=(mb == 0), stop=(mb == 1),
                     reads=[sm_t, c_t], writes=[pf_t[p_d]])
            P.recip(SM[0:4, 24:25], PF[p_d][0:4, 0:1], reads=[pf_t[p_d]], writes=[sm_t])
            P.ts(hosb[0:4, :], PF[p_x][0:4, 0:256], SM[0:4, 24:25], None, ALU.mult, reads=[pf_t[p_x], sm_t],
                 writes=[hosb_t])
            for h in range(4):
                P.dma(otok[n:n + 1, 768 + h * 64:768 + (h + 1) * 64], hosb[h:h + 1, h * 64:(h + 1) * 64],
                      reads=[hosb_t], writes=[otok_t])
        if STAGE == 10.7:
            return
        HG = otok[:, 512:768]
        P.tt(hosb[:], HG, HG, ALU.mult, reads=[otok_t], writes=[hosb_t])
        P.reduce(SM[:, 0:4], hosb[:].rearrange("p (h d) -> p h d", h=4), ALU.add, reads=[hosb_t], writes=[sm_t])
        P.ts(SM[:, 0:4], SM[:, 0:4], 1.0 / 64, EPS, ALU.mult, ALU.add, reads=[sm_t], writes=[sm_t])
        P.act(SM[:, 4:8], SM[:, 0:4], AF.Ln, reads=[sm_t], writes=[sm_t])
        P.act(SM[:, 8:12], SM[:, 4:8], AF.Exp, reads=[sm_t], writes=[sm_t], scale=-0.5)
        for h in range(4):
            P.ts(otok[:, 512 + h * 64:512 + (h + 1) * 64], otok[:, 512 + h * 64:512 + (h + 1) * 64],
                 SM[:, 8 + h:9 + h], None, ALU.mult, reads=[otok_t, sm_t], writes=[otok_t])
        P.tt(HG, HG, hgnb, ALU.mult, reads=[otok_t, c_t], writes=[otok_t])
        for (gsrc, gtok, c0, c1, o0) in ((G[3], g_t[3], 0, 512, 0), (G[5], g_t[5], 256, 512, 512),
                                         (G[6], g_t[6], 256, 512, 768)):
            w_ = c1 - c0
            t_ = HIb[:, 0:w_]
            P.act(t_, gsrc[:, c0:c1], AF.Exp, reads=[gtok], writes=[hib_t], scale=-1.0)
            P.ts(t_, t_, 1.0, None, ALU.add, reads=[hib_t], writes=[hib_t])
            P.recip(t_, t_, reads=[hib_t], writes=[hib_t])
            P.tt(t_, t_, gsrc[:, c0:c1], ALU.mult, reads=[hib_t, gtok], writes=[hib_t])
            P.tt(otok[:, o0:o0 + w_], otok[:, o0:o0 + w_], t_, ALU.mult, reads=[otok_t, hib_t], writes=[otok_t])
        P.copy(xsb[0][:], otok[:], reads=[otok_t], writes=[xsb_t[0]])
        transpose_to(lambda: MIX[:, :, 0:128], 0, mix_t)
        wg0 = load_wgroup(7)
        wg1 = load_wgroup(8)
        st, stt_ = st4[0], st4_t[0]
        for half, wg in ((0, wg0), (1, wg1)):
            pi = pf_next()
            for kb in range(8):
                P.mm(PF[pi][:, :], MIX[:, kb, 0:128], WG[wg][:, kb, :], start=(kb == 0), stop=(kb == 7),
                     reads=[mix_t[kb], wg_t[wg]], writes=[pf_t[pi]])
            P.tt(G[half][:], PF[pi][:, :], yst[:, half * 512:(half + 1) * 512], ALU.add,
                 reads=[pf_t[pi], yst_t], writes=[g_t[half]])
            P.act(xsb[1][:, half * 512:(half + 1) * 512], G[half][:], AF.Square, reads=[g_t[half]],
                  writes=[xsb_t[1], stt_], accum_out=st[:, half:half + 1])
        P.tt(st[:, 0:1], st[:, 0:1], st[:, 1:2], ALU.add, reads=[stt_], writes=[stt_])
        P.ts(st[:, 1:2], st[:, 0:1], 1.0 / D, EPS, ALU.mult, ALU.add, reads=[stt_], writes=[stt_])
        P.act(st[:, 2:3], st[:, 1:2], AF.Ln, reads=[stt_], writes=[stt_])
        P.act(st[:, 3:4], st[:, 2:3], AF.Exp, reads=[stt_], writes=[stt_], scale=-0.5)
        for half in range(2):
            P.stt(G[half][:], G[half][:], st[:, 3:4], fgb[:, half * 512:(half + 1) * 512], ALU.mult, ALU.mult,
                  reads=[g_t[half], stt_, c_t], writes=[g_t[half]])
            P.dma(ys_d[:, half * 512:(half + 1) * 512], G[half][0:NSAMP, :], reads=[g_t[half]])

    gen = [None]

    def pump(k):
        if gen[0] is None:
            return
        for _ in range(k):
            try:
                next(gen[0])
            except StopIteration:
                gen[0] = None
                return

    if ENABLE_SAMPLE:
        sample_front()
        gen[0] = sample_stream()
        rr["npf"] = 4
    for ti in range(N_TILES):
        if ti == HALF or N_TILES < HALF:
            pump(10 ** 9)
            rr["npf"] = 6
        tile(ti)
    pump(10 ** 9)
    rr["npf"] = 6
    P.dma(hgs_d, Sst[:], reads=[s_t])


    if ENABLE_SAMPLE:
        sample_back()

    P.emit()
    return nc


_NC_CACHE = {}


def _consts():
    s = np.arange(128)[:, None]
    t = np.arange(128)[None, :]
    same = (s // 64) == (t // 64)
    ident = np.eye(128, dtype=np.float32)
    tri = ((s <= t) & same).astype(np.float32)
    su = ((s > t) & same).astype(np.float32)
    ob = same.astype(np.float32)
    cf = np.concatenate([ident, tri, su, ob, np.arange(128, dtype=np.float32)[:, None], np.ones((128, 1), np.float32)],
                        axis=1).astype(np.float32)
    uin = (s >= t).astype(np.float32)
    tq = np.arange(512)[None, :]
    msk = [((d * 128 + s) < tq).astype(np.float32) for d in range(4)]
    lst = (s < t).astype(np.float32)
    cb = np.concatenate([ident, uin, lst] + msk, axis=1).astype(ml_dtypes.bfloat16)
    return cf, cb


def kernel(x_prompt, x_sample, mem_prompt, cache_k, cache_v, page_table, state_hgrn, cache_mem_k, cache_mem_v,
           norm_gain, w_in, sb_bias, hg_lb_logits, hg_norm_gain, mem_norm_gain, w_mem_kv, w_out, final_norm_gain):
    if "nc" not in _NC_CACHE:
        _NC_CACHE["nc"] = build_program()
    nc = _NC_CACHE["nc"]
    f = lambda a: np.ascontiguousarray(np.asarray(a, dtype=np.float32))
    cf, cb = _consts()
    common = {
        "w_in": f(w_in[0]), "w_out": f(w_out[0]), "w_mem": f(w_mem_kv[0]),
        "gin": f(np.asarray(norm_gain[0]).reshape(8, 128).T),
        "gmem": f(np.asarray(mem_norm_gain[0]).reshape(8, 128).T),
        "fg": f(np.asarray(final_norm_gain).reshape(1, D)),
        "hgn": f(np.asarray(hg_norm_gain[0]).reshape(2, 128).T),
        "sbb": f(np.asarray(sb_bias[0]).reshape(1, 8)),
        "lbl": f(np.asarray(hg_lb_logits).reshape(1, 512)),
        "cf": cf, "cb": cb,
    }
    if ENABLE_SAMPLE:
        ckv = np.concatenate([f(cache_k[0]).reshape(POOL_PAGES * 128, 512),
                              f(cache_v[0]).reshape(POOL_PAGES * 128, 512)], axis=1)
    in_maps = []
    for c in range(8):
        b = c // 2
        m = dict(common)
        r = c % 2
        xb = f(x_prompt[b])
        m["x"] = np.ascontiguousarray(xb[:S // 2]) if r == 1 else np.zeros((S // 2, D), np.float32)
        m["xo"] = np.ascontiguousarray(xb[r * (S // 2):(r + 1) * (S // 2)])
        m["kbias"] = np.full((128, 1), 0.0 if r == 1 else -30000.0, np.float32)
        m["mem"] = f(mem_prompt[b])
        if ENABLE_SAMPLE:
            sl = slice(NSAMP * c, NSAMP * (c + 1))
            m["xs"] = f(x_sample[sl, 0])
            m["pt"] = np.ascontiguousarray(np.asarray(page_table[sl], dtype=np.int32).reshape(1, NSAMP * NPAGE))
            m["ckv"] = ckv
            st = f(state_hgrn[0, sl]).reshape(NSAMP, 2, 2, 64, 64).transpose(0, 2, 3, 1, 4).reshape(NSAMP, 128, 128)
            m["sst"] = np.ascontiguousarray(st)
            m["cmk"] = f(cache_mem_k[0, sl]).reshape(NSAMP, 256, 256)
            m["cmv"] = f(cache_mem_v[0, sl]).reshape(NSAMP, 256, 256)
            m["hgr"] = f(np.asarray(hg_norm_gain[0]).reshape(1, 256))
        in_maps.append(m)
    res = run_bass_kernel_spmd(nc, in_maps[:NCORES], core_ids=list(range(NCORES))).results
    res = list(res) + [res[0], res[1 % NCORES]] * ((8 - NCORES) // 2 + 1)

    def unstate(a):
        return a.reshape(2, 64, 2, 64).transpose(2, 0, 1, 3).reshape(4, 64, 64)

    cat = lambda b, n_: np.concatenate([res[2 * b][n_], res[2 * b + 1][n_]], axis=0)
    y_prompt = np.stack([cat(b, "y") for b in range(4)]).astype(np.float32)
    k_prompt = np.stack([cat(b, "k") for b in range(4)]).reshape(1, 4, S, 8, 64).astype(np.float32)
    v_prompt = np.stack([cat(b, "v") for b in range(4)]).reshape(1, 4, S, 8, 64).astype(np.float32)
    hgrn_prompt = np.stack([unstate(res[2 * b + 1]["hgs"]) for b in range(4)])[None].astype(np.float32)
    mem_k = np.stack([res[2 * b]["mk"] for b in range(4)]).reshape(1, 4, 256, 4, 64).astype(np.float32)
    mem_v = np.stack([res[2 * b]["mv"] for b in range(4)]).reshape(1, 4, 256, 4, 64).astype(np.float32)
    if ENABLE_SAMPLE:
        y_sample = np.concatenate([res[c]["ys"] for c in range(8)]).reshape(32, 1, D).astype(np.float32)
        k_sample = np.concatenate([res[c]["ks"] for c in range(8)]).reshape(1, 32, 1, 8, 64).astype(np.float32)
        v_sample = np.concatenate([res[c]["vs"] for c in range(8)]).reshape(1, 32, 1, 8, 64).astype(np.float32)
        hgrn_sample = np.concatenate([np.stack([unstate(res[c]["hss"][i]) for i in range(NSAMP)])
                                      for c in range(8)])[None].astype(np.float32)
    else:
        y_sample = np.zeros((32, 1, D), np.float32)
        k_sample = np.zeros((1, 32, 1, 8, 64), np.float32)
        v_sample = np.zeros((1, 32, 1, 8, 64), np.float32)
        hgrn_sample = np.zeros((1, 32, 4, 64, 64), np.float32)
    return (y_prompt, y_sample, k_prompt, v_prompt, hgrn_prompt, mem_k, mem_v, k_sample, v_sample, hgrn_sample)
```

```python
import contextlib
import numpy as np
import ml_dtypes
import concourse.bass as bass
import concourse.mybir as mybir
from concourse.bass_utils import run_bass_kernel_spmd

F32 = mybir.dt.float32
BF16 = mybir.dt.bfloat16
I32 = mybir.dt.int32
AF = mybir.ActivationFunctionType
ALU = mybir.AluOpType
AX = mybir.AxisListType

D = 1024
S = 4096
NT = S // 512
NB = S // 128
DIN = 3584
EPS = 1e-6
NSAMP = 4
NPAGE = 64
N_DMA_SEM = 12
RING = 3 * N_DMA_SEM

ENABLE_SAMPLE = True
N_TILES = NT
STAGE = 99
NCORES = 8
POOL_PAGES = 2560
NO_SCRATCH_READ = False
DBG_PRE = False
DBG_KBS = None


class Tok:
    __slots__ = ("w", "r", "ps", "old")

    def __init__(self, ps=False, old=None):
        self.w = None
        self.r = []
        self.ps = ps
        self.old = old


class Prog:
    def __init__(self, nc, es):
        self.nc = nc
        self.es = es
        self.ops = []
        self.eng = {"pe": nc.tensor, "act": nc.scalar, "dve": nc.vector, "pool": nc.gpsimd, "sp": nc.sync}

    def op(self, eng, fn, reads=(), writes=(), dma=False):
        idx = len(self.ops)
        deps = set()
        for t in reads:
            if t.w is not None:
                deps.add(t.w)
            if t.ps:
                deps.update(r for r in t.r if self.ops[r][0] != eng)
        for t in writes:
            if t.w is not None:
                deps.add(t.w)
            deps.update(t.r)
            if t.old:
                for o in t.old:
                    if o.w is not None:
                        deps.add(o.w)
                    deps.update(o.r)
                t.old = None
        deps.discard(idx)
        self.ops.append([eng, fn, deps, dma])
        for t in reads:
            t.r.append(idx)
        for t in writes:
            t.w = idx
            t.r = []
        return idx

    def dma(self, out, in_, reads=(), writes=(), q="sp", **kw):
        e = self.eng[q]
        return self.op(q, lambda: e.dma_start(out=out, in_=in_, **kw), reads, writes, dma=True)

    def gather(self, out, in_, idx_ap, reads=(), writes=()):
        g = self.nc.gpsimd
        return self.op("pool", lambda: g.indirect_dma_start(
            out=out, out_offset=None, in_=in_, in_offset=bass.IndirectOffsetOnAxis(ap=idx_ap, axis=0)),
            reads, writes, dma=True)

    def act(self, out, in_, func, reads=(), writes=(), **kw):
        a = self.nc.scalar
        return self.op("act", lambda: a.activation(out=out, in_=in_, func=func, **kw), reads, writes)

    def tt(self, out, in0, in1, op, reads=(), writes=(), eng="dve"):
        e = self.eng[eng]
        return self.op(eng, lambda: e.tensor_tensor(out=out, in0=in0, in1=in1, op=op), reads, writes)

    def ts(self, out, in0, s1, s2, op0, op1=None, reads=(), writes=(), eng="dve"):
        e = self.eng[eng]
        if op1 is None:
            return self.op(eng, lambda: e.tensor_scalar(out=out, in0=in0, scalar1=s1, scalar2=None, op0=op0),
                           reads, writes)
        return self.op(eng, lambda: e.tensor_scalar(out=out, in0=in0, scalar1=s1, scalar2=s2, op0=op0, op1=op1),
                       reads, writes)

    def stt(self, out, in0, scalar, in1, op0, op1, reads=(), writes=()):
        e = self.nc.vector
        return self.op("dve", lambda: e.scalar_tensor_tensor(out=out, in0=in0, scalar=scalar, in1=in1,
                                                             op0=op0, op1=op1), reads, writes)

    def copy(self, out, in_, reads=(), writes=(), eng="dve"):
        e = self.eng[eng]
        if eng == "act":
            return self.op(eng, lambda: e.activation(out=out, in_=in_, func=AF.Copy), reads, writes)
        return self.op(eng, lambda: e.tensor_copy(out=out, in_=in_), reads, writes)

    def recip(self, out, in_, reads=(), writes=()):
        e = self.nc.vector
        return self.op("dve", lambda: e.reciprocal(out=out, in_=in_), reads, writes)

    def reduce(self, out, in_, op, reads=(), writes=()):
        e = self.nc.vector
        return self.op("dve", lambda: e.tensor_reduce(out=out, in_=in_, axis=AX.X, op=op), reads, writes)

    def scan(self, out, d0, d1, reads=(), writes=()):
        e = self.nc.vector
        return self.op("dve", lambda: e.tensor_tensor_scan(out=out, data0=d0, data1=d1, initial=0.0,
                                                           op0=ALU.mult, op1=ALU.add), reads, writes)

    def memset(self, ap, val, writes=(), eng="pool"):
        e = self.eng[eng]
        return self.op(eng, lambda: e.memset(ap, val), (), writes)

    def mm(self, out, lhsT, rhs, start=True, stop=True, reads=(), writes=(), tp=None):
        t = self.nc.tensor
        if tp is None:
            return self.op("pe", lambda: t.matmul(out, lhsT=lhsT, rhs=rhs, start=start, stop=stop), reads, writes)
        return self.op("pe", lambda: t.matmul(out, lhsT=lhsT, rhs=rhs, start=start, stop=stop, tile_position=tp),
                       reads, writes)

    def tr(self, out, in_, ident, reads=(), writes=()):
        t = self.nc.tensor
        return self.op("pe", lambda: t.transpose(out=out, in_=in_, identity=ident), reads, writes)

    def emit(self):
        nc, es = self.nc, self.es
        ops = self.ops
        n = len(ops)
        comp = ("pe", "act", "dve", "pool")
        pos = [0] * n
        cnt = {e: 0 for e in comp}
        qbase = {"sp": 0, "pool": N_DMA_SEM, "act": 2 * N_DMA_SEM}
        dq = {q: [] for q in qbase}
        for i, (eng, fn, deps, dma) in enumerate(ops):
            if dma:
                k = len(dq[eng])
                pos[i] = (k // N_DMA_SEM) * RING + qbase[eng] + (k % N_DMA_SEM)
                dq[eng].append(i)
            else:
                pos[i] = cnt[eng]
                cnt[eng] += 1
        dma_ops = [i for i in range(n) if ops[i][3]]
        for q, lst in dq.items():
            for k, i in enumerate(lst):
                if k >= N_DMA_SEM:
                    ops[i][2].add(lst[k - N_DMA_SEM])
        waited = {}
        marked = [False] * n
        waits = [None] * n
        for i, (eng, fn, deps, dma) in enumerate(ops):
            wl = []
            for d in sorted(deps):
                deng, _, _, ddma = ops[d]
                if ddma:
                    key = (eng, "dma", pos[d] % RING)
                    val = pos[d] // RING
                else:
                    if deng == eng:
                        if eng == "pe":
                            continue
                        if eng != "pool" and pos[i] - pos[d] > 2:
                            continue
                    key = (eng, deng)
                    val = pos[d]
                if waited.get(key, -1) >= val:
                    continue
                waited[key] = val
                wl.append(d)
                marked[d] = True
            waits[i] = wl
        sem = {e: es.enter_context(nc.semaphore("s_" + e)) for e in comp}
        dsem = [es.enter_context(nc.semaphore("s_dma%d" % k)) for k in range(RING)]
        val = [0] * n
        c2 = {e: 0 for e in comp}
        for i, (eng, fn, deps, dma) in enumerate(ops):
            if dma:
                val[i] = 16 * (pos[i] // RING + 1)
            elif marked[i]:
                c2[eng] += 1
                val[i] = c2[eng]
        for i, (eng, fn, deps, dma) in enumerate(ops):
            e = self.eng[eng]
            for d in waits[i]:
                deng, _, _, ddma = ops[d]
                if ddma:
                    e.wait_ge(dsem[pos[d] % RING], val[d])
                else:
                    e.wait_ge(sem[deng], val[d])
            ins = fn()
            if dma:
                ins.then_inc(dsem[pos[i] % RING], 16)
            elif marked[i]:
                ins.then_inc(sem[eng], 1)
        last = {}
        for i in dma_ops:
            last[pos[i] % RING] = val[i]
        for k, v in last.items():
            nc.sync.wait_ge(dsem[k], v)
        for e in comp:
            if c2[e] > 0:
                nc.sync.wait_ge(sem[e], c2[e])


def build_program():
    nc = bass.Bass("TRN2", target_bir_lowering=False)
    es = contextlib.ExitStack()
    P = Prog(nc, es)

    def din(name, shape, dt=F32):
        return nc.dram_tensor(name, list(shape), dt, kind="ExternalInput").ap()

    def dout(name, shape, dt=F32):
        return nc.dram_tensor(name, list(shape), dt, kind="ExternalOutput").ap()

    x_d = din("x", [S // 2, D])
    xo_d = din("xo", [S // 2, D])
    kb_d = din("kbias", [128, 1])
    mem_d = din("mem", [256, D])
    w_in_d = din("w_in", [D, DIN])
    w_out_d = din("w_out", [D, D])
    w_mem_d = din("w_mem", [D, 512])
    gin_d = din("gin", [128, 8])
    gmem_d = din("gmem", [128, 8])
    fg_d = din("fg", [1, D])
    hgn_d = din("hgn", [128, 2])
    sbb_d = din("sbb", [1, 8])
    lbl_d = din("lbl", [1, 512])
    cf_d = din("cf", [128, 514])
    cb_d = din("cb", [128, 384 + 2048], BF16)
    y_d = dout("y", [S // 2, D])
    k_d = dout("k", [S // 2, 512])
    v_d = dout("v", [S // 2, 512])
    hgs_d = dout("hgs", [128, 128])
    mk_d = dout("mk", [256, 256])
    mv_d = dout("mv", [256, 256])
    wsc_d = nc.dram_tensor("wsc", [9, 128, 4096], BF16, kind="Internal").ap()
    if ENABLE_SAMPLE:
        xs_d = din("xs", [NSAMP, D])
        pt_d = din("pt", [1, NSAMP * NPAGE], I32)
        ckv_d = din("ckv", [POOL_PAGES * 128, 1024])
        sst_d = din("sst", [NSAMP, 128, 128])
        cmk_d = din("cmk", [NSAMP, 256, 256])
        cmv_d = din("cmv", [NSAMP, 256, 256])
        ys_d = dout("ys", [NSAMP, D])
        ks_d = dout("ks", [NSAMP, 512])
        vs_d = dout("vs", [NSAMP, 512])
        hss_d = dout("hss", [NSAMP, 128, 128])
        hgr_d = din("hgr", [1, 256])
        scr_d = nc.dram_tensor("scr", [NSAMP, DIN], F32, kind="Internal").ap()

    cnt = [0]

    def sb(shape, dt=F32, name=None):
        cnt[0] += 1
        return es.enter_context(nc.sbuf_tensor("sb_" + (name or ("t%d" % cnt[0])), list(shape), dt))

    def psum(shape, dt=F32):
        cnt[0] += 1
        return es.enter_context(nc.psum_tensor("p%d" % cnt[0], list(shape), dt))

    def T():
        return Tok()

    KT = sb([128, 4, S], BF16, "KT")
    kt_t = [[T() for _ in range(NB)] for _ in range(4)]
    VA = sb([128, NB, 512], BF16, "VA")
    va_t = [T() for _ in range(NB)]
    MKT = sb([128, 2, 256], BF16, "MKT")
    MV = sb([128, 2, 256], BF16, "MV")
    mk_t = T()
    cf = sb([128, 514], F32, "cf")
    cb = sb([128, 384 + 2048], BF16, "cb")
    c_t = T()
    ident_f, TRI, SU, OB = cf[:, 0:128], cf[:, 128:256], cf[:, 256:384], cf[:, 384:512]
    IOTA, ONEC = cf[:, 512:513], cf[:, 513:514]
    ident_b, UIN, LST = cb[:, 0:128], cb[:, 128:256], cb[:, 256:384]
    MSK = [cb[:, 384 + 512 * d: 384 + 512 * (d + 1)] for d in range(4)]
    ONESB = sb([128, 128], BF16, "onesb")
    gin = sb([128, 8], F32, "gin")
    gmem = sb([128, 8], F32, "gmem")
    fgb = sb([128, D], F32, "fgb")
    hgn = sb([128, 2], F32, "hgn")
    sbb = sb([128, 8], F32, "sbb")
    sbbp = sb([128, 8], F32, "sbbp")
    kbias = sb([128, 1], F32, "kbias")
    lbl = sb([128, 512], F32, "lbl")
    lb = sb([128, 256], F32, "lb")
    oml = sb([128, 256], F32, "oml")
    Sst = sb([128, 128], F32, "Sst")
    s_t = T()

    PF = [psum([128, 512], F32) for _ in range(6)]
    pf_t = [Tok(ps=True) for _ in range(6)]
    PB = [psum([128, 1024], BF16) for _ in range(2)]
    pb_t = [Tok(ps=True) for _ in range(2)]

    P.dma(cf[:], cf_d, writes=[c_t])
    P.dma(cb[:], cb_d, writes=[c_t])
    P.dma(gin[:], gin_d, writes=[c_t])
    P.dma(gmem[:], gmem_d, writes=[c_t])
    P.dma(fgb[:], fg_d.partition_broadcast(128), writes=[c_t])
    P.dma(hgn[:], hgn_d, writes=[c_t])
    P.dma(sbb[:], sbb_d.partition_broadcast(128), writes=[c_t])
    P.dma(lbl[:], lbl_d.partition_broadcast(128), writes=[c_t])
    P.dma(kbias[:], kb_d, writes=[c_t])
    P.ts(sbbp[:], sbb[:], kbias[:, 0:1], None, ALU.add, reads=[c_t], writes=[c_t])
    P.memset(ONESB[:], 1.0, writes=[c_t], eng="dve")
    P.memset(Sst[:], 0.0, writes=[s_t], eng="dve")
    P.tt(lb[:], lbl[:, 256:512], lbl[:, 0:256], ALU.subtract, reads=[c_t], writes=[c_t])
    P.act(lb[:], lb[:], AF.Exp, reads=[c_t], writes=[c_t])
    P.ts(lb[:], lb[:], 1.0, None, ALU.add, reads=[c_t], writes=[c_t])
    P.recip(lb[:], lb[:], reads=[c_t], writes=[c_t])
    P.ts(oml[:], lb[:], -1.0, 1.0, ALU.mult, ALU.add, reads=[c_t], writes=[c_t])

    if STAGE == 0:
        P.emit()
        return nc
    xst = [sb([128, D], F32) for _ in range(2)]
    xst_t = [T() for _ in range(2)]
    xsb = [sb([128, D], BF16) for _ in range(2)]
    xsb_t = [T() for _ in range(2)]
    st4 = [sb([128, 4], F32) for _ in range(2)]
    st4_t = [T() for _ in range(2)]
    wst = [sb([128, 512], F32) for _ in range(2)]
    wst_t = [T() for _ in range(2)]
    WG = [sb([128, 8, 512], BF16) for _ in range(2)]
    wg_t = [T() for _ in range(2)]
    wsc_t = [T() for _ in range(9)]
    rr = {"x": 0, "w": 0, "wg": 0, "pf": 0, "pb": 0, "npf": 6}

    def rmsnorm_to_bf(src_ap, reads, which):
        st = st4[which]
        stt_ = st4_t[which]
        P.act(xsb[which][:], src_ap, AF.Square, reads=reads, writes=[xsb_t[which], stt_], accum_out=st[:, 0:1])
        P.ts(st[:, 1:2], st[:, 0:1], 1.0 / D, EPS, ALU.mult, ALU.add, reads=[stt_], writes=[stt_])
        P.act(st[:, 2:3], st[:, 1:2], AF.Ln, reads=[stt_], writes=[stt_])
        P.act(st[:, 3:4], st[:, 2:3], AF.Exp, reads=[stt_], writes=[stt_], scale=-0.5)
        P.act(xsb[which][:], src_ap, AF.Copy, reads=list(reads) + [stt_], writes=[xsb_t[which]], scale=st[:, 3:4])

    def transpose_to(dst_ap_fn, which, dst_toks):
        pb = rr["pb"] % 2
        rr["pb"] += 1
        for kb in range(8):
            P.tr(PB[pb][:, kb * 128:(kb + 1) * 128], xsb[which][:, kb * 128:(kb + 1) * 128], ident_b,
                 reads=[xsb_t[which], c_t], writes=[pb_t[pb]])
        P.copy(dst_ap_fn(), PB[pb][:, :].rearrange("p (k t) -> p k t", k=8), reads=[pb_t[pb]], writes=dst_toks)

    for g in range(9):
        wgi = rr["wg"] % 2
        rr["wg"] += 1
        for kb in range(8):
            wi = rr["w"] % 2
            rr["w"] += 1
            if g < 7:
                src = w_in_d[kb * 128:(kb + 1) * 128, g * 512:(g + 1) * 512]
            else:
                src = w_out_d[kb * 128:(kb + 1) * 128, (g - 7) * 512:(g - 6) * 512]
            P.dma(wst[wi][:], src, writes=[wst_t[wi]])
            if g < 7:
                if kb % 2 == 0:
                    P.ts(WG[wgi][:, kb, :], wst[wi][:], gin[:, kb:kb + 1], None, ALU.mult,
                         reads=[wst_t[wi], c_t], writes=[wg_t[wgi]])
                else:
                    P.act(WG[wgi][:, kb, :], wst[wi][:], AF.Copy, reads=[wst_t[wi], c_t], writes=[wg_t[wgi]],
                          scale=gin[:, kb:kb + 1])
            else:
                P.copy(WG[wgi][:, kb, :], wst[wi][:], reads=[wst_t[wi]], writes=[wg_t[wgi]],
                       eng="dve" if kb % 2 == 0 else "pool")
        P.dma(wsc_d[g], WG[wgi][:, :, :].rearrange("p k c -> p (k c)"), reads=[wg_t[wgi]], writes=[wsc_t[g]])

    if STAGE == 1:
        P.emit()
        return nc

    def load_wgroup(g):
        wgi = rr["wg"] % 2
        rr["wg"] += 1
        if not NO_SCRATCH_READ:
            P.dma(WG[wgi][:, :, :].rearrange("p k c -> p (k c)"), wsc_d[g], reads=[wsc_t[g]], writes=[wg_t[wgi]])
        return wgi

    xnT = sb([128, 8, 512], BF16, "xnT")
    xnT_t = [T() for _ in range(4)]
    memT = [xnT[:, :, mb_ * 128:(mb_ + 1) * 128] for mb_ in range(2)]
    memT_t = [xnT_t[0], xnT_t[1]]
    for mb in range(2):
        xi = rr["x"] % 2
        rr["x"] += 1
        P.dma(xst[xi][:], mem_d[mb * 128:(mb + 1) * 128, :], writes=[xst_t[xi]])
        rmsnorm_to_bf(xst[xi][:], [xst_t[xi]], xi)
        transpose_to(lambda mb=mb: memT[mb], xi, [memT_t[mb]])
    wmb = [sb([128, 512], BF16) for _ in range(2)]
    wmb_t = [T() for _ in range(2)]
    for kb in range(8):
        wi = rr["w"] % 2
        rr["w"] += 1
        P.dma(wst[wi][:], w_mem_d[kb * 128:(kb + 1) * 128, :], writes=[wst_t[wi]])
        P.ts(wmb[kb % 2][:], wst[wi][:], gmem[:, kb:kb + 1], None, ALU.mult, reads=[wst_t[wi], c_t],
             writes=[wmb_t[kb % 2]])
        for mb in range(2):
            P.mm(PF[mb][:, :], xnT[:, kb, mb * 128:(mb + 1) * 128], wmb[kb % 2][:], start=(kb == 0), stop=(kb == 7),
                 reads=[memT_t[mb], wmb_t[kb % 2]], writes=[pf_t[mb]])
    kvst = [sb([128, 512], F32) for _ in range(2)]
    kvst_t = [T() for _ in range(2)]
    kbf = [sb([128, 512], BF16) for _ in range(2)]
    kbf_t = [T() for _ in range(2)]
    for mb in range(2):
        P.copy(kvst[mb][:], PF[mb][:, :], reads=[pf_t[mb]], writes=[kvst_t[mb]], eng="act" if mb else "dve")
        P.dma(mk_d[mb * 128:(mb + 1) * 128, :], kvst[mb][:, 0:256], reads=[kvst_t[mb]])
        P.dma(mv_d[mb * 128:(mb + 1) * 128, :], kvst[mb][:, 256:512], reads=[kvst_t[mb]])
        P.copy(kbf[mb][:, 0:256], kvst[mb][:, 0:256], reads=[kvst_t[mb]], writes=[kbf_t[mb]])
        P.copy(MV[:, mb, :], kvst[mb][:, 256:512], reads=[kvst_t[mb]], writes=[mk_t])
        pb = rr["pb"] % 2
        rr["pb"] += 1
        for hp in range(2):
            P.tr(PB[pb][:, hp * 128:(hp + 1) * 128], kbf[mb][:, hp * 128:(hp + 1) * 128], ident_b,
                 reads=[kbf_t[mb], c_t], writes=[pb_t[pb]])
        P.copy(MKT[:, :, mb * 128:(mb + 1) * 128], PB[pb][:, 0:256].rearrange("p (k t) -> p k t", k=2),
               reads=[pb_t[pb]], writes=[mk_t])

    if STAGE == 2:
        P.emit()
        return nc
    QT = sb([128, 4, 512], BF16, "QT")
    qt_t = [T() for _ in range(4)]
    SG = sb([128, 8, 512], BF16, "SG")
    sg_t = [T() for _ in range(8)]
    HQT = sb([128, 2, 512], F32, "HQT")
    hq_t = [T() for _ in range(2)]
    XQT = sb([128, 2, 512], BF16, "XQT")
    xq_t = [T() for _ in range(2)]
    HF = sb([128, 4, 256], F32, "HF")
    HI = sb([128, 4, 256], F32, "HI")
    hf_t = [T() for _ in range(4)]
    hi_t = [T() for _ in range(4)]
    MIX = sb([128, 8, 512], BF16, "MIX")
    mix_t = [T() for _ in range(8)]
    gtmp = [sb([128, 512], F32) for _ in range(2)]
    gtmp_t = [T() for _ in range(2)]
    EB = [sb([128, 512], F32) for _ in range(4)]
    eb_t = [T() for _ in range(4)]
    SPB = wmb + [sb([128, 512], BF16) for _ in range(2)]
    spb_t = wmb_t + [T() for _ in range(2)]
    WB = [sb([128, 512], BF16) for _ in range(2)]
    wb_t = [T() for _ in range(2)]
    AB = [sb([128, 512], BF16) for _ in range(4)]
    ab_t = [T() for _ in range(4)]
    KTf = KT[:, 0, :].bitcast(F32)
    SPACC = [KTf[:, 0:512], KTf[:, 512:1024]]
    spacc_t = [Tok(old=[kt_t[0][kb_] for kb_ in range(NB)]) for _ in range(2)]
    hw = {n_: sb([128, 256], F32, "hw_" + n_) for n_ in ("a", "f", "lf", "k", "kd")}
    hw_t = {n_: T() for n_ in hw}
    hx = {n_: sb([128, 2, 128], F32, "hx_" + n_) for n_ in ("ep", "en", "qe", "ke")}
    hx_t = {n_: T() for n_ in hx}
    hattn = [sb([128, 128], F32) for _ in range(2)]
    hattn_t = [T() for _ in range(2)]
    hdec = sb([128, 4], F32, "hdec")
    hdec_t = T()
    hosb = sb([128, 256], F32, "hosb")
    hosb_t = T()
    hsq = sb([128, 128], F32, "hsq")
    hsq_t = T()
    hrs = sb([128, 128], F32, "hrs")
    hrs_t = T()
    yst = sb([128, D], F32, "yst")
    yst_t = T()

    def pf_next():
        i = rr["pf"] % rr["npf"]
        rr["pf"] += 1
        return i

    def silu_from_psum(pi, dst_ap, dst_tok):
        gi = rr["x"] % 2
        rr["x"] += 1
        P.act(gtmp[gi][:], PF[pi][:, :], AF.Exp, reads=[pf_t[pi]], writes=[gtmp_t[gi]], scale=-1.0)
        P.ts(gtmp[gi][:], gtmp[gi][:], 1.0, None, ALU.add, reads=[gtmp_t[gi]], writes=[gtmp_t[gi]])
        P.recip(gtmp[gi][:], gtmp[gi][:], reads=[gtmp_t[gi]], writes=[gtmp_t[gi]])
        P.tt(dst_ap, PF[pi][:, :], gtmp[gi][:], ALU.mult, reads=[pf_t[pi], gtmp_t[gi]], writes=[dst_tok])

    def feat_proj(wgi, cbs, evac):
        for cbi in cbs:
            pi = pf_next()
            for kb in range(8):
                P.mm(PF[pi][:, :], WG[wgi][:, kb, cbi * 128:(cbi + 1) * 128], xnT[:, kb, :], start=(kb == 0),
                     stop=(kb == 7), reads=[wg_t[wgi]] + xnT_t, writes=[pf_t[pi]])
            evac(cbi, pi)

    def tok_proj(wgi, c0, c1, blk):
        pi = pf_next()
        for kb in range(8):
            P.mm(PF[pi][:, 0:c1 - c0], xnT[:, kb, blk * 128:(blk + 1) * 128], WG[wgi][:, kb, c0:c1],
                 start=(kb == 0), stop=(kb == 7), reads=[wg_t[wgi], xnT_t[blk]], writes=[pf_t[pi]])
        return pi

    def hgrn_block(blk, state_only=False):
        c0 = blk * 128
        tmpb = [0, 1, 2] if state_only else [0, 1, 2, 5]

        def tmp_next():
            i = tmpb[rr["pf"] % len(tmpb)]
            rr["pf"] += 1
            return i
        a, f, lf, k, kd = hw["a"], hw["f"], hw["lf"], hw["k"], hw["kd"]
        P.act(a[:], HF[:, blk, :], AF.Exp, reads=[hf_t[blk]], writes=[hw_t["a"]], scale=-1.0)
        P.ts(a[:], a[:], 1.0, None, ALU.add, reads=[hw_t["a"]], writes=[hw_t["a"]])
        P.recip(a[:], a[:], reads=[hw_t["a"]], writes=[hw_t["a"]])
        P.tt(f[:], a[:], oml[:], ALU.mult, reads=[hw_t["a"], c_t], writes=[hw_t["f"]])
        P.tt(f[:], f[:], lb[:], ALU.add, reads=[hw_t["f"], c_t], writes=[hw_t["f"]])
        P.act(lf[:], f[:], AF.Ln, reads=[hw_t["f"]], writes=[hw_t["lf"]])
        P.ts(k[:], f[:], -1.0, 1.0, ALU.mult, ALU.add, reads=[hw_t["f"]], writes=[hw_t["k"]])
        if STAGE == 3.1:
            return
        p_rev = tmp_next()
        P.mm(PF[p_rev][:, 0:256], SU, lf[:], reads=[c_t, hw_t["lf"]], writes=[pf_t[p_rev]])
        P.act(kd[:], PF[p_rev][:, 0:256], AF.Exp, reads=[pf_t[p_rev]], writes=[hw_t["kd"]])
        P.tt(kd[:], kd[:], k[:], ALU.mult, reads=[hw_t["kd"], hw_t["k"]], writes=[hw_t["kd"]])
        if STAGE == 3.2:
            return
        p_bc = tmp_next()
        for hp in range(2):
            P.mm(PF[p_bc][:, hp * 128:(hp + 1) * 128], lf[:, hp * 128:(hp + 1) * 128], TRI,
                 reads=[hw_t["lf"], c_t], writes=[pf_t[p_bc]])
        P.act(hx["ep"][:, :, :], PF[p_bc][:, 0:256].rearrange("p (k t) -> p k t", k=2), AF.Exp,
              reads=[pf_t[p_bc]], writes=[hx_t["ep"]])
        if not state_only:
            P.act(hx["en"][:, :, :], PF[p_bc][:, 0:256].rearrange("p (k t) -> p k t", k=2), AF.Exp,
                  reads=[pf_t[p_bc]], writes=[hx_t["en"]], scale=-1.0)
            P.tt(hx["qe"][:, :, :], hx["ep"][:, :, :], HQT[:, :, c0:c0 + 128], ALU.mult,
                 reads=[hx_t["ep"]] + hq_t, writes=[hx_t["qe"]])
            if STAGE == 3.3:
                return
            p_kt = tmp_next()
            for hp in range(2):
                P.tr(PF[p_kt][:, hp * 128:(hp + 1) * 128], k[:, hp * 128:(hp + 1) * 128], ident_f,
                     reads=[hw_t["k"], c_t], writes=[pf_t[p_kt]])
            P.tt(hx["ke"][:, :, :], PF[p_kt][:, 0:256].rearrange("p (k t) -> p k t", k=2), hx["en"][:, :, :], ALU.mult,
                 reads=[pf_t[p_kt], hx_t["en"]], writes=[hx_t["ke"]])
        for c in range(2):
            for hp in range(2):
                P.copy(hdec[:, c * 2 + hp: c * 2 + hp + 1], hx["ep"][:, hp, c * 64 + 63: c * 64 + 64],
                       reads=[hx_t["ep"]], writes=[hdec_t])
        if STAGE == 3.4:
            return
        if not state_only:
            p_o = 3
            for h in range(4):
                hp, rb = h // 2, (h % 2) * 64
                p_at = tmp_next()
                P.mm(PF[p_at][:, 0:128], hx["ke"][rb:rb + 64, hp, :], hx["qe"][rb:rb + 64, hp, :],
                     reads=[hx_t["ke"], hx_t["qe"]], writes=[pf_t[p_at]], tp=(rb, 0))
                ai = h % 2
                P.tt(hattn[ai][:], PF[p_at][:, 0:128], TRI, ALU.mult, reads=[pf_t[p_at], c_t], writes=[hattn_t[ai]])
                P.mm(PF[p_o][rb:rb + 64, hp * 128:(hp + 1) * 128], HI[:, blk, h * 64:(h + 1) * 64], hattn[ai][:],
                     start=True, stop=True, reads=[hi_t[blk], hattn_t[ai]], writes=[pf_t[p_o]], tp=(0, rb))
            if STAGE == 3.5:
                return
        p_i = 4
        for c in range(2):
            for h in range(4):
                if STAGE == 3.55 or state_only:
                    break
                hp, rb = h // 2, (h % 2) * 64
                P.mm(PF[p_i][rb:rb + 64, hp * 128 + c * 64: hp * 128 + c * 64 + 64],
                     Sst[rb:rb + 64, hp * 64:(hp + 1) * 64], hx["qe"][rb:rb + 64, hp, c * 64:(c + 1) * 64],
                     start=True, stop=True, reads=[s_t, hx_t["qe"]], writes=[pf_t[p_i]], tp=(rb, rb))
            if STAGE == 3.57:
                continue
            p_s = tmp_next()
            for h in range(4):
                hp, rb = h // 2, (h % 2) * 64
                P.mm(PF[p_s][rb:rb + 64, hp * 64:(hp + 1) * 64], kd[c * 64:(c + 1) * 64, h * 64:(h + 1) * 64],
                     HI[c * 64:(c + 1) * 64, blk, h * 64:(h + 1) * 64], reads=[hw_t["kd"], hi_t[blk]],
                     writes=[pf_t[p_s]], tp=(c * 64, rb))
            for hp in range(2):
                P.stt(Sst[:, hp * 64:(hp + 1) * 64], Sst[:, hp * 64:(hp + 1) * 64],
                      hdec[:, c * 2 + hp: c * 2 + hp + 1], PF[p_s][:, hp * 64:(hp + 1) * 64], ALU.mult, ALU.add,
                      reads=[s_t, hdec_t, pf_t[p_s]], writes=[s_t])
        if STAGE in (3.55, 3.57, 3.6) or state_only:
            return
        P.copy(hosb[:], PF[p_i][:, 0:256], reads=[pf_t[p_i]], writes=[hosb_t], eng="act")
        P.tt(hosb[:], hosb[:], PF[p_o][:, 0:256], ALU.add, reads=[hosb_t, pf_t[p_o]], writes=[hosb_t])
        for hp in range(2):
            P.act(hsq[:], hosb[:, hp * 128:(hp + 1) * 128], AF.Square, reads=[hosb_t], writes=[hsq_t])
            p_n = tmp_next()
            P.mm(PF[p_n][:, 0:128], OB, hsq[:], reads=[c_t, hsq_t], writes=[pf_t[p_n]])
            P.ts(hrs[:], PF[p_n][:, 0:128], 1.0 / 64, EPS, ALU.mult, ALU.add, reads=[pf_t[p_n]], writes=[hrs_t])
            P.act(hrs[:], hrs[:], AF.Ln, reads=[hrs_t], writes=[hrs_t])
            P.act(hrs[:], hrs[:], AF.Exp, reads=[hrs_t], writes=[hrs_t], scale=-0.5)
            P.tt(hrs[:], hosb[:, hp * 128:(hp + 1) * 128], hrs[:], ALU.mult, reads=[hosb_t, hrs_t],
                 writes=[hrs_t])
            P.stt(MIX[:, 4 + hp, c0:c0 + 128], hrs[:], hgn[:, hp:hp + 1], SG[:, 4 + hp, c0:c0 + 128], ALU.mult,
                  ALU.mult, reads=[hrs_t, c_t, sg_t[4 + hp]], writes=[mix_t[4 + hp]])

    def xattn():
        for hp in range(2):
            p_o, p_d = pf_next(), pf_next()
            for hh in range(2):
                h, rb = hp * 2 + hh, hh * 64
                for mb in range(2):
                    p_s = pf_next()
                    P.mm(PF[p_s][:, :], MKT[rb:rb + 64, hp, mb * 128:(mb + 1) * 128], XQT[rb:rb + 64, hp, :],
                         reads=[mk_t, xq_t[hp]], writes=[pf_t[p_s]], tp=(rb, 0))
                    ai = rr["x"] % 2
                    rr["x"] += 1
                    P.act(AB[ai][:], PF[p_s][:, :], AF.Exp, reads=[pf_t[p_s]], writes=[ab_t[ai]])
                    P.mm(PF[p_o][rb:rb + 64, :], MV[:, mb, h * 64:(h + 1) * 64], AB[ai][:], start=(mb == 0),
                         stop=(mb == 1), reads=[mk_t, ab_t[ai]], writes=[pf_t[p_o]], tp=(0, rb))
                    P.mm(PF[p_d][rb:rb + 64, :], ONESB[:, 0:64], AB[ai][:], start=(mb == 0), stop=(mb == 1),
                         reads=[c_t, ab_t[ai]], writes=[pf_t[p_d]], tp=(0, rb))
            gi = rr["x"] % 2
            rr["x"] += 1
            P.recip(gtmp[gi][:], PF[p_d][:, :], reads=[pf_t[p_d]], writes=[gtmp_t[gi]])
            P.tt(gtmp[gi][:], PF[p_o][:, :], gtmp[gi][:], ALU.mult, reads=[pf_t[p_o], gtmp_t[gi]],
                 writes=[gtmp_t[gi]])
            P.tt(MIX[:, 6 + hp, :], gtmp[gi][:], SG[:, 6 + hp, :], ALU.mult, reads=[gtmp_t[gi], sg_t[6 + hp]],
                 writes=[mix_t[6 + hp]])

    def sb_attention(qi):
        nkb = 4 * qi + 4
        for hp in range(4):
            av = 4 + (hp % 2)
            kbs = list(range(nkb - 1, -1, -1))
            n = len(kbs)

            def stage_a1(i):
                kb = kbs[i]
                r = i % 2
                for hh in range(2):
                    rb = hh * 64
                    P.mm(PF[hh][:, :], KT[rb:rb + 64, hp, kb * 128:(kb + 1) * 128], QT[rb:rb + 64, hp, :],
                         reads=[kt_t[hp][kb], qt_t[hp]], writes=[pf_t[hh]], tp=(rb, 0))
                z = kb - 4 * qi
                bsrc = sbbp if kb < 4 * HALF else sbb
                for hh in range(2):
                    h = hp * 2 + hh
                    e = hh * 2 + r
                    P.act(EB[e][:], PF[hh][:, :], AF.Exp, reads=[pf_t[hh], c_t], writes=[eb_t[e]],
                          bias=bsrc[:, h:h + 1])
                    if z >= 0:
                        P.tt(EB[e][:], EB[e][:], MSK[z], ALU.mult, reads=[eb_t[e], c_t], writes=[eb_t[e]])

            def stage_a2(i):
                r = i % 2
                for hh in range(2):
                    e = hh * 2 + r
                    P.act(SPB[e][:], EB[e][:], AF.Ln, reads=[eb_t[e]], writes=[spb_t[e]], bias=1.0)

            def stage_b(i):
                kb = kbs[i]
                r = i % 2
                for hh in range(2):
                    e = hh * 2 + r
                    P.mm(PF[2 + hh][:, :], UIN, SPB[e][:], start=(kb == nkb - 1), stop=(kb == 0),
                         reads=[c_t, spb_t[e]], writes=[pf_t[2 + hh]])
                for hh in range(2):
                    e = hh * 2 + r
                    P.act(WB[hh][:], PF[2 + hh][:, :], AF.Exp, reads=[pf_t[2 + hh]], writes=[wb_t[hh]], scale=-1.0)
                    P.tt(AB[e][:], EB[e][:], WB[hh][:], ALU.mult, reads=[eb_t[e], wb_t[hh]], writes=[ab_t[e]])

            def stage_b2(i):
                kb = kbs[i]
                r = i % 2
                if kb > 0:
                    for hh in range(2):
                        e = hh * 2 + r
                        P.mm(PF[2 + hh][:, :], LST, SPB[e][:], start=False, stop=False, reads=[c_t, spb_t[e]],
                             writes=[pf_t[2 + hh]])

            def stage_c(i):
                kb = kbs[i]
                r = i % 2
                for hh in range(2):
                    rb = hh * 64
                    e = hh * 2 + r
                    h = hp * 2 + hh
                    P.mm(PF[av][rb:rb + 64, :], VA[:, kb, h * 64:(h + 1) * 64], AB[e][:], start=(kb == nkb - 1),
                         stop=(kb == 0), reads=[va_t[kb], ab_t[e]], writes=[pf_t[av]], tp=(0, rb))

            for i in range(n + 2):
                if i < n:
                    stage_a1(i)
                if 0 <= i - 2 < n:
                    stage_b2(i - 2)
                if 0 <= i - 1 < n:
                    stage_b(i - 1)
                if i < n:
                    stage_a2(i)
                if 0 <= i - 2 < n:
                    stage_c(i - 2)
            P.tt(MIX[:, hp, :], PF[av][:, :], SG[:, hp, :], ALU.mult, reads=[pf_t[av], sg_t[hp]],
                 writes=[mix_t[hp]])

    HALF = NT // 2

    def tile(ti):
        own = ti >= HALF
        src = xo_d if own else x_d
        r0 = (ti - HALF) * 512 if own else ti * 512
        for blk in range(4):
            xi = rr["x"] % 2
            rr["x"] += 1
            P.dma(xst[xi][:], src[r0 + blk * 128: r0 + (blk + 1) * 128, :], writes=[xst_t[xi]])
            rmsnorm_to_bf(xst[xi][:], [xst_t[xi]], xi)
            transpose_to(lambda blk=blk: xnT[:, :, blk * 128:(blk + 1) * 128], xi, [xnT_t[blk]])
        rr["pf"] = 0
        wgi = load_wgroup(1)
        for blk in range(4):
            gb = ti * 4 + blk
            pi = tok_proj(wgi, 0, 512, blk)
            si = gb % 2
            P.copy(kvst[si][:], PF[pi][:, :], reads=[pf_t[pi]], writes=[kvst_t[si]], eng="act")
            if own:
                P.dma(k_d[r0 + blk * 128: r0 + (blk + 1) * 128, :], kvst[si][:], reads=[kvst_t[si]])
            P.copy(kbf[si][:], kvst[si][:], reads=[kvst_t[si]], writes=[kbf_t[si]])
            pb = rr["pb"] % 2
            rr["pb"] += 1
            for hp in range(4):
                P.tr(PB[pb][:, hp * 128:(hp + 1) * 128], kbf[si][:, hp * 128:(hp + 1) * 128], ident_b,
                     reads=[kbf_t[si], c_t], writes=[pb_t[pb]])
            P.copy(KT[:, :, gb * 128:(gb + 1) * 128], PB[pb][:, 0:512].rearrange("p (k t) -> p k t", k=4),
                   reads=[pb_t[pb]], writes=[kt_t[hp_][gb] for hp_ in range(4)])
            if not own:
                pump(4)
        wgi = load_wgroup(2)
        for blk in range(4):
            gb = ti * 4 + blk
            pi = tok_proj(wgi, 0, 512, blk)
            si = gb % 2
            P.copy(kvst[si][:], PF[pi][:, :], reads=[pf_t[pi]], writes=[kvst_t[si]], eng="act")
            if own:
                P.dma(v_d[r0 + blk * 128: r0 + (blk + 1) * 128, :], kvst[si][:], reads=[kvst_t[si]])
            P.copy(VA[:, gb, :], kvst[si][:], reads=[kvst_t[si]], writes=[va_t[gb]])
            if not own:
                pump(4)
        if own:
            wgi = load_wgroup(0)
            feat_proj(wgi, range(4), lambda cbi, pi: P.act(QT[:, cbi, :], PF[pi][:, :], AF.Copy, reads=[pf_t[pi]],
                                                          writes=[qt_t[cbi]], scale=0.125))
            wgi = load_wgroup(3)
            feat_proj(wgi, range(4), lambda cbi, pi: silu_from_psum(pi, SG[:, cbi, :], sg_t[cbi]))
        wgi = load_wgroup(4)
        if own:
            feat_proj(wgi, range(2), lambda cbi, pi: P.copy(HQT[:, cbi, :], PF[pi][:, :], reads=[pf_t[pi]],
                                                            writes=[hq_t[cbi]], eng="act"))
        for blk in range(4):
            pi = tok_proj(wgi, 256, 512, blk)
            P.copy(HF[:, blk, :], PF[pi][:, 0:256], reads=[pf_t[pi]], writes=[hf_t[blk]])
        wgi = load_wgroup(5)
        for blk in range(4):
            pi = tok_proj(wgi, 0, 256, blk)
            P.copy(HI[:, blk, :], PF[pi][:, 0:256], reads=[pf_t[pi]], writes=[hi_t[blk]], eng="act")
        if own:
            feat_proj(wgi, range(2, 4), lambda cbi, pi: silu_from_psum(pi, SG[:, 2 + cbi, :], sg_t[2 + cbi]))
            wgi = load_wgroup(6)
            feat_proj(wgi, range(2), lambda cbi, pi: P.act(XQT[:, cbi, :], PF[pi][:, :], AF.Copy, reads=[pf_t[pi]],
                                                           writes=[xq_t[cbi]], scale=0.125))
            feat_proj(wgi, range(2, 4), lambda cbi, pi: silu_from_psum(pi, SG[:, 4 + cbi, :], sg_t[4 + cbi]))
        for blk in range(4):
            hgrn_block(blk, state_only=not own)
            if not own:
                pump(8)
        if not own:
            return
        xattn()
        sb_attention(ti)
        wg0 = load_wgroup(7)
        wg1 = load_wgroup(8)
        for blk in range(4):
            xi = rr["x"] % 2
            rr["x"] += 1
            P.dma(xst[xi][:], src[r0 + blk * 128: r0 + (blk + 1) * 128, :], writes=[xst_t[xi]])
            for half, wg in ((0, wg0), (1, wg1)):
                pi = pf_next()
                kbs = list(range(8)) if DBG_KBS is None else list(DBG_KBS)
                for kb in kbs:
                    P.mm(PF[pi][:, :], MIX[:, kb, blk * 128:(blk + 1) * 128], WG[wg][:, kb, :], start=(kb == kbs[0]),
                         stop=(kb == kbs[-1]), reads=[mix_t[kb], wg_t[wg]], writes=[pf_t[pi]])
                P.tt(yst[:, half * 512:(half + 1) * 512], PF[pi][:, :], xst[xi][:, half * 512:(half + 1) * 512],
                     ALU.add, reads=[pf_t[pi], xst_t[xi]], writes=[yst_t])
            st, stt_ = st4[xi], st4_t[xi]
            P.act(xsb[xi][:], yst[:], AF.Square, reads=[yst_t], writes=[xsb_t[xi], stt_], accum_out=st[:, 0:1])
            P.ts(st[:, 1:2], st[:, 0:1], 1.0 / D, EPS, ALU.mult, ALU.add, reads=[stt_], writes=[stt_])
            P.act(st[:, 2:3], st[:, 1:2], AF.Ln, reads=[stt_], writes=[stt_])
            P.act(st[:, 3:4], st[:, 2:3], AF.Exp, reads=[stt_], writes=[stt_], scale=-0.5)
            if DBG_PRE:
                P.copy(xst[xi][:], yst[:], reads=[yst_t], writes=[xst_t[xi]])
            else:
                P.stt(xst[xi][:], yst[:], st[:, 3:4], fgb[:], ALU.mult, ALU.mult, reads=[yst_t, stt_, c_t],
                      writes=[xst_t[xi]])
            P.dma(y_d[r0 + blk * 128: r0 + (blk + 1) * 128, :], xst[xi][:], reads=[xst_t[xi]])

    SS = {}

    def sample_front():
        G = [EB[0], EB[1], kvst[0], kvst[1], wst[0], wst[1], gtmp[0]]
        g_t = [eb_t[0], eb_t[1], kvst_t[0], kvst_t[1], wst_t[0], wst_t[1], gtmp_t[0]]
        TMP = [gtmp[1][:, :], HQT[:, 1, :]]
        tmp_t = [gtmp_t[1], Tok(old=hq_t)]
        QB, qb_t = HQT[:, 0, :], Tok(old=hq_t)
        HFv = HF[:, :, :].rearrange("p a b -> p (a b)")
        HIv = HI[:, :, :].rearrange("p a b -> p (a b)")
        HFa, HFb, HIa, HIb = HFv[:, 0:512], HFv[:, 512:1024], HIv[:, 0:512], HIv[:, 512:1024]
        hfa_t, hfb_t, hia_t, hib_t = Tok(old=hf_t), Tok(old=hf_t), Tok(old=hi_t), Tok(old=hi_t)
        ZALL, E_ = SPACC[0], SPACC[1]
        v3 = lambda ap: ap.rearrange("p (g h) -> p g h", h=8)
        otok = sb([128, D], F32, "otok")
        otok_t = T()
        ptb = sb([128, NSAMP * NPAGE], I32, "ptb")
        idxa = sb([128, NSAMP * NPAGE], I32, "idxa")
        idx_t = T()
        hgnb = lbl[:, 0:256]
        FKQ = sb([128, 24], F32, "fkq")
        fkq_t = T()
        SM = sb([128, 32], F32, "sm")
        sm_t = T()
        scr_t = T()

        P.memset(otok[:], 0.0, writes=[otok_t], eng="dve")
        P.dma(hgnb, hgr_d.partition_broadcast(128), writes=[c_t])
        P.dma(ptb[:], pt_d.partition_broadcast(128), writes=[idx_t])
        P.ts(idxa[:], ptb[:], 128.0, IOTA, ALU.mult, ALU.add, reads=[idx_t, c_t], writes=[idx_t])
        P.memset(yst[:], 0.0, writes=[yst_t], eng="dve")
        P.dma(yst[0:NSAMP, :], xs_d, writes=[yst_t])
        rmsnorm_to_bf(yst[:], [yst_t], 0)
        transpose_to(lambda: xnT[:, :, 0:128], 0, [xnT_t[0]])
        for g in range(7):
            wgi = load_wgroup(g)
            pi = tok_proj(wgi, 0, 512, 0)
            P.copy(G[g][:], PF[pi][:, :], reads=[pf_t[pi]], writes=[g_t[g]], eng="act" if g % 2 else "dve")
            P.dma(scr_d[:, g * 512:(g + 1) * 512], G[g][0:NSAMP, :], reads=[g_t[g]], writes=[scr_t])
        P.dma(ks_d, G[1][0:NSAMP, :], reads=[g_t[1]])
        P.dma(vs_d, G[2][0:NSAMP, :], reads=[g_t[2]])
        if STAGE == 10.1:
            return
        a, f, k = hw["a"], hw["f"], hw["k"]
        P.act(a[:], G[4][:, 256:512], AF.Exp, reads=[g_t[4]], writes=[hw_t["a"]], scale=-1.0)
        P.ts(a[:], a[:], 1.0, None, ALU.add, reads=[hw_t["a"]], writes=[hw_t["a"]])
        P.recip(a[:], a[:], reads=[hw_t["a"]], writes=[hw_t["a"]])
        P.tt(f[:], a[:], oml[:], ALU.mult, reads=[hw_t["a"], c_t], writes=[hw_t["f"]])
        P.tt(f[:], f[:], lb[:], ALU.add, reads=[hw_t["f"], c_t], writes=[hw_t["f"]])
        P.ts(k[:], f[:], -1.0, 1.0, ALU.mult, ALU.add, reads=[hw_t["f"]], writes=[hw_t["k"]])
        p_f = pf_next()
        for j, (src, st_) in enumerate(((f, hw_t["f"]), (k, hw_t["k"]), (G[4], g_t[4]))):
            for hp in range(2):
                c = (j * 2 + hp) * 4
                P.tr(PF[p_f][:, c:c + 4], src[0:4, hp * 128:(hp + 1) * 128], ident_f[0:4, 0:4],
                     reads=[st_, c_t], writes=[pf_t[p_f]])
        P.copy(FKQ[:], PF[p_f][:, 0:24], reads=[pf_t[p_f]], writes=[fkq_t])

        SS.update(dict(G=G, g_t=g_t, TMP=TMP, tmp_t=tmp_t, HFa=HFa, HFb=HFb, HIa=HIa, HIb=HIb, hfa_t=hfa_t, hfb_t=hfb_t,
                       hia_t=hia_t, hib_t=hib_t, otok=otok, otok_t=otok_t, idxa=idxa, idx_t=idx_t, hgnb=hgnb, FKQ=FKQ,
                       fkq_t=fkq_t, SM=SM, sm_t=sm_t, scr_t=scr_t))

    def sample_stream():
        otok, otok_t, idxa, idx_t, scr_t = SS["otok"], SS["otok_t"], SS["idxa"], SS["idx_t"], SS["scr_t"]
        PG = [KT[:, hp_, S // 2:S].bitcast(F32) for hp_ in range(4)]
        VAf = VA[:, NB // 2:NB, :].rearrange("p a b -> p (a b)").bitcast(F32)
        PG += [VAf[:, 2048:3072]]
        pg_t = [T() for _ in range(5)]
        NPB = 5
        VBF = [VA[:, NB - 4 + j_, :] for j_ in range(4)]
        vbf_t = [T() for _ in range(4)]
        A8b = sb([128, 4, 8], BF16, "a8b")

        TMPs = [VAf[:, 0:512], VAf[:, 512:1024]]
        tmps_t = [T(), T()]
        QBs, qbs_t = VAf[:, 1024:1536], T()
        RES, res_t = VAf[:, 1536:2048], T()
        for hp_ in range(4):
            for kb_ in range(NB // 2, NB):
                kt_t[hp_][kb_].old = [pg_t[hp_]]
        for kb_ in range(NB // 2, NB):
            va_t[kb_].old = [pg_t[4], tmps_t[0], tmps_t[1], qbs_t, res_t] + vbf_t
        S8 = sb([128, 4, 48], F32, "s8")
        S8b = sb([128, 4, 8], BF16, "s8b")
        s8_t = [T() for _ in range(4)]
        NPG = NPAGE
        CB, AVB = 5, 4
        npg = [0]
        for n in range(NSAMP):
            P.dma(QBs, scr_d[n:n + 1, 0:512].partition_broadcast(128), reads=[scr_t], writes=[qbs_t])
            pages = list(range(NPG - 1, -1, -1))

            def st1(ii):
                p = pages[ii]
                i = npg[0] + ii
                col = n * NPAGE + p
                b8 = i % 4
                P.gather(PG[i % NPB], ckv_d[:, :], idxa[:, col:col + 1], reads=[idx_t], writes=[pg_t[i % NPB]])
                P.tt(TMPs[i % 2], PG[i % NPB][:, 0:512], QBs, ALU.mult, reads=[pg_t[i % NPB], qbs_t], writes=[tmps_t[i % 2]])
                P.copy(VBF[b8], PG[i % NPB][:, 512:1024], reads=[pg_t[i % NPB]], writes=[vbf_t[b8]], eng="act")
                P.reduce(S8[:, b8, 0:8], TMPs[i % 2].rearrange("p (h d) -> p h d", h=8), ALU.add,
                         reads=[tmps_t[i % 2]], writes=[s8_t[b8]])
                P.stt(S8[:, b8, 0:8], S8[:, b8, 0:8], 0.125, sbb[:, 0:8], ALU.mult, ALU.add,
                      reads=[s8_t[b8], c_t], writes=[s8_t[b8]])
                P.act(S8[:, b8, 8:16], S8[:, b8, 0:8], AF.Exp, reads=[s8_t[b8]], writes=[s8_t[b8]])
                P.act(S8b[:, b8, :], S8[:, b8, 8:16], AF.Ln, reads=[s8_t[b8]], writes=[s8_t[b8]], bias=1.0)

            def st2(ii):
                i = npg[0] + ii
                b8 = i % 4
                P.mm(PF[CB][:, 0:8], UIN, S8b[:, b8, :], start=(ii == 0), stop=(ii == NPG - 1),
                     reads=[c_t, s8_t[b8]], writes=[pf_t[CB]])
                P.act(S8[:, b8, 24:32], PF[CB][:, 0:8], AF.Exp, reads=[pf_t[CB]], writes=[s8_t[b8]], scale=-1.0)
                P.tt(A8b[:, b8, :], S8[:, b8, 8:16], S8[:, b8, 24:32], ALU.mult, reads=[s8_t[b8]],
                     writes=[s8_t[b8]])
                if ii < NPG - 1:
                    P.mm(PF[CB][:, 0:8], LST, S8b[:, b8, :], start=False, stop=False, reads=[c_t, s8_t[b8]],
                         writes=[pf_t[CB]])
                P.mm(PF[AVB][0:8, :], A8b[:, b8, :], VBF[b8], start=(ii == 0), stop=(ii == NPG - 1),
                     reads=[s8_t[b8], vbf_t[b8]], writes=[pf_t[AVB]])

            for ii in range(NPG + 1):
                if ii < NPG:
                    st1(ii)
                if ii >= 1:
                    st2(ii - 1)
                yield
            npg[0] += NPG
            P.copy(RES[0:8, :], PF[AVB][0:8, :], reads=[pf_t[AVB]], writes=[res_t])
            for h in range(8):
                P.dma(otok[n:n + 1, h * 64:(h + 1) * 64], RES[h:h + 1, h * 64:(h + 1) * 64], reads=[res_t],
                      writes=[otok_t])
            yield

    def sample_back():
        G, g_t, TMP, tmp_t = SS["G"], SS["g_t"], SS["TMP"], SS["tmp_t"]
        HFa, HFb, HIa, HIb = SS["HFa"], SS["HFb"], SS["HIa"], SS["HIb"]
        hfa_t, hfb_t, hia_t, hib_t = SS["hfa_t"], SS["hfb_t"], SS["hia_t"], SS["hib_t"]
        otok, otok_t, hgnb, FKQ, fkq_t, SM, sm_t, scr_t = (SS["otok"], SS["otok_t"], SS["hgnb"], SS["FKQ"], SS["fkq_t"],
                                                          SS["SM"], SS["sm_t"], SS["scr_t"])
        for g in (3, 5, 6):
            P.dma(G[g][0:NSAMP, :], scr_d[:, g * 512:(g + 1) * 512], reads=[scr_t], writes=[g_t[g]])
        P.memset(yst[:], 0.0, writes=[yst_t], eng="dve")
        P.dma(yst[0:NSAMP, :], xs_d, writes=[yst_t])
        for n in range(NSAMP):
            VB, S0, SN, KV = hw["kd"], hsq, hrs, hattn[0]
            P.dma(VB[:], scr_d[n:n + 1, 5 * 512:5 * 512 + 256].partition_broadcast(128), reads=[scr_t],
                  writes=[hw_t["kd"]])
            P.dma(S0[:], sst_d[n], writes=[hsq_t])
            for half in range(2):
                r0 = half * 64
                for hp in range(2):
                    h = hp * 2 + half
                    ck_, cf_ = (1 * 2 + hp) * 4 + n, (0 * 2 + hp) * 4 + n
                    P.ts(KV[r0:r0 + 64, hp * 64:(hp + 1) * 64], VB[r0:r0 + 64, h * 64:(h + 1) * 64],
                         FKQ[r0:r0 + 64, ck_:ck_ + 1], None, ALU.mult, reads=[hw_t["kd"], fkq_t],
                         writes=[hattn_t[0]])
                    P.stt(SN[r0:r0 + 64, hp * 64:(hp + 1) * 64], S0[r0:r0 + 64, hp * 64:(hp + 1) * 64],
                          FKQ[r0:r0 + 64, cf_:cf_ + 1], KV[r0:r0 + 64, hp * 64:(hp + 1) * 64], ALU.mult, ALU.add,
                          reads=[hsq_t, fkq_t, hattn_t[0]], writes=[hrs_t])
            P.dma(hss_d[n], SN[:], reads=[hrs_t])
            if STAGE == 10.55:
                continue
            p_h = 5
            for hp in range(2):
                cq = (2 * 2 + hp) * 4 + n
                P.ts(KV[:, hp * 64:(hp + 1) * 64], SN[:, hp * 64:(hp + 1) * 64], FKQ[:, cq:cq + 1], None, ALU.mult,
                     reads=[hrs_t, fkq_t], writes=[hattn_t[0]])
            P.mm(PF[p_h][:, 0:128], OB, KV[:, :], reads=[c_t, hattn_t[0]], writes=[pf_t[p_h]])
            P.copy(hosb[:, 0:128], PF[p_h][:, 0:128], reads=[pf_t[p_h]], writes=[hosb_t])
            for half in range(2):
                for hp in range(2):
                    h = hp * 2 + half
                    P.dma(otok[n:n + 1, 512 + h * 64:512 + (h + 1) * 64],
                          hosb[half * 64:half * 64 + 1, hp * 64:(hp + 1) * 64], reads=[hosb_t], writes=[otok_t])
            if STAGE == 10.6:
                continue
            XQB, CMK, CMV, XT = hw["lf"], TMP[0], TMP[1], HFb
            P.dma(XQB[:], scr_d[n:n + 1, 6 * 512:6 * 512 + 256].partition_broadcast(128), reads=[scr_t],
                  writes=[hw_t["lf"]])
            P.dma(CMK.rearrange("p (b c) -> p b c", b=2), cmk_d[n].rearrange("(b m) c -> m b c", b=2),
                  writes=[tmp_t[0]])
            P.dma(CMV.rearrange("p (b c) -> p b c", b=2), cmv_d[n].rearrange("(b m) c -> m b c", b=2),
                  writes=[tmp_t[1]])
            for mb in range(2):
                P.tt(XT[:, mb * 256:(mb + 1) * 256], CMK[:, mb * 256:(mb + 1) * 256], XQB[:], ALU.mult,
                     reads=[tmp_t[0], hw_t["lf"]], writes=[hfb_t])
                P.reduce(SM[:, 8 + mb * 4:12 + mb * 4],
                         XT[:, mb * 256:(mb + 1) * 256].rearrange("p (h d) -> p h d", h=4), ALU.add,
                         reads=[hfb_t], writes=[sm_t])
            P.act(SM[:, 16:24], SM[:, 8:16], AF.Exp, reads=[sm_t], writes=[sm_t], scale=0.125)
            p_x, p_d = 0, 1
            for mb in range(2):
                P.mm(PF[p_x][0:4, 0:256], SM[:, 16 + mb * 4:20 + mb * 4], CMV[:, mb * 256:(mb + 1) * 256],
                     start=(mb == 0), stop=(mb == 1), reads=[sm_t, tmp_t[1]], writes=[pf_t[p_x]])
            for mb in range(2):
                P.mm(PF[p_d][0:4, 0:1], SM[:, 16 + mb * 4:20 + mb * 4], ONEC, start=(mb == 0), stop=(mb == 1),
                     reads=[sm_t, c_t], writes=[pf_t[p_d]])
            P.recip(SM[0:4, 24:25], PF[p_d][0:4, 0:1], reads=[pf_t[p_d]], writes=[sm_t])
            P.ts(hosb[0:4, :], PF[p_x][0:4, 0:256], SM[0:4, 24:25], None, ALU.mult, reads=[pf_t[p_x], sm_t],
                 writes=[hosb_t])
            for h in range(4):
                P.dma(otok[n:n + 1, 768 + h * 64:768 + (h + 1) * 64], hosb[h:h + 1, h * 64:(h + 1) * 64],
                      reads=[hosb_t], writes=[otok_t])
        if STAGE == 10.7:
            return
        HG = otok[:, 512:768]
        P.tt(hosb[:], HG, HG, ALU.mult, reads=[otok_t], writes=[hosb_t])
        P.reduce(SM[:, 0:4], hosb[:].rearrange("p (h d) -> p h d", h=4), ALU.add, reads=[hosb_t], writes=[sm_t])
        P.ts(SM[:, 0:4], SM[:, 0:4], 1.0 / 64, EPS, ALU.mult, ALU.add, reads=[sm_t], writes=[sm_t])
        P.act(SM[:, 4:8], SM[:, 0:4], AF.Ln, reads=[sm_t], writes=[sm_t])
        P.act(SM[:, 8:12], SM[:, 4:8], AF.Exp, reads=[sm_t], writes=[sm_t], scale=-0.5)
        for h in range(4):
            P.ts(otok[:, 512 + h * 64:512 + (h + 1) * 64], otok[:, 512 + h * 64:512 + (h + 1) * 64],
                 SM[:, 8 + h:9 + h], None, ALU.mult, reads=[otok_t, sm_t], writes=[otok_t])
        P.tt(HG, HG, hgnb, ALU.mult, reads=[otok_t, c_t], writes=[otok_t])
        for (gsrc, gtok, c0, c1, o0) in ((G[3], g_t[3], 0, 512, 0), (G[5], g_t[5], 256, 512, 512),
                                         (G[6], g_t[6], 256, 512, 768)):
            w_ = c1 - c0
            t_ = HIb[:, 0:w_]
            P.act(t_, gsrc[:, c0:c1], AF.Exp, reads=[gtok], writes=[hib_t], scale=-1.0)
            P.ts(t_, t_, 1.0, None, ALU.add, reads=[hib_t], writes=[hib_t])
            P.recip(t_, t_, reads=[hib_t], writes=[hib_t])
            P.tt(t_, t_, gsrc[:, c0:c1], ALU.mult, reads=[hib_t, gtok], writes=[hib_t])
            P.tt(otok[:, o0:o0 + w_], otok[:, o0:o0 + w_], t_, ALU.mult, reads=[otok_t, hib_t], writes=[otok_t])
        P.copy(xsb[0][:], otok[:], reads=[otok_t], writes=[xsb_t[0]])
        transpose_to(lambda: MIX[:, :, 0:128], 0, mix_t)
        wg0 = load_wgroup(7)
        wg1 = load_wgroup(8)
        st, stt_ = st4[0], st4_t[0]
        for half, wg in ((0, wg0), (1, wg1)):
            pi = pf_next()
            for kb in range(8):
                P.mm(PF[pi][:, :], MIX[:, kb, 0:128], WG[wg][:, kb, :], start=(kb == 0), stop=(kb == 7),
                     reads=[mix_t[kb], wg_t[wg]], writes=[pf_t[pi]])
            P.tt(G[half][:], PF[pi][:, :], yst[:, half * 512:(half + 1) * 512], ALU.add,
                 reads=[pf_t[pi], yst_t], writes=[g_t[half]])
            P.act(xsb[1][:, half * 512:(half + 1) * 512], G[half][:], AF.Square, reads=[g_t[half]],
                  writes=[xsb_t[1], stt_], accum_out=st[:, half:half + 1])
        P.tt(st[:, 0:1], st[:, 0:1], st[:, 1:2], ALU.add, reads=[stt_], writes=[stt_])
        P.ts(st[:, 1:2], st[:, 0:1], 1.0 / D, EPS, ALU.mult, ALU.add, reads=[stt_], writes=[stt_])
        P.act(st[:, 2:3], st[:, 1:2], AF.Ln, reads=[stt_], writes=[stt_])
        P.act(st[:, 3:4], st[:, 2:3], AF.Exp, reads=[stt_], writes=[stt_], scale=-0.5)
        for half in range(2):
            P.stt(G[half][:], G[half][:], st[:, 3:4], fgb[:, half * 512:(half + 1) * 512], ALU.mult, ALU.mult,
                  reads=[g_t[half], stt_, c_t], writes=[g_t[half]])
            P.dma(ys_d[:, half * 512:(half + 1) * 512], G[half][0:NSAMP, :], reads=[g_t[half]])

    gen = [None]

    def pump(k):
        if gen[0] is None:
            return
        for _ in range(k):
            try:
                next(gen[0])
            except StopIteration:
                gen[0] = None
                return

    if ENABLE_SAMPLE:
        sample_front()
        gen[0] = sample_stream()
        rr["npf"] = 4
    for ti in range(N_TILES):
        if ti == HALF or N_TILES < HALF:
            pump(10 ** 9)
            rr["npf"] = 6
        tile(ti)
    pump(10 ** 9)
    rr["npf"] = 6
    P.dma(hgs_d, Sst[:], reads=[s_t])


    if ENABLE_SAMPLE:
        sample_back()

    P.emit()
    return nc


_NC_CACHE = {}


def _consts():
    s = np.arange(128)[:, None]
    t = np.arange(128)[None, :]
    same = (s // 64) == (t // 64)
    ident = np.eye(128, dtype=np.float32)
    tri = ((s <= t) & same).astype(np.float32)
    su = ((s > t) & same).astype(np.float32)
    ob = same.astype(np.float32)
    cf = np.concatenate([ident, tri, su, ob, np.arange(128, dtype=np.float32)[:, None], np.ones((128, 1), np.float32)],
                        axis=1).astype(np.float32)
    uin = (s >= t).astype(np.float32)
    tq = np.arange(512)[None, :]
    msk = [((d * 128 + s) < tq).astype(np.float32) for d in range(4)]
    lst = (s < t).astype(np.float32)
    cb = np.concatenate([ident, uin, lst] + msk, axis=1).astype(ml_dtypes.bfloat16)
    return cf, cb


def kernel(x_prompt, x_sample, mem_prompt, cache_k, cache_v, page_table, state_hgrn, cache_mem_k, cache_mem_v,
           norm_gain, w_in, sb_bias, hg_lb_logits, hg_norm_gain, mem_norm_gain, w_mem_kv, w_out, final_norm_gain):
    if "nc" not in _NC_CACHE:
        _NC_CACHE["nc"] = build_program()
    nc = _NC_CACHE["nc"]
    f = lambda a: np.ascontiguousarray(np.asarray(a, dtype=np.float32))
    cf, cb = _consts()
    common = {
        "w_in": f(w_in[0]), "w_out": f(w_out[0]), "w_mem": f(w_mem_kv[0]),
        "gin": f(np.asarray(norm_gain[0]).reshape(8, 128).T),
        "gmem": f(np.asarray(mem_norm_gain[0]).reshape(8, 128).T),
        "fg": f(np.asarray(final_norm_gain).reshape(1, D)),
        "hgn": f(np.asarray(hg_norm_gain[0]).reshape(2, 128).T),
        "sbb": f(np.asarray(sb_bias[0]).reshape(1, 8)),
        "lbl": f(np.asarray(hg_lb_logits).reshape(1, 512)),
        "cf": cf, "cb": cb,
    }
    if ENABLE_SAMPLE:
        ckv = np.concatenate([f(cache_k[0]).reshape(POOL_PAGES * 128, 512),
                              f(cache_v[0]).reshape(POOL_PAGES * 128, 512)], axis=1)
    in_maps = []
    for c in range(8):
        b = c // 2
        m = dict(common)
        r = c % 2
        xb = f(x_prompt[b])
        m["x"] = np.ascontiguousarray(xb[:S // 2]) if r == 1 else np.zeros((S // 2, D), np.float32)
        m["xo"] = np.ascontiguousarray(xb[r * (S // 2):(r + 1) * (S // 2)])
        m["kbias"] = np.full((128, 1), 0.0 if r == 1 else -30000.0, np.float32)
        m["mem"] = f(mem_prompt[b])
        if ENABLE_SAMPLE:
            sl = slice(NSAMP * c, NSAMP * (c + 1))
            m["xs"] = f(x_sample[sl, 0])
            m["pt"] = np.ascontiguousarray(np.asarray(page_table[sl], dtype=np.int32).reshape(1, NSAMP * NPAGE))
            m["ckv"] = ckv
            st = f(state_hgrn[0, sl]).reshape(NSAMP, 2, 2, 64, 64).transpose(0, 2, 3, 1, 4).reshape(NSAMP, 128, 128)
            m["sst"] = np.ascontiguousarray(st)
            m["cmk"] = f(cache_mem_k[0, sl]).reshape(NSAMP, 256, 256)
            m["cmv"] = f(cache_mem_v[0, sl]).reshape(NSAMP, 256, 256)
            m["hgr"] = f(np.asarray(hg_norm_gain[0]).reshape(1, 256))
        in_maps.append(m)
    res = run_bass_kernel_spmd(nc, in_maps[:NCORES], core_ids=list(range(NCORES))).results
    res = list(res) + [res[0], res[1 % NCORES]] * ((8 - NCORES) // 2 + 1)

    def unstate(a):
        return a.reshape(2, 64, 2, 64).transpose(2, 0, 1, 3).reshape(4, 64, 64)

    cat = lambda b, n_: np.concatenate([res[2 * b][n_], res[2 * b + 1][n_]], axis=0)
    y_prompt = np.stack([cat(b, "y") for b in range(4)]).astype(np.float32)
    k_prompt = np.stack([cat(b, "k") for b in range(4)]).reshape(1, 4, S, 8, 64).astype(np.float32)
    v_prompt = np.stack([cat(b, "v") for b in range(4)]).reshape(1, 4, S, 8, 64).astype(np.float32)
    hgrn_prompt = np.stack([unstate(res[2 * b + 1]["hgs"]) for b in range(4)])[None].astype(np.float32)
    mem_k = np.stack([res[2 * b]["mk"] for b in range(4)]).reshape(1, 4, 256, 4, 64).astype(np.float32)
    mem_v = np.stack([res[2 * b]["mv"] for b in range(4)]).reshape(1, 4, 256, 4, 64).astype(np.float32)
    if ENABLE_SAMPLE:
        y_sample = np.concatenate([res[c]["ys"] for c in range(8)]).reshape(32, 1, D).astype(np.float32)
        k_sample = np.concatenate([res[c]["ks"] for c in range(8)]).reshape(1, 32, 1, 8, 64).astype(np.float32)
        v_sample = np.concatenate([res[c]["vs"] for c in range(8)]).reshape(1, 32, 1, 8, 64).astype(np.float32)
        hgrn_sample = np.concatenate([np.stack([unstate(res[c]["hss"][i]) for i in range(NSAMP)])
                                      for c in range(8)])[None].astype(np.float32)
    else:
        y_sample = np.zeros((32, 1, D), np.float32)
        k_sample = np.zeros((1, 32, 1, 8, 64), np.float32)
        v_sample = np.zeros((1, 32, 1, 8, 64), np.float32)
        hgrn_sample = np.zeros((1, 32, 4, 64, 64), np.float32)
    return (y_prompt, y_sample, k_prompt, v_prompt, hgrn_prompt, mem_k, mem_v, k_sample, v_sample, hgrn_sample)
```

```python
import contextlib
import numpy as np
import ml_dtypes
import concourse.bass as bass
import concourse.mybir as mybir
from concourse.bass_utils import run_bass_kernel_spmd

F32 = mybir.dt.float32
BF16 = mybir.dt.bfloat16
I32 = mybir.dt.int32
AF = mybir.ActivationFunctionType
ALU = mybir.AluOpType
AX = mybir.AxisListType

D = 1024
S = 4096
NT = S // 512
NB = S // 128
DIN = 3584
EPS = 1e-6
NSAMP = 4
NPAGE = 64
N_DMA_SEM = 12
RING = 3 * N_DMA_SEM

ENABLE_SAMPLE = True
N_TILES = NT
STAGE = 99
NCORES = 8
POOL_PAGES = 2560
NO_SCRATCH_READ = False
DBG_PRE = False
DBG_KBS = None


class Tok:
    __slots__ = ("w", "r", "ps", "old")

    def __init__(self, ps=False, old=None):
        self.w = None
        self.r = []
        self.ps = ps
        self.old = old


class Prog:
    def __init__(self, nc, es):
        self.nc = nc
        self.es = es
        self.ops = []
        self.hook = None
        self.every = 6
        self._n = 0
        self._in_hook = False
        self.eng = {"pe": nc.tensor, "act": nc.scalar, "dve": nc.vector, "pool": nc.gpsimd, "sp": nc.sync}

    def op(self, eng, fn, reads=(), writes=(), dma=False):
        idx = len(self.ops)
        deps = set()
        for t in reads:
            if t.w is not None:
                deps.add(t.w)
            if t.ps:
                deps.update(r for r in t.r if self.ops[r][0] != eng)
        for t in writes:
            if t.w is not None:
                deps.add(t.w)
            deps.update(t.r)
            if t.old:
                for o in t.old:
                    if o.w is not None:
                        deps.add(o.w)
                    deps.update(o.r)
                t.old = None
        deps.discard(idx)
        self.ops.append([eng, fn, deps, dma])
        for t in reads:
            t.r.append(idx)
        for t in writes:
            t.w = idx
            t.r = []
        if self.hook is not None and not self._in_hook:
            self._n += 1
            if self._n % self.every == 0:
                self._in_hook = True
                try:
                    self.hook()
                finally:
                    self._in_hook = False
        return idx

    def dma(self, out, in_, reads=(), writes=(), q="sp", **kw):
        e = self.eng[q]
        return self.op(q, lambda: e.dma_start(out=out, in_=in_, **kw), reads, writes, dma=True)

    def gather(self, out, in_, idx_ap, reads=(), writes=()):
        g = self.nc.gpsimd
        return self.op("pool", lambda: g.indirect_dma_start(
            out=out, out_offset=None, in_=in_, in_offset=bass.IndirectOffsetOnAxis(ap=idx_ap, axis=0)),
            reads, writes, dma=True)

    def act(self, out, in_, func, reads=(), writes=(), **kw):
        a = self.nc.scalar
        return self.op("act", lambda: a.activation(out=out, in_=in_, func=func, **kw), reads, writes)

    def tt(self, out, in0, in1, op, reads=(), writes=(), eng="dve"):
        e = self.eng[eng]
        return self.op(eng, lambda: e.tensor_tensor(out=out, in0=in0, in1=in1, op=op), reads, writes)

    def ts(self, out, in0, s1, s2, op0, op1=None, reads=(), writes=(), eng="dve"):
        e = self.eng[eng]
        if op1 is None:
            return self.op(eng, lambda: e.tensor_scalar(out=out, in0=in0, scalar1=s1, scalar2=None, op0=op0),
                           reads, writes)
        return self.op(eng, lambda: e.tensor_scalar(out=out, in0=in0, scalar1=s1, scalar2=s2, op0=op0, op1=op1),
                       reads, writes)

    def stt(self, out, in0, scalar, in1, op0, op1, reads=(), writes=()):
        e = self.nc.vector
        return self.op("dve", lambda: e.scalar_tensor_tensor(out=out, in0=in0, scalar=scalar, in1=in1,
                                                             op0=op0, op1=op1), reads, writes)

    def copy(self, out, in_, reads=(), writes=(), eng="dve"):
        e = self.eng[eng]
        if eng == "act":
            return self.op(eng, lambda: e.activation(out=out, in_=in_, func=AF.Copy), reads, writes)
        return self.op(eng, lambda: e.tensor_copy(out=out, in_=in_), reads, writes)

    def recip(self, out, in_, reads=(), writes=()):
        e = self.nc.vector
        return self.op("dve", lambda: e.reciprocal(out=out, in_=in_), reads, writes)

    def reduce(self, out, in_, op, reads=(), writes=()):
        e = self.nc.vector
        return self.op("dve", lambda: e.tensor_reduce(out=out, in_=in_, axis=AX.X, op=op), reads, writes)

    def scan(self, out, d0, d1, reads=(), writes=()):
        e = self.nc.vector
        return self.op("dve", lambda: e.tensor_tensor_scan(out=out, data0=d0, data1=d1, initial=0.0,
                                                           op0=ALU.mult, op1=ALU.add), reads, writes)

    def memset(self, ap, val, writes=(), eng="pool"):
        e = self.eng[eng]
        return self.op(eng, lambda: e.memset(ap, val), (), writes)

    def mm(self, out, lhsT, rhs, start=True, stop=True, reads=(), writes=(), tp=None):
        t = self.nc.tensor
        if tp is None:
            return self.op("pe", lambda: t.matmul(out, lhsT=lhsT, rhs=rhs, start=start, stop=stop), reads, writes)
        return self.op("pe", lambda: t.matmul(out, lhsT=lhsT, rhs=rhs, start=start, stop=stop, tile_position=tp),
                       reads, writes)

    def tr(self, out, in_, ident, reads=(), writes=()):
        t = self.nc.tensor
        return self.op("pe", lambda: t.transpose(out=out, in_=in_, identity=ident), reads, writes)

    def emit(self):
        nc, es = self.nc, self.es
        ops = self.ops
        n = len(ops)
        comp = ("pe", "act", "dve", "pool")
        pos = [0] * n
        cnt = {e: 0 for e in comp}
        qbase = {"sp": 0, "pool": N_DMA_SEM, "act": 2 * N_DMA_SEM}
        dq = {q: [] for q in qbase}
        for i, (eng, fn, deps, dma) in enumerate(ops):
            if dma:
                k = len(dq[eng])
                pos[i] = (k // N_DMA_SEM) * RING + qbase[eng] + (k % N_DMA_SEM)
                dq[eng].append(i)
            else:
                pos[i] = cnt[eng]
                cnt[eng] += 1
        dma_ops = [i for i in range(n) if ops[i][3]]
        for q, lst in dq.items():
            for k, i in enumerate(lst):
                if k >= N_DMA_SEM:
                    ops[i][2].add(lst[k - N_DMA_SEM])
        waited = {}
        marked = [False] * n
        waits = [None] * n
        for i, (eng, fn, deps, dma) in enumerate(ops):
            wl = []
            for d in sorted(deps):
                deng, _, _, ddma = ops[d]
                if ddma:
                    key = (eng, "dma", pos[d] % RING)
                    val = pos[d] // RING
                else:
                    if deng == eng:
                        if eng == "pe":
                            continue
                        if eng != "pool" and pos[i] - pos[d] > 2:
                            continue
                    key = (eng, deng)
                    val = pos[d]
                if waited.get(key, -1) >= val:
                    continue
                waited[key] = val
                wl.append(d)
                marked[d] = True
            waits[i] = wl
        sem = {e: es.enter_context(nc.semaphore("s_" + e)) for e in comp}
        dsem = [es.enter_context(nc.semaphore("s_dma%d" % k)) for k in range(RING)]
        val = [0] * n
        c2 = {e: 0 for e in comp}
        for i, (eng, fn, deps, dma) in enumerate(ops):
            if dma:
                val[i] = 16 * (pos[i] // RING + 1)
            elif marked[i]:
                c2[eng] += 1
                val[i] = c2[eng]
        for i, (eng, fn, deps, dma) in enumerate(ops):
            e = self.eng[eng]
            for d in waits[i]:
                deng, _, _, ddma = ops[d]
                if ddma:
                    e.wait_ge(dsem[pos[d] % RING], val[d])
                else:
                    e.wait_ge(sem[deng], val[d])
            ins = fn()
            if dma:
                ins.then_inc(dsem[pos[i] % RING], 16)
            elif marked[i]:
                ins.then_inc(sem[eng], 1)
        last = {}
        for i in dma_ops:
            last[pos[i] % RING] = val[i]
        for k, v in last.items():
            nc.sync.wait_ge(dsem[k], v)
        for e in comp:
            if c2[e] > 0:
                nc.sync.wait_ge(sem[e], c2[e])


def build_program():
    nc = bass.Bass("TRN2", target_bir_lowering=False)
    es = contextlib.ExitStack()
    P = Prog(nc, es)

    def din(name, shape, dt=F32):
        return nc.dram_tensor(name, list(shape), dt, kind="ExternalInput").ap()

    def dout(name, shape, dt=F32):
        return nc.dram_tensor(name, list(shape), dt, kind="ExternalOutput").ap()

    x_d = din("x", [S // 2, D])
    xo_d = din("xo", [S // 2, D])
    kb_d = din("kbias", [128, 1])
    mem_d = din("mem", [256, D])
    w_in_d = din("w_in", [D, DIN])
    w_out_d = din("w_out", [D, D])
    w_mem_d = din("w_mem", [D, 512])
    gin_d = din("gin", [128, 8])
    gmem_d = din("gmem", [128, 8])
    fg_d = din("fg", [1, D])
    hgn_d = din("hgn", [128, 2])
    sbb_d = din("sbb", [1, 8])
    lbl_d = din("lbl", [1, 512])
    cf_d = din("cf", [128, 514])
    cb_d = din("cb", [128, 384 + 2048], BF16)
    y_d = dout("y", [S // 2, D])
    k_d = dout("k", [S // 2, 512])
    v_d = dout("v", [S // 2, 512])
    hgs_d = dout("hgs", [128, 128])
    mk_d = dout("mk", [256, 256])
    mv_d = dout("mv", [256, 256])
    wsc_d = nc.dram_tensor("wsc", [9, 128, 4096], BF16, kind="Internal").ap()
    if ENABLE_SAMPLE:
        xs_d = din("xs", [NSAMP, D])
        pt_d = din("pt", [1, NSAMP * NPAGE], I32)
        ckv_d = din("ckv", [POOL_PAGES * 128, 1024])
        sst_d = din("sst", [NSAMP, 128, 128])
        cmk_d = din("cmk", [NSAMP, 256, 256])
        cmv_d = din("cmv", [NSAMP, 256, 256])
        ys_d = dout("ys", [NSAMP, D])
        ks_d = dout("ks", [NSAMP, 512])
        vs_d = dout("vs", [NSAMP, 512])
        hss_d = dout("hss", [NSAMP, 128, 128])
        hgr_d = din("hgr", [1, 256])
        scr_d = nc.dram_tensor("scr", [NSAMP, DIN], F32, kind="Internal").ap()

    cnt = [0]

    def sb(shape, dt=F32, name=None):
        cnt[0] += 1
        return es.enter_context(nc.sbuf_tensor("sb_" + (name or ("t%d" % cnt[0])), list(shape), dt))

    def psum(shape, dt=F32):
        cnt[0] += 1
        return es.enter_context(nc.psum_tensor("p%d" % cnt[0], list(shape), dt))

    def T():
        return Tok()

    KT = sb([128, 4, S], BF16, "KT")
    kt_t = [[T() for _ in range(NB)] for _ in range(4)]
    VA = sb([128, NB, 512], BF16, "VA")
    va_t = [T() for _ in range(NB)]
    MKT = sb([128, 2, 256], BF16, "MKT")
    MV = sb([128, 2, 256], BF16, "MV")
    mk_t = T()
    cf = sb([128, 514], F32, "cf")
    cb = sb([128, 384 + 2048], BF16, "cb")
    c_t = T()
    ident_f, TRI, SU, OB = cf[:, 0:128], cf[:, 128:256], cf[:, 256:384], cf[:, 384:512]
    IOTA, ONEC = cf[:, 512:513], cf[:, 513:514]
    ident_b, UIN, LST = cb[:, 0:128], cb[:, 128:256], cb[:, 256:384]
    MSK = [cb[:, 384 + 512 * d: 384 + 512 * (d + 1)] for d in range(4)]
    ONESB = sb([128, 128], BF16, "onesb")
    gin = sb([128, 8], F32, "gin")
    gmem = sb([128, 8], F32, "gmem")
    fgb = sb([128, D], F32, "fgb")
    hgn = sb([128, 2], F32, "hgn")
    sbb = sb([128, 8], F32, "sbb")
    sbbp = sb([128, 8], F32, "sbbp")
    kbias = sb([128, 1], F32, "kbias")
    lbl = sb([128, 512], F32, "lbl")
    lb = sb([128, 256], F32, "lb")
    oml = sb([128, 256], F32, "oml")
    Sst = sb([128, 128], F32, "Sst")
    s_t = T()

    PF = [psum([128, 512], F32) for _ in range(6)]
    pf_t = [Tok(ps=True) for _ in range(6)]
    PB = [psum([128, 1024], BF16) for _ in range(2)]
    pb_t = [Tok(ps=True) for _ in range(2)]

    P.dma(cf[:], cf_d, writes=[c_t])
    P.dma(cb[:], cb_d, writes=[c_t])
    P.dma(gin[:], gin_d, writes=[c_t])
    P.dma(gmem[:], gmem_d, writes=[c_t])
    P.dma(fgb[:], fg_d.partition_broadcast(128), writes=[c_t])
    P.dma(hgn[:], hgn_d, writes=[c_t])
    P.dma(sbb[:], sbb_d.partition_broadcast(128), writes=[c_t])
    P.dma(lbl[:], lbl_d.partition_broadcast(128), writes=[c_t])
    P.dma(kbias[:], kb_d, writes=[c_t])
    P.ts(sbbp[:], sbb[:], kbias[:, 0:1], None, ALU.add, reads=[c_t], writes=[c_t])
    P.memset(ONESB[:], 1.0, writes=[c_t], eng="dve")
    P.memset(Sst[:], 0.0, writes=[s_t], eng="dve")
    P.tt(lb[:], lbl[:, 256:512], lbl[:, 0:256], ALU.subtract, reads=[c_t], writes=[c_t])
    P.act(lb[:], lb[:], AF.Exp, reads=[c_t], writes=[c_t])
    P.ts(lb[:], lb[:], 1.0, None, ALU.add, reads=[c_t], writes=[c_t])
    P.recip(lb[:], lb[:], reads=[c_t], writes=[c_t])
    P.ts(oml[:], lb[:], -1.0, 1.0, ALU.mult, ALU.add, reads=[c_t], writes=[c_t])

    if STAGE == 0:
        P.emit()
        return nc
    xst = [sb([128, D], F32) for _ in range(2)]
    xst_t = [T() for _ in range(2)]
    xsb = [sb([128, D], BF16) for _ in range(2)]
    xsb_t = [T() for _ in range(2)]
    st4 = [sb([128, 4], F32) for _ in range(2)]
    st4_t = [T() for _ in range(2)]
    wst = [sb([128, 512], F32) for _ in range(2)]
    wst_t = [T() for _ in range(2)]
    WG = [sb([128, 8, 512], BF16) for _ in range(2)]
    wg_t = [T() for _ in range(2)]
    wsc_t = [T() for _ in range(9)]
    rr = {"x": 0, "w": 0, "wg": 0, "pf": 0, "pb": 0, "npf": 6}

    def rmsnorm_to_bf(src_ap, reads, which):
        st = st4[which]
        stt_ = st4_t[which]
        P.act(xsb[which][:], src_ap, AF.Square, reads=reads, writes=[xsb_t[which], stt_], accum_out=st[:, 0:1])
        P.ts(st[:, 1:2], st[:, 0:1], 1.0 / D, EPS, ALU.mult, ALU.add, reads=[stt_], writes=[stt_])
        P.act(st[:, 2:3], st[:, 1:2], AF.Ln, reads=[stt_], writes=[stt_])
        P.act(st[:, 3:4], st[:, 2:3], AF.Exp, reads=[stt_], writes=[stt_], scale=-0.5)
        P.act(xsb[which][:], src_ap, AF.Copy, reads=list(reads) + [stt_], writes=[xsb_t[which]], scale=st[:, 3:4])

    def transpose_to(dst_ap_fn, which, dst_toks):
        pb = rr["pb"] % 2
        rr["pb"] += 1
        for kb in range(8):
            P.tr(PB[pb][:, kb * 128:(kb + 1) * 128], xsb[which][:, kb * 128:(kb + 1) * 128], ident_b,
                 reads=[xsb_t[which], c_t], writes=[pb_t[pb]])
        P.copy(dst_ap_fn(), PB[pb][:, :].rearrange("p (k t) -> p k t", k=8), reads=[pb_t[pb]], writes=dst_toks)

    for g in range(9):
        wgi = rr["wg"] % 2
        rr["wg"] += 1
        for kb in range(8):
            wi = rr["w"] % 2
            rr["w"] += 1
            if g < 7:
                src = w_in_d[kb * 128:(kb + 1) * 128, g * 512:(g + 1) * 512]
            else:
                src = w_out_d[kb * 128:(kb + 1) * 128, (g - 7) * 512:(g - 6) * 512]
            P.dma(wst[wi][:], src, writes=[wst_t[wi]])
            if g < 7:
                if kb % 2 == 0:
                    P.ts(WG[wgi][:, kb, :], wst[wi][:], gin[:, kb:kb + 1], None, ALU.mult,
                         reads=[wst_t[wi], c_t], writes=[wg_t[wgi]])
                else:
                    P.act(WG[wgi][:, kb, :], wst[wi][:], AF.Copy, reads=[wst_t[wi], c_t], writes=[wg_t[wgi]],
                          scale=gin[:, kb:kb + 1])
            else:
                P.copy(WG[wgi][:, kb, :], wst[wi][:], reads=[wst_t[wi]], writes=[wg_t[wgi]],
                       eng="dve" if kb % 2 == 0 else "pool")
        P.dma(wsc_d[g], WG[wgi][:, :, :].rearrange("p k c -> p (k c)"), reads=[wg_t[wgi]], writes=[wsc_t[g]])

    if STAGE == 1:
        P.emit()
        return nc

    def load_wgroup(g):
        wgi = rr["wg"] % 2
        rr["wg"] += 1
        if not NO_SCRATCH_READ:
            P.dma(WG[wgi][:, :, :].rearrange("p k c -> p (k c)"), wsc_d[g], reads=[wsc_t[g]], writes=[wg_t[wgi]])
        return wgi

    xnT = sb([128, 8, 512], BF16, "xnT")
    xnT_t = [T() for _ in range(4)]
    memT = [xnT[:, :, mb_ * 128:(mb_ + 1) * 128] for mb_ in range(2)]
    memT_t = [xnT_t[0], xnT_t[1]]
    for mb in range(2):
        xi = rr["x"] % 2
        rr["x"] += 1
        P.dma(xst[xi][:], mem_d[mb * 128:(mb + 1) * 128, :], writes=[xst_t[xi]])
        rmsnorm_to_bf(xst[xi][:], [xst_t[xi]], xi)
        transpose_to(lambda mb=mb: memT[mb], xi, [memT_t[mb]])
    wmb = [sb([128, 512], BF16) for _ in range(2)]
    wmb_t = [T() for _ in range(2)]
    for kb in range(8):
        wi = rr["w"] % 2
        rr["w"] += 1
        P.dma(wst[wi][:], w_mem_d[kb * 128:(kb + 1) * 128, :], writes=[wst_t[wi]])
        P.ts(wmb[kb % 2][:], wst[wi][:], gmem[:, kb:kb + 1], None, ALU.mult, reads=[wst_t[wi], c_t],
             writes=[wmb_t[kb % 2]])
        for mb in range(2):
            P.mm(PF[mb][:, :], xnT[:, kb, mb * 128:(mb + 1) * 128], wmb[kb % 2][:], start=(kb == 0), stop=(kb == 7),
                 reads=[memT_t[mb], wmb_t[kb % 2]], writes=[pf_t[mb]])
    kvst = [sb([128, 512], F32) for _ in range(2)]
    kvst_t = [T() for _ in range(2)]
    kbf = [sb([128, 512], BF16) for _ in range(2)]
    kbf_t = [T() for _ in range(2)]
    for mb in range(2):
        P.copy(kvst[mb][:], PF[mb][:, :], reads=[pf_t[mb]], writes=[kvst_t[mb]], eng="act" if mb else "dve")
        P.dma(mk_d[mb * 128:(mb + 1) * 128, :], kvst[mb][:, 0:256], reads=[kvst_t[mb]])
        P.dma(mv_d[mb * 128:(mb + 1) * 128, :], kvst[mb][:, 256:512], reads=[kvst_t[mb]])
        P.copy(kbf[mb][:, 0:256], kvst[mb][:, 0:256], reads=[kvst_t[mb]], writes=[kbf_t[mb]])
        P.copy(MV[:, mb, :], kvst[mb][:, 256:512], reads=[kvst_t[mb]], writes=[mk_t])
        pb = rr["pb"] % 2
        rr["pb"] += 1
        for hp in range(2):
            P.tr(PB[pb][:, hp * 128:(hp + 1) * 128], kbf[mb][:, hp * 128:(hp + 1) * 128], ident_b,
                 reads=[kbf_t[mb], c_t], writes=[pb_t[pb]])
        P.copy(MKT[:, :, mb * 128:(mb + 1) * 128], PB[pb][:, 0:256].rearrange("p (k t) -> p k t", k=2),
               reads=[pb_t[pb]], writes=[mk_t])

    if STAGE == 2:
        P.emit()
        return nc
    QT = sb([128, 4, 512], BF16, "QT")
    qt_t = [T() for _ in range(4)]
    SG = sb([128, 8, 512], BF16, "SG")
    sg_t = [T() for _ in range(8)]
    HQT = sb([128, 2, 512], F32, "HQT")
    hq_t = [T() for _ in range(2)]
    XQT = sb([128, 2, 512], BF16, "XQT")
    xq_t = [T() for _ in range(2)]
    HF = sb([128, 4, 256], F32, "HF")
    HI = sb([128, 4, 256], F32, "HI")
    hf_t = [T() for _ in range(4)]
    hi_t = [T() for _ in range(4)]
    MIX = sb([128, 8, 512], BF16, "MIX")
    mix_t = [T() for _ in range(8)]
    gtmp = [sb([128, 512], F32) for _ in range(2)]
    gtmp_t = [T() for _ in range(2)]
    EB = [sb([128, 512], F32) for _ in range(4)]
    eb_t = [T() for _ in range(4)]
    SPB = wmb + [sb([128, 512], BF16) for _ in range(2)]
    spb_t = wmb_t + [T() for _ in range(2)]
    WB = [sb([128, 512], BF16) for _ in range(2)]
    wb_t = [T() for _ in range(2)]
    AB = [sb([128, 512], BF16) for _ in range(4)]
    ab_t = [T() for _ in range(4)]
    KTf = KT[:, 0, :].bitcast(F32)
    SPACC = [KTf[:, 0:512], KTf[:, 512:1024]]
    spacc_t = [Tok(old=[kt_t[0][kb_] for kb_ in range(NB)]) for _ in range(2)]
    hw = {n_: sb([128, 256], F32, "hw_" + n_) for n_ in ("a", "f", "lf", "k", "kd")}
    hw_t = {n_: T() for n_ in hw}
    hx = {n_: sb([128, 2, 128], F32, "hx_" + n_) for n_ in ("ep", "en", "qe", "ke")}
    hx_t = {n_: T() for n_ in hx}
    hattn = [sb([128, 128], F32) for _ in range(2)]
    hattn_t = [T() for _ in range(2)]
    hdec = sb([128, 4], F32, "hdec")
    hdec_t = T()
    hosb = sb([128, 256], F32, "hosb")
    hosb_t = T()
    hsq = sb([128, 128], F32, "hsq")
    hsq_t = T()
    hrs = sb([128, 128], F32, "hrs")
    hrs_t = T()
    yst = sb([128, D], F32, "yst")
    yst_t = T()

    def pf_next():
        i = rr["pf"] % rr["npf"]
        rr["pf"] += 1
        return i

    def silu_from_psum(pi, dst_ap, dst_tok):
        gi = rr["x"] % 2
        rr["x"] += 1
        P.act(gtmp[gi][:], PF[pi][:, :], AF.Exp, reads=[pf_t[pi]], writes=[gtmp_t[gi]], scale=-1.0)
        P.ts(gtmp[gi][:], gtmp[gi][:], 1.0, None, ALU.add, reads=[gtmp_t[gi]], writes=[gtmp_t[gi]])
        P.recip(gtmp[gi][:], gtmp[gi][:], reads=[gtmp_t[gi]], writes=[gtmp_t[gi]])
        P.tt(dst_ap, PF[pi][:, :], gtmp[gi][:], ALU.mult, reads=[pf_t[pi], gtmp_t[gi]], writes=[dst_tok])

    def feat_proj(wgi, cbs, evac):
        for cbi in cbs:
            pi = pf_next()
            for kb in range(8):
                P.mm(PF[pi][:, :], WG[wgi][:, kb, cbi * 128:(cbi + 1) * 128], xnT[:, kb, :], start=(kb == 0),
                     stop=(kb == 7), reads=[wg_t[wgi]] + xnT_t, writes=[pf_t[pi]])
            evac(cbi, pi)

    def tok_proj(wgi, c0, c1, blk):
        pi = pf_next()
        for kb in range(8):
            P.mm(PF[pi][:, 0:c1 - c0], xnT[:, kb, blk * 128:(blk + 1) * 128], WG[wgi][:, kb, c0:c1],
                 start=(kb == 0), stop=(kb == 7), reads=[wg_t[wgi], xnT_t[blk]], writes=[pf_t[pi]])
        return pi

    def hgrn_block(blk, state_only=False):
        c0 = blk * 128
        tmpb = [0, 1, 2] if state_only else [0, 1, 2, 5]

        def tmp_next():
            i = tmpb[rr["pf"] % len(tmpb)]
            rr["pf"] += 1
            return i
        a, f, lf, k, kd = hw["a"], hw["f"], hw["lf"], hw["k"], hw["kd"]
        P.act(a[:], HF[:, blk, :], AF.Exp, reads=[hf_t[blk]], writes=[hw_t["a"]], scale=-1.0)
        P.ts(a[:], a[:], 1.0, None, ALU.add, reads=[hw_t["a"]], writes=[hw_t["a"]])
        P.recip(a[:], a[:], reads=[hw_t["a"]], writes=[hw_t["a"]])
        P.tt(f[:], a[:], oml[:], ALU.mult, reads=[hw_t["a"], c_t], writes=[hw_t["f"]])
        P.tt(f[:], f[:], lb[:], ALU.add, reads=[hw_t["f"], c_t], writes=[hw_t["f"]])
        P.act(lf[:], f[:], AF.Ln, reads=[hw_t["f"]], writes=[hw_t["lf"]])
        P.ts(k[:], f[:], -1.0, 1.0, ALU.mult, ALU.add, reads=[hw_t["f"]], writes=[hw_t["k"]])
        if STAGE == 3.1:
            return
        p_rev = tmp_next()
        P.mm(PF[p_rev][:, 0:256], SU, lf[:], reads=[c_t, hw_t["lf"]], writes=[pf_t[p_rev]])
        P.act(kd[:], PF[p_rev][:, 0:256], AF.Exp, reads=[pf_t[p_rev]], writes=[hw_t["kd"]])
        P.tt(kd[:], kd[:], k[:], ALU.mult, reads=[hw_t["kd"], hw_t["k"]], writes=[hw_t["kd"]])
        if STAGE == 3.2:
            return
        p_bc = tmp_next()
        for hp in range(2):
            P.mm(PF[p_bc][:, hp * 128:(hp + 1) * 128], lf[:, hp * 128:(hp + 1) * 128], TRI,
                 reads=[hw_t["lf"], c_t], writes=[pf_t[p_bc]])
        P.act(hx["ep"][:, :, :], PF[p_bc][:, 0:256].rearrange("p (k t) -> p k t", k=2), AF.Exp,
              reads=[pf_t[p_bc]], writes=[hx_t["ep"]])
        if not state_only:
            P.act(hx["en"][:, :, :], PF[p_bc][:, 0:256].rearrange("p (k t) -> p k t", k=2), AF.Exp,
                  reads=[pf_t[p_bc]], writes=[hx_t["en"]], scale=-1.0)
            P.tt(hx["qe"][:, :, :], hx["ep"][:, :, :], HQT[:, :, c0:c0 + 128], ALU.mult,
                 reads=[hx_t["ep"]] + hq_t, writes=[hx_t["qe"]])
            if STAGE == 3.3:
                return
            p_kt = tmp_next()
            for hp in range(2):
                P.tr(PF[p_kt][:, hp * 128:(hp + 1) * 128], k[:, hp * 128:(hp + 1) * 128], ident_f,
                     reads=[hw_t["k"], c_t], writes=[pf_t[p_kt]])
            P.tt(hx["ke"][:, :, :], PF[p_kt][:, 0:256].rearrange("p (k t) -> p k t", k=2), hx["en"][:, :, :], ALU.mult,
                 reads=[pf_t[p_kt], hx_t["en"]], writes=[hx_t["ke"]])
        for c in range(2):
            for hp in range(2):
                P.copy(hdec[:, c * 2 + hp: c * 2 + hp + 1], hx["ep"][:, hp, c * 64 + 63: c * 64 + 64],
                       reads=[hx_t["ep"]], writes=[hdec_t])
        if STAGE == 3.4:
            return
        if not state_only:
            p_o = 3
            for h in range(4):
                hp, rb = h // 2, (h % 2) * 64
                p_at = tmp_next()
                P.mm(PF[p_at][:, 0:128], hx["ke"][rb:rb + 64, hp, :], hx["qe"][rb:rb + 64, hp, :],
                     reads=[hx_t["ke"], hx_t["qe"]], writes=[pf_t[p_at]], tp=(rb, 0))
                ai = h % 2
                P.tt(hattn[ai][:], PF[p_at][:, 0:128], TRI, ALU.mult, reads=[pf_t[p_at], c_t], writes=[hattn_t[ai]])
                P.mm(PF[p_o][rb:rb + 64, hp * 128:(hp + 1) * 128], HI[:, blk, h * 64:(h + 1) * 64], hattn[ai][:],
                     start=True, stop=True, reads=[hi_t[blk], hattn_t[ai]], writes=[pf_t[p_o]], tp=(0, rb))
            if STAGE == 3.5:
                return
        p_i = 4
        for c in range(2):
            for h in range(4):
                if STAGE == 3.55 or state_only:
                    break
                hp, rb = h // 2, (h % 2) * 64
                P.mm(PF[p_i][rb:rb + 64, hp * 128 + c * 64: hp * 128 + c * 64 + 64],
                     Sst[rb:rb + 64, hp * 64:(hp + 1) * 64], hx["qe"][rb:rb + 64, hp, c * 64:(c + 1) * 64],
                     start=True, stop=True, reads=[s_t, hx_t["qe"]], writes=[pf_t[p_i]], tp=(rb, rb))
            if STAGE == 3.57:
                continue
            p_s = tmp_next()
            for h in range(4):
                hp, rb = h // 2, (h % 2) * 64
                P.mm(PF[p_s][rb:rb + 64, hp * 64:(hp + 1) * 64], kd[c * 64:(c + 1) * 64, h * 64:(h + 1) * 64],
                     HI[c * 64:(c + 1) * 64, blk, h * 64:(h + 1) * 64], reads=[hw_t["kd"], hi_t[blk]],
                     writes=[pf_t[p_s]], tp=(c * 64, rb))
            for hp in range(2):
                P.stt(Sst[:, hp * 64:(hp + 1) * 64], Sst[:, hp * 64:(hp + 1) * 64],
                      hdec[:, c * 2 + hp: c * 2 + hp + 1], PF[p_s][:, hp * 64:(hp + 1) * 64], ALU.mult, ALU.add,
                      reads=[s_t, hdec_t, pf_t[p_s]], writes=[s_t])
        if STAGE in (3.55, 3.57, 3.6) or state_only:
            return
        P.copy(hosb[:], PF[p_i][:, 0:256], reads=[pf_t[p_i]], writes=[hosb_t], eng="act")
        P.tt(hosb[:], hosb[:], PF[p_o][:, 0:256], ALU.add, reads=[hosb_t, pf_t[p_o]], writes=[hosb_t])
        for hp in range(2):
            P.act(hsq[:], hosb[:, hp * 128:(hp + 1) * 128], AF.Square, reads=[hosb_t], writes=[hsq_t])
            p_n = tmp_next()
            P.mm(PF[p_n][:, 0:128], OB, hsq[:], reads=[c_t, hsq_t], writes=[pf_t[p_n]])
            P.ts(hrs[:], PF[p_n][:, 0:128], 1.0 / 64, EPS, ALU.mult, ALU.add, reads=[pf_t[p_n]], writes=[hrs_t])
            P.act(hrs[:], hrs[:], AF.Ln, reads=[hrs_t], writes=[hrs_t])
            P.act(hrs[:], hrs[:], AF.Exp, reads=[hrs_t], writes=[hrs_t], scale=-0.5)
            P.tt(hrs[:], hosb[:, hp * 128:(hp + 1) * 128], hrs[:], ALU.mult, reads=[hosb_t, hrs_t],
                 writes=[hrs_t])
            P.stt(MIX[:, 4 + hp, c0:c0 + 128], hrs[:], hgn[:, hp:hp + 1], SG[:, 4 + hp, c0:c0 + 128], ALU.mult,
                  ALU.mult, reads=[hrs_t, c_t, sg_t[4 + hp]], writes=[mix_t[4 + hp]])

    def xattn():
        for hp in range(2):
            p_o, p_d = pf_next(), pf_next()
            for hh in range(2):
                h, rb = hp * 2 + hh, hh * 64
                for mb in range(2):
                    p_s = pf_next()
                    P.mm(PF[p_s][:, :], MKT[rb:rb + 64, hp, mb * 128:(mb + 1) * 128], XQT[rb:rb + 64, hp, :],
                         reads=[mk_t, xq_t[hp]], writes=[pf_t[p_s]], tp=(rb, 0))
                    ai = rr["x"] % 2
                    rr["x"] += 1
                    P.act(AB[ai][:], PF[p_s][:, :], AF.Exp, reads=[pf_t[p_s]], writes=[ab_t[ai]])
                    P.mm(PF[p_o][rb:rb + 64, :], MV[:, mb, h * 64:(h + 1) * 64], AB[ai][:], start=(mb == 0),
                         stop=(mb == 1), reads=[mk_t, ab_t[ai]], writes=[pf_t[p_o]], tp=(0, rb))
                    P.mm(PF[p_d][rb:rb + 64, :], ONESB[:, 0:64], AB[ai][:], start=(mb == 0), stop=(mb == 1),
                         reads=[c_t, ab_t[ai]], writes=[pf_t[p_d]], tp=(0, rb))
            gi = rr["x"] % 2
            rr["x"] += 1
            P.recip(gtmp[gi][:], PF[p_d][:, :], reads=[pf_t[p_d]], writes=[gtmp_t[gi]])
            P.tt(gtmp[gi][:], PF[p_o][:, :], gtmp[gi][:], ALU.mult, reads=[pf_t[p_o], gtmp_t[gi]],
                 writes=[gtmp_t[gi]])
            P.tt(MIX[:, 6 + hp, :], gtmp[gi][:], SG[:, 6 + hp, :], ALU.mult, reads=[gtmp_t[gi], sg_t[6 + hp]],
                 writes=[mix_t[6 + hp]])

    def sb_attention(qi):
        nkb = 4 * qi + 4
        for hp in range(4):
            av = 4 + (hp % 2)
            kbs = list(range(nkb - 1, -1, -1))
            n = len(kbs)

            def stage_a1(i):
                kb = kbs[i]
                r = i % 2
                for hh in range(2):
                    rb = hh * 64
                    P.mm(PF[hh][:, :], KT[rb:rb + 64, hp, kb * 128:(kb + 1) * 128], QT[rb:rb + 64, hp, :],
                         reads=[kt_t[hp][kb], qt_t[hp]], writes=[pf_t[hh]], tp=(rb, 0))
                z = kb - 4 * qi
                bsrc = sbbp if kb < 4 * HALF else sbb
                for hh in range(2):
                    h = hp * 2 + hh
                    e = hh * 2 + r
                    P.act(EB[e][:], PF[hh][:, :], AF.Exp, reads=[pf_t[hh], c_t], writes=[eb_t[e]],
                          bias=bsrc[:, h:h + 1])
                    if z >= 0:
                        P.tt(EB[e][:], EB[e][:], MSK[z], ALU.mult, reads=[eb_t[e], c_t], writes=[eb_t[e]])

            def stage_a2(i):
                r = i % 2
                for hh in range(2):
                    e = hh * 2 + r
                    P.act(SPB[e][:], EB[e][:], AF.Ln, reads=[eb_t[e]], writes=[spb_t[e]], bias=1.0)

            def stage_b(i):
                kb = kbs[i]
                r = i % 2
                for hh in range(2):
                    e = hh * 2 + r
                    P.mm(PF[2 + hh][:, :], UIN, SPB[e][:], start=(kb == nkb - 1), stop=(kb == 0),
                         reads=[c_t, spb_t[e]], writes=[pf_t[2 + hh]])
                for hh in range(2):
                    e = hh * 2 + r
                    P.act(WB[hh][:], PF[2 + hh][:, :], AF.Exp, reads=[pf_t[2 + hh]], writes=[wb_t[hh]], scale=-1.0)
                    P.tt(AB[e][:], EB[e][:], WB[hh][:], ALU.mult, reads=[eb_t[e], wb_t[hh]], writes=[ab_t[e]])

            def stage_b2(i):
                kb = kbs[i]
                r = i % 2
                if kb > 0:
                    for hh in range(2):
                        e = hh * 2 + r
                        P.mm(PF[2 + hh][:, :], LST, SPB[e][:], start=False, stop=False, reads=[c_t, spb_t[e]],
                             writes=[pf_t[2 + hh]])

            def stage_c(i):
                kb = kbs[i]
                r = i % 2
                for hh in range(2):
                    rb = hh * 64
                    e = hh * 2 + r
                    h = hp * 2 + hh
                    P.mm(PF[av][rb:rb + 64, :], VA[:, kb, h * 64:(h + 1) * 64], AB[e][:], start=(kb == nkb - 1),
                         stop=(kb == 0), reads=[va_t[kb], ab_t[e]], writes=[pf_t[av]], tp=(0, rb))

            for i in range(n + 2):
                if i < n:
                    stage_a1(i)
                if 0 <= i - 2 < n:
                    stage_b2(i - 2)
                if 0 <= i - 1 < n:
                    stage_b(i - 1)
                if i < n:
                    stage_a2(i)
                if 0 <= i - 2 < n:
                    stage_c(i - 2)
            P.tt(MIX[:, hp, :], PF[av][:, :], SG[:, hp, :], ALU.mult, reads=[pf_t[av], sg_t[hp]],
                 writes=[mix_t[hp]])

    HALF = NT // 2

    def tile(ti):
        own = ti >= HALF
        src = xo_d if own else x_d
        r0 = (ti - HALF) * 512 if own else ti * 512
        for blk in range(4):
            xi = rr["x"] % 2
            rr["x"] += 1
            P.dma(xst[xi][:], src[r0 + blk * 128: r0 + (blk + 1) * 128, :], writes=[xst_t[xi]])
            rmsnorm_to_bf(xst[xi][:], [xst_t[xi]], xi)
            transpose_to(lambda blk=blk: xnT[:, :, blk * 128:(blk + 1) * 128], xi, [xnT_t[blk]])
        rr["pf"] = 0
        wgi = load_wgroup(1)
        for blk in range(4):
            gb = ti * 4 + blk
            pi = tok_proj(wgi, 0, 512, blk)
            si = gb % 2
            P.copy(kvst[si][:], PF[pi][:, :], reads=[pf_t[pi]], writes=[kvst_t[si]], eng="act")
            if own:
                P.dma(k_d[r0 + blk * 128: r0 + (blk + 1) * 128, :], kvst[si][:], reads=[kvst_t[si]])
            P.copy(kbf[si][:], kvst[si][:], reads=[kvst_t[si]], writes=[kbf_t[si]])
            pb = rr["pb"] % 2
            rr["pb"] += 1
            for hp in range(4):
                P.tr(PB[pb][:, hp * 128:(hp + 1) * 128], kbf[si][:, hp * 128:(hp + 1) * 128], ident_b,
                     reads=[kbf_t[si], c_t], writes=[pb_t[pb]])
            P.copy(KT[:, :, gb * 128:(gb + 1) * 128], PB[pb][:, 0:512].rearrange("p (k t) -> p k t", k=4),
                   reads=[pb_t[pb]], writes=[kt_t[hp_][gb] for hp_ in range(4)])
        wgi = load_wgroup(2)
        for blk in range(4):
            gb = ti * 4 + blk
            pi = tok_proj(wgi, 0, 512, blk)
            si = gb % 2
            P.copy(kvst[si][:], PF[pi][:, :], reads=[pf_t[pi]], writes=[kvst_t[si]], eng="act")
            if own:
                P.dma(v_d[r0 + blk * 128: r0 + (blk + 1) * 128, :], kvst[si][:], reads=[kvst_t[si]])
            P.copy(VA[:, gb, :], kvst[si][:], reads=[kvst_t[si]], writes=[va_t[gb]])
        if own:
            wgi = load_wgroup(0)
            feat_proj(wgi, range(4), lambda cbi, pi: P.act(QT[:, cbi, :], PF[pi][:, :], AF.Copy, reads=[pf_t[pi]],
                                                          writes=[qt_t[cbi]], scale=0.125))
            wgi = load_wgroup(3)
            feat_proj(wgi, range(4), lambda cbi, pi: silu_from_psum(pi, SG[:, cbi, :], sg_t[cbi]))
        wgi = load_wgroup(4)
        if own:
            feat_proj(wgi, range(2), lambda cbi, pi: P.copy(HQT[:, cbi, :], PF[pi][:, :], reads=[pf_t[pi]],
                                                            writes=[hq_t[cbi]], eng="act"))
        for blk in range(4):
            pi = tok_proj(wgi, 256, 512, blk)
            P.copy(HF[:, blk, :], PF[pi][:, 0:256], reads=[pf_t[pi]], writes=[hf_t[blk]])
        wgi = load_wgroup(5)
        for blk in range(4):
            pi = tok_proj(wgi, 0, 256, blk)
            P.copy(HI[:, blk, :], PF[pi][:, 0:256], reads=[pf_t[pi]], writes=[hi_t[blk]], eng="act")
        if own:
            feat_proj(wgi, range(2, 4), lambda cbi, pi: silu_from_psum(pi, SG[:, 2 + cbi, :], sg_t[2 + cbi]))
            wgi = load_wgroup(6)
            feat_proj(wgi, range(2), lambda cbi, pi: P.act(XQT[:, cbi, :], PF[pi][:, :], AF.Copy, reads=[pf_t[pi]],
                                                           writes=[xq_t[cbi]], scale=0.125))
            feat_proj(wgi, range(2, 4), lambda cbi, pi: silu_from_psum(pi, SG[:, 4 + cbi, :], sg_t[4 + cbi]))
        for blk in range(4):
            hgrn_block(blk, state_only=not own)
        if not own:
            return
        xattn()
        sb_attention(ti)
        wg0 = load_wgroup(7)
        wg1 = load_wgroup(8)
        for blk in range(4):
            xi = rr["x"] % 2
            rr["x"] += 1
            P.dma(xst[xi][:], src[r0 + blk * 128: r0 + (blk + 1) * 128, :], writes=[xst_t[xi]])
            for half, wg in ((0, wg0), (1, wg1)):
                pi = pf_next()
                kbs = list(range(8)) if DBG_KBS is None else list(DBG_KBS)
                for kb in kbs:
                    P.mm(PF[pi][:, :], MIX[:, kb, blk * 128:(blk + 1) * 128], WG[wg][:, kb, :], start=(kb == kbs[0]),
                         stop=(kb == kbs[-1]), reads=[mix_t[kb], wg_t[wg]], writes=[pf_t[pi]])
                P.tt(yst[:, half * 512:(half + 1) * 512], PF[pi][:, :], xst[xi][:, half * 512:(half + 1) * 512],
                     ALU.add, reads=[pf_t[pi], xst_t[xi]], writes=[yst_t])
            st, stt_ = st4[xi], st4_t[xi]
            P.act(xsb[xi][:], yst[:], AF.Square, reads=[yst_t], writes=[xsb_t[xi], stt_], accum_out=st[:, 0:1])
            P.ts(st[:, 1:2], st[:, 0:1], 1.0 / D, EPS, ALU.mult, ALU.add, reads=[stt_], writes=[stt_])
            P.act(st[:, 2:3], st[:, 1:2], AF.Ln, reads=[stt_], writes=[stt_])
            P.act(st[:, 3:4], st[:, 2:3], AF.Exp, reads=[stt_], writes=[stt_], scale=-0.5)
            if DBG_PRE:
                P.copy(xst[xi][:], yst[:], reads=[yst_t], writes=[xst_t[xi]])
            else:
                P.stt(xst[xi][:], yst[:], st[:, 3:4], fgb[:], ALU.mult, ALU.mult, reads=[yst_t, stt_, c_t],
                      writes=[xst_t[xi]])
            P.dma(y_d[r0 + blk * 128: r0 + (blk + 1) * 128, :], xst[xi][:], reads=[xst_t[xi]])

    SS = {}

    def sample_front():
        G = [EB[0], EB[1], kvst[0], kvst[1], wst[0], wst[1], gtmp[0]]
        g_t = [eb_t[0], eb_t[1], kvst_t[0], kvst_t[1], wst_t[0], wst_t[1], gtmp_t[0]]
        TMP = [gtmp[1][:, :], HQT[:, 1, :]]
        tmp_t = [gtmp_t[1], Tok(old=hq_t)]
        QB, qb_t = HQT[:, 0, :], Tok(old=hq_t)
        HFv = HF[:, :, :].rearrange("p a b -> p (a b)")
        HIv = HI[:, :, :].rearrange("p a b -> p (a b)")
        HFa, HFb, HIa, HIb = HFv[:, 0:512], HFv[:, 512:1024], HIv[:, 0:512], HIv[:, 512:1024]
        hfa_t, hfb_t, hia_t, hib_t = Tok(old=hf_t), Tok(old=hf_t), Tok(old=hi_t), Tok(old=hi_t)
        ZALL, E_ = SPACC[0], SPACC[1]
        v3 = lambda ap: ap.rearrange("p (g h) -> p g h", h=8)
        otok = sb([128, D], F32, "otok")
        otok_t = T()
        ptb = sb([128, NSAMP * NPAGE], I32, "ptb")
        idxa = sb([128, NSAMP * NPAGE], I32, "idxa")
        idx_t = T()
        hgnb = lbl[:, 0:256]
        FKQ = sb([128, 24], F32, "fkq")
        fkq_t = T()
        SM = sb([128, 32], F32, "sm")
        sm_t = T()
        scr_t = T()

        P.memset(otok[:], 0.0, writes=[otok_t], eng="dve")
        P.dma(hgnb, hgr_d.partition_broadcast(128), writes=[c_t])
        P.dma(ptb[:], pt_d.partition_broadcast(128), writes=[idx_t])
        P.ts(idxa[:], ptb[:], 128.0, IOTA, ALU.mult, ALU.add, reads=[idx_t, c_t], writes=[idx_t])
        P.memset(yst[:], 0.0, writes=[yst_t], eng="dve")
        P.dma(yst[0:NSAMP, :], xs_d, writes=[yst_t])
        rmsnorm_to_bf(yst[:], [yst_t], 0)
        transpose_to(lambda: xnT[:, :, 0:128], 0, [xnT_t[0]])
        for g in range(7):
            wgi = load_wgroup(g)
            pi = tok_proj(wgi, 0, 512, 0)
            P.copy(G[g][:], PF[pi][:, :], reads=[pf_t[pi]], writes=[g_t[g]], eng="act" if g % 2 else "dve")
            P.dma(scr_d[:, g * 512:(g + 1) * 512], G[g][0:NSAMP, :], reads=[g_t[g]], writes=[scr_t])
        P.dma(ks_d, G[1][0:NSAMP, :], reads=[g_t[1]])
        P.dma(vs_d, G[2][0:NSAMP, :], reads=[g_t[2]])
        if STAGE == 10.1:
            return
        a, f, k = hw["a"], hw["f"], hw["k"]
        P.act(a[:], G[4][:, 256:512], AF.Exp, reads=[g_t[4]], writes=[hw_t["a"]], scale=-1.0)
        P.ts(a[:], a[:], 1.0, None, ALU.add, reads=[hw_t["a"]], writes=[hw_t["a"]])
        P.recip(a[:], a[:], reads=[hw_t["a"]], writes=[hw_t["a"]])
        P.tt(f[:], a[:], oml[:], ALU.mult, reads=[hw_t["a"], c_t], writes=[hw_t["f"]])
        P.tt(f[:], f[:], lb[:], ALU.add, reads=[hw_t["f"], c_t], writes=[hw_t["f"]])
        P.ts(k[:], f[:], -1.0, 1.0, ALU.mult, ALU.add, reads=[hw_t["f"]], writes=[hw_t["k"]])
        p_f = pf_next()
        for j, (src, st_) in enumerate(((f, hw_t["f"]), (k, hw_t["k"]), (G[4], g_t[4]))):
            for hp in range(2):
                c = (j * 2 + hp) * 4
                P.tr(PF[p_f][:, c:c + 4], src[0:4, hp * 128:(hp + 1) * 128], ident_f[0:4, 0:4],
                     reads=[st_, c_t], writes=[pf_t[p_f]])
        P.copy(FKQ[:], PF[p_f][:, 0:24], reads=[pf_t[p_f]], writes=[fkq_t])

        SS.update(dict(G=G, g_t=g_t, TMP=TMP, tmp_t=tmp_t, HFa=HFa, HFb=HFb, HIa=HIa, HIb=HIb, hfa_t=hfa_t, hfb_t=hfb_t,
                       hia_t=hia_t, hib_t=hib_t, otok=otok, otok_t=otok_t, idxa=idxa, idx_t=idx_t, hgnb=hgnb, FKQ=FKQ,
                       fkq_t=fkq_t, SM=SM, sm_t=sm_t, scr_t=scr_t))

    def sample_stream():
        otok, otok_t, idxa, idx_t, scr_t = SS["otok"], SS["otok_t"], SS["idxa"], SS["idx_t"], SS["scr_t"]
        PG = [KT[:, hp_, S // 2:S].bitcast(F32) for hp_ in range(4)]
        VAf = VA[:, NB // 2:NB, :].rearrange("p a b -> p (a b)").bitcast(F32)
        PG += [VAf[:, 2048:3072]]
        pg_t = [T() for _ in range(5)]
        NPB = 5
        VBF = [VA[:, NB - 4 + j_, :] for j_ in range(4)]
        vbf_t = [T() for _ in range(4)]
        A8b = sb([128, 4, 8], BF16, "a8b")

        TMPs = [VAf[:, 0:512], VAf[:, 512:1024]]
        tmps_t = [T(), T()]
        QBs, qbs_t = VAf[:, 1024:1536], T()
        RES, res_t = VAf[:, 1536:2048], T()
        for hp_ in range(4):
            for kb_ in range(NB // 2, NB):
                kt_t[hp_][kb_].old = [pg_t[hp_]]
        for kb_ in range(NB // 2, NB):
            va_t[kb_].old = [pg_t[4], tmps_t[0], tmps_t[1], qbs_t, res_t] + vbf_t
        S8 = sb([128, 4, 48], F32, "s8")
        S8b = sb([128, 4, 8], BF16, "s8b")
        s8_t = [T() for _ in range(4)]
        NPG = NPAGE
        CB, AVB = 5, 4
        npg = [0]
        for n in range(NSAMP):
            P.dma(QBs, scr_d[n:n + 1, 0:512].partition_broadcast(128), reads=[scr_t], writes=[qbs_t])
            pages = list(range(NPG - 1, -1, -1))

            def st1(ii):
                p = pages[ii]
                i = npg[0] + ii
                col = n * NPAGE + p
                b8 = i % 4
                P.gather(PG[i % NPB], ckv_d[:, :], idxa[:, col:col + 1], reads=[idx_t], writes=[pg_t[i % NPB]])
                P.tt(TMPs[i % 2], PG[i % NPB][:, 0:512], QBs, ALU.mult, reads=[pg_t[i % NPB], qbs_t], writes=[tmps_t[i % 2]])
                P.copy(VBF[b8], PG[i % NPB][:, 512:1024], reads=[pg_t[i % NPB]], writes=[vbf_t[b8]], eng="act")
                P.reduce(S8[:, b8, 0:8], TMPs[i % 2].rearrange("p (h d) -> p h d", h=8), ALU.add,
                         reads=[tmps_t[i % 2]], writes=[s8_t[b8]])
                P.stt(S8[:, b8, 0:8], S8[:, b8, 0:8], 0.125, sbb[:, 0:8], ALU.mult, ALU.add,
                      reads=[s8_t[b8], c_t], writes=[s8_t[b8]])
                P.act(S8[:, b8, 8:16], S8[:, b8, 0:8], AF.Exp, reads=[s8_t[b8]], writes=[s8_t[b8]])
                P.act(S8b[:, b8, :], S8[:, b8, 8:16], AF.Ln, reads=[s8_t[b8]], writes=[s8_t[b8]], bias=1.0)

            def st2(ii):
                i = npg[0] + ii
                b8 = i % 4
                P.mm(PF[CB][:, 0:8], UIN, S8b[:, b8, :], start=(ii == 0), stop=(ii == NPG - 1),
                     reads=[c_t, s8_t[b8]], writes=[pf_t[CB]])
                P.act(S8[:, b8, 24:32], PF[CB][:, 0:8], AF.Exp, reads=[pf_t[CB]], writes=[s8_t[b8]], scale=-1.0)
                P.tt(A8b[:, b8, :], S8[:, b8, 8:16], S8[:, b8, 24:32], ALU.mult, reads=[s8_t[b8]],
                     writes=[s8_t[b8]])
                if ii < NPG - 1:
                    P.mm(PF[CB][:, 0:8], LST, S8b[:, b8, :], start=False, stop=False, reads=[c_t, s8_t[b8]],
                         writes=[pf_t[CB]])
                P.mm(PF[AVB][0:8, :], A8b[:, b8, :], VBF[b8], start=(ii == 0), stop=(ii == NPG - 1),
                     reads=[s8_t[b8], vbf_t[b8]], writes=[pf_t[AVB]])

            for ii in range(NPG + 1):
                if ii < NPG:
                    st1(ii)
                if ii >= 1:
                    st2(ii - 1)
                yield
            npg[0] += NPG
            P.copy(RES[0:8, :], PF[AVB][0:8, :], reads=[pf_t[AVB]], writes=[res_t])
            for h in range(8):
                P.dma(otok[n:n + 1, h * 64:(h + 1) * 64], RES[h:h + 1, h * 64:(h + 1) * 64], reads=[res_t],
                      writes=[otok_t])
            yield

    def sample_back():
        G, g_t, TMP, tmp_t = SS["G"], SS["g_t"], SS["TMP"], SS["tmp_t"]
        HFa, HFb, HIa, HIb = SS["HFa"], SS["HFb"], SS["HIa"], SS["HIb"]
        hfa_t, hfb_t, hia_t, hib_t = SS["hfa_t"], SS["hfb_t"], SS["hia_t"], SS["hib_t"]
        otok, otok_t, hgnb, FKQ, fkq_t, SM, sm_t, scr_t = (SS["otok"], SS["otok_t"], SS["hgnb"], SS["FKQ"], SS["fkq_t"],
                                                          SS["SM"], SS["sm_t"], SS["scr_t"])
        for g in (3, 5, 6):
            P.dma(G[g][0:NSAMP, :], scr_d[:, g * 512:(g + 1) * 512], reads=[scr_t], writes=[g_t[g]])
        P.memset(yst[:], 0.0, writes=[yst_t], eng="dve")
        P.dma(yst[0:NSAMP, :], xs_d, writes=[yst_t])
        for n in range(NSAMP):
            VB, S0, SN, KV = hw["kd"], hsq, hrs, hattn[0]
            P.dma(VB[:], scr_d[n:n + 1, 5 * 512:5 * 512 + 256].partition_broadcast(128), reads=[scr_t],
                  writes=[hw_t["kd"]])
            P.dma(S0[:], sst_d[n], writes=[hsq_t])
            for half in range(2):
                r0 = half * 64
                for hp in range(2):
                    h = hp * 2 + half
                    ck_, cf_ = (1 * 2 + hp) * 4 + n, (0 * 2 + hp) * 4 + n
                    P.ts(KV[r0:r0 + 64, hp * 64:(hp + 1) * 64], VB[r0:r0 + 64, h * 64:(h + 1) * 64],
                         FKQ[r0:r0 + 64, ck_:ck_ + 1], None, ALU.mult, reads=[hw_t["kd"], fkq_t],
                         writes=[hattn_t[0]])
                    P.stt(SN[r0:r0 + 64, hp * 64:(hp + 1) * 64], S0[r0:r0 + 64, hp * 64:(hp + 1) * 64],
                          FKQ[r0:r0 + 64, cf_:cf_ + 1], KV[r0:r0 + 64, hp * 64:(hp + 1) * 64], ALU.mult, ALU.add,
                          reads=[hsq_t, fkq_t, hattn_t[0]], writes=[hrs_t])
            P.dma(hss_d[n], SN[:], reads=[hrs_t])
            if STAGE == 10.55:
                continue
            p_h = 5
            for hp in range(2):
                cq = (2 * 2 + hp) * 4 + n
                P.ts(KV[:, hp * 64:(hp + 1) * 64], SN[:, hp * 64:(hp + 1) * 64], FKQ[:, cq:cq + 1], None, ALU.mult,
                     reads=[hrs_t, fkq_t], writes=[hattn_t[0]])
            P.mm(PF[p_h][:, 0:128], OB, KV[:, :], reads=[c_t, hattn_t[0]], writes=[pf_t[p_h]])
            P.copy(hosb[:, 0:128], PF[p_h][:, 0:128], reads=[pf_t[p_h]], writes=[hosb_t])
            for half in range(2):
                for hp in range(2):
                    h = hp * 2 + half
                    P.dma(otok[n:n + 1, 512 + h * 64:512 + (h + 1) * 64],
                          hosb[half * 64:half * 64 + 1, hp * 64:(hp + 1) * 64], reads=[hosb_t], writes=[otok_t])
            if STAGE == 10.6:
                continue
            XQB, CMK, CMV, XT = hw["lf"], TMP[0], TMP[1], HFb
            P.dma(XQB[:], scr_d[n:n + 1, 6 * 512:6 * 512 + 256].partition_broadcast(128), reads=[scr_t],
                  writes=[hw_t["lf"]])
            P.dma(CMK.rearrange("p (b c) -> p b c", b=2), cmk_d[n].rearrange("(b m) c -> m b c", b=2),
                  writes=[tmp_t[0]])
            P.dma(CMV.rearrange("p (b c) -> p b c", b=2), cmv_d[n].rearrange("(b m) c -> m b c", b=2),
                  writes=[tmp_t[1]])
            for mb in range(2):
                P.tt(XT[:, mb * 256:(mb + 1) * 256], CMK[:, mb * 256:(mb + 1) * 256], XQB[:], ALU.mult,
                     reads=[tmp_t[0], hw_t["lf"]], writes=[hfb_t])
                P.reduce(SM[:, 8 + mb * 4:12 + mb * 4],
                         XT[:, mb * 256:(mb + 1) * 256].rearrange("p (h d) -> p h d", h=4), ALU.add,
                         reads=[hfb_t], writes=[sm_t])
            P.act(SM[:, 16:24], SM[:, 8:16], AF.Exp, reads=[sm_t], writes=[sm_t], scale=0.125)
            p_x, p_d = 0, 1
            for mb in range(2):
                P.mm(PF[p_x][0:4, 0:256], SM[:, 16 + mb * 4:20 + mb * 4], CMV[:, mb * 256:(mb + 1) * 256],
                     start=(mb == 0), stop=(mb == 1), reads=[sm_t, tmp_t[1]], writes=[pf_t[p_x]])
            for mb in range(2):
                P.mm(PF[p_d][0:4, 0:1], SM[:, 16 + mb * 4:20 + mb * 4], ONEC, start=(mb == 0), stop=(mb == 1),
                     reads=[sm_t, c_t], writes=[pf_t[p_d]])
            P.recip(SM[0:4, 24:25], PF[p_d][0:4, 0:1], reads=[pf_t[p_d]], writes=[sm_t])
            P.ts(hosb[0:4, :], PF[p_x][0:4, 0:256], SM[0:4, 24:25], None, ALU.mult, reads=[pf_t[p_x], sm_t],
                 writes=[hosb_t])
            for h in range(4):
                P.dma(otok[n:n + 1, 768 + h * 64:768 + (h + 1) * 64], hosb[h:h + 1, h * 64:(h + 1) * 64],
                      reads=[hosb_t], writes=[otok_t])
        if STAGE == 10.7:
            return
        HG = otok[:, 512:768]
        P.tt(hosb[:], HG, HG, ALU.mult, reads=[otok_t], writes=[hosb_t])
        P.reduce(SM[:, 0:4], hosb[:].rearrange("p (h d) -> p h d", h=4), ALU.add, reads=[hosb_t], writes=[sm_t])
        P.ts(SM[:, 0:4], SM[:, 0:4], 1.0 / 64, EPS, ALU.mult, ALU.add, reads=[sm_t], writes=[sm_t])
        P.act(SM[:, 4:8], SM[:, 0:4], AF.Ln, reads=[sm_t], writes=[sm_t])
        P.act(SM[:, 8:12], SM[:, 4:8], AF.Exp, reads=[sm_t], writes=[sm_t], scale=-0.5)
        for h in range(4):
            P.ts(otok[:, 512 + h * 64:512 + (h + 1) * 64], otok[:, 512 + h * 64:512 + (h + 1) * 64],
                 SM[:, 8 + h:9 + h], None, ALU.mult, reads=[otok_t, sm_t], writes=[otok_t])
        P.tt(HG, HG, hgnb, ALU.mult, reads=[otok_t, c_t], writes=[otok_t])
        for (gsrc, gtok, c0, c1, o0) in ((G[3], g_t[3], 0, 512, 0), (G[5], g_t[5], 256, 512, 512),
                                         (G[6], g_t[6], 256, 512, 768)):
            w_ = c1 - c0
            t_ = HIb[:, 0:w_]
            P.act(t_, gsrc[:, c0:c1], AF.Exp, reads=[gtok], writes=[hib_t], scale=-1.0)
            P.ts(t_, t_, 1.0, None, ALU.add, reads=[hib_t], writes=[hib_t])
            P.recip(t_, t_, reads=[hib_t], writes=[hib_t])
            P.tt(t_, t_, gsrc[:, c0:c1], ALU.mult, reads=[hib_t, gtok], writes=[hib_t])
            P.tt(otok[:, o0:o0 + w_], otok[:, o0:o0 + w_], t_, ALU.mult, reads=[otok_t, hib_t], writes=[otok_t])
        P.copy(xsb[0][:], otok[:], reads=[otok_t], writes=[xsb_t[0]])
        transpose_to(lambda: MIX[:, :, 0:128], 0, mix_t)
        wg0 = load_wgroup(7)
        wg1 = load_wgroup(8)
        st, stt_ = st4[0], st4_t[0]
        for half, wg in ((0, wg0), (1, wg1)):
            pi = pf_next()
            for kb in range(8):
                P.mm(PF[pi][:, :], MIX[:, kb, 0:128], WG[wg][:, kb, :], start=(kb == 0), stop=(kb == 7),
                     reads=[mix_t[kb], wg_t[wg]], writes=[pf_t[pi]])
            P.tt(G[half][:], PF[pi][:, :], yst[:, half * 512:(half + 1) * 512], ALU.add,
                 reads=[pf_t[pi], yst_t], writes=[g_t[half]])
            P.act(xsb[1][:, half * 512:(half + 1) * 512], G[half][:], AF.Square, reads=[g_t[half]],
                  writes=[xsb_t[1], stt_], accum_out=st[:, half:half + 1])
        P.tt(st[:, 0:1], st[:, 0:1], st[:, 1:2], ALU.add, reads=[stt_], writes=[stt_])
        P.ts(st[:, 1:2], st[:, 0:1], 1.0 / D, EPS, ALU.mult, ALU.add, reads=[stt_], writes=[stt_])
        P.act(st[:, 2:3], st[:, 1:2], AF.Ln, reads=[stt_], writes=[stt_])
        P.act(st[:, 3:4], st[:, 2:3], AF.Exp, reads=[stt_], writes=[stt_], scale=-0.5)
        for half in range(2):
            P.stt(G[half][:], G[half][:], st[:, 3:4], fgb[:, half * 512:(half + 1) * 512], ALU.mult, ALU.mult,
                  reads=[g_t[half], stt_, c_t], writes=[g_t[half]])
            P.dma(ys_d[:, half * 512:(half + 1) * 512], G[half][0:NSAMP, :], reads=[g_t[half]])

    gen = [None]

    def pump(k):
        if gen[0] is None:
            return
        for _ in range(k):
            try:
                next(gen[0])
            except StopIteration:
                gen[0] = None
                return

    if ENABLE_SAMPLE:
        sample_front()
        gen[0] = sample_stream()
        rr["npf"] = 4
        P.hook = lambda: pump(1)
        P.every = 6
    for ti in range(N_TILES):
        if ti == HALF or N_TILES < HALF:
            P.hook = None
            pump(10 ** 9)
            rr["npf"] = 6
        tile(ti)
    P.hook = None
    pump(10 ** 9)
    rr["npf"] = 6
    P.dma(hgs_d, Sst[:], reads=[s_t])


    if ENABLE_SAMPLE:
        sample_back()

    P.emit()
    return nc


_NC_CACHE = {}


def _consts():
    s = np.arange(128)[:, None]
    t = np.arange(128)[None, :]
    same = (s // 64) == (t // 64)
    ident = np.eye(128, dtype=np.float32)
    tri = ((s <= t) & same).astype(np.float32)
    su = ((s > t) & same).astype(np.float32)
    ob = same.astype(np.float32)
    cf = np.concatenate([ident, tri, su, ob, np.arange(128, dtype=np.float32)[:, None], np.ones((128, 1), np.float32)],
                        axis=1).astype(np.float32)
    uin = (s >= t).astype(np.float32)
    tq = np.arange(512)[None, :]
    msk = [((d * 128 + s) < tq).astype(np.float32) for d in range(4)]
    lst = (s < t).astype(np.float32)
    cb = np.concatenate([ident, uin, lst] + msk, axis=1).astype(ml_dtypes.bfloat16)
    return cf, cb


def kernel(x_prompt, x_sample, mem_prompt, cache_k, cache_v, page_table, state_hgrn, cache_mem_k, cache_mem_v,
           norm_gain, w_in, sb_bias, hg_lb_logits, hg_norm_gain, mem_norm_gain, w_mem_kv, w_out, final_norm_gain):
    if "nc" not in _NC_CACHE:
        _NC_CACHE["nc"] = build_program()
    nc = _NC_CACHE["nc"]
    f = lambda a: np.ascontiguousarray(np.asarray(a, dtype=np.float32))
    cf, cb = _consts()
    common = {
        "w_in": f(w_in[0]), "w_out": f(w_out[0]), "w_mem": f(w_mem_kv[0]),
        "gin": f(np.asarray(norm_gain[0]).reshape(8, 128).T),
        "gmem": f(np.asarray(mem_norm_gain[0]).reshape(8, 128).T),
        "fg": f(np.asarray(final_norm_gain).reshape(1, D)),
        "hgn": f(np.asarray(hg_norm_gain[0]).reshape(2, 128).T),
        "sbb": f(np.asarray(sb_bias[0]).reshape(1, 8)),
        "lbl": f(np.asarray(hg_lb_logits).reshape(1, 512)),
        "cf": cf, "cb": cb,
    }
    if ENABLE_SAMPLE:
        ckv = np.concatenate([f(cache_k[0]).reshape(POOL_PAGES * 128, 512),
                              f(cache_v[0]).reshape(POOL_PAGES * 128, 512)], axis=1)
    in_maps = []
    for c in range(8):
        b = c // 2
        m = dict(common)
        r = c % 2
        xb = f(x_prompt[b])
        m["x"] = np.ascontiguousarray(xb[:S // 2]) if r == 1 else np.zeros((S // 2, D), np.float32)
        m["xo"] = np.ascontiguousarray(xb[r * (S // 2):(r + 1) * (S // 2)])
        m["kbias"] = np.full((128, 1), 0.0 if r == 1 else -30000.0, np.float32)
        m["mem"] = f(mem_prompt[b])
        if ENABLE_SAMPLE:
            sl = slice(NSAMP * c, NSAMP * (c + 1))
            m["xs"] = f(x_sample[sl, 0])
            m["pt"] = np.ascontiguousarray(np.asarray(page_table[sl], dtype=np.int32).reshape(1, NSAMP * NPAGE))
            m["ckv"] = ckv
            st = f(state_hgrn[0, sl]).reshape(NSAMP, 2, 2, 64, 64).transpose(0, 2, 3, 1, 4).reshape(NSAMP, 128, 128)
            m["sst"] = np.ascontiguousarray(st)
            m["cmk"] = f(cache_mem_k[0, sl]).reshape(NSAMP, 256, 256)
            m["cmv"] = f(cache_mem_v[0, sl]).reshape(NSAMP, 256, 256)
            m["hgr"] = f(np.asarray(hg_norm_gain[0]).reshape(1, 256))
        in_maps.append(m)
    res = run_bass_kernel_spmd(nc, in_maps[:NCORES], core_ids=list(range(NCORES))).results
    res = list(res) + [res[0], res[1 % NCORES]] * ((8 - NCORES) // 2 + 1)

    def unstate(a):
        return a.reshape(2, 64, 2, 64).transpose(2, 0, 1, 3).reshape(4, 64, 64)

    cat = lambda b, n_: np.concatenate([res[2 * b][n_], res[2 * b + 1][n_]], axis=0)
    y_prompt = np.stack([cat(b, "y") for b in range(4)]).astype(np.float32)
    k_prompt = np.stack([cat(b, "k") for b in range(4)]).reshape(1, 4, S, 8, 64).astype(np.float32)
    v_prompt = np.stack([cat(b, "v") for b in range(4)]).reshape(1, 4, S, 8, 64).astype(np.float32)
    hgrn_prompt = np.stack([unstate(res[2 * b + 1]["hgs"]) for b in range(4)])[None].astype(np.float32)
    mem_k = np.stack([res[2 * b]["mk"] for b in range(4)]).reshape(1, 4, 256, 4, 64).astype(np.float32)
    mem_v = np.stack([res[2 * b]["mv"] for b in range(4)]).reshape(1, 4, 256, 4, 64).astype(np.float32)
    if ENABLE_SAMPLE:
        y_sample = np.concatenate([res[c]["ys"] for c in range(8)]).reshape(32, 1, D).astype(np.float32)
        k_sample = np.concatenate([res[c]["ks"] for c in range(8)]).reshape(1, 32, 1, 8, 64).astype(np.float32)
        v_sample = np.concatenate([res[c]["vs"] for c in range(8)]).reshape(1, 32, 1, 8, 64).astype(np.float32)
        hgrn_sample = np.concatenate([np.stack([unstate(res[c]["hss"][i]) for i in range(NSAMP)])
                                      for c in range(8)])[None].astype(np.float32)
    else:
        y_sample = np.zeros((32, 1, D), np.float32)
        k_sample = np.zeros((1, 32, 1, 8, 64), np.float32)
        v_sample = np.zeros((1, 32, 1, 8, 64), np.float32)
        hgrn_sample = np.zeros((1, 32, 4, 64, 64), np.float32)
    return (y_prompt, y_sample, k_prompt, v_prompt, hgrn_prompt, mem_k, mem_v, k_sample, v_sample, hgrn_sample)
```

```python
import contextlib
import numpy as np
import ml_dtypes
import concourse.bass as bass
import concourse.mybir as mybir
from concourse.bass_utils import run_bass_kernel_spmd

F32 = mybir.dt.float32
BF16 = mybir.dt.bfloat16
I32 = mybir.dt.int32
AF = mybir.ActivationFunctionType
ALU = mybir.AluOpType
AX = mybir.AxisListType

D = 1024
S = 4096
NT = S // 512
NB = S // 128
DIN = 3584
EPS = 1e-6
NSAMP = 4
NPAGE = 64
N_DMA_SEM = 12
RING = 3 * N_DMA_SEM

ENABLE_SAMPLE = True
N_TILES = NT
STAGE = 99
NCORES = 8
POOL_PAGES = 2560
NO_SCRATCH_READ = False
DBG_PRE = False
DBG_KBS = None


class Tok:
    __slots__ = ("w", "r", "ps", "old")

    def __init__(self, ps=False, old=None):
        self.w = None
        self.r = []
        self.ps = ps
        self.old = old


class Prog:
    def __init__(self, nc, es):
        self.nc = nc
        self.es = es
        self.ops = []
        self.hook = None
        self.every = 6
        self._n = 0
        self._in_hook = False
        self.eng = {"pe": nc.tensor, "act": nc.scalar, "dve": nc.vector, "pool": nc.gpsimd, "sp": nc.sync}

    def op(self, eng, fn, reads=(), writes=(), dma=False):
        idx = len(self.ops)
        deps = set()
        for t in reads:
            if t.w is not None:
                deps.add(t.w)
            if t.ps:
                deps.update(r for r in t.r if self.ops[r][0] != eng)
        for t in writes:
            if t.w is not None:
                deps.add(t.w)
            deps.update(t.r)
            if t.old:
                for o in t.old:
                    if o.w is not None:
                        deps.add(o.w)
                    deps.update(o.r)
                t.old = None
        deps.discard(idx)
        self.ops.append([eng, fn, deps, dma])
        for t in reads:
            t.r.append(idx)
        for t in writes:
            t.w = idx
            t.r = []
        if self.hook is not None and not self._in_hook:
            self._n += 1
            if self._n % self.every == 0:
                self._in_hook = True
                try:
                    self.hook()
                finally:
                    self._in_hook = False
        return idx

    def dma(self, out, in_, reads=(), writes=(), q="sp", **kw):
        e = self.eng[q]
        return self.op(q, lambda: e.dma_start(out=out, in_=in_, **kw), reads, writes, dma=True)

    def gather(self, out, in_, idx_ap, reads=(), writes=()):
        g = self.nc.gpsimd
        return self.op("pool", lambda: g.indirect_dma_start(
            out=out, out_offset=None, in_=in_, in_offset=bass.IndirectOffsetOnAxis(ap=idx_ap, axis=0)),
            reads, writes, dma=True)

    def act(self, out, in_, func, reads=(), writes=(), **kw):
        a = self.nc.scalar
        return self.op("act", lambda: a.activation(out=out, in_=in_, func=func, **kw), reads, writes)

    def tt(self, out, in0, in1, op, reads=(), writes=(), eng="dve"):
        e = self.eng[eng]
        return self.op(eng, lambda: e.tensor_tensor(out=out, in0=in0, in1=in1, op=op), reads, writes)

    def ts(self, out, in0, s1, s2, op0, op1=None, reads=(), writes=(), eng="dve"):
        e = self.eng[eng]
        if op1 is None:
            return self.op(eng, lambda: e.tensor_scalar(out=out, in0=in0, scalar1=s1, scalar2=None, op0=op0),
                           reads, writes)
        return self.op(eng, lambda: e.tensor_scalar(out=out, in0=in0, scalar1=s1, scalar2=s2, op0=op0, op1=op1),
                       reads, writes)

    def stt(self, out, in0, scalar, in1, op0, op1, reads=(), writes=()):
        e = self.nc.vector
        return self.op("dve", lambda: e.scalar_tensor_tensor(out=out, in0=in0, scalar=scalar, in1=in1,
                                                             op0=op0, op1=op1), reads, writes)

    def copy(self, out, in_, reads=(), writes=(), eng="dve"):
        e = self.eng[eng]
        if eng == "act":
            return self.op(eng, lambda: e.activation(out=out, in_=in_, func=AF.Copy), reads, writes)
        return self.op(eng, lambda: e.tensor_copy(out=out, in_=in_), reads, writes)

    def recip(self, out, in_, reads=(), writes=()):
        e = self.nc.vector
        return self.op("dve", lambda: e.reciprocal(out=out, in_=in_), reads, writes)

    def reduce(self, out, in_, op, reads=(), writes=()):
        e = self.nc.vector
        return self.op("dve", lambda: e.tensor_reduce(out=out, in_=in_, axis=AX.X, op=op), reads, writes)

    def scan(self, out, d0, d1, reads=(), writes=()):
        e = self.nc.vector
        return self.op("dve", lambda: e.tensor_tensor_scan(out=out, data0=d0, data1=d1, initial=0.0,
                                                           op0=ALU.mult, op1=ALU.add), reads, writes)

    def memset(self, ap, val, writes=(), eng="pool"):
        e = self.eng[eng]
        return self.op(eng, lambda: e.memset(ap, val), (), writes)

    def mm(self, out, lhsT, rhs, start=True, stop=True, reads=(), writes=(), tp=None):
        t = self.nc.tensor
        if tp is None:
            return self.op("pe", lambda: t.matmul(out, lhsT=lhsT, rhs=rhs, start=start, stop=stop), reads, writes)
        return self.op("pe", lambda: t.matmul(out, lhsT=lhsT, rhs=rhs, start=start, stop=stop, tile_position=tp),
                       reads, writes)

    def tr(self, out, in_, ident, reads=(), writes=()):
        t = self.nc.tensor
        return self.op("pe", lambda: t.transpose(out=out, in_=in_, identity=ident), reads, writes)

    def emit(self):
        nc, es = self.nc, self.es
        ops = self.ops
        n = len(ops)
        comp = ("pe", "act", "dve", "pool")
        pos = [0] * n
        cnt = {e: 0 for e in comp}
        qbase = {"sp": 0, "pool": N_DMA_SEM, "act": 2 * N_DMA_SEM}
        dq = {q: [] for q in qbase}
        for i, (eng, fn, deps, dma) in enumerate(ops):
            if dma:
                k = len(dq[eng])
                pos[i] = (k // N_DMA_SEM) * RING + qbase[eng] + (k % N_DMA_SEM)
                dq[eng].append(i)
            else:
                pos[i] = cnt[eng]
                cnt[eng] += 1
        dma_ops = [i for i in range(n) if ops[i][3]]
        for q, lst in dq.items():
            for k, i in enumerate(lst):
                if k >= N_DMA_SEM:
                    ops[i][2].add(lst[k - N_DMA_SEM])
        waited = {}
        marked = [False] * n
        waits = [None] * n
        for i, (eng, fn, deps, dma) in enumerate(ops):
            wl = []
            for d in sorted(deps):
                deng, _, _, ddma = ops[d]
                if ddma:
                    key = (eng, "dma", pos[d] % RING)
                    val = pos[d] // RING
                else:
                    if deng == eng:
                        if eng == "pe":
                            continue
                        if eng != "pool" and pos[i] - pos[d] > 2:
                            continue
                    key = (eng, deng)
                    val = pos[d]
                if waited.get(key, -1) >= val:
                    continue
                waited[key] = val
                wl.append(d)
                marked[d] = True
            waits[i] = wl
        sem = {e: es.enter_context(nc.semaphore("s_" + e)) for e in comp}
        dsem = [es.enter_context(nc.semaphore("s_dma%d" % k)) for k in range(RING)]
        val = [0] * n
        c2 = {e: 0 for e in comp}
        for i, (eng, fn, deps, dma) in enumerate(ops):
            if dma:
                val[i] = 16 * (pos[i] // RING + 1)
            elif marked[i]:
                c2[eng] += 1
                val[i] = c2[eng]
        for i, (eng, fn, deps, dma) in enumerate(ops):
            e = self.eng[eng]
            for d in waits[i]:
                deng, _, _, ddma = ops[d]
                if ddma:
                    e.wait_ge(dsem[pos[d] % RING], val[d])
                else:
                    e.wait_ge(sem[deng], val[d])
            ins = fn()
            if dma:
                ins.then_inc(dsem[pos[i] % RING], 16)
            elif marked[i]:
                ins.then_inc(sem[eng], 1)
        last = {}
        for i in dma_ops:
            last[pos[i] % RING] = val[i]
        for k, v in last.items():
            nc.sync.wait_ge(dsem[k], v)
        for e in comp:
            if c2[e] > 0:
                nc.sync.wait_ge(sem[e], c2[e])


def build_program():
    nc = bass.Bass("TRN2", target_bir_lowering=False)
    es = contextlib.ExitStack()
    P = Prog(nc, es)

    def din(name, shape, dt=F32):
        return nc.dram_tensor(name, list(shape), dt, kind="ExternalInput").ap()

    def dout(name, shape, dt=F32):
        return nc.dram_tensor(name, list(shape), dt, kind="ExternalOutput").ap()

    x_d = din("x", [S // 2, D])
    xo_d = din("xo", [S // 2, D])
    kb_d = din("kbias", [128, 1])
    mem_d = din("mem", [256, D])
    w_in_d = din("w_in", [D, DIN])
    w_out_d = din("w_out", [D, D])
    w_mem_d = din("w_mem", [D, 512])
    gin_d = din("gin", [128, 8])
    gmem_d = din("gmem", [128, 8])
    fg_d = din("fg", [1, D])
    hgn_d = din("hgn", [128, 2])
    sbb_d = din("sbb", [1, 8])
    lbl_d = din("lbl", [1, 512])
    cf_d = din("cf", [128, 514])
    cb_d = din("cb", [128, 384 + 2048], BF16)
    y_d = dout("y", [S // 2, D])
    k_d = dout("k", [S // 2, 512])
    v_d = dout("v", [S // 2, 512])
    hgs_d = dout("hgs", [128, 128])
    mk_d = dout("mk", [256, 256])
    mv_d = dout("mv", [256, 256])
    wsc_d = nc.dram_tensor("wsc", [9, 128, 4096], BF16, kind="Internal").ap()
    if ENABLE_SAMPLE:
        xs_d = din("xs", [NSAMP, D])
        pt_d = din("pt", [1, NSAMP * NPAGE], I32)
        ckv_d = din("ckv", [POOL_PAGES * 128, 1024])
        sst_d = din("sst", [NSAMP, 128, 128])
        cmk_d = din("cmk", [NSAMP, 256, 256])
        cmv_d = din("cmv", [NSAMP, 256, 256])
        ys_d = dout("ys", [NSAMP, D])
        ks_d = dout("ks", [NSAMP, 512])
        vs_d = dout("vs", [NSAMP, 512])
        hss_d = dout("hss", [NSAMP, 128, 128])
        hgr_d = din("hgr", [1, 256])
        scr_d = nc.dram_tensor("scr", [NSAMP, DIN], F32, kind="Internal").ap()

    cnt = [0]

    def sb(shape, dt=F32, name=None):
        cnt[0] += 1
        return es.enter_context(nc.sbuf_tensor("sb_" + (name or ("t%d" % cnt[0])), list(shape), dt))

    def psum(shape, dt=F32):
        cnt[0] += 1
        return es.enter_context(nc.psum_tensor("p%d" % cnt[0], list(shape), dt))

    def T():
        return Tok()

    KT = sb([128, 4, S], BF16, "KT")
    kt_t = [[T() for _ in range(NB)] for _ in range(4)]
    VA = sb([128, NB, 512], BF16, "VA")
    va_t = [T() for _ in range(NB)]
    MKT = sb([128, 2, 256], BF16, "MKT")
    MV = sb([128, 2, 256], BF16, "MV")
    mk_t = T()
    cf = sb([128, 514], F32, "cf")
    cb = sb([128, 384 + 2048], BF16, "cb")
    c_t = T()
    ident_f, TRI, SU, OB = cf[:, 0:128], cf[:, 128:256], cf[:, 256:384], cf[:, 384:512]
    IOTA, ONEC = cf[:, 512:513], cf[:, 513:514]
    ident_b, UIN, LST = cb[:, 0:128], cb[:, 128:256], cb[:, 256:384]
    MSK = [cb[:, 384 + 512 * d: 384 + 512 * (d + 1)] for d in range(4)]
    ONESB = sb([128, 128], BF16, "onesb")
    gin = sb([128, 8], F32, "gin")
    gmem = sb([128, 8], F32, "gmem")
    fgb = sb([128, D], F32, "fgb")
    hgn = sb([128, 2], F32, "hgn")
    sbb = sb([128, 8], F32, "sbb")
    sbbp = sb([128, 8], F32, "sbbp")
    kbias = sb([128, 1], F32, "kbias")
    lbl = sb([128, 512], F32, "lbl")
    lb = sb([128, 256], F32, "lb")
    oml = sb([128, 256], F32, "oml")
    Sst = sb([128, 128], F32, "Sst")
    s_t = T()

    PF = [psum([128, 512], F32) for _ in range(6)]
    pf_t = [Tok(ps=True) for _ in range(6)]
    PB = [psum([128, 1024], BF16) for _ in range(2)]
    pb_t = [Tok(ps=True) for _ in range(2)]
    PBf = [PB[i_][:, :].bitcast(F32) for i_ in range(2)]

    class Deferred:
        def __init__(self):
            self.q = []

        def __getattr__(self, name):
            return lambda *a, **k: self.q.append((name, a, k))

        def pump(self, n_):
            for _ in range(min(n_, len(self.q))):
                name, a, k = self.q.pop(0)
                getattr(P, name)(*a, **k)


    P.dma(cf[:], cf_d, writes=[c_t])
    P.dma(cb[:], cb_d, writes=[c_t])
    P.dma(gin[:], gin_d, writes=[c_t])
    P.dma(gmem[:], gmem_d, writes=[c_t])
    P.dma(fgb[:], fg_d.partition_broadcast(128), writes=[c_t])
    P.dma(hgn[:], hgn_d, writes=[c_t])
    P.dma(sbb[:], sbb_d.partition_broadcast(128), writes=[c_t])
    P.dma(lbl[:], lbl_d.partition_broadcast(128), writes=[c_t])
    P.dma(kbias[:], kb_d, writes=[c_t])
    P.ts(sbbp[:], sbb[:], kbias[:, 0:1], None, ALU.add, reads=[c_t], writes=[c_t])
    P.memset(ONESB[:], 1.0, writes=[c_t], eng="dve")
    P.memset(Sst[:], 0.0, writes=[s_t], eng="dve")
    P.tt(lb[:], lbl[:, 256:512], lbl[:, 0:256], ALU.subtract, reads=[c_t], writes=[c_t])
    P.act(lb[:], lb[:], AF.Exp, reads=[c_t], writes=[c_t])
    P.ts(lb[:], lb[:], 1.0, None, ALU.add, reads=[c_t], writes=[c_t])
    P.recip(lb[:], lb[:], reads=[c_t], writes=[c_t])
    P.ts(oml[:], lb[:], -1.0, 1.0, ALU.mult, ALU.add, reads=[c_t], writes=[c_t])

    if STAGE == 0:
        P.emit()
        return nc
    xst = [sb([128, D], F32) for _ in range(2)]
    xst_t = [T() for _ in range(2)]
    xsb = [sb([128, D], BF16) for _ in range(2)]
    xsb_t = [T() for _ in range(2)]
    st4 = [sb([128, 4], F32) for _ in range(2)]
    st4_t = [T() for _ in range(2)]
    wst = [sb([128, 512], F32) for _ in range(2)]
    wst_t = [T() for _ in range(2)]
    WG = [sb([128, 8, 512], BF16) for _ in range(2)]
    wg_t = [T() for _ in range(2)]
    wsc_t = [T() for _ in range(9)]
    rr = {"x": 0, "w": 0, "wg": 0, "pf": 0, "pb": 0, "npf": 6}

    def rmsnorm_to_bf(src_ap, reads, which):
        st = st4[which]
        stt_ = st4_t[which]
        P.act(xsb[which][:], src_ap, AF.Square, reads=reads, writes=[xsb_t[which], stt_], accum_out=st[:, 0:1])
        P.ts(st[:, 1:2], st[:, 0:1], 1.0 / D, EPS, ALU.mult, ALU.add, reads=[stt_], writes=[stt_])
        P.act(st[:, 2:3], st[:, 1:2], AF.Ln, reads=[stt_], writes=[stt_])
        P.act(st[:, 3:4], st[:, 2:3], AF.Exp, reads=[stt_], writes=[stt_], scale=-0.5)
        P.act(xsb[which][:], src_ap, AF.Copy, reads=list(reads) + [stt_], writes=[xsb_t[which]], scale=st[:, 3:4])

    def transpose_to(dst_ap_fn, which, dst_toks):
        pb = rr["pb"] % 2
        rr["pb"] += 1
        for kb in range(8):
            P.tr(PB[pb][:, kb * 128:(kb + 1) * 128], xsb[which][:, kb * 128:(kb + 1) * 128], ident_b,
                 reads=[xsb_t[which], c_t], writes=[pb_t[pb]])
        P.copy(dst_ap_fn(), PB[pb][:, :].rearrange("p (k t) -> p k t", k=8), reads=[pb_t[pb]], writes=dst_toks)

    for g in range(9):
        wgi = rr["wg"] % 2
        rr["wg"] += 1
        for kb in range(8):
            wi = rr["w"] % 2
            rr["w"] += 1
            if g < 7:
                src = w_in_d[kb * 128:(kb + 1) * 128, g * 512:(g + 1) * 512]
            else:
                src = w_out_d[kb * 128:(kb + 1) * 128, (g - 7) * 512:(g - 6) * 512]
            P.dma(wst[wi][:], src, writes=[wst_t[wi]])
            if g < 7:
                if kb % 2 == 0:
                    P.ts(WG[wgi][:, kb, :], wst[wi][:], gin[:, kb:kb + 1], None, ALU.mult,
                         reads=[wst_t[wi], c_t], writes=[wg_t[wgi]])
                else:
                    P.act(WG[wgi][:, kb, :], wst[wi][:], AF.Copy, reads=[wst_t[wi], c_t], writes=[wg_t[wgi]],
                          scale=gin[:, kb:kb + 1])
            else:
                P.copy(WG[wgi][:, kb, :], wst[wi][:], reads=[wst_t[wi]], writes=[wg_t[wgi]],
                       eng="dve" if kb % 2 == 0 else "pool")
        P.dma(wsc_d[g], WG[wgi][:, :, :].rearrange("p k c -> p (k c)"), reads=[wg_t[wgi]], writes=[wsc_t[g]])

    if STAGE == 1:
        P.emit()
        return nc

    def load_wgroup(g):
        wgi = rr["wg"] % 2
        rr["wg"] += 1
        if not NO_SCRATCH_READ:
            P.dma(WG[wgi][:, :, :].rearrange("p k c -> p (k c)"), wsc_d[g], reads=[wsc_t[g]], writes=[wg_t[wgi]])
        return wgi

    xnT = sb([128, 8, 512], BF16, "xnT")
    xnT_t = [T() for _ in range(4)]
    memT = [xnT[:, :, mb_ * 128:(mb_ + 1) * 128] for mb_ in range(2)]
    memT_t = [xnT_t[0], xnT_t[1]]
    for mb in range(2):
        xi = rr["x"] % 2
        rr["x"] += 1
        P.dma(xst[xi][:], mem_d[mb * 128:(mb + 1) * 128, :], writes=[xst_t[xi]])
        rmsnorm_to_bf(xst[xi][:], [xst_t[xi]], xi)
        transpose_to(lambda mb=mb: memT[mb], xi, [memT_t[mb]])
    wmb = [sb([128, 512], BF16) for _ in range(2)]
    wmb_t = [T() for _ in range(2)]
    for kb in range(8):
        wi = rr["w"] % 2
        rr["w"] += 1
        P.dma(wst[wi][:], w_mem_d[kb * 128:(kb + 1) * 128, :], writes=[wst_t[wi]])
        P.ts(wmb[kb % 2][:], wst[wi][:], gmem[:, kb:kb + 1], None, ALU.mult, reads=[wst_t[wi], c_t],
             writes=[wmb_t[kb % 2]])
        for mb in range(2):
            P.mm(PF[mb][:, :], xnT[:, kb, mb * 128:(mb + 1) * 128], wmb[kb % 2][:], start=(kb == 0), stop=(kb == 7),
                 reads=[memT_t[mb], wmb_t[kb % 2]], writes=[pf_t[mb]])
    kvst = [sb([128, 512], F32) for _ in range(2)]
    kvst_t = [T() for _ in range(2)]
    kbf = [sb([128, 512], BF16) for _ in range(2)]
    kbf_t = [T() for _ in range(2)]
    for mb in range(2):
        P.copy(kvst[mb][:], PF[mb][:, :], reads=[pf_t[mb]], writes=[kvst_t[mb]], eng="act" if mb else "dve")
        P.dma(mk_d[mb * 128:(mb + 1) * 128, :], kvst[mb][:, 0:256], reads=[kvst_t[mb]])
        P.dma(mv_d[mb * 128:(mb + 1) * 128, :], kvst[mb][:, 256:512], reads=[kvst_t[mb]])
        P.copy(kbf[mb][:, 0:256], kvst[mb][:, 0:256], reads=[kvst_t[mb]], writes=[kbf_t[mb]])
        P.copy(MV[:, mb, :], kvst[mb][:, 256:512], reads=[kvst_t[mb]], writes=[mk_t])
        pb = rr["pb"] % 2
        rr["pb"] += 1
        for hp in range(2):
            P.tr(PB[pb][:, hp * 128:(hp + 1) * 128], kbf[mb][:, hp * 128:(hp + 1) * 128], ident_b,
                 reads=[kbf_t[mb], c_t], writes=[pb_t[pb]])
        P.copy(MKT[:, :, mb * 128:(mb + 1) * 128], PB[pb][:, 0:256].rearrange("p (k t) -> p k t", k=2),
               reads=[pb_t[pb]], writes=[mk_t])

    if STAGE == 2:
        P.emit()
        return nc
    QT = sb([128, 4, 512], BF16, "QT")
    qt_t = [T() for _ in range(4)]
    SG = sb([128, 8, 512], BF16, "SG")
    sg_t = [T() for _ in range(8)]
    HQT = sb([128, 2, 512], F32, "HQT")
    hq_t = [T() for _ in range(2)]
    XQT = sb([128, 2, 512], BF16, "XQT")
    xq_t = [T() for _ in range(2)]
    HF = sb([128, 4, 256], F32, "HF")
    HI = sb([128, 4, 256], F32, "HI")
    hf_t = [T() for _ in range(4)]
    hi_t = [T() for _ in range(4)]
    MIX = sb([128, 8, 512], BF16, "MIX")
    mix_t = [T() for _ in range(8)]
    gtmp = [sb([128, 512], F32) for _ in range(2)]
    gtmp_t = [T() for _ in range(2)]
    EB = [sb([128, 512], F32) for _ in range(4)]
    eb_t = [T() for _ in range(4)]
    SPB = wmb + [sb([128, 512], BF16) for _ in range(2)]
    spb_t = wmb_t + [T() for _ in range(2)]
    WB = [sb([128, 512], BF16) for _ in range(2)]
    wb_t = [T() for _ in range(2)]
    AB = [sb([128, 512], BF16) for _ in range(4)]
    ab_t = [T() for _ in range(4)]
    KTf = KT[:, 0, :].bitcast(F32)
    SPACC = [KTf[:, 0:512], KTf[:, 512:1024]]
    spacc_t = [Tok(old=[kt_t[0][kb_] for kb_ in range(NB)]) for _ in range(2)]
    hw = {n_: sb([128, 256], F32, "hw_" + n_) for n_ in ("a", "f", "lf", "k", "kd")}
    hw_t = {n_: T() for n_ in hw}
    hx = {n_: sb([128, 2, 128], F32, "hx_" + n_) for n_ in ("ep", "en", "qe", "ke")}
    hx_t = {n_: T() for n_ in hx}
    hattn = [sb([128, 128], F32) for _ in range(2)]
    hattn_t = [T() for _ in range(2)]
    hdec = sb([128, 4], F32, "hdec")
    hdec_t = T()
    hosb = sb([128, 256], F32, "hosb")
    hosb_t = T()
    hsq = sb([128, 128], F32, "hsq")
    hsq_t = T()
    hrs = sb([128, 128], F32, "hrs")
    hrs_t = T()
    yst = sb([128, D], F32, "yst")
    yst_t = T()

    def pf_next():
        i = rr["pf"] % rr["npf"]
        rr["pf"] += 1
        return i

    def silu_from_psum(pi, dst_ap, dst_tok):
        gi = rr["x"] % 2
        rr["x"] += 1
        P.act(gtmp[gi][:], PF[pi][:, :], AF.Exp, reads=[pf_t[pi]], writes=[gtmp_t[gi]], scale=-1.0)
        P.ts(gtmp[gi][:], gtmp[gi][:], 1.0, None, ALU.add, reads=[gtmp_t[gi]], writes=[gtmp_t[gi]])
        P.recip(gtmp[gi][:], gtmp[gi][:], reads=[gtmp_t[gi]], writes=[gtmp_t[gi]])
        P.tt(dst_ap, PF[pi][:, :], gtmp[gi][:], ALU.mult, reads=[pf_t[pi], gtmp_t[gi]], writes=[dst_tok])

    def feat_proj(wgi, cbs, evac):
        for cbi in cbs:
            pi = pf_next()
            for kb in range(8):
                P.mm(PF[pi][:, :], WG[wgi][:, kb, cbi * 128:(cbi + 1) * 128], xnT[:, kb, :], start=(kb == 0),
                     stop=(kb == 7), reads=[wg_t[wgi]] + xnT_t, writes=[pf_t[pi]])
            evac(cbi, pi)

    def tok_proj(wgi, c0, c1, blk):
        pi = pf_next()
        for kb in range(8):
            P.mm(PF[pi][:, 0:c1 - c0], xnT[:, kb, blk * 128:(blk + 1) * 128], WG[wgi][:, kb, c0:c1],
                 start=(kb == 0), stop=(kb == 7), reads=[wg_t[wgi], xnT_t[blk]], writes=[pf_t[pi]])
        return pi

    def hgrn_block(blk, state_only=False, HP=None, defer=False):
        if HP is None:
            HP = P
        if defer:
            HB = {0: PBf[0], 1: PBf[0], 2: PBf[0], 5: PBf[0], 3: PF[5], 4: PBf[1]}
            hb_t = {0: pb_t[0], 1: pb_t[0], 2: pb_t[0], 5: pb_t[0], 3: pf_t[5], 4: pb_t[1]}
        else:
            HB, hb_t = PF, pf_t

        c0 = blk * 128
        tmpb = [0, 1, 2] if state_only else [0, 1, 2, 5]

        def tmp_next():
            i = tmpb[rr["pf"] % len(tmpb)]
            rr["pf"] += 1
            return i
        a, f, lf, k, kd = hw["a"], hw["f"], hw["lf"], hw["k"], hw["kd"]
        HP.act(a[:], HF[:, blk, :], AF.Exp, reads=[hf_t[blk]], writes=[hw_t["a"]], scale=-1.0)
        HP.ts(a[:], a[:], 1.0, None, ALU.add, reads=[hw_t["a"]], writes=[hw_t["a"]])
        HP.recip(a[:], a[:], reads=[hw_t["a"]], writes=[hw_t["a"]])
        HP.tt(f[:], a[:], oml[:], ALU.mult, reads=[hw_t["a"], c_t], writes=[hw_t["f"]])
        HP.tt(f[:], f[:], lb[:], ALU.add, reads=[hw_t["f"], c_t], writes=[hw_t["f"]])
        HP.act(lf[:], f[:], AF.Ln, reads=[hw_t["f"]], writes=[hw_t["lf"]])
        HP.ts(k[:], f[:], -1.0, 1.0, ALU.mult, ALU.add, reads=[hw_t["f"]], writes=[hw_t["k"]])
        if STAGE == 3.1:
            return
        p_rev = tmp_next()
        HP.mm(HB[p_rev][:, 0:256], SU, lf[:], reads=[c_t, hw_t["lf"]], writes=[hb_t[p_rev]])
        HP.act(kd[:], HB[p_rev][:, 0:256], AF.Exp, reads=[hb_t[p_rev]], writes=[hw_t["kd"]])
        HP.tt(kd[:], kd[:], k[:], ALU.mult, reads=[hw_t["kd"], hw_t["k"]], writes=[hw_t["kd"]])
        if STAGE == 3.2:
            return
        p_bc = tmp_next()
        for hp in range(2):
            HP.mm(HB[p_bc][:, hp * 128:(hp + 1) * 128], lf[:, hp * 128:(hp + 1) * 128], TRI,
                 reads=[hw_t["lf"], c_t], writes=[hb_t[p_bc]])
        HP.act(hx["ep"][:, :, :], HB[p_bc][:, 0:256].rearrange("p (k t) -> p k t", k=2), AF.Exp,
              reads=[hb_t[p_bc]], writes=[hx_t["ep"]])
        if not state_only:
            HP.act(hx["en"][:, :, :], HB[p_bc][:, 0:256].rearrange("p (k t) -> p k t", k=2), AF.Exp,
                  reads=[hb_t[p_bc]], writes=[hx_t["en"]], scale=-1.0)
            HP.tt(hx["qe"][:, :, :], hx["ep"][:, :, :], HQT[:, :, c0:c0 + 128], ALU.mult,
                 reads=[hx_t["ep"]] + hq_t, writes=[hx_t["qe"]])
            if STAGE == 3.3:
                return
            p_kt = tmp_next()
            for hp in range(2):
                HP.tr(HB[p_kt][:, hp * 128:(hp + 1) * 128], k[:, hp * 128:(hp + 1) * 128], ident_f,
                     reads=[hw_t["k"], c_t], writes=[hb_t[p_kt]])
            HP.tt(hx["ke"][:, :, :], HB[p_kt][:, 0:256].rearrange("p (k t) -> p k t", k=2), hx["en"][:, :, :], ALU.mult,
                 reads=[hb_t[p_kt], hx_t["en"]], writes=[hx_t["ke"]])
        for c in range(2):
            for hp in range(2):
                HP.copy(hdec[:, c * 2 + hp: c * 2 + hp + 1], hx["ep"][:, hp, c * 64 + 63: c * 64 + 64],
                       reads=[hx_t["ep"]], writes=[hdec_t])
        if STAGE == 3.4:
            return
        if not state_only:
            p_o = 3
            for h in range(4):
                hp, rb = h // 2, (h % 2) * 64
                p_at = tmp_next()
                HP.mm(HB[p_at][:, 0:128], hx["ke"][rb:rb + 64, hp, :], hx["qe"][rb:rb + 64, hp, :],
                     reads=[hx_t["ke"], hx_t["qe"]], writes=[hb_t[p_at]], tp=(rb, 0))
                ai = h % 2
                HP.tt(hattn[ai][:], HB[p_at][:, 0:128], TRI, ALU.mult, reads=[hb_t[p_at], c_t], writes=[hattn_t[ai]])
                HP.mm(HB[p_o][rb:rb + 64, hp * 128:(hp + 1) * 128], HI[:, blk, h * 64:(h + 1) * 64], hattn[ai][:],
                     start=True, stop=True, reads=[hi_t[blk], hattn_t[ai]], writes=[hb_t[p_o]], tp=(0, rb))
            if STAGE == 3.5:
                return
        p_i = 4
        for c in range(2):
            for h in range(4):
                if STAGE == 3.55 or state_only:
                    break
                hp, rb = h // 2, (h % 2) * 64
                HP.mm(HB[p_i][rb:rb + 64, hp * 128 + c * 64: hp * 128 + c * 64 + 64],
                     Sst[rb:rb + 64, hp * 64:(hp + 1) * 64], hx["qe"][rb:rb + 64, hp, c * 64:(c + 1) * 64],
                     start=True, stop=True, reads=[s_t, hx_t["qe"]], writes=[hb_t[p_i]], tp=(rb, rb))
            if STAGE == 3.57:
                continue
            p_s = tmp_next()
            for h in range(4):
                hp, rb = h // 2, (h % 2) * 64
                HP.mm(HB[p_s][rb:rb + 64, hp * 64:(hp + 1) * 64], kd[c * 64:(c + 1) * 64, h * 64:(h + 1) * 64],
                     HI[c * 64:(c + 1) * 64, blk, h * 64:(h + 1) * 64], reads=[hw_t["kd"], hi_t[blk]],
                     writes=[hb_t[p_s]], tp=(c * 64, rb))
            for hp in range(2):
                HP.stt(Sst[:, hp * 64:(hp + 1) * 64], Sst[:, hp * 64:(hp + 1) * 64],
                      hdec[:, c * 2 + hp: c * 2 + hp + 1], HB[p_s][:, hp * 64:(hp + 1) * 64], ALU.mult, ALU.add,
                      reads=[s_t, hdec_t, hb_t[p_s]], writes=[s_t])
        if STAGE in (3.55, 3.57, 3.6) or state_only:
            return
        HP.copy(hosb[:], HB[p_i][:, 0:256], reads=[hb_t[p_i]], writes=[hosb_t], eng="act")
        HP.tt(hosb[:], hosb[:], HB[p_o][:, 0:256], ALU.add, reads=[hosb_t, hb_t[p_o]], writes=[hosb_t])
        for hp in range(2):
            HP.act(hsq[:], hosb[:, hp * 128:(hp + 1) * 128], AF.Square, reads=[hosb_t], writes=[hsq_t])
            p_n = tmp_next()
            HP.mm(HB[p_n][:, 0:128], OB, hsq[:], reads=[c_t, hsq_t], writes=[hb_t[p_n]])
            HP.ts(hrs[:], HB[p_n][:, 0:128], 1.0 / 64, EPS, ALU.mult, ALU.add, reads=[hb_t[p_n]], writes=[hrs_t])
            HP.act(hrs[:], hrs[:], AF.Ln, reads=[hrs_t], writes=[hrs_t])
            HP.act(hrs[:], hrs[:], AF.Exp, reads=[hrs_t], writes=[hrs_t], scale=-0.5)
            HP.tt(hrs[:], hosb[:, hp * 128:(hp + 1) * 128], hrs[:], ALU.mult, reads=[hosb_t, hrs_t],
                 writes=[hrs_t])
            HP.stt(MIX[:, 4 + hp, c0:c0 + 128], hrs[:], hgn[:, hp:hp + 1], SG[:, 4 + hp, c0:c0 + 128], ALU.mult,
                  ALU.mult, reads=[hrs_t, c_t, sg_t[4 + hp]], writes=[mix_t[4 + hp]])

    def xattn():
        for hp in range(2):
            p_o, p_d = pf_next(), pf_next()
            for hh in range(2):
                h, rb = hp * 2 + hh, hh * 64
                for mb in range(2):
                    p_s = pf_next()
                    P.mm(PF[p_s][:, :], MKT[rb:rb + 64, hp, mb * 128:(mb + 1) * 128], XQT[rb:rb + 64, hp, :],
                         reads=[mk_t, xq_t[hp]], writes=[pf_t[p_s]], tp=(rb, 0))
                    ai = rr["x"] % 2
                    rr["x"] += 1
                    P.act(AB[ai][:], PF[p_s][:, :], AF.Exp, reads=[pf_t[p_s]], writes=[ab_t[ai]])
                    P.mm(PF[p_o][rb:rb + 64, :], MV[:, mb, h * 64:(h + 1) * 64], AB[ai][:], start=(mb == 0),
                         stop=(mb == 1), reads=[mk_t, ab_t[ai]], writes=[pf_t[p_o]], tp=(0, rb))
                    P.mm(PF[p_d][rb:rb + 64, :], ONESB[:, 0:64], AB[ai][:], start=(mb == 0), stop=(mb == 1),
                         reads=[c_t, ab_t[ai]], writes=[pf_t[p_d]], tp=(0, rb))
            gi = rr["x"] % 2
            rr["x"] += 1
            P.recip(gtmp[gi][:], PF[p_d][:, :], reads=[pf_t[p_d]], writes=[gtmp_t[gi]])
            P.tt(gtmp[gi][:], PF[p_o][:, :], gtmp[gi][:], ALU.mult, reads=[pf_t[p_o], gtmp_t[gi]],
                 writes=[gtmp_t[gi]])
            P.tt(MIX[:, 6 + hp, :], gtmp[gi][:], SG[:, 6 + hp, :], ALU.mult, reads=[gtmp_t[gi], sg_t[6 + hp]],
                 writes=[mix_t[6 + hp]])

    def sb_attention(qi, side=None):
        nkb = 4 * qi + 4
        for hp in range(4):
            av = 4
            kbs = list(range(nkb - 1, -1, -1))
            n = len(kbs)

            def stage_a1(i):
                kb = kbs[i]
                r = i % 2
                for hh in range(2):
                    rb = hh * 64
                    P.mm(PF[hh][:, :], KT[rb:rb + 64, hp, kb * 128:(kb + 1) * 128], QT[rb:rb + 64, hp, :],
                         reads=[kt_t[hp][kb], qt_t[hp]], writes=[pf_t[hh]], tp=(rb, 0))
                z = kb - 4 * qi
                bsrc = sbbp if kb < 4 * HALF else sbb
                for hh in range(2):
                    h = hp * 2 + hh
                    e = hh * 2 + r
                    P.act(EB[e][:], PF[hh][:, :], AF.Exp, reads=[pf_t[hh], c_t], writes=[eb_t[e]],
                          bias=bsrc[:, h:h + 1])
                    if z >= 0:
                        P.tt(EB[e][:], EB[e][:], MSK[z], ALU.mult, reads=[eb_t[e], c_t], writes=[eb_t[e]])

            def stage_a2(i):
                r = i % 2
                for hh in range(2):
                    e = hh * 2 + r
                    P.act(SPB[e][:], EB[e][:], AF.Ln, reads=[eb_t[e]], writes=[spb_t[e]], bias=1.0)

            def stage_b(i):
                kb = kbs[i]
                r = i % 2
                for hh in range(2):
                    e = hh * 2 + r
                    P.mm(PF[2 + hh][:, :], UIN, SPB[e][:], start=(kb == nkb - 1), stop=(kb == 0),
                         reads=[c_t, spb_t[e]], writes=[pf_t[2 + hh]])
                for hh in range(2):
                    e = hh * 2 + r
                    P.act(WB[hh][:], PF[2 + hh][:, :], AF.Exp, reads=[pf_t[2 + hh]], writes=[wb_t[hh]], scale=-1.0)
                    P.tt(AB[e][:], EB[e][:], WB[hh][:], ALU.mult, reads=[eb_t[e], wb_t[hh]], writes=[ab_t[e]])

            def stage_b2(i):
                kb = kbs[i]
                r = i % 2
                if kb > 0:
                    for hh in range(2):
                        e = hh * 2 + r
                        P.mm(PF[2 + hh][:, :], LST, SPB[e][:], start=False, stop=False, reads=[c_t, spb_t[e]],
                             writes=[pf_t[2 + hh]])

            def stage_c(i):
                kb = kbs[i]
                r = i % 2
                for hh in range(2):
                    rb = hh * 64
                    e = hh * 2 + r
                    h = hp * 2 + hh
                    P.mm(PF[av][rb:rb + 64, :], VA[:, kb, h * 64:(h + 1) * 64], AB[e][:], start=(kb == nkb - 1),
                         stop=(kb == 0), reads=[va_t[kb], ab_t[e]], writes=[pf_t[av]], tp=(0, rb))

            for i in range(n + 2):
                if i < n:
                    stage_a1(i)
                if 0 <= i - 2 < n:
                    stage_b2(i - 2)
                if 0 <= i - 1 < n:
                    stage_b(i - 1)
                if i < n:
                    stage_a2(i)
                if 0 <= i - 2 < n:
                    stage_c(i - 2)
                if side is not None:
                    side(i)
            P.tt(MIX[:, hp, :], PF[av][:, :], SG[:, hp, :], ALU.mult, reads=[pf_t[av], sg_t[hp]],
                 writes=[mix_t[hp]])

    HALF = NT // 2

    def tile(ti):
        own = ti >= HALF
        src = xo_d if own else x_d
        r0 = (ti - HALF) * 512 if own else ti * 512
        for blk in range(4):
            xi = rr["x"] % 2
            rr["x"] += 1
            P.dma(xst[xi][:], src[r0 + blk * 128: r0 + (blk + 1) * 128, :], writes=[xst_t[xi]])
            rmsnorm_to_bf(xst[xi][:], [xst_t[xi]], xi)
            transpose_to(lambda blk=blk: xnT[:, :, blk * 128:(blk + 1) * 128], xi, [xnT_t[blk]])
        rr["pf"] = 0
        wgi = load_wgroup(1)
        for blk in range(4):
            gb = ti * 4 + blk
            pi = tok_proj(wgi, 0, 512, blk)
            si = gb % 2
            P.copy(kvst[si][:], PF[pi][:, :], reads=[pf_t[pi]], writes=[kvst_t[si]], eng="act")
            if own:
                P.dma(k_d[r0 + blk * 128: r0 + (blk + 1) * 128, :], kvst[si][:], reads=[kvst_t[si]])
            P.copy(kbf[si][:], kvst[si][:], reads=[kvst_t[si]], writes=[kbf_t[si]])
            pb = rr["pb"] % 2
            rr["pb"] += 1
            for hp in range(4):
                P.tr(PB[pb][:, hp * 128:(hp + 1) * 128], kbf[si][:, hp * 128:(hp + 1) * 128], ident_b,
                     reads=[kbf_t[si], c_t], writes=[pb_t[pb]])
            P.copy(KT[:, :, gb * 128:(gb + 1) * 128], PB[pb][:, 0:512].rearrange("p (k t) -> p k t", k=4),
                   reads=[pb_t[pb]], writes=[kt_t[hp_][gb] for hp_ in range(4)])
        wgi = load_wgroup(2)
        for blk in range(4):
            gb = ti * 4 + blk
            pi = tok_proj(wgi, 0, 512, blk)
            si = gb % 2
            P.copy(kvst[si][:], PF[pi][:, :], reads=[pf_t[pi]], writes=[kvst_t[si]], eng="act")
            if own:
                P.dma(v_d[r0 + blk * 128: r0 + (blk + 1) * 128, :], kvst[si][:], reads=[kvst_t[si]])
            P.copy(VA[:, gb, :], kvst[si][:], reads=[kvst_t[si]], writes=[va_t[gb]])
        if own:
            wgi = load_wgroup(0)
            feat_proj(wgi, range(4), lambda cbi, pi: P.act(QT[:, cbi, :], PF[pi][:, :], AF.Copy, reads=[pf_t[pi]],
                                                          writes=[qt_t[cbi]], scale=0.125))
            wgi = load_wgroup(3)
            feat_proj(wgi, range(4), lambda cbi, pi: silu_from_psum(pi, SG[:, cbi, :], sg_t[cbi]))
        wgi = load_wgroup(4)
        if own:
            feat_proj(wgi, range(2), lambda cbi, pi: P.copy(HQT[:, cbi, :], PF[pi][:, :], reads=[pf_t[pi]],
                                                            writes=[hq_t[cbi]], eng="act"))
        for blk in range(4):
            pi = tok_proj(wgi, 256, 512, blk)
            P.copy(HF[:, blk, :], PF[pi][:, 0:256], reads=[pf_t[pi]], writes=[hf_t[blk]])
        wgi = load_wgroup(5)
        for blk in range(4):
            pi = tok_proj(wgi, 0, 256, blk)
            P.copy(HI[:, blk, :], PF[pi][:, 0:256], reads=[pf_t[pi]], writes=[hi_t[blk]], eng="act")
        if own:
            feat_proj(wgi, range(2, 4), lambda cbi, pi: silu_from_psum(pi, SG[:, 2 + cbi, :], sg_t[2 + cbi]))
            wgi = load_wgroup(6)
            feat_proj(wgi, range(2), lambda cbi, pi: P.act(XQT[:, cbi, :], PF[pi][:, :], AF.Copy, reads=[pf_t[pi]],
                                                           writes=[xq_t[cbi]], scale=0.125))
            feat_proj(wgi, range(2, 4), lambda cbi, pi: silu_from_psum(pi, SG[:, 4 + cbi, :], sg_t[4 + cbi]))
        if not own:
            for blk in range(4):
                hgrn_block(blk, state_only=True)
            return
        HD = Deferred()
        for blk in range(4):
            hgrn_block(blk, HP=HD, defer=True)
        nq = len(HD.q)
        n_it = 4 * (4 * ti + 6)
        per = (nq + n_it - 1) // n_it + 1
        xattn()
        sb_attention(ti, side=lambda i_: HD.pump(per))
        HD.pump(10 ** 9)
        wg0 = load_wgroup(7)
        wg1 = load_wgroup(8)
        for blk in range(4):
            xi = rr["x"] % 2
            rr["x"] += 1
            P.dma(xst[xi][:], src[r0 + blk * 128: r0 + (blk + 1) * 128, :], writes=[xst_t[xi]])
            for half, wg in ((0, wg0), (1, wg1)):
                pi = pf_next()
                kbs = list(range(8)) if DBG_KBS is None else list(DBG_KBS)
                for kb in kbs:
                    P.mm(PF[pi][:, :], MIX[:, kb, blk * 128:(blk + 1) * 128], WG[wg][:, kb, :], start=(kb == kbs[0]),
                         stop=(kb == kbs[-1]), reads=[mix_t[kb], wg_t[wg]], writes=[pf_t[pi]])
                P.tt(yst[:, half * 512:(half + 1) * 512], PF[pi][:, :], xst[xi][:, half * 512:(half + 1) * 512],
                     ALU.add, reads=[pf_t[pi], xst_t[xi]], writes=[yst_t])
            st, stt_ = st4[xi], st4_t[xi]
            P.act(xsb[xi][:], yst[:], AF.Square, reads=[yst_t], writes=[xsb_t[xi], stt_], accum_out=st[:, 0:1])
            P.ts(st[:, 1:2], st[:, 0:1], 1.0 / D, EPS, ALU.mult, ALU.add, reads=[stt_], writes=[stt_])
            P.act(st[:, 2:3], st[:, 1:2], AF.Ln, reads=[stt_], writes=[stt_])
            P.act(st[:, 3:4], st[:, 2:3], AF.Exp, reads=[stt_], writes=[stt_], scale=-0.5)
            if DBG_PRE:
                P.copy(xst[xi][:], yst[:], reads=[yst_t], writes=[xst_t[xi]])
            else:
                P.stt(xst[xi][:], yst[:], st[:, 3:4], fgb[:], ALU.mult, ALU.mult, reads=[yst_t, stt_, c_t],
                      writes=[xst_t[xi]])
            P.dma(y_d[r0 + blk * 128: r0 + (blk + 1) * 128, :], xst[xi][:], reads=[xst_t[xi]])

    SS = {}

    def sample_front():
        G = [EB[0], EB[1], kvst[0], kvst[1], wst[0], wst[1], gtmp[0]]
        g_t = [eb_t[0], eb_t[1], kvst_t[0], kvst_t[1], wst_t[0], wst_t[1], gtmp_t[0]]
        TMP = [gtmp[1][:, :], HQT[:, 1, :]]
        tmp_t = [gtmp_t[1], Tok(old=hq_t)]
        QB, qb_t = HQT[:, 0, :], Tok(old=hq_t)
        HFv = HF[:, :, :].rearrange("p a b -> p (a b)")
        HIv = HI[:, :, :].rearrange("p a b -> p (a b)")
        HFa, HFb, HIa, HIb = HFv[:, 0:512], HFv[:, 512:1024], HIv[:, 0:512], HIv[:, 512:1024]
        hfa_t, hfb_t, hia_t, hib_t = Tok(old=hf_t), Tok(old=hf_t), Tok(old=hi_t), Tok(old=hi_t)
        ZALL, E_ = SPACC[0], SPACC[1]
        v3 = lambda ap: ap.rearrange("p (g h) -> p g h", h=8)
        otok = sb([128, D], F32, "otok")
        otok_t = T()
        ptb = sb([128, NSAMP * NPAGE], I32, "ptb")
        idxa = sb([128, NSAMP * NPAGE], I32, "idxa")
        idx_t = T()
        hgnb = lbl[:, 0:256]
        FKQ = sb([128, 24], F32, "fkq")
        fkq_t = T()
        SM = sb([128, 32], F32, "sm")
        sm_t = T()
        scr_t = T()

        P.memset(otok[:], 0.0, writes=[otok_t], eng="dve")
        P.dma(hgnb, hgr_d.partition_broadcast(128), writes=[c_t])
        P.dma(ptb[:], pt_d.partition_broadcast(128), writes=[idx_t])
        P.ts(idxa[:], ptb[:], 128.0, IOTA, ALU.mult, ALU.add, reads=[idx_t, c_t], writes=[idx_t])
        P.memset(yst[:], 0.0, writes=[yst_t], eng="dve")
        P.dma(yst[0:NSAMP, :], xs_d, writes=[yst_t])
        rmsnorm_to_bf(yst[:], [yst_t], 0)
        transpose_to(lambda: xnT[:, :, 0:128], 0, [xnT_t[0]])
        for g in range(7):
            wgi = load_wgroup(g)
            pi = tok_proj(wgi, 0, 512, 0)
            P.copy(G[g][:], PF[pi][:, :], reads=[pf_t[pi]], writes=[g_t[g]], eng="act" if g % 2 else "dve")
            P.dma(scr_d[:, g * 512:(g + 1) * 512], G[g][0:NSAMP, :], reads=[g_t[g]], writes=[scr_t])
        P.dma(ks_d, G[1][0:NSAMP, :], reads=[g_t[1]])
        P.dma(vs_d, G[2][0:NSAMP, :], reads=[g_t[2]])
        if STAGE == 10.1:
            return
        a, f, k = hw["a"], hw["f"], hw["k"]
        P.act(a[:], G[4][:, 256:512], AF.Exp, reads=[g_t[4]], writes=[hw_t["a"]], scale=-1.0)
        P.ts(a[:], a[:], 1.0, None, ALU.add, reads=[hw_t["a"]], writes=[hw_t["a"]])
        P.recip(a[:], a[:], reads=[hw_t["a"]], writes=[hw_t["a"]])
        P.tt(f[:], a[:], oml[:], ALU.mult, reads=[hw_t["a"], c_t], writes=[hw_t["f"]])
        P.tt(f[:], f[:], lb[:], ALU.add, reads=[hw_t["f"], c_t], writes=[hw_t["f"]])
        P.ts(k[:], f[:], -1.0, 1.0, ALU.mult, ALU.add, reads=[hw_t["f"]], writes=[hw_t["k"]])
        p_f = pf_next()
        for j, (src, st_) in enumerate(((f, hw_t["f"]), (k, hw_t["k"]), (G[4], g_t[4]))):
            for hp in range(2):
                c = (j * 2 + hp) * 4
                P.tr(PF[p_f][:, c:c + 4], src[0:4, hp * 128:(hp + 1) * 128], ident_f[0:4, 0:4],
                     reads=[st_, c_t], writes=[pf_t[p_f]])
        P.copy(FKQ[:], PF[p_f][:, 0:24], reads=[pf_t[p_f]], writes=[fkq_t])

        SS.update(dict(G=G, g_t=g_t, TMP=TMP, tmp_t=tmp_t, HFa=HFa, HFb=HFb, HIa=HIa, HIb=HIb, hfa_t=hfa_t, hfb_t=hfb_t,
                       hia_t=hia_t, hib_t=hib_t, otok=otok, otok_t=otok_t, idxa=idxa, idx_t=idx_t, hgnb=hgnb, FKQ=FKQ,
                       fkq_t=fkq_t, SM=SM, sm_t=sm_t, scr_t=scr_t))

    def sample_stream():
        otok, otok_t, idxa, idx_t, scr_t = SS["otok"], SS["otok_t"], SS["idxa"], SS["idx_t"], SS["scr_t"]
        PG = [KT[:, hp_, S // 2:S].bitcast(F32) for hp_ in range(4)]
        VAf = VA[:, NB // 2:NB, :].rearrange("p a b -> p (a b)").bitcast(F32)
        PG += [VAf[:, 2048:3072]]
        pg_t = [T() for _ in range(5)]
        NPB = 5
        VBF = [VA[:, NB - 4 + j_, :] for j_ in range(4)]
        vbf_t = [T() for _ in range(4)]
        A8b = sb([128, 4, 8], BF16, "a8b")

        TMPs = [VAf[:, 0:512], VAf[:, 512:1024]]
        tmps_t = [T(), T()]
        QBs, qbs_t = VAf[:, 1024:1536], T()
        RES, res_t = VAf[:, 1536:2048], T()
        for hp_ in range(4):
            for kb_ in range(NB // 2, NB):
                kt_t[hp_][kb_].old = [pg_t[hp_]]
        for kb_ in range(NB // 2, NB):
            va_t[kb_].old = [pg_t[4], tmps_t[0], tmps_t[1], qbs_t, res_t] + vbf_t
        S8 = sb([128, 4, 48], F32, "s8")
        S8b = sb([128, 4, 8], BF16, "s8b")
        s8_t = [T() for _ in range(4)]
        NPG = NPAGE
        CB, AVB = 5, 4
        npg = [0]
        for n in range(NSAMP):
            P.dma(QBs, scr_d[n:n + 1, 0:512].partition_broadcast(128), reads=[scr_t], writes=[qbs_t])
            pages = list(range(NPG - 1, -1, -1))

            def st1(ii):
                p = pages[ii]
                i = npg[0] + ii
                col = n * NPAGE + p
                b8 = i % 4
                P.gather(PG[i % NPB], ckv_d[:, :], idxa[:, col:col + 1], reads=[idx_t], writes=[pg_t[i % NPB]])
                P.tt(TMPs[i % 2], PG[i % NPB][:, 0:512], QBs, ALU.mult, reads=[pg_t[i % NPB], qbs_t], writes=[tmps_t[i % 2]])
                P.copy(VBF[b8], PG[i % NPB][:, 512:1024], reads=[pg_t[i % NPB]], writes=[vbf_t[b8]], eng="act")
                P.reduce(S8[:, b8, 0:8], TMPs[i % 2].rearrange("p (h d) -> p h d", h=8), ALU.add,
                         reads=[tmps_t[i % 2]], writes=[s8_t[b8]])
                P.stt(S8[:, b8, 0:8], S8[:, b8, 0:8], 0.125, sbb[:, 0:8], ALU.mult, ALU.add,
                      reads=[s8_t[b8], c_t], writes=[s8_t[b8]])
                P.act(S8[:, b8, 8:16], S8[:, b8, 0:8], AF.Exp, reads=[s8_t[b8]], writes=[s8_t[b8]])
                P.act(S8b[:, b8, :], S8[:, b8, 8:16], AF.Ln, reads=[s8_t[b8]], writes=[s8_t[b8]], bias=1.0)

            def st2(ii):
                i = npg[0] + ii
                b8 = i % 4
                P.mm(PF[CB][:, 0:8], UIN, S8b[:, b8, :], start=(ii == 0), stop=(ii == NPG - 1),
                     reads=[c_t, s8_t[b8]], writes=[pf_t[CB]])
                P.act(S8[:, b8, 24:32], PF[CB][:, 0:8], AF.Exp, reads=[pf_t[CB]], writes=[s8_t[b8]], scale=-1.0)
                P.tt(A8b[:, b8, :], S8[:, b8, 8:16], S8[:, b8, 24:32], ALU.mult, reads=[s8_t[b8]],
                     writes=[s8_t[b8]])
                if ii < NPG - 1:
                    P.mm(PF[CB][:, 0:8], LST, S8b[:, b8, :], start=False, stop=False, reads=[c_t, s8_t[b8]],
                         writes=[pf_t[CB]])
                P.mm(PF[AVB][0:8, :], A8b[:, b8, :], VBF[b8], start=(ii == 0), stop=(ii == NPG - 1),
                     reads=[s8_t[b8], vbf_t[b8]], writes=[pf_t[AVB]])

            for ii in range(NPG + 1):
                if ii < NPG:
                    st1(ii)
                if ii >= 1:
                    st2(ii - 1)
                yield
            npg[0] += NPG
            P.copy(RES[0:8, :], PF[AVB][0:8, :], reads=[pf_t[AVB]], writes=[res_t])
            for h in range(8):
                P.dma(otok[n:n + 1, h * 64:(h + 1) * 64], RES[h:h + 1, h * 64:(h + 1) * 64], reads=[res_t],
                      writes=[otok_t])
            yield

    def sample_back():
        G, g_t, TMP, tmp_t = SS["G"], SS["g_t"], SS["TMP"], SS["tmp_t"]
        HFa, HFb, HIa, HIb = SS["HFa"], SS["HFb"], SS["HIa"], SS["HIb"]
        hfa_t, hfb_t, hia_t, hib_t = SS["hfa_t"], SS["hfb_t"], SS["hia_t"], SS["hib_t"]
        otok, otok_t, hgnb, FKQ, fkq_t, SM, sm_t, scr_t = (SS["otok"], SS["otok_t"], SS["hgnb"], SS["FKQ"], SS["fkq_t"],
                                                          SS["SM"], SS["sm_t"], SS["scr_t"])
        for g in (3, 5, 6):
            P.dma(G[g][0:NSAMP, :], scr_d[:, g * 512:(g + 1) * 512], reads=[scr_t], writes=[g_t[g]])
        P.memset(yst[:], 0.0, writes=[yst_t], eng="dve")
        P.dma(yst[0:NSAMP, :], xs_d, writes=[yst_t])
        for n in range(NSAMP):
            VB, S0, SN, KV = hw["kd"], hsq, hrs, hattn[0]
            P.dma(VB[:], scr_d[n:n + 1, 5 * 512:5 * 512 + 256].partition_broadcast(128), reads=[scr_t],
                  writes=[hw_t["kd"]])
            P.dma(S0[:], sst_d[n], writes=[hsq_t])
            for half in range(2):
                r0 = half * 64
                for hp in range(2):
                    h = hp * 2 + half
                    ck_, cf_ = (1 * 2 + hp) * 4 + n, (0 * 2 + hp) * 4 + n
                    P.ts(KV[r0:r0 + 64, hp * 64:(hp + 1) * 64], VB[r0:r0 + 64, h * 64:(h + 1) * 64],
                         FKQ[r0:r0 + 64, ck_:ck_ + 1], None, ALU.mult, reads=[hw_t["kd"], fkq_t],
                         writes=[hattn_t[0]])
                    P.stt(SN[r0:r0 + 64, hp * 64:(hp + 1) * 64], S0[r0:r0 + 64, hp * 64:(hp + 1) * 64],
                          FKQ[r0:r0 + 64, cf_:cf_ + 1], KV[r0:r0 + 64, hp * 64:(hp + 1) * 64], ALU.mult, ALU.add,
                          reads=[hsq_t, fkq_t, hattn_t[0]], writes=[hrs_t])
            P.dma(hss_d[n], SN[:], reads=[hrs_t])
            if STAGE == 10.55:
                continue
            p_h = 5
            for hp in range(2):
                cq = (2 * 2 + hp) * 4 + n
                P.ts(KV[:, hp * 64:(hp + 1) * 64], SN[:, hp * 64:(hp + 1) * 64], FKQ[:, cq:cq + 1], None, ALU.mult,
                     reads=[hrs_t, fkq_t], writes=[hattn_t[0]])
            P.mm(PF[p_h][:, 0:128], OB, KV[:, :], reads=[c_t, hattn_t[0]], writes=[pf_t[p_h]])
            P.copy(hosb[:, 0:128], PF[p_h][:, 0:128], reads=[pf_t[p_h]], writes=[hosb_t])
            for half in range(2):
                for hp in range(2):
                    h = hp * 2 + half
                    P.dma(otok[n:n + 1, 512 + h * 64:512 + (h + 1) * 64],
                          hosb[half * 64:half * 64 + 1, hp * 64:(hp + 1) * 64], reads=[hosb_t], writes=[otok_t])
            if STAGE == 10.6:
                continue
            XQB, CMK, CMV, XT = hw["lf"], TMP[0], TMP[1], HFb
            P.dma(XQB[:], scr_d[n:n + 1, 6 * 512:6 * 512 + 256].partition_broadcast(128), reads=[scr_t],
                  writes=[hw_t["lf"]])
            P.dma(CMK.rearrange("p (b c) -> p b c", b=2), cmk_d[n].rearrange("(b m) c -> m b c", b=2),
                  writes=[tmp_t[0]])
            P.dma(CMV.rearrange("p (b c) -> p b c", b=2), cmv_d[n].rearrange("(b m) c -> m b c", b=2),
                  writes=[tmp_t[1]])
            for mb in range(2):
                P.tt(XT[:, mb * 256:(mb + 1) * 256], CMK[:, mb * 256:(mb + 1) * 256], XQB[:], ALU.mult,
                     reads=[tmp_t[0], hw_t["lf"]], writes=[hfb_t])
                P.reduce(SM[:, 8 + mb * 4:12 + mb * 4],
                         XT[:, mb * 256:(mb + 1) * 256].rearrange("p (h d) -> p h d", h=4), ALU.add,
                         reads=[hfb_t], writes=[sm_t])
            P.act(SM[:, 16:24], SM[:, 8:16], AF.Exp, reads=[sm_t], writes=[sm_t], scale=0.125)
            p_x, p_d = 0, 1
            for mb in range(2):
                P.mm(PF[p_x][0:4, 0:256], SM[:, 16 + mb * 4:20 + mb * 4], CMV[:, mb * 256:(mb + 1) * 256],
                     start=(mb == 0), stop=(mb == 1), reads=[sm_t, tmp_t[1]], writes=[pf_t[p_x]])
            for mb in range(2):
                P.mm(PF[p_d][0:4, 0:1], SM[:, 16 + mb * 4:20 + mb * 4], ONEC, start=(mb == 0), stop=(mb == 1),
                     reads=[sm_t, c_t], writes=[pf_t[p_d]])
            P.recip(SM[0:4, 24:25], PF[p_d][0:4, 0:1], reads=[pf_t[p_d]], writes=[sm_t])
            P.ts(hosb[0:4, :], PF[p_x][0:4, 0:256], SM[0:4, 24:25], None, ALU.mult, reads=[pf_t[p_x], sm_t],
                 writes=[hosb_t])
            for h in range(4):
                P.dma(otok[n:n + 1, 768 + h * 64:768 + (h + 1) * 64], hosb[h:h + 1, h * 64:(h + 1) * 64],
                      reads=[hosb_t], writes=[otok_t])
        if STAGE == 10.7:
            return
        HG = otok[:, 512:768]
        P.tt(hosb[:], HG, HG, ALU.mult, reads=[otok_t], writes=[hosb_t])
        P.reduce(SM[:, 0:4], hosb[:].rearrange("p (h d) -> p h d", h=4), ALU.add, reads=[hosb_t], writes=[sm_t])
        P.ts(SM[:, 0:4], SM[:, 0:4], 1.0 / 64, EPS, ALU.mult, ALU.add, reads=[sm_t], writes=[sm_t])
        P.act(SM[:, 4:8], SM[:, 0:4], AF.Ln, reads=[sm_t], writes=[sm_t])
        P.act(SM[:, 8:12], SM[:, 4:8], AF.Exp, reads=[sm_t], writes=[sm_t], scale=-0.5)
        for h in range(4):
            P.ts(otok[:, 512 + h * 64:512 + (h + 1) * 64], otok[:, 512 + h * 64:512 + (h + 1) * 64],
                 SM[:, 8 + h:9 + h], None, ALU.mult, reads=[otok_t, sm_t], writes=[otok_t])
        P.tt(HG, HG, hgnb, ALU.mult, reads=[otok_t, c_t], writes=[otok_t])
        for (gsrc, gtok, c0, c1, o0) in ((G[3], g_t[3], 0, 512, 0), (G[5], g_t[5], 256, 512, 512),
                                         (G[6], g_t[6], 256, 512, 768)):
            w_ = c1 - c0
            t_ = HIb[:, 0:w_]
            P.act(t_, gsrc[:, c0:c1], AF.Exp, reads=[gtok], writes=[hib_t], scale=-1.0)
            P.ts(t_, t_, 1.0, None, ALU.add, reads=[hib_t], writes=[hib_t])
            P.recip(t_, t_, reads=[hib_t], writes=[hib_t])
            P.tt(t_, t_, gsrc[:, c0:c1], ALU.mult, reads=[hib_t, gtok], writes=[hib_t])
            P.tt(otok[:, o0:o0 + w_], otok[:, o0:o0 + w_], t_, ALU.mult, reads=[otok_t, hib_t], writes=[otok_t])
        P.copy(xsb[0][:], otok[:], reads=[otok_t], writes=[xsb_t[0]])
        transpose_to(lambda: MIX[:, :, 0:128], 0, mix_t)
        wg0 = load_wgroup(7)
        wg1 = load_wgroup(8)
        st, stt_ = st4[0], st4_t[0]
        for half, wg in ((0, wg0), (1, wg1)):
            pi = pf_next()
            for kb in range(8):
                P.mm(PF[pi][:, :], MIX[:, kb, 0:128], WG[wg][:, kb, :], start=(kb == 0), stop=(kb == 7),
                     reads=[mix_t[kb], wg_t[wg]], writes=[pf_t[pi]])
            P.tt(G[half][:], PF[pi][:, :], yst[:, half * 512:(half + 1) * 512], ALU.add,
                 reads=[pf_t[pi], yst_t], writes=[g_t[half]])
            P.act(xsb[1][:, half * 512:(half + 1) * 512], G[half][:], AF.Square, reads=[g_t[half]],
                  writes=[xsb_t[1], stt_], accum_out=st[:, half:half + 1])
        P.tt(st[:, 0:1], st[:, 0:1], st[:, 1:2], ALU.add, reads=[stt_], writes=[stt_])
        P.ts(st[:, 1:2], st[:, 0:1], 1.0 / D, EPS, ALU.mult, ALU.add, reads=[stt_], writes=[stt_])
        P.act(st[:, 2:3], st[:, 1:2], AF.Ln, reads=[stt_], writes=[stt_])
        P.act(st[:, 3:4], st[:, 2:3], AF.Exp, reads=[stt_], writes=[stt_], scale=-0.5)
        for half in range(2):
            P.stt(G[half][:], G[half][:], st[:, 3:4], fgb[:, half * 512:(half + 1) * 512], ALU.mult, ALU.mult,
                  reads=[g_t[half], stt_, c_t], writes=[g_t[half]])
            P.dma(ys_d[:, half * 512:(half + 1) * 512], G[half][0:NSAMP, :], reads=[g_t[half]])

    gen = [None]

    def pump(k):
        if gen[0] is None:
            return
        for _ in range(k):
            try:
                next(gen[0])
            except StopIteration:
                gen[0] = None
                return

    if ENABLE_SAMPLE:
        sample_front()
        gen[0] = sample_stream()
        rr["npf"] = 4
        P.hook = lambda: pump(1)
        P.every = 6
    for ti in range(N_TILES):
        if ti == HALF or N_TILES < HALF:
            P.hook = None
            pump(10 ** 9)
            rr["npf"] = 6
        tile(ti)
    P.hook = None
    pump(10 ** 9)
    rr["npf"] = 6
    P.dma(hgs_d, Sst[:], reads=[s_t])


    if ENABLE_SAMPLE:
        sample_back()

    P.emit()
    return nc


_NC_CACHE = {}


def _consts():
    s = np.arange(128)[:, None]
    t = np.arange(128)[None, :]
    same = (s // 64) == (t // 64)
    ident = np.eye(128, dtype=np.float32)
    tri = ((s <= t) & same).astype(np.float32)
    su = ((s > t) & same).astype(np.float32)
    ob = same.astype(np.float32)
    cf = np.concatenate([ident, tri, su, ob, np.arange(128, dtype=np.float32)[:, None], np.ones((128, 1), np.float32)],
                        axis=1).astype(np.float32)
    uin = (s >= t).astype(np.float32)
    tq = np.arange(512)[None, :]
    msk = [((d * 128 + s) < tq).astype(np.float32) for d in range(4)]
    lst = (s < t).astype(np.float32)
    cb = np.concatenate([ident, uin, lst] + msk, axis=1).astype(ml_dtypes.bfloat16)
    return cf, cb


def kernel(x_prompt, x_sample, mem_prompt, cache_k, cache_v, page_table, state_hgrn, cache_mem_k, cache_mem_v,
           norm_gain, w_in, sb_bias, hg_lb_logits, hg_norm_gain, mem_norm_gain, w_mem_kv, w_out, final_norm_gain):
    if "nc" not in _NC_CACHE:
        _NC_CACHE["nc"] = build_program()
    nc = _NC_CACHE["nc"]
    f = lambda a: np.ascontiguousarray(np.asarray(a, dtype=np.float32))
    cf, cb = _consts()
    common = {
        "w_in": f(w_in[0]), "w_out": f(w_out[0]), "w_mem": f(w_mem_kv[0]),
        "gin": f(np.asarray(norm_gain[0]).reshape(8, 128).T),
        "gmem": f(np.asarray(mem_norm_gain[0]).reshape(8, 128).T),
        "fg": f(np.asarray(final_norm_gain).reshape(1, D)),
        "hgn": f(np.asarray(hg_norm_gain[0]).reshape(2, 128).T),
        "sbb": f(np.asarray(sb_bias[0]).reshape(1, 8)),
        "lbl": f(np.asarray(hg_lb_logits).reshape(1, 512)),
        "cf": cf, "cb": cb,
    }
    if ENABLE_SAMPLE:
        ckv = np.concatenate([f(cache_k[0]).reshape(POOL_PAGES * 128, 512),
                              f(cache_v[0]).reshape(POOL_PAGES * 128, 512)], axis=1)
    in_maps = []
    for c in range(8):
        b = c // 2
        m = dict(common)
        r = c % 2
        xb = f(x_prompt[b])
        m["x"] = np.ascontiguousarray(xb[:S // 2]) if r == 1 else np.zeros((S // 2, D), np.float32)
        m["xo"] = np.ascontiguousarray(xb[r * (S // 2):(r + 1) * (S // 2)])
        m["kbias"] = np.full((128, 1), 0.0 if r == 1 else -30000.0, np.float32)
        m["mem"] = f(mem_prompt[b])
        if ENABLE_SAMPLE:
            sl = slice(NSAMP * c, NSAMP * (c + 1))
            m["xs"] = f(x_sample[sl, 0])
            m["pt"] = np.ascontiguousarray(np.asarray(page_table[sl], dtype=np.int32).reshape(1, NSAMP * NPAGE))
            m["ckv"] = ckv
            st = f(state_hgrn[0, sl]).reshape(NSAMP, 2, 2, 64, 64).transpose(0, 2, 3, 1, 4).reshape(NSAMP, 128, 128)
            m["sst"] = np.ascontiguousarray(st)
            m["cmk"] = f(cache_mem_k[0, sl]).reshape(NSAMP, 256, 256)
            m["cmv"] = f(cache_mem_v[0, sl]).reshape(NSAMP, 256, 256)
            m["hgr"] = f(np.asarray(hg_norm_gain[0]).reshape(1, 256))
        in_maps.append(m)
    res = run_bass_kernel_spmd(nc, in_maps[:NCORES], core_ids=list(range(NCORES))).results
    res = list(res) + [res[0], res[1 % NCORES]] * ((8 - NCORES) // 2 + 1)

    def unstate(a):
        return a.reshape(2, 64, 2, 64).transpose(2, 0, 1, 3).reshape(4, 64, 64)

    cat = lambda b, n_: np.concatenate([res[2 * b][n_], res[2 * b + 1][n_]], axis=0)
    y_prompt = np.stack([cat(b, "y") for b in range(4)]).astype(np.float32)
    k_prompt = np.stack([cat(b, "k") for b in range(4)]).reshape(1, 4, S, 8, 64).astype(np.float32)
    v_prompt = np.stack([cat(b, "v") for b in range(4)]).reshape(1, 4, S, 8, 64).astype(np.float32)
    hgrn_prompt = np.stack([unstate(res[2 * b + 1]["hgs"]) for b in range(4)])[None].astype(np.float32)
    mem_k = np.stack([res[2 * b]["mk"] for b in range(4)]).reshape(1, 4, 256, 4, 64).astype(np.float32)
    mem_v = np.stack([res[2 * b]["mv"] for b in range(4)]).reshape(1, 4, 256, 4, 64).astype(np.float32)
    if ENABLE_SAMPLE:
        y_sample = np.concatenate([res[c]["ys"] for c in range(8)]).reshape(32, 1, D).astype(np.float32)
        k_sample = np.concatenate([res[c]["ks"] for c in range(8)]).reshape(1, 32, 1, 8, 64).astype(np.float32)
        v_sample = np.concatenate([res[c]["vs"] for c in range(8)]).reshape(1, 32, 1, 8, 64).astype(np.float32)
        hgrn_sample = np.concatenate([np.stack([unstate(res[c]["hss"][i]) for i in range(NSAMP)])
                                      for c in range(8)])[None].astype(np.float32)
    else:
        y_sample = np.zeros((32, 1, D), np.float32)
        k_sample = np.zeros((1, 32, 1, 8, 64), np.float32)
        v_sample = np.zeros((1, 32, 1, 8, 64), np.float32)
        hgrn_sample = np.zeros((1, 32, 4, 64, 64), np.float32)
    return (y_prompt, y_sample, k_prompt, v_prompt, hgrn_prompt, mem_k, mem_v, k_sample, v_sample, hgrn_sample)
```

```python
import contextlib
import numpy as np
import ml_dtypes
import concourse.bass as bass
import concourse.mybir as mybir
from concourse.bass_utils import run_bass_kernel_spmd

F32 = mybir.dt.float32
BF16 = mybir.dt.bfloat16
I32 = mybir.dt.int32
AF = mybir.ActivationFunctionType
ALU = mybir.AluOpType
AX = mybir.AxisListType

D = 1024
S = 4096
NT = S // 512
NB = S // 128
DIN = 3584
EPS = 1e-6
NSAMP = 4
NPAGE = 64
N_DMA_SEM = 12
RING = 3 * N_DMA_SEM

ENABLE_SAMPLE = True
N_TILES = NT
STAGE = 99
NCORES = 8
POOL_PAGES = 2560
NO_SCRATCH_READ = False
DBG_PRE = False
DBG_KBS = None


class Tok:
    __slots__ = ("w", "r", "ps", "old")

    def __init__(self, ps=False, old=None):
        self.w = None
        self.r = []
        self.ps = ps
        self.old = old


class Prog:
    def __init__(self, nc, es):
        self.nc = nc
        self.es = es
        self.ops = []
        self.hook = None
        self.every = 6
        self._n = 0
        self._in_hook = False
        self.eng = {"pe": nc.tensor, "act": nc.scalar, "dve": nc.vector, "pool": nc.gpsimd, "sp": nc.sync}

    def op(self, eng, fn, reads=(), writes=(), dma=False):
        idx = len(self.ops)
        deps = set()
        for t in reads:
            if t.w is not None:
                deps.add(t.w)
            if t.ps:
                deps.update(r for r in t.r if self.ops[r][0] != eng)
        for t in writes:
            if t.w is not None:
                deps.add(t.w)
            deps.update(t.r)
            if t.old:
                for o in t.old:
                    if o.w is not None:
                        deps.add(o.w)
                    deps.update(o.r)
                t.old = None
        deps.discard(idx)
        self.ops.append([eng, fn, deps, dma])
        for t in reads:
            t.r.append(idx)
        for t in writes:
            t.w = idx
            t.r = []
        if self.hook is not None and not self._in_hook:
            self._n += 1
            if self._n % self.every == 0:
                self._in_hook = True
                try:
                    self.hook()
                finally:
                    self._in_hook = False
        return idx

    def dma(self, out, in_, reads=(), writes=(), q="sp", **kw):
        e = self.eng[q]
        return self.op(q, lambda: e.dma_start(out=out, in_=in_, **kw), reads, writes, dma=True)

    def gather(self, out, in_, idx_ap, reads=(), writes=()):
        g = self.nc.gpsimd
        return self.op("pool", lambda: g.indirect_dma_start(
            out=out, out_offset=None, in_=in_, in_offset=bass.IndirectOffsetOnAxis(ap=idx_ap, axis=0)),
            reads, writes, dma=True)

    def act(self, out, in_, func, reads=(), writes=(), **kw):
        a = self.nc.scalar
        return self.op("act", lambda: a.activation(out=out, in_=in_, func=func, **kw), reads, writes)

    def tt(self, out, in0, in1, op, reads=(), writes=(), eng="dve"):
        e = self.eng[eng]
        return self.op(eng, lambda: e.tensor_tensor(out=out, in0=in0, in1=in1, op=op), reads, writes)

    def ts(self, out, in0, s1, s2, op0, op1=None, reads=(), writes=(), eng="dve"):
        e = self.eng[eng]
        if op1 is None:
            return self.op(eng, lambda: e.tensor_scalar(out=out, in0=in0, scalar1=s1, scalar2=None, op0=op0),
                           reads, writes)
        return self.op(eng, lambda: e.tensor_scalar(out=out, in0=in0, scalar1=s1, scalar2=s2, op0=op0, op1=op1),
                       reads, writes)

    def stt(self, out, in0, scalar, in1, op0, op1, reads=(), writes=()):
        e = self.nc.vector
        return self.op("dve", lambda: e.scalar_tensor_tensor(out=out, in0=in0, scalar=scalar, in1=in1,
                                                             op0=op0, op1=op1), reads, writes)

    def copy(self, out, in_, reads=(), writes=(), eng="dve"):
        e = self.eng[eng]
        if eng == "act":
            return self.op(eng, lambda: e.activation(out=out, in_=in_, func=AF.Copy), reads, writes)
        return self.op(eng, lambda: e.tensor_copy(out=out, in_=in_), reads, writes)

    def recip(self, out, in_, reads=(), writes=()):
        e = self.nc.vector
        return self.op("dve", lambda: e.reciprocal(out=out, in_=in_), reads, writes)

    def reduce(self, out, in_, op, reads=(), writes=()):
        e = self.nc.vector
        return self.op("dve", lambda: e.tensor_reduce(out=out, in_=in_, axis=AX.X, op=op), reads, writes)

    def scan(self, out, d0, d1, reads=(), writes=()):
        e = self.nc.vector
        return self.op("dve", lambda: e.tensor_tensor_scan(out=out, data0=d0, data1=d1, initial=0.0,
                                                           op0=ALU.mult, op1=ALU.add), reads, writes)

    def memset(self, ap, val, writes=(), eng="pool"):
        e = self.eng[eng]
        return self.op(eng, lambda: e.memset(ap, val), (), writes)

    def mm(self, out, lhsT, rhs, start=True, stop=True, reads=(), writes=(), tp=None):
        t = self.nc.tensor
        if tp is None:
            return self.op("pe", lambda: t.matmul(out, lhsT=lhsT, rhs=rhs, start=start, stop=stop), reads, writes)
        return self.op("pe", lambda: t.matmul(out, lhsT=lhsT, rhs=rhs, start=start, stop=stop, tile_position=tp),
                       reads, writes)

    def tr(self, out, in_, ident, reads=(), writes=()):
        t = self.nc.tensor
        return self.op("pe", lambda: t.transpose(out=out, in_=in_, identity=ident), reads, writes)

    def emit(self):
        nc, es = self.nc, self.es
        ops = self.ops
        n = len(ops)
        comp = ("pe", "act", "dve", "pool")
        pos = [0] * n
        cnt = {e: 0 for e in comp}
        qbase = {"sp": 0, "pool": N_DMA_SEM, "act": 2 * N_DMA_SEM}
        dq = {q: [] for q in qbase}
        for i, (eng, fn, deps, dma) in enumerate(ops):
            if dma:
                k = len(dq[eng])
                pos[i] = (k // N_DMA_SEM) * RING + qbase[eng] + (k % N_DMA_SEM)
                dq[eng].append(i)
            else:
                pos[i] = cnt[eng]
                cnt[eng] += 1
        dma_ops = [i for i in range(n) if ops[i][3]]
        for q, lst in dq.items():
            for k, i in enumerate(lst):
                if k >= N_DMA_SEM:
                    ops[i][2].add(lst[k - N_DMA_SEM])
        waited = {}
        marked = [False] * n
        waits = [None] * n
        for i, (eng, fn, deps, dma) in enumerate(ops):
            wl = []
            for d in sorted(deps):
                deng, _, _, ddma = ops[d]
                if ddma:
                    key = (eng, "dma", pos[d] % RING)
                    val = pos[d] // RING
                else:
                    if deng == eng:
                        if eng == "pe":
                            continue
                        if eng != "pool" and pos[i] - pos[d] > 2:
                            continue
                    key = (eng, deng)
                    val = pos[d]
                if waited.get(key, -1) >= val:
                    continue
                waited[key] = val
                wl.append(d)
                marked[d] = True
            waits[i] = wl
        sem = {e: es.enter_context(nc.semaphore("s_" + e)) for e in comp}
        dsem = [es.enter_context(nc.semaphore("s_dma%d" % k)) for k in range(RING)]
        val = [0] * n
        c2 = {e: 0 for e in comp}
        for i, (eng, fn, deps, dma) in enumerate(ops):
            if dma:
                val[i] = 16 * (pos[i] // RING + 1)
            elif marked[i]:
                c2[eng] += 1
                val[i] = c2[eng]
        for i, (eng, fn, deps, dma) in enumerate(ops):
            e = self.eng[eng]
            for d in waits[i]:
                deng, _, _, ddma = ops[d]
                if ddma:
                    e.wait_ge(dsem[pos[d] % RING], val[d])
                else:
                    e.wait_ge(sem[deng], val[d])
            ins = fn()
            if dma:
                ins.then_inc(dsem[pos[i] % RING], 16)
            elif marked[i]:
                ins.then_inc(sem[eng], 1)
        last = {}
        for i in dma_ops:
            last[pos[i] % RING] = val[i]
        for k, v in last.items():
            nc.sync.wait_ge(dsem[k], v)
        for e in comp:
            if c2[e] > 0:
                nc.sync.wait_ge(sem[e], c2[e])


def build_program():
    nc = bass.Bass("TRN2", target_bir_lowering=False)
    es = contextlib.ExitStack()
    P = Prog(nc, es)

    def din(name, shape, dt=F32):
        return nc.dram_tensor(name, list(shape), dt, kind="ExternalInput").ap()

    def dout(name, shape, dt=F32):
        return nc.dram_tensor(name, list(shape), dt, kind="ExternalOutput").ap()

    x_d = din("x", [S // 2, D])
    xo_d = din("xo", [S // 2, D])
    kb_d = din("kbias", [128, 1])
    mem_d = din("mem", [256, D])
    w_in_d = din("w_in", [D, DIN])
    w_out_d = din("w_out", [D, D])
    w_mem_d = din("w_mem", [D, 512])
    gin_d = din("gin", [128, 8])
    gmem_d = din("gmem", [128, 8])
    fg_d = din("fg", [1, D])
    hgn_d = din("hgn", [128, 2])
    sbb_d = din("sbb", [1, 8])
    lbl_d = din("lbl", [1, 512])
    cf_d = din("cf", [128, 514])
    cb_d = din("cb", [128, 384 + 2048], BF16)
    y_d = dout("y", [S // 2, D])
    k_d = dout("k", [S // 2, 512])
    v_d = dout("v", [S // 2, 512])
    hgs_d = dout("hgs", [128, 128])
    mk_d = dout("mk", [256, 256])
    mv_d = dout("mv", [256, 256])
    wsc_d = nc.dram_tensor("wsc", [9, 128, 4096], BF16, kind="Internal").ap()
    if ENABLE_SAMPLE:
        xs_d = din("xs", [NSAMP, D])
        pt_d = din("pt", [1, NSAMP * NPAGE], I32)
        ckv_d = din("ckv", [POOL_PAGES * 128, 1024])
        sst_d = din("sst", [NSAMP, 128, 128])
        cmk_d = din("cmk", [NSAMP, 256, 256])
        cmv_d = din("cmv", [NSAMP, 256, 256])
        ys_d = dout("ys", [NSAMP, D])
        ks_d = dout("ks", [NSAMP, 512])
        vs_d = dout("vs", [NSAMP, 512])
        hss_d = dout("hss", [NSAMP, 128, 128])
        hgr_d = din("hgr", [1, 256])
        scr_d = nc.dram_tensor("scr", [NSAMP, DIN], F32, kind="Internal").ap()

    cnt = [0]

    def sb(shape, dt=F32, name=None):
        cnt[0] += 1
        return es.enter_context(nc.sbuf_tensor("sb_" + (name or ("t%d" % cnt[0])), list(shape), dt))

    def psum(shape, dt=F32):
        cnt[0] += 1
        return es.enter_context(nc.psum_tensor("p%d" % cnt[0], list(shape), dt))

    def T():
        return Tok()

    KT = sb([128, 4, S], BF16, "KT")
    kt_t = [[T() for _ in range(NB)] for _ in range(4)]
    VA = sb([128, NB, 512], BF16, "VA")
    va_t = [T() for _ in range(NB)]
    MKT = sb([128, 2, 256], BF16, "MKT")
    MV = sb([128, 2, 256], BF16, "MV")
    mk_t = T()
    cf = sb([128, 514], F32, "cf")
    cb = sb([128, 384 + 2048], BF16, "cb")
    c_t = T()
    ident_f, TRI, SU, OB = cf[:, 0:128], cf[:, 128:256], cf[:, 256:384], cf[:, 384:512]
    IOTA, ONEC = cf[:, 512:513], cf[:, 513:514]
    ident_b, UIN, LST = cb[:, 0:128], cb[:, 128:256], cb[:, 256:384]
    MSK = [cb[:, 384 + 512 * d: 384 + 512 * (d + 1)] for d in range(4)]
    ONESB = sb([128, 128], BF16, "onesb")
    gin = sb([128, 8], F32, "gin")
    gmem = sb([128, 8], F32, "gmem")
    fgb = sb([128, D], F32, "fgb")
    hgn = sb([128, 2], F32, "hgn")
    sbb = sb([128, 8], F32, "sbb")
    sbbp = sb([128, 8], F32, "sbbp")
    kbias = sb([128, 1], F32, "kbias")
    lbl = sb([128, 512], F32, "lbl")
    lb = sb([128, 256], F32, "lb")
    oml = sb([128, 256], F32, "oml")
    Sst = sb([128, 128], F32, "Sst")
    s_t = T()

    PF = [psum([128, 512], F32) for _ in range(6)]
    pf_t = [Tok(ps=True) for _ in range(6)]
    PB = [psum([128, 1024], BF16) for _ in range(2)]
    pb_t = [Tok(ps=True) for _ in range(2)]
    PBf = [PB[i_][:, :].bitcast(F32) for i_ in range(2)]

    class Deferred:
        def __init__(self):
            self.q = []

        def __getattr__(self, name):
            return lambda *a, **k: self.q.append((name, a, k))

        def pump(self, n_):
            for _ in range(min(n_, len(self.q))):
                name, a, k = self.q.pop(0)
                getattr(P, name)(*a, **k)


    P.dma(cf[:], cf_d, writes=[c_t])
    P.dma(cb[:], cb_d, writes=[c_t])
    P.dma(gin[:], gin_d, writes=[c_t])
    P.dma(gmem[:], gmem_d, writes=[c_t])
    P.dma(fgb[:], fg_d.partition_broadcast(128), writes=[c_t])
    P.dma(hgn[:], hgn_d, writes=[c_t])
    P.dma(sbb[:], sbb_d.partition_broadcast(128), writes=[c_t])
    P.dma(lbl[:], lbl_d.partition_broadcast(128), writes=[c_t])
    P.dma(kbias[:], kb_d, writes=[c_t])
    P.ts(sbbp[:], sbb[:], kbias[:, 0:1], None, ALU.add, reads=[c_t], writes=[c_t])
    P.memset(ONESB[:], 1.0, writes=[c_t], eng="dve")
    P.memset(Sst[:], 0.0, writes=[s_t], eng="dve")
    P.tt(lb[:], lbl[:, 256:512], lbl[:, 0:256], ALU.subtract, reads=[c_t], writes=[c_t])
    P.act(lb[:], lb[:], AF.Exp, reads=[c_t], writes=[c_t])
    P.ts(lb[:], lb[:], 1.0, None, ALU.add, reads=[c_t], writes=[c_t])
    P.recip(lb[:], lb[:], reads=[c_t], writes=[c_t])
    P.ts(oml[:], lb[:], -1.0, 1.0, ALU.mult, ALU.add, reads=[c_t], writes=[c_t])

    if STAGE == 0:
        P.emit()
        return nc
    xst = [sb([128, D], F32) for _ in range(2)]
    xst_t = [T() for _ in range(2)]
    xsb = [sb([128, D], BF16) for _ in range(2)]
    xsb_t = [T() for _ in range(2)]
    st4 = [sb([128, 4], F32) for _ in range(2)]
    st4_t = [T() for _ in range(2)]
    wst = [sb([128, 512], F32) for _ in range(2)]
    wst_t = [T() for _ in range(2)]
    WG = [sb([128, 8, 512], BF16) for _ in range(2)]
    wg_t = [T() for _ in range(2)]
    wsc_t = [T() for _ in range(9)]
    rr = {"x": 0, "w": 0, "wg": 0, "pf": 0, "pb": 0, "npf": 6}

    def rmsnorm_to_bf(src_ap, reads, which):
        st = st4[which]
        stt_ = st4_t[which]
        P.act(xsb[which][:], src_ap, AF.Square, reads=reads, writes=[xsb_t[which], stt_], accum_out=st[:, 0:1])
        P.ts(st[:, 1:2], st[:, 0:1], 1.0 / D, EPS, ALU.mult, ALU.add, reads=[stt_], writes=[stt_])
        P.act(st[:, 2:3], st[:, 1:2], AF.Ln, reads=[stt_], writes=[stt_])
        P.act(st[:, 3:4], st[:, 2:3], AF.Exp, reads=[stt_], writes=[stt_], scale=-0.5)
        P.act(xsb[which][:], src_ap, AF.Copy, reads=list(reads) + [stt_], writes=[xsb_t[which]], scale=st[:, 3:4])

    def transpose_to(dst_ap_fn, which, dst_toks):
        pb = rr["pb"] % 2
        rr["pb"] += 1
        for kb in range(8):
            P.tr(PB[pb][:, kb * 128:(kb + 1) * 128], xsb[which][:, kb * 128:(kb + 1) * 128], ident_b,
                 reads=[xsb_t[which], c_t], writes=[pb_t[pb]])
        P.copy(dst_ap_fn(), PB[pb][:, :].rearrange("p (k t) -> p k t", k=8), reads=[pb_t[pb]], writes=dst_toks)

    for g in range(9):
        wgi = rr["wg"] % 2
        rr["wg"] += 1
        for kb in range(8):
            wi = rr["w"] % 2
            rr["w"] += 1
            if g < 7:
                src = w_in_d[kb * 128:(kb + 1) * 128, g * 512:(g + 1) * 512]
            else:
                src = w_out_d[kb * 128:(kb + 1) * 128, (g - 7) * 512:(g - 6) * 512]
            P.dma(wst[wi][:], src, writes=[wst_t[wi]])
            if g < 7:
                if kb % 2 == 0:
                    P.ts(WG[wgi][:, kb, :], wst[wi][:], gin[:, kb:kb + 1], None, ALU.mult,
                         reads=[wst_t[wi], c_t], writes=[wg_t[wgi]])
                else:
                    P.act(WG[wgi][:, kb, :], wst[wi][:], AF.Copy, reads=[wst_t[wi], c_t], writes=[wg_t[wgi]],
                          scale=gin[:, kb:kb + 1])
            else:
                P.copy(WG[wgi][:, kb, :], wst[wi][:], reads=[wst_t[wi]], writes=[wg_t[wgi]],
                       eng="dve" if kb % 2 == 0 else "pool")
        P.dma(wsc_d[g], WG[wgi][:, :, :].rearrange("p k c -> p (k c)"), reads=[wg_t[wgi]], writes=[wsc_t[g]])

    if STAGE == 1:
        P.emit()
        return nc

    def load_wgroup(g):
        wgi = rr["wg"] % 2
        rr["wg"] += 1
        if not NO_SCRATCH_READ:
            P.dma(WG[wgi][:, :, :].rearrange("p k c -> p (k c)"), wsc_d[g], reads=[wsc_t[g]], writes=[wg_t[wgi]])
        return wgi

    xnT = sb([128, 8, 512], BF16, "xnT")
    xnT_t = [T() for _ in range(4)]
    memT = [xnT[:, :, mb_ * 128:(mb_ + 1) * 128] for mb_ in range(2)]
    memT_t = [xnT_t[0], xnT_t[1]]
    for mb in range(2):
        xi = rr["x"] % 2
        rr["x"] += 1
        P.dma(xst[xi][:], mem_d[mb * 128:(mb + 1) * 128, :], writes=[xst_t[xi]])
        rmsnorm_to_bf(xst[xi][:], [xst_t[xi]], xi)
        transpose_to(lambda mb=mb: memT[mb], xi, [memT_t[mb]])
    wmb = [sb([128, 512], BF16) for _ in range(2)]
    wmb_t = [T() for _ in range(2)]
    for kb in range(8):
        wi = rr["w"] % 2
        rr["w"] += 1
        P.dma(wst[wi][:], w_mem_d[kb * 128:(kb + 1) * 128, :], writes=[wst_t[wi]])
        P.ts(wmb[kb % 2][:], wst[wi][:], gmem[:, kb:kb + 1], None, ALU.mult, reads=[wst_t[wi], c_t],
             writes=[wmb_t[kb % 2]])
        for mb in range(2):
            P.mm(PF[mb][:, :], xnT[:, kb, mb * 128:(mb + 1) * 128], wmb[kb % 2][:], start=(kb == 0), stop=(kb == 7),
                 reads=[memT_t[mb], wmb_t[kb % 2]], writes=[pf_t[mb]])
    kvst = [sb([128, 512], F32) for _ in range(2)]
    kvst_t = [T() for _ in range(2)]
    kbf = [sb([128, 512], BF16) for _ in range(2)]
    kbf_t = [T() for _ in range(2)]
    for mb in range(2):
        P.copy(kvst[mb][:], PF[mb][:, :], reads=[pf_t[mb]], writes=[kvst_t[mb]], eng="act" if mb else "dve")
        P.dma(mk_d[mb * 128:(mb + 1) * 128, :], kvst[mb][:, 0:256], reads=[kvst_t[mb]])
        P.dma(mv_d[mb * 128:(mb + 1) * 128, :], kvst[mb][:, 256:512], reads=[kvst_t[mb]])
        P.copy(kbf[mb][:, 0:256], kvst[mb][:, 0:256], reads=[kvst_t[mb]], writes=[kbf_t[mb]])
        P.copy(MV[:, mb, :], kvst[mb][:, 256:512], reads=[kvst_t[mb]], writes=[mk_t])
        pb = rr["pb"] % 2
        rr["pb"] += 1
        for hp in range(2):
            P.tr(PB[pb][:, hp * 128:(hp + 1) * 128], kbf[mb][:, hp * 128:(hp + 1) * 128], ident_b,
                 reads=[kbf_t[mb], c_t], writes=[pb_t[pb]])
        P.copy(MKT[:, :, mb * 128:(mb + 1) * 128], PB[pb][:, 0:256].rearrange("p (k t) -> p k t", k=2),
               reads=[pb_t[pb]], writes=[mk_t])

    if STAGE == 2:
        P.emit()
        return nc
    QT = sb([128, 4, 512], BF16, "QT")
    qt_t = [T() for _ in range(4)]
    SG = sb([128, 8, 512], BF16, "SG")
    sg_t = [T() for _ in range(8)]
    HQT = sb([128, 2, 512], F32, "HQT")
    hq_t = [T() for _ in range(2)]
    XQT = sb([128, 2, 512], BF16, "XQT")
    xq_t = [T() for _ in range(2)]
    HF = sb([128, 4, 256], F32, "HF")
    HI = sb([128, 4, 256], F32, "HI")
    hf_t = [T() for _ in range(4)]
    hi_t = [T() for _ in range(4)]
    MIX = sb([128, 8, 512], BF16, "MIX")
    mix_t = [T() for _ in range(8)]
    gtmp = [sb([128, 512], F32) for _ in range(2)]
    gtmp_t = [T() for _ in range(2)]
    EB = [sb([128, 512], F32) for _ in range(4)]
    eb_t = [T() for _ in range(4)]
    SPB = wmb + [sb([128, 512], BF16) for _ in range(2)]
    spb_t = wmb_t + [T() for _ in range(2)]
    WB = [sb([128, 512], BF16) for _ in range(2)]
    wb_t = [T() for _ in range(2)]
    AB = [sb([128, 512], BF16) for _ in range(4)]
    ab_t = [T() for _ in range(4)]
    KTf = KT[:, 0, :].bitcast(F32)
    SPACC = [KTf[:, 0:512], KTf[:, 512:1024]]
    spacc_t = [Tok(old=[kt_t[0][kb_] for kb_ in range(NB)]) for _ in range(2)]
    hw = {n_: sb([128, 256], F32, "hw_" + n_) for n_ in ("a", "f", "lf", "k", "kd")}
    hw_t = {n_: T() for n_ in hw}
    hx = {n_: sb([128, 2, 128], F32, "hx_" + n_) for n_ in ("ep", "en", "qe", "ke")}
    hx_t = {n_: T() for n_ in hx}
    hattn = [sb([128, 128], F32) for _ in range(2)]
    hattn_t = [T() for _ in range(2)]
    hdec = sb([128, 4], F32, "hdec")
    hdec_t = T()
    hosb = sb([128, 256], F32, "hosb")
    hosb_t = T()
    hsq = sb([128, 128], F32, "hsq")
    hsq_t = T()
    hrs = sb([128, 128], F32, "hrs")
    hrs_t = T()
    yst = sb([128, D], F32, "yst")
    yst_t = T()

    def pf_next():
        i = rr["pf"] % rr["npf"]
        rr["pf"] += 1
        return i

    def silu_from_psum(pi, dst_ap, dst_tok):
        gi = rr["x"] % 2
        rr["x"] += 1
        P.act(gtmp[gi][:], PF[pi][:, :], AF.Exp, reads=[pf_t[pi]], writes=[gtmp_t[gi]], scale=-1.0)
        P.ts(gtmp[gi][:], gtmp[gi][:], 1.0, None, ALU.add, reads=[gtmp_t[gi]], writes=[gtmp_t[gi]])
        P.recip(gtmp[gi][:], gtmp[gi][:], reads=[gtmp_t[gi]], writes=[gtmp_t[gi]])
        P.tt(dst_ap, PF[pi][:, :], gtmp[gi][:], ALU.mult, reads=[pf_t[pi], gtmp_t[gi]], writes=[dst_tok])

    def feat_proj(wgi, cbs, evac):
        for cbi in cbs:
            pi = pf_next()
            for kb in range(8):
                P.mm(PF[pi][:, :], WG[wgi][:, kb, cbi * 128:(cbi + 1) * 128], xnT[:, kb, :], start=(kb == 0),
                     stop=(kb == 7), reads=[wg_t[wgi]] + xnT_t, writes=[pf_t[pi]])
            evac(cbi, pi)

    def tok_proj(wgi, c0, c1, blk):
        pi = pf_next()
        for kb in range(8):
            P.mm(PF[pi][:, 0:c1 - c0], xnT[:, kb, blk * 128:(blk + 1) * 128], WG[wgi][:, kb, c0:c1],
                 start=(kb == 0), stop=(kb == 7), reads=[wg_t[wgi], xnT_t[blk]], writes=[pf_t[pi]])
        return pi

    def hgrn_block(blk, state_only=False, HP=None, defer=False):
        if HP is None:
            HP = P
        if defer:
            HB = {0: PBf[0], 1: PBf[0], 2: PBf[0], 5: PBf[0], 3: PF[5], 4: PBf[1]}
            hb_t = {0: pb_t[0], 1: pb_t[0], 2: pb_t[0], 5: pb_t[0], 3: pf_t[5], 4: pb_t[1]}
        else:
            HB, hb_t = PF, pf_t

        c0 = blk * 128
        tmpb = [0, 1, 2] if state_only else [0, 1, 2, 5]

        def tmp_next():
            i = tmpb[rr["pf"] % len(tmpb)]
            rr["pf"] += 1
            return i
        a, f, lf, k, kd = hw["a"], hw["f"], hw["lf"], hw["k"], hw["kd"]
        HP.act(a[:], HF[:, blk, :], AF.Exp, reads=[hf_t[blk]], writes=[hw_t["a"]], scale=-1.0)
        HP.ts(a[:], a[:], 1.0, None, ALU.add, reads=[hw_t["a"]], writes=[hw_t["a"]])
        HP.recip(a[:], a[:], reads=[hw_t["a"]], writes=[hw_t["a"]])
        HP.tt(f[:], a[:], oml[:], ALU.mult, reads=[hw_t["a"], c_t], writes=[hw_t["f"]])
        HP.tt(f[:], f[:], lb[:], ALU.add, reads=[hw_t["f"], c_t], writes=[hw_t["f"]])
        HP.act(lf[:], f[:], AF.Ln, reads=[hw_t["f"]], writes=[hw_t["lf"]])
        HP.ts(k[:], f[:], -1.0, 1.0, ALU.mult, ALU.add, reads=[hw_t["f"]], writes=[hw_t["k"]])
        if STAGE == 3.1:
            return
        p_rev = tmp_next()
        HP.mm(HB[p_rev][:, 0:256], SU, lf[:], reads=[c_t, hw_t["lf"]], writes=[hb_t[p_rev]])
        HP.act(kd[:], HB[p_rev][:, 0:256], AF.Exp, reads=[hb_t[p_rev]], writes=[hw_t["kd"]])
        HP.tt(kd[:], kd[:], k[:], ALU.mult, reads=[hw_t["kd"], hw_t["k"]], writes=[hw_t["kd"]])
        if STAGE == 3.2:
            return
        p_bc = tmp_next()
        for hp in range(2):
            HP.mm(HB[p_bc][:, hp * 128:(hp + 1) * 128], lf[:, hp * 128:(hp + 1) * 128], TRI,
                 reads=[hw_t["lf"], c_t], writes=[hb_t[p_bc]])
        HP.act(hx["ep"][:, :, :], HB[p_bc][:, 0:256].rearrange("p (k t) -> p k t", k=2), AF.Exp,
              reads=[hb_t[p_bc]], writes=[hx_t["ep"]])
        if not state_only:
            HP.act(hx["en"][:, :, :], HB[p_bc][:, 0:256].rearrange("p (k t) -> p k t", k=2), AF.Exp,
                  reads=[hb_t[p_bc]], writes=[hx_t["en"]], scale=-1.0)
            HP.tt(hx["qe"][:, :, :], hx["ep"][:, :, :], HQT[:, :, c0:c0 + 128], ALU.mult,
                 reads=[hx_t["ep"]] + hq_t, writes=[hx_t["qe"]])
            if STAGE == 3.3:
                return
            p_kt = tmp_next()
            for hp in range(2):
                HP.tr(HB[p_kt][:, hp * 128:(hp + 1) * 128], k[:, hp * 128:(hp + 1) * 128], ident_f,
                     reads=[hw_t["k"], c_t], writes=[hb_t[p_kt]])
            HP.tt(hx["ke"][:, :, :], HB[p_kt][:, 0:256].rearrange("p (k t) -> p k t", k=2), hx["en"][:, :, :], ALU.mult,
                 reads=[hb_t[p_kt], hx_t["en"]], writes=[hx_t["ke"]])
        for c in range(2):
            for hp in range(2):
                HP.copy(hdec[:, c * 2 + hp: c * 2 + hp + 1], hx["ep"][:, hp, c * 64 + 63: c * 64 + 64],
                       reads=[hx_t["ep"]], writes=[hdec_t])
        if STAGE == 3.4:
            return
        if not state_only:
            p_o = 3
            for h in range(4):
                hp, rb = h // 2, (h % 2) * 64
                p_at = tmp_next()
                HP.mm(HB[p_at][:, 0:128], hx["ke"][rb:rb + 64, hp, :], hx["qe"][rb:rb + 64, hp, :],
                     reads=[hx_t["ke"], hx_t["qe"]], writes=[hb_t[p_at]], tp=(rb, 0))
                ai = h % 2
                HP.tt(hattn[ai][:], HB[p_at][:, 0:128], TRI, ALU.mult, reads=[hb_t[p_at], c_t], writes=[hattn_t[ai]])
                HP.mm(HB[p_o][rb:rb + 64, hp * 128:(hp + 1) * 128], HI[:, blk, h * 64:(h + 1) * 64], hattn[ai][:],
                     start=True, stop=True, reads=[hi_t[blk], hattn_t[ai]], writes=[hb_t[p_o]], tp=(0, rb))
            if STAGE == 3.5:
                return
        p_i = 4
        for c in range(2):
            for h in range(4):
                if STAGE == 3.55 or state_only:
                    break
                hp, rb = h // 2, (h % 2) * 64
                HP.mm(HB[p_i][rb:rb + 64, hp * 128 + c * 64: hp * 128 + c * 64 + 64],
                     Sst[rb:rb + 64, hp * 64:(hp + 1) * 64], hx["qe"][rb:rb + 64, hp, c * 64:(c + 1) * 64],
                     start=True, stop=True, reads=[s_t, hx_t["qe"]], writes=[hb_t[p_i]], tp=(rb, rb))
            if STAGE == 3.57:
                continue
            p_s = tmp_next()
            for h in range(4):
                hp, rb = h // 2, (h % 2) * 64
                HP.mm(HB[p_s][rb:rb + 64, hp * 64:(hp + 1) * 64], kd[c * 64:(c + 1) * 64, h * 64:(h + 1) * 64],
                     HI[c * 64:(c + 1) * 64, blk, h * 64:(h + 1) * 64], reads=[hw_t["kd"], hi_t[blk]],
                     writes=[hb_t[p_s]], tp=(c * 64, rb))
            for hp in range(2):
                HP.stt(Sst[:, hp * 64:(hp + 1) * 64], Sst[:, hp * 64:(hp + 1) * 64],
                      hdec[:, c * 2 + hp: c * 2 + hp + 1], HB[p_s][:, hp * 64:(hp + 1) * 64], ALU.mult, ALU.add,
                      reads=[s_t, hdec_t, hb_t[p_s]], writes=[s_t])
        if STAGE in (3.55, 3.57, 3.6) or state_only:
            return
        HP.copy(hosb[:], HB[p_i][:, 0:256], reads=[hb_t[p_i]], writes=[hosb_t], eng="act")
        HP.tt(hosb[:], hosb[:], HB[p_o][:, 0:256], ALU.add, reads=[hosb_t, hb_t[p_o]], writes=[hosb_t])
        for hp in range(2):
            HP.act(hsq[:], hosb[:, hp * 128:(hp + 1) * 128], AF.Square, reads=[hosb_t], writes=[hsq_t])
            p_n = tmp_next()
            HP.mm(HB[p_n][:, 0:128], OB, hsq[:], reads=[c_t, hsq_t], writes=[hb_t[p_n]])
            HP.ts(hrs[:], HB[p_n][:, 0:128], 1.0 / 64, EPS, ALU.mult, ALU.add, reads=[hb_t[p_n]], writes=[hrs_t])
            HP.act(hrs[:], hrs[:], AF.Ln, reads=[hrs_t], writes=[hrs_t])
            HP.act(hrs[:], hrs[:], AF.Exp, reads=[hrs_t], writes=[hrs_t], scale=-0.5)
            HP.tt(hrs[:], hosb[:, hp * 128:(hp + 1) * 128], hrs[:], ALU.mult, reads=[hosb_t, hrs_t],
                 writes=[hrs_t])
            HP.stt(MIX[:, 4 + hp, c0:c0 + 128], hrs[:], hgn[:, hp:hp + 1], SG[:, 4 + hp, c0:c0 + 128], ALU.mult,
                  ALU.mult, reads=[hrs_t, c_t, sg_t[4 + hp]], writes=[mix_t[4 + hp]])

    def xattn(HP=None, defer=False):
        if HP is None:
            HP = P
        for hp in range(2):
            if defer:
                bk = {"o": (PF[5], pf_t[5]), "d": (PBf[1], pb_t[1])}
                PT, pt_t = kbf, kbf_t
            else:
                i_o, i_d = pf_next(), pf_next()
                bk = {"o": (PF[i_o], pf_t[i_o]), "d": (PF[i_d], pf_t[i_d])}
                PT, pt_t = AB, ab_t
            (PO, po_t), (PD, pd_t) = bk["o"], bk["d"]
            for hh in range(2):
                h, rb = hp * 2 + hh, hh * 64
                for mb in range(2):
                    if defer:
                        PS_, ps_t = PBf[0], pb_t[0]
                    else:
                        i_s = pf_next()
                        PS_, ps_t = PF[i_s], pf_t[i_s]
                    HP.mm(PS_[:, :], MKT[rb:rb + 64, hp, mb * 128:(mb + 1) * 128], XQT[rb:rb + 64, hp, :],
                          reads=[mk_t, xq_t[hp]], writes=[ps_t], tp=(rb, 0))
                    ai = rr["x"] % 2
                    rr["x"] += 1
                    HP.act(PT[ai][:], PS_[:, :], AF.Exp, reads=[ps_t], writes=[pt_t[ai]])
                    HP.mm(PO[rb:rb + 64, :], MV[:, mb, h * 64:(h + 1) * 64], PT[ai][:], start=(mb == 0),
                          stop=(mb == 1), reads=[mk_t, pt_t[ai]], writes=[po_t], tp=(0, rb))
                    HP.mm(PD[rb:rb + 64, :], ONESB[:, 0:64], PT[ai][:], start=(mb == 0), stop=(mb == 1),
                          reads=[c_t, pt_t[ai]], writes=[pd_t], tp=(0, rb))
            gi = rr["x"] % 2
            rr["x"] += 1
            HP.recip(gtmp[gi][:], PD[:, :], reads=[pd_t], writes=[gtmp_t[gi]])
            HP.tt(gtmp[gi][:], PO[:, :], gtmp[gi][:], ALU.mult, reads=[po_t, gtmp_t[gi]],
                  writes=[gtmp_t[gi]])
            HP.tt(MIX[:, 6 + hp, :], gtmp[gi][:], SG[:, 6 + hp, :], ALU.mult, reads=[gtmp_t[gi], sg_t[6 + hp]],
                  writes=[mix_t[6 + hp]])

    def sb_attention(qi, side=None):
        nkb = 4 * qi + 4
        for hp in range(4):
            av = 4
            kbs = list(range(nkb - 1, -1, -1))
            n = len(kbs)

            def stage_a1(i):
                kb = kbs[i]
                r = i % 2
                for hh in range(2):
                    rb = hh * 64
                    P.mm(PF[hh][:, :], KT[rb:rb + 64, hp, kb * 128:(kb + 1) * 128], QT[rb:rb + 64, hp, :],
                         reads=[kt_t[hp][kb], qt_t[hp]], writes=[pf_t[hh]], tp=(rb, 0))
                z = kb - 4 * qi
                bsrc = sbbp if kb < 4 * HALF else sbb
                for hh in range(2):
                    h = hp * 2 + hh
                    e = hh * 2 + r
                    P.act(EB[e][:], PF[hh][:, :], AF.Exp, reads=[pf_t[hh], c_t], writes=[eb_t[e]],
                          bias=bsrc[:, h:h + 1])
                    if z >= 0:
                        P.tt(EB[e][:], EB[e][:], MSK[z], ALU.mult, reads=[eb_t[e], c_t], writes=[eb_t[e]])

            def stage_a2(i):
                r = i % 2
                for hh in range(2):
                    e = hh * 2 + r
                    P.act(SPB[e][:], EB[e][:], AF.Ln, reads=[eb_t[e]], writes=[spb_t[e]], bias=1.0)

            def stage_b(i):
                kb = kbs[i]
                r = i % 2
                for hh in range(2):
                    e = hh * 2 + r
                    P.mm(PF[2 + hh][:, :], UIN, SPB[e][:], start=(kb == nkb - 1), stop=(kb == 0),
                         reads=[c_t, spb_t[e]], writes=[pf_t[2 + hh]])
                for hh in range(2):
                    e = hh * 2 + r
                    P.act(WB[hh][:], PF[2 + hh][:, :], AF.Exp, reads=[pf_t[2 + hh]], writes=[wb_t[hh]], scale=-1.0)
                    P.tt(AB[e][:], EB[e][:], WB[hh][:], ALU.mult, reads=[eb_t[e], wb_t[hh]], writes=[ab_t[e]])

            def stage_b2(i):
                kb = kbs[i]
                r = i % 2
                if kb > 0:
                    for hh in range(2):
                        e = hh * 2 + r
                        P.mm(PF[2 + hh][:, :], LST, SPB[e][:], start=False, stop=False, reads=[c_t, spb_t[e]],
                             writes=[pf_t[2 + hh]])

            def stage_c(i):
                kb = kbs[i]
                r = i % 2
                for hh in range(2):
                    rb = hh * 64
                    e = hh * 2 + r
                    h = hp * 2 + hh
                    P.mm(PF[av][rb:rb + 64, :], VA[:, kb, h * 64:(h + 1) * 64], AB[e][:], start=(kb == nkb - 1),
                         stop=(kb == 0), reads=[va_t[kb], ab_t[e]], writes=[pf_t[av]], tp=(0, rb))

            for i in range(n + 2):
                if i < n:
                    stage_a1(i)
                if 0 <= i - 2 < n:
                    stage_b2(i - 2)
                if 0 <= i - 1 < n:
                    stage_b(i - 1)
                if i < n:
                    stage_a2(i)
                if 0 <= i - 2 < n:
                    stage_c(i - 2)
                if side is not None:
                    side(i)
            P.tt(MIX[:, hp, :], PF[av][:, :], SG[:, hp, :], ALU.mult, reads=[pf_t[av], sg_t[hp]],
                 writes=[mix_t[hp]])

    HALF = NT // 2

    def tile(ti):
        own = ti >= HALF
        src = xo_d if own else x_d
        r0 = (ti - HALF) * 512 if own else ti * 512
        for blk in range(4):
            xi = rr["x"] % 2
            rr["x"] += 1
            P.dma(xst[xi][:], src[r0 + blk * 128: r0 + (blk + 1) * 128, :], writes=[xst_t[xi]])
            rmsnorm_to_bf(xst[xi][:], [xst_t[xi]], xi)
            transpose_to(lambda blk=blk: xnT[:, :, blk * 128:(blk + 1) * 128], xi, [xnT_t[blk]])
        rr["pf"] = 0
        wgi = load_wgroup(1)
        for blk in range(4):
            gb = ti * 4 + blk
            pi = tok_proj(wgi, 0, 512, blk)
            si = gb % 2
            P.copy(kvst[si][:], PF[pi][:, :], reads=[pf_t[pi]], writes=[kvst_t[si]], eng="act")
            if own:
                P.dma(k_d[r0 + blk * 128: r0 + (blk + 1) * 128, :], kvst[si][:], reads=[kvst_t[si]])
            P.copy(kbf[si][:], kvst[si][:], reads=[kvst_t[si]], writes=[kbf_t[si]])
            pb = rr["pb"] % 2
            rr["pb"] += 1
            for hp in range(4):
                P.tr(PB[pb][:, hp * 128:(hp + 1) * 128], kbf[si][:, hp * 128:(hp + 1) * 128], ident_b,
                     reads=[kbf_t[si], c_t], writes=[pb_t[pb]])
            P.copy(KT[:, :, gb * 128:(gb + 1) * 128], PB[pb][:, 0:512].rearrange("p (k t) -> p k t", k=4),
                   reads=[pb_t[pb]], writes=[kt_t[hp_][gb] for hp_ in range(4)])
        wgi = load_wgroup(2)
        for blk in range(4):
            gb = ti * 4 + blk
            pi = tok_proj(wgi, 0, 512, blk)
            si = gb % 2
            P.copy(kvst[si][:], PF[pi][:, :], reads=[pf_t[pi]], writes=[kvst_t[si]], eng="act")
            if own:
                P.dma(v_d[r0 + blk * 128: r0 + (blk + 1) * 128, :], kvst[si][:], reads=[kvst_t[si]])
            P.copy(VA[:, gb, :], kvst[si][:], reads=[kvst_t[si]], writes=[va_t[gb]])
        if own:
            wgi = load_wgroup(0)
            feat_proj(wgi, range(4), lambda cbi, pi: P.act(QT[:, cbi, :], PF[pi][:, :], AF.Copy, reads=[pf_t[pi]],
                                                          writes=[qt_t[cbi]], scale=0.125))
            wgi = load_wgroup(3)
            feat_proj(wgi, range(4), lambda cbi, pi: silu_from_psum(pi, SG[:, cbi, :], sg_t[cbi]))
        wgi = load_wgroup(4)
        if own:
            feat_proj(wgi, range(2), lambda cbi, pi: P.copy(HQT[:, cbi, :], PF[pi][:, :], reads=[pf_t[pi]],
                                                            writes=[hq_t[cbi]], eng="act"))
        for blk in range(4):
            pi = tok_proj(wgi, 256, 512, blk)
            P.copy(HF[:, blk, :], PF[pi][:, 0:256], reads=[pf_t[pi]], writes=[hf_t[blk]])
        wgi = load_wgroup(5)
        for blk in range(4):
            pi = tok_proj(wgi, 0, 256, blk)
            P.copy(HI[:, blk, :], PF[pi][:, 0:256], reads=[pf_t[pi]], writes=[hi_t[blk]], eng="act")
        if own:
            feat_proj(wgi, range(2, 4), lambda cbi, pi: silu_from_psum(pi, SG[:, 2 + cbi, :], sg_t[2 + cbi]))
            wgi = load_wgroup(6)
            feat_proj(wgi, range(2), lambda cbi, pi: P.act(XQT[:, cbi, :], PF[pi][:, :], AF.Copy, reads=[pf_t[pi]],
                                                           writes=[xq_t[cbi]], scale=0.125))
            feat_proj(wgi, range(2, 4), lambda cbi, pi: silu_from_psum(pi, SG[:, 4 + cbi, :], sg_t[4 + cbi]))
        if not own:
            for blk in range(4):
                hgrn_block(blk, state_only=True)
            return
        HD = Deferred()
        for blk in range(4):
            hgrn_block(blk, HP=HD, defer=True)
        xattn(HP=HD, defer=True)
        nq = len(HD.q)
        n_it = 4 * (4 * ti + 6)
        per = (nq + n_it - 1) // n_it + 1
        sb_attention(ti, side=lambda i_: HD.pump(per))
        HD.pump(10 ** 9)
        wg0 = load_wgroup(7)
        wg1 = load_wgroup(8)
        for blk in range(4):
            xi = rr["x"] % 2
            rr["x"] += 1
            P.dma(xst[xi][:], src[r0 + blk * 128: r0 + (blk + 1) * 128, :], writes=[xst_t[xi]])
            for half, wg in ((0, wg0), (1, wg1)):
                pi = pf_next()
                kbs = list(range(8)) if DBG_KBS is None else list(DBG_KBS)
                for kb in kbs:
                    P.mm(PF[pi][:, :], MIX[:, kb, blk * 128:(blk + 1) * 128], WG[wg][:, kb, :], start=(kb == kbs[0]),
                         stop=(kb == kbs[-1]), reads=[mix_t[kb], wg_t[wg]], writes=[pf_t[pi]])
                P.tt(yst[:, half * 512:(half + 1) * 512], PF[pi][:, :], xst[xi][:, half * 512:(half + 1) * 512],
                     ALU.add, reads=[pf_t[pi], xst_t[xi]], writes=[yst_t])
            st, stt_ = st4[xi], st4_t[xi]
            P.act(xsb[xi][:], yst[:], AF.Square, reads=[yst_t], writes=[xsb_t[xi], stt_], accum_out=st[:, 0:1])
            P.ts(st[:, 1:2], st[:, 0:1], 1.0 / D, EPS, ALU.mult, ALU.add, reads=[stt_], writes=[stt_])
            P.act(st[:, 2:3], st[:, 1:2], AF.Ln, reads=[stt_], writes=[stt_])
            P.act(st[:, 3:4], st[:, 2:3], AF.Exp, reads=[stt_], writes=[stt_], scale=-0.5)
            if DBG_PRE:
                P.copy(xst[xi][:], yst[:], reads=[yst_t], writes=[xst_t[xi]])
            else:
                P.stt(xst[xi][:], yst[:], st[:, 3:4], fgb[:], ALU.mult, ALU.mult, reads=[yst_t, stt_, c_t],
                      writes=[xst_t[xi]])
            P.dma(y_d[r0 + blk * 128: r0 + (blk + 1) * 128, :], xst[xi][:], reads=[xst_t[xi]])

    SS = {}

    def sample_front():
        G = [EB[0], EB[1], kvst[0], kvst[1], wst[0], wst[1], gtmp[0]]
        g_t = [eb_t[0], eb_t[1], kvst_t[0], kvst_t[1], wst_t[0], wst_t[1], gtmp_t[0]]
        TMP = [gtmp[1][:, :], HQT[:, 1, :]]
        tmp_t = [gtmp_t[1], Tok(old=hq_t)]
        QB, qb_t = HQT[:, 0, :], Tok(old=hq_t)
        HFv = HF[:, :, :].rearrange("p a b -> p (a b)")
        HIv = HI[:, :, :].rearrange("p a b -> p (a b)")
        HFa, HFb, HIa, HIb = HFv[:, 0:512], HFv[:, 512:1024], HIv[:, 0:512], HIv[:, 512:1024]
        hfa_t, hfb_t, hia_t, hib_t = Tok(old=hf_t), Tok(old=hf_t), Tok(old=hi_t), Tok(old=hi_t)
        ZALL, E_ = SPACC[0], SPACC[1]
        v3 = lambda ap: ap.rearrange("p (g h) -> p g h", h=8)
        otok = sb([128, D], F32, "otok")
        otok_t = T()
        ptb = sb([128, NSAMP * NPAGE], I32, "ptb")
        idxa = sb([128, NSAMP * NPAGE], I32, "idxa")
        idx_t = T()
        hgnb = lbl[:, 0:256]
        FKQ = sb([128, 24], F32, "fkq")
        fkq_t = T()
        SM = sb([128, 32], F32, "sm")
        sm_t = T()
        scr_t = T()

        P.memset(otok[:], 0.0, writes=[otok_t], eng="dve")
        P.dma(hgnb, hgr_d.partition_broadcast(128), writes=[c_t])
        P.dma(ptb[:], pt_d.partition_broadcast(128), writes=[idx_t])
        P.ts(idxa[:], ptb[:], 128.0, IOTA, ALU.mult, ALU.add, reads=[idx_t, c_t], writes=[idx_t])
        P.memset(yst[:], 0.0, writes=[yst_t], eng="dve")
        P.dma(yst[0:NSAMP, :], xs_d, writes=[yst_t])
        rmsnorm_to_bf(yst[:], [yst_t], 0)
        transpose_to(lambda: xnT[:, :, 0:128], 0, [xnT_t[0]])
        for g in range(7):
            wgi = load_wgroup(g)
            pi = tok_proj(wgi, 0, 512, 0)
            P.copy(G[g][:], PF[pi][:, :], reads=[pf_t[pi]], writes=[g_t[g]], eng="act" if g % 2 else "dve")
            P.dma(scr_d[:, g * 512:(g + 1) * 512], G[g][0:NSAMP, :], reads=[g_t[g]], writes=[scr_t])
        P.dma(ks_d, G[1][0:NSAMP, :], reads=[g_t[1]])
        P.dma(vs_d, G[2][0:NSAMP, :], reads=[g_t[2]])
        if STAGE == 10.1:
            return
        a, f, k = hw["a"], hw["f"], hw["k"]
        P.act(a[:], G[4][:, 256:512], AF.Exp, reads=[g_t[4]], writes=[hw_t["a"]], scale=-1.0)
        P.ts(a[:], a[:], 1.0, None, ALU.add, reads=[hw_t["a"]], writes=[hw_t["a"]])
        P.recip(a[:], a[:], reads=[hw_t["a"]], writes=[hw_t["a"]])
        P.tt(f[:], a[:], oml[:], ALU.mult, reads=[hw_t["a"], c_t], writes=[hw_t["f"]])
        P.tt(f[:], f[:], lb[:], ALU.add, reads=[hw_t["f"], c_t], writes=[hw_t["f"]])
        P.ts(k[:], f[:], -1.0, 1.0, ALU.mult, ALU.add, reads=[hw_t["f"]], writes=[hw_t["k"]])
        p_f = pf_next()
        for j, (src, st_) in enumerate(((f, hw_t["f"]), (k, hw_t["k"]), (G[4], g_t[4]))):
            for hp in range(2):
                c = (j * 2 + hp) * 4
                P.tr(PF[p_f][:, c:c + 4], src[0:4, hp * 128:(hp + 1) * 128], ident_f[0:4, 0:4],
                     reads=[st_, c_t], writes=[pf_t[p_f]])
        P.copy(FKQ[:], PF[p_f][:, 0:24], reads=[pf_t[p_f]], writes=[fkq_t])

        SS.update(dict(G=G, g_t=g_t, TMP=TMP, tmp_t=tmp_t, HFa=HFa, HFb=HFb, HIa=HIa, HIb=HIb, hfa_t=hfa_t, hfb_t=hfb_t,
                       hia_t=hia_t, hib_t=hib_t, otok=otok, otok_t=otok_t, idxa=idxa, idx_t=idx_t, hgnb=hgnb, FKQ=FKQ,
                       fkq_t=fkq_t, SM=SM, sm_t=sm_t, scr_t=scr_t))

    def sample_stream():
        otok, otok_t, idxa, idx_t, scr_t = SS["otok"], SS["otok_t"], SS["idxa"], SS["idx_t"], SS["scr_t"]
        PG = [KT[:, hp_, S // 2:S].bitcast(F32) for hp_ in range(4)]
        VAf = VA[:, NB // 2:NB, :].rearrange("p a b -> p (a b)").bitcast(F32)
        PG += [VAf[:, 2048:3072]]
        pg_t = [T() for _ in range(5)]
        NPB = 5
        VBF = [VA[:, NB - 4 + j_, :] for j_ in range(4)]
        vbf_t = [T() for _ in range(4)]
        A8b = sb([128, 4, 8], BF16, "a8b")

        TMPs = [VAf[:, 0:512], VAf[:, 512:1024]]
        tmps_t = [T(), T()]
        QBs, qbs_t = VAf[:, 1024:1536], T()
        RES, res_t = VAf[:, 1536:2048], T()
        for hp_ in range(4):
            for kb_ in range(NB // 2, NB):
                kt_t[hp_][kb_].old = [pg_t[hp_]]
        for kb_ in range(NB // 2, NB):
            va_t[kb_].old = [pg_t[4], tmps_t[0], tmps_t[1], qbs_t, res_t] + vbf_t
        S8 = sb([128, 4, 48], F32, "s8")
        S8b = sb([128, 4, 8], BF16, "s8b")
        s8_t = [T() for _ in range(4)]
        NPG = NPAGE
        CB, AVB = 5, 4
        npg = [0]
        for n in range(NSAMP):
            P.dma(QBs, scr_d[n:n + 1, 0:512].partition_broadcast(128), reads=[scr_t], writes=[qbs_t])
            pages = list(range(NPG - 1, -1, -1))

            def st1(ii):
                p = pages[ii]
                i = npg[0] + ii
                col = n * NPAGE + p
                b8 = i % 4
                P.gather(PG[i % NPB], ckv_d[:, :], idxa[:, col:col + 1], reads=[idx_t], writes=[pg_t[i % NPB]])
                P.tt(TMPs[i % 2], PG[i % NPB][:, 0:512], QBs, ALU.mult, reads=[pg_t[i % NPB], qbs_t], writes=[tmps_t[i % 2]])
                P.copy(VBF[b8], PG[i % NPB][:, 512:1024], reads=[pg_t[i % NPB]], writes=[vbf_t[b8]], eng="act")
                P.reduce(S8[:, b8, 0:8], TMPs[i % 2].rearrange("p (h d) -> p h d", h=8), ALU.add,
                         reads=[tmps_t[i % 2]], writes=[s8_t[b8]])
                P.stt(S8[:, b8, 0:8], S8[:, b8, 0:8], 0.125, sbb[:, 0:8], ALU.mult, ALU.add,
                      reads=[s8_t[b8], c_t], writes=[s8_t[b8]])
                P.act(S8[:, b8, 8:16], S8[:, b8, 0:8], AF.Exp, reads=[s8_t[b8]], writes=[s8_t[b8]])
                P.act(S8b[:, b8, :], S8[:, b8, 8:16], AF.Ln, reads=[s8_t[b8]], writes=[s8_t[b8]], bias=1.0)

            def st2(ii):
                i = npg[0] + ii
                b8 = i % 4
                P.mm(PF[CB][:, 0:8], UIN, S8b[:, b8, :], start=(ii == 0), stop=(ii == NPG - 1),
                     reads=[c_t, s8_t[b8]], writes=[pf_t[CB]])
                P.act(S8[:, b8, 24:32], PF[CB][:, 0:8], AF.Exp, reads=[pf_t[CB]], writes=[s8_t[b8]], scale=-1.0)
                P.tt(A8b[:, b8, :], S8[:, b8, 8:16], S8[:, b8, 24:32], ALU.mult, reads=[s8_t[b8]],
                     writes=[s8_t[b8]])
                if ii < NPG - 1:
                    P.mm(PF[CB][:, 0:8], LST, S8b[:, b8, :], start=False, stop=False, reads=[c_t, s8_t[b8]],
                         writes=[pf_t[CB]])
                P.mm(PF[AVB][0:8, :], A8b[:, b8, :], VBF[b8], start=(ii == 0), stop=(ii == NPG - 1),
                     reads=[s8_t[b8], vbf_t[b8]], writes=[pf_t[AVB]])

            for ii in range(NPG + 1):
                if ii < NPG:
                    st1(ii)
                if ii >= 1:
                    st2(ii - 1)
                yield
            npg[0] += NPG
            P.copy(RES[0:8, :], PF[AVB][0:8, :], reads=[pf_t[AVB]], writes=[res_t])
            for h in range(8):
                P.dma(otok[n:n + 1, h * 64:(h + 1) * 64], RES[h:h + 1, h * 64:(h + 1) * 64], reads=[res_t],
                      writes=[otok_t])
            yield

    def sample_back():
        G, g_t, TMP, tmp_t = SS["G"], SS["g_t"], SS["TMP"], SS["tmp_t"]
        HFa, HFb, HIa, HIb = SS["HFa"], SS["HFb"], SS["HIa"], SS["HIb"]
        hfa_t, hfb_t, hia_t, hib_t = SS["hfa_t"], SS["hfb_t"], SS["hia_t"], SS["hib_t"]
        otok, otok_t, hgnb, FKQ, fkq_t, SM, sm_t, scr_t = (SS["otok"], SS["otok_t"], SS["hgnb"], SS["FKQ"], SS["fkq_t"],
                                                          SS["SM"], SS["sm_t"], SS["scr_t"])
        for g in (3, 5, 6):
            P.dma(G[g][0:NSAMP, :], scr_d[:, g * 512:(g + 1) * 512], reads=[scr_t], writes=[g_t[g]])
        P.memset(yst[:], 0.0, writes=[yst_t], eng="dve")
        P.dma(yst[0:NSAMP, :], xs_d, writes=[yst_t])
        for n in range(NSAMP):
            VB, S0, SN, KV = hw["kd"], hsq, hrs, hattn[0]
            P.dma(VB[:], scr_d[n:n + 1, 5 * 512:5 * 512 + 256].partition_broadcast(128), reads=[scr_t],
                  writes=[hw_t["kd"]])
            P.dma(S0[:], sst_d[n], writes=[hsq_t])
            for half in range(2):
                r0 = half * 64
                for hp in range(2):
                    h = hp * 2 + half
                    ck_, cf_ = (1 * 2 + hp) * 4 + n, (0 * 2 + hp) * 4 + n
                    P.ts(KV[r0:r0 + 64, hp * 64:(hp + 1) * 64], VB[r0:r0 + 64, h * 64:(h + 1) * 64],
                         FKQ[r0:r0 + 64, ck_:ck_ + 1], None, ALU.mult, reads=[hw_t["kd"], fkq_t],
                         writes=[hattn_t[0]])
                    P.stt(SN[r0:r0 + 64, hp * 64:(hp + 1) * 64], S0[r0:r0 + 64, hp * 64:(hp + 1) * 64],
                          FKQ[r0:r0 + 64, cf_:cf_ + 1], KV[r0:r0 + 64, hp * 64:(hp + 1) * 64], ALU.mult, ALU.add,
                          reads=[hsq_t, fkq_t, hattn_t[0]], writes=[hrs_t])
            P.dma(hss_d[n], SN[:], reads=[hrs_t])
            if STAGE == 10.55:
                continue
            p_h = 5
            for hp in range(2):
                cq = (2 * 2 + hp) * 4 + n
                P.ts(KV[:, hp * 64:(hp + 1) * 64], SN[:, hp * 64:(hp + 1) * 64], FKQ[:, cq:cq + 1], None, ALU.mult,
                     reads=[hrs_t, fkq_t], writes=[hattn_t[0]])
            P.mm(PF[p_h][:, 0:128], OB, KV[:, :], reads=[c_t, hattn_t[0]], writes=[pf_t[p_h]])
            P.copy(hosb[:, 0:128], PF[p_h][:, 0:128], reads=[pf_t[p_h]], writes=[hosb_t])
            for half in range(2):
                for hp in range(2):
                    h = hp * 2 + half
                    P.dma(otok[n:n + 1, 512 + h * 64:512 + (h + 1) * 64],
                          hosb[half * 64:half * 64 + 1, hp * 64:(hp + 1) * 64], reads=[hosb_t], writes=[otok_t])
            if STAGE == 10.6:
                continue
            XQB, CMK, CMV, XT = hw["lf"], TMP[0], TMP[1], HFb
            P.dma(XQB[:], scr_d[n:n + 1, 6 * 512:6 * 512 + 256].partition_broadcast(128), reads=[scr_t],
                  writes=[hw_t["lf"]])
            P.dma(CMK.rearrange("p (b c) -> p b c", b=2), cmk_d[n].rearrange("(b m) c -> m b c", b=2),
                  writes=[tmp_t[0]])
            P.dma(CMV.rearrange("p (b c) -> p b c", b=2), cmv_d[n].rearrange("(b m) c -> m b c", b=2),
                  writes=[tmp_t[1]])
            for mb in range(2):
                P.tt(XT[:, mb * 256:(mb + 1) * 256], CMK[:, mb * 256:(mb + 1) * 256], XQB[:], ALU.mult,
                     reads=[tmp_t[0], hw_t["lf"]], writes=[hfb_t])
                P.reduce(SM[:, 8 + mb * 4:12 + mb * 4],
                         XT[:, mb * 256:(mb + 1) * 256].rearrange("p (h d) -> p h d", h=4), ALU.add,
                         reads=[hfb_t], writes=[sm_t])
            P.act(SM[:, 16:24], SM[:, 8:16], AF.Exp, reads=[sm_t], writes=[sm_t], scale=0.125)
            p_x, p_d = 0, 1
            for mb in range(2):
                P.mm(PF[p_x][0:4, 0:256], SM[:, 16 + mb * 4:20 + mb * 4], CMV[:, mb * 256:(mb + 1) * 256],
                     start=(mb == 0), stop=(mb == 1), reads=[sm_t, tmp_t[1]], writes=[pf_t[p_x]])
            for mb in range(2):
                P.mm(PF[p_d][0:4, 0:1], SM[:, 16 + mb * 4:20 + mb * 4], ONEC, start=(mb == 0), stop=(mb == 1),
                     reads=[sm_t, c_t], writes=[pf_t[p_d]])
            P.recip(SM[0:4, 24:25], PF[p_d][0:4, 0:1], reads=[pf_t[p_d]], writes=[sm_t])
            P.ts(hosb[0:4, :], PF[p_x][0:4, 0:256], SM[0:4, 24:25], None, ALU.mult, reads=[pf_t[p_x], sm_t],
                 writes=[hosb_t])
            for h in range(4):
                P.dma(otok[n:n + 1, 768 + h * 64:768 + (h + 1) * 64], hosb[h:h + 1, h * 64:(h + 1) * 64],
                      reads=[hosb_t], writes=[otok_t])
        if STAGE == 10.7:
            return
        HG = otok[:, 512:768]
        P.tt(hosb[:], HG, HG, ALU.mult, reads=[otok_t], writes=[hosb_t])
        P.reduce(SM[:, 0:4], hosb[:].rearrange("p (h d) -> p h d", h=4), ALU.add, reads=[hosb_t], writes=[sm_t])
        P.ts(SM[:, 0:4], SM[:, 0:4], 1.0 / 64, EPS, ALU.mult, ALU.add, reads=[sm_t], writes=[sm_t])
        P.act(SM[:, 4:8], SM[:, 0:4], AF.Ln, reads=[sm_t], writes=[sm_t])
        P.act(SM[:, 8:12], SM[:, 4:8], AF.Exp, reads=[sm_t], writes=[sm_t], scale=-0.5)
        for h in range(4):
            P.ts(otok[:, 512 + h * 64:512 + (h + 1) * 64], otok[:, 512 + h * 64:512 + (h + 1) * 64],
                 SM[:, 8 + h:9 + h], None, ALU.mult, reads=[otok_t, sm_t], writes=[otok_t])
        P.tt(HG, HG, hgnb, ALU.mult, reads=[otok_t, c_t], writes=[otok_t])
        for (gsrc, gtok, c0, c1, o0) in ((G[3], g_t[3], 0, 512, 0), (G[5], g_t[5], 256, 512, 512),
                                         (G[6], g_t[6], 256, 512, 768)):
            w_ = c1 - c0
            t_ = HIb[:, 0:w_]
            P.act(t_, gsrc[:, c0:c1], AF.Exp, reads=[gtok], writes=[hib_t], scale=-1.0)
            P.ts(t_, t_, 1.0, None, ALU.add, reads=[hib_t], writes=[hib_t])
            P.recip(t_, t_, reads=[hib_t], writes=[hib_t])
            P.tt(t_, t_, gsrc[:, c0:c1], ALU.mult, reads=[hib_t, gtok], writes=[hib_t])
            P.tt(otok[:, o0:o0 + w_], otok[:, o0:o0 + w_], t_, ALU.mult, reads=[otok_t, hib_t], writes=[otok_t])
        P.copy(xsb[0][:], otok[:], reads=[otok_t], writes=[xsb_t[0]])
        transpose_to(lambda: MIX[:, :, 0:128], 0, mix_t)
        wg0 = load_wgroup(7)
        wg1 = load_wgroup(8)
        st, stt_ = st4[0], st4_t[0]
        for half, wg in ((0, wg0), (1, wg1)):
            pi = pf_next()
            for kb in range(8):
                P.mm(PF[pi][:, :], MIX[:, kb, 0:128], WG[wg][:, kb, :], start=(kb == 0), stop=(kb == 7),
                     reads=[mix_t[kb], wg_t[wg]], writes=[pf_t[pi]])
            P.tt(G[half][:], PF[pi][:, :], yst[:, half * 512:(half + 1) * 512], ALU.add,
                 reads=[pf_t[pi], yst_t], writes=[g_t[half]])
            P.act(xsb[1][:, half * 512:(half + 1) * 512], G[half][:], AF.Square, reads=[g_t[half]],
                  writes=[xsb_t[1], stt_], accum_out=st[:, half:half + 1])
        P.tt(st[:, 0:1], st[:, 0:1], st[:, 1:2], ALU.add, reads=[stt_], writes=[stt_])
        P.ts(st[:, 1:2], st[:, 0:1], 1.0 / D, EPS, ALU.mult, ALU.add, reads=[stt_], writes=[stt_])
        P.act(st[:, 2:3], st[:, 1:2], AF.Ln, reads=[stt_], writes=[stt_])
        P.act(st[:, 3:4], st[:, 2:3], AF.Exp, reads=[stt_], writes=[stt_], scale=-0.5)
        for half in range(2):
            P.stt(G[half][:], G[half][:], st[:, 3:4], fgb[:, half * 512:(half + 1) * 512], ALU.mult, ALU.mult,
                  reads=[g_t[half], stt_, c_t], writes=[g_t[half]])
            P.dma(ys_d[:, half * 512:(half + 1) * 512], G[half][0:NSAMP, :], reads=[g_t[half]])

    gen = [None]

    def pump(k):
        if gen[0] is None:
            return
        for _ in range(k):
            try:
                next(gen[0])
            except StopIteration:
                gen[0] = None
                return

    if ENABLE_SAMPLE:
        sample_front()
        gen[0] = sample_stream()
        rr["npf"] = 4
        P.hook = lambda: pump(1)
        P.every = 6
    for ti in range(N_TILES):
        if ti == HALF or N_TILES < HALF:
            P.hook = None
            pump(10 ** 9)
            rr["npf"] = 6
        tile(ti)
    P.hook = None
    pump(10 ** 9)
    rr["npf"] = 6
    P.dma(hgs_d, Sst[:], reads=[s_t])


    if ENABLE_SAMPLE:
        sample_back()

    P.emit()
    return nc


_NC_CACHE = {}


def _consts():
    s = np.arange(128)[:, None]
    t = np.arange(128)[None, :]
    same = (s // 64) == (t // 64)
    ident = np.eye(128, dtype=np.float32)
    tri = ((s <= t) & same).astype(np.float32)
    su = ((s > t) & same).astype(np.float32)
    ob = same.astype(np.float32)
    cf = np.concatenate([ident, tri, su, ob, np.arange(128, dtype=np.float32)[:, None], np.ones((128, 1), np.float32)],
                        axis=1).astype(np.float32)
    uin = (s >= t).astype(np.float32)
    tq = np.arange(512)[None, :]
    msk = [((d * 128 + s) < tq).astype(np.float32) for d in range(4)]
    lst = (s < t).astype(np.float32)
    cb = np.concatenate([ident, uin, lst] + msk, axis=1).astype(ml_dtypes.bfloat16)
    return cf, cb


def kernel(x_prompt, x_sample, mem_prompt, cache_k, cache_v, page_table, state_hgrn, cache_mem_k, cache_mem_v,
           norm_gain, w_in, sb_bias, hg_lb_logits, hg_norm_gain, mem_norm_gain, w_mem_kv, w_out, final_norm_gain):
    if "nc" not in _NC_CACHE:
        _NC_CACHE["nc"] = build_program()
    nc = _NC_CACHE["nc"]
    f = lambda a: np.ascontiguousarray(np.asarray(a, dtype=np.float32))
    cf, cb = _consts()
    common = {
        "w_in": f(w_in[0]), "w_out": f(w_out[0]), "w_mem": f(w_mem_kv[0]),
        "gin": f(np.asarray(norm_gain[0]).reshape(8, 128).T),
        "gmem": f(np.asarray(mem_norm_gain[0]).reshape(8, 128).T),
        "fg": f(np.asarray(final_norm_gain).reshape(1, D)),
        "hgn": f(np.asarray(hg_norm_gain[0]).reshape(2, 128).T),
        "sbb": f(np.asarray(sb_bias[0]).reshape(1, 8)),
        "lbl": f(np.asarray(hg_lb_logits).reshape(1, 512)),
        "cf": cf, "cb": cb,
    }
    if ENABLE_SAMPLE:
        ckv = np.concatenate([f(cache_k[0]).reshape(POOL_PAGES * 128, 512),
                              f(cache_v[0]).reshape(POOL_PAGES * 128, 512)], axis=1)
    in_maps = []
    for c in range(8):
        b = c // 2
        m = dict(common)
        r = c % 2
        xb = f(x_prompt[b])
        m["x"] = np.ascontiguousarray(xb[:S // 2]) if r == 1 else np.zeros((S // 2, D), np.float32)
        m["xo"] = np.ascontiguousarray(xb[r * (S // 2):(r + 1) * (S // 2)])
        m["kbias"] = np.full((128, 1), 0.0 if r == 1 else -30000.0, np.float32)
        m["mem"] = f(mem_prompt[b])
        if ENABLE_SAMPLE:
            sl = slice(NSAMP * c, NSAMP * (c + 1))
            m["xs"] = f(x_sample[sl, 0])
            m["pt"] = np.ascontiguousarray(np.asarray(page_table[sl], dtype=np.int32).reshape(1, NSAMP * NPAGE))
            m["ckv"] = ckv
            st = f(state_hgrn[0, sl]).reshape(NSAMP, 2, 2, 64, 64).transpose(0, 2, 3, 1, 4).reshape(NSAMP, 128, 128)
            m["sst"] = np.ascontiguousarray(st)
            m["cmk"] = f(cache_mem_k[0, sl]).reshape(NSAMP, 256, 256)
            m["cmv"] = f(cache_mem_v[0, sl]).reshape(NSAMP, 256, 256)
            m["hgr"] = f(np.asarray(hg_norm_gain[0]).reshape(1, 256))
        in_maps.append(m)
    res = run_bass_kernel_spmd(nc, in_maps[:NCORES], core_ids=list(range(NCORES))).results
    res = list(res) + [res[0], res[1 % NCORES]] * ((8 - NCORES) // 2 + 1)

    def unstate(a):
        return a.reshape(2, 64, 2, 64).transpose(2, 0, 1, 3).reshape(4, 64, 64)

    cat = lambda b, n_: np.concatenate([res[2 * b][n_], res[2 * b + 1][n_]], axis=0)
    y_prompt = np.stack([cat(b, "y") for b in range(4)]).astype(np.float32)
    k_prompt = np.stack([cat(b, "k") for b in range(4)]).reshape(1, 4, S, 8, 64).astype(np.float32)
    v_prompt = np.stack([cat(b, "v") for b in range(4)]).reshape(1, 4, S, 8, 64).astype(np.float32)
    hgrn_prompt = np.stack([unstate(res[2 * b + 1]["hgs"]) for b in range(4)])[None].astype(np.float32)
    mem_k = np.stack([res[2 * b]["mk"] for b in range(4)]).reshape(1, 4, 256, 4, 64).astype(np.float32)
    mem_v = np.stack([res[2 * b]["mv"] for b in range(4)]).reshape(1, 4, 256, 4, 64).astype(np.float32)
    if ENABLE_SAMPLE:
        y_sample = np.concatenate([res[c]["ys"] for c in range(8)]).reshape(32, 1, D).astype(np.float32)
        k_sample = np.concatenate([res[c]["ks"] for c in range(8)]).reshape(1, 32, 1, 8, 64).astype(np.float32)
        v_sample = np.concatenate([res[c]["vs"] for c in range(8)]).reshape(1, 32, 1, 8, 64).astype(np.float32)
        hgrn_sample = np.concatenate([np.stack([unstate(res[c]["hss"][i]) for i in range(NSAMP)])
                                      for c in range(8)])[None].astype(np.float32)
    else:
        y_sample = np.zeros((32, 1, D), np.float32)
        k_sample = np.zeros((1, 32, 1, 8, 64), np.float32)
        v_sample = np.zeros((1, 32, 1, 8, 64), np.float32)
        hgrn_sample = np.zeros((1, 32, 4, 64, 64), np.float32)
    return (y_prompt, y_sample, k_prompt, v_prompt, hgrn_prompt, mem_k, mem_v, k_sample, v_sample, hgrn_sample)
```

```python
import contextlib
import numpy as np
import ml_dtypes
import concourse.bass as bass
import concourse.mybir as mybir
from concourse.bass_utils import run_bass_kernel_spmd

F32 = mybir.dt.float32
BF16 = mybir.dt.bfloat16
I32 = mybir.dt.int32
AF = mybir.ActivationFunctionType
ALU = mybir.AluOpType
AX = mybir.AxisListType

D = 1024
S = 4096
NT = S // 512
NB = S // 128
DIN = 3584
EPS = 1e-6
NSAMP = 4
NPAGE = 64
N_DMA_SEM = 12
RING = 3 * N_DMA_SEM

ENABLE_SAMPLE = True
N_TILES = NT
STAGE = 99
NCORES = 8
POOL_PAGES = 2560
NO_SCRATCH_READ = False
DBG_PRE = False
DBG_KBS = None


class Tok:
    __slots__ = ("w", "r", "ps", "old")

    def __init__(self, ps=False, old=None):
        self.w = None
        self.r = []
        self.ps = ps
        self.old = old


class Prog:
    def __init__(self, nc, es):
        self.nc = nc
        self.es = es
        self.ops = []
        self.hook = None
        self.every = 6
        self._n = 0
        self._in_hook = False
        self.eng = {"pe": nc.tensor, "act": nc.scalar, "dve": nc.vector, "pool": nc.gpsimd, "sp": nc.sync}

    def op(self, eng, fn, reads=(), writes=(), dma=False):
        idx = len(self.ops)
        deps = set()
        for t in reads:
            if t.w is not None:
                deps.add(t.w)
            if t.ps:
                deps.update(r for r in t.r if self.ops[r][0] != eng)
        for t in writes:
            if t.w is not None:
                deps.add(t.w)
            deps.update(t.r)
            if t.old:
                for o in t.old:
                    if o.w is not None:
                        deps.add(o.w)
                    deps.update(o.r)
                t.old = None
        deps.discard(idx)
        self.ops.append([eng, fn, deps, dma])
        for t in reads:
            t.r.append(idx)
        for t in writes:
            t.w = idx
            t.r = []
        if self.hook is not None and not self._in_hook:
            self._n += 1
            if self._n % self.every == 0:
                self._in_hook = True
                try:
                    self.hook()
                finally:
                    self._in_hook = False
        return idx

    def dma(self, out, in_, reads=(), writes=(), q="sp", **kw):
        e = self.eng[q]
        return self.op(q, lambda: e.dma_start(out=out, in_=in_, **kw), reads, writes, dma=True)

    def gather(self, out, in_, idx_ap, reads=(), writes=()):
        g = self.nc.gpsimd
        return self.op("pool", lambda: g.indirect_dma_start(
            out=out, out_offset=None, in_=in_, in_offset=bass.IndirectOffsetOnAxis(ap=idx_ap, axis=0)),
            reads, writes, dma=True)

    def act(self, out, in_, func, reads=(), writes=(), **kw):
        a = self.nc.scalar
        return self.op("act", lambda: a.activation(out=out, in_=in_, func=func, **kw), reads, writes)

    def tt(self, out, in0, in1, op, reads=(), writes=(), eng="dve"):
        e = self.eng[eng]
        return self.op(eng, lambda: e.tensor_tensor(out=out, in0=in0, in1=in1, op=op), reads, writes)

    def ts(self, out, in0, s1, s2, op0, op1=None, reads=(), writes=(), eng="dve"):
        e = self.eng[eng]
        if op1 is None:
            return self.op(eng, lambda: e.tensor_scalar(out=out, in0=in0, scalar1=s1, scalar2=None, op0=op0),
                           reads, writes)
        return self.op(eng, lambda: e.tensor_scalar(out=out, in0=in0, scalar1=s1, scalar2=s2, op0=op0, op1=op1),
                       reads, writes)

    def stt(self, out, in0, scalar, in1, op0, op1, reads=(), writes=()):
        e = self.nc.vector
        return self.op("dve", lambda: e.scalar_tensor_tensor(out=out, in0=in0, scalar=scalar, in1=in1,
                                                             op0=op0, op1=op1), reads, writes)

    def copy(self, out, in_, reads=(), writes=(), eng="dve"):
        e = self.eng[eng]
        if eng == "act":
            return self.op(eng, lambda: e.activation(out=out, in_=in_, func=AF.Copy), reads, writes)
        return self.op(eng, lambda: e.tensor_copy(out=out, in_=in_), reads, writes)

    def recip(self, out, in_, reads=(), writes=()):
        e = self.nc.vector
        return self.op("dve", lambda: e.reciprocal(out=out, in_=in_), reads, writes)

    def reduce(self, out, in_, op, reads=(), writes=()):
        e = self.nc.vector
        return self.op("dve", lambda: e.tensor_reduce(out=out, in_=in_, axis=AX.X, op=op), reads, writes)

    def scan(self, out, d0, d1, reads=(), writes=()):
        e = self.nc.vector
        return self.op("dve", lambda: e.tensor_tensor_scan(out=out, data0=d0, data1=d1, initial=0.0,
                                                           op0=ALU.mult, op1=ALU.add), reads, writes)

    def memset(self, ap, val, writes=(), eng="pool"):
        e = self.eng[eng]
        return self.op(eng, lambda: e.memset(ap, val), (), writes)

    def mm(self, out, lhsT, rhs, start=True, stop=True, reads=(), writes=(), tp=None):
        t = self.nc.tensor
        if tp is None:
            return self.op("pe", lambda: t.matmul(out, lhsT=lhsT, rhs=rhs, start=start, stop=stop), reads, writes)
        return self.op("pe", lambda: t.matmul(out, lhsT=lhsT, rhs=rhs, start=start, stop=stop, tile_position=tp),
                       reads, writes)

    def tr(self, out, in_, ident, reads=(), writes=()):
        t = self.nc.tensor
        return self.op("pe", lambda: t.transpose(out=out, in_=in_, identity=ident), reads, writes)

    def emit(self):
        nc, es = self.nc, self.es
        ops = self.ops
        n = len(ops)
        comp = ("pe", "act", "dve", "pool")
        pos = [0] * n
        cnt = {e: 0 for e in comp}
        qbase = {"sp": 0, "pool": N_DMA_SEM, "act": 2 * N_DMA_SEM}
        dq = {q: [] for q in qbase}
        for i, (eng, fn, deps, dma) in enumerate(ops):
            if dma:
                k = len(dq[eng])
                pos[i] = (k // N_DMA_SEM) * RING + qbase[eng] + (k % N_DMA_SEM)
                dq[eng].append(i)
            else:
                pos[i] = cnt[eng]
                cnt[eng] += 1
        dma_ops = [i for i in range(n) if ops[i][3]]
        for q, lst in dq.items():
            for k, i in enumerate(lst):
                if k >= N_DMA_SEM:
                    ops[i][2].add(lst[k - N_DMA_SEM])
        waited = {}
        marked = [False] * n
        waits = [None] * n
        for i, (eng, fn, deps, dma) in enumerate(ops):
            wl = []
            for d in sorted(deps):
                deng, _, _, ddma = ops[d]
                if ddma:
                    key = (eng, "dma", pos[d] % RING)
                    val = pos[d] // RING
                else:
                    if deng == eng:
                        if eng == "pe":
                            continue
                        if eng != "pool" and pos[i] - pos[d] > 2:
                            continue
                    key = (eng, deng)
                    val = pos[d]
                if waited.get(key, -1) >= val:
                    continue
                waited[key] = val
                wl.append(d)
                marked[d] = True
            waits[i] = wl
        sem = {e: es.enter_context(nc.semaphore("s_" + e)) for e in comp}
        dsem = [es.enter_context(nc.semaphore("s_dma%d" % k)) for k in range(RING)]
        val = [0] * n
        c2 = {e: 0 for e in comp}
        for i, (eng, fn, deps, dma) in enumerate(ops):
            if dma:
                val[i] = 16 * (pos[i] // RING + 1)
            elif marked[i]:
                c2[eng] += 1
                val[i] = c2[eng]
        for i, (eng, fn, deps, dma) in enumerate(ops):
            e = self.eng[eng]
            for d in waits[i]:
                deng, _, _, ddma = ops[d]
                if ddma:
                    e.wait_ge(dsem[pos[d] % RING], val[d])
                else:
                    e.wait_ge(sem[deng], val[d])
            ins = fn()
            if dma:
                ins.then_inc(dsem[pos[i] % RING], 16)
            elif marked[i]:
                ins.then_inc(sem[eng], 1)
        last = {}
        for i in dma_ops:
            last[pos[i] % RING] = val[i]
        for k, v in last.items():
            nc.sync.wait_ge(dsem[k], v)
        for e in comp:
            if c2[e] > 0:
                nc.sync.wait_ge(sem[e], c2[e])


def build_program():
    nc = bass.Bass("TRN2", target_bir_lowering=False)
    es = contextlib.ExitStack()
    P = Prog(nc, es)

    def din(name, shape, dt=F32):
        return nc.dram_tensor(name, list(shape), dt, kind="ExternalInput").ap()

    def dout(name, shape, dt=F32):
        return nc.dram_tensor(name, list(shape), dt, kind="ExternalOutput").ap()

    x_d = din("x", [S // 2, D])
    xo_d = din("xo", [S // 2, D])
    kb_d = din("kbias", [128, 1])
    mem_d = din("mem", [256, D])
    w_in_d = din("w_in", [D, DIN])
    w_out_d = din("w_out", [D, D])
    w_mem_d = din("w_mem", [D, 512])
    gin_d = din("gin", [128, 8])
    gmem_d = din("gmem", [128, 8])
    fg_d = din("fg", [1, D])
    hgn_d = din("hgn", [128, 2])
    sbb_d = din("sbb", [1, 8])
    lbl_d = din("lbl", [1, 512])
    cf_d = din("cf", [128, 514])
    cb_d = din("cb", [128, 384 + 2048], BF16)
    y_d = dout("y", [S // 2, D])
    k_d = dout("k", [S // 2, 512])
    v_d = dout("v", [S // 2, 512])
    hgs_d = dout("hgs", [128, 128])
    mk_d = dout("mk", [256, 256])
    mv_d = dout("mv", [256, 256])
    wsc_d = nc.dram_tensor("wsc", [9, 128, 4096], BF16, kind="Internal").ap()
    if ENABLE_SAMPLE:
        xs_d = din("xs", [NSAMP, D])
        pt_d = din("pt", [1, NSAMP * NPAGE], I32)
        ckv_d = din("ckv", [POOL_PAGES * 128, 1024])
        sst_d = din("sst", [NSAMP, 128, 128])
        cmk_d = din("cmk", [NSAMP, 256, 256])
        cmv_d = din("cmv", [NSAMP, 256, 256])
        ys_d = dout("ys", [NSAMP, D])
        ks_d = dout("ks", [NSAMP, 512])
        vs_d = dout("vs", [NSAMP, 512])
        hss_d = dout("hss", [NSAMP, 128, 128])
        hgr_d = din("hgr", [1, 256])
        scr_d = nc.dram_tensor("scr", [NSAMP, DIN], F32, kind="Internal").ap()

    cnt = [0]

    def sb(shape, dt=F32, name=None):
        cnt[0] += 1
        return es.enter_context(nc.sbuf_tensor("sb_" + (name or ("t%d" % cnt[0])), list(shape), dt))

    def psum(shape, dt=F32):
        cnt[0] += 1
        return es.enter_context(nc.psum_tensor("p%d" % cnt[0], list(shape), dt))

    def T():
        return Tok()

    KT = sb([128, 4, S], BF16, "KT")
    kt_t = [[T() for _ in range(NB)] for _ in range(4)]
    VA = sb([128, NB, 512], BF16, "VA")
    va_t = [T() for _ in range(NB)]
    MKT = sb([128, 2, 256], BF16, "MKT")
    MV = sb([128, 2, 256], BF16, "MV")
    mk_t = T()
    cf = sb([128, 514], F32, "cf")
    cb = sb([128, 384 + 2048], BF16, "cb")
    c_t = T()
    ident_f, TRI, SU, OB = cf[:, 0:128], cf[:, 128:256], cf[:, 256:384], cf[:, 384:512]
    IOTA, ONEC = cf[:, 512:513], cf[:, 513:514]
    ident_b, UIN, LST = cb[:, 0:128], cb[:, 128:256], cb[:, 256:384]
    MSK = [cb[:, 384 + 512 * d: 384 + 512 * (d + 1)] for d in range(4)]
    ONESB = sb([128, 128], BF16, "onesb")
    gin = sb([128, 8], F32, "gin")
    gmem = sb([128, 8], F32, "gmem")
    fgb = sb([128, D], F32, "fgb")
    hgn = sb([128, 2], F32, "hgn")
    sbb = sb([128, 8], F32, "sbb")
    sbbp = sb([128, 8], F32, "sbbp")
    kbias = sb([128, 1], F32, "kbias")
    lbl = sb([128, 512], F32, "lbl")
    lb = sb([128, 256], F32, "lb")
    oml = sb([128, 256], F32, "oml")
    Sst = sb([128, 128], F32, "Sst")
    s_t = T()

    PF = [psum([128, 512], F32) for _ in range(6)]
    pf_t = [Tok(ps=True) for _ in range(6)]
    PB = [psum([128, 1024], BF16) for _ in range(2)]
    pb_t = [Tok(ps=True) for _ in range(2)]
    PBf = [PB[i_][:, :].bitcast(F32) for i_ in range(2)]

    class Deferred:
        def __init__(self):
            self.q = []

        def __getattr__(self, name):
            return lambda *a, **k: self.q.append((name, a, k))

        def pump(self, n_):
            for _ in range(min(n_, len(self.q))):
                name, a, k = self.q.pop(0)
                getattr(P, name)(*a, **k)


    P.dma(cf[:], cf_d, writes=[c_t])
    P.dma(cb[:], cb_d, writes=[c_t])
    P.dma(gin[:], gin_d, writes=[c_t])
    P.dma(gmem[:], gmem_d, writes=[c_t])
    P.dma(fgb[:], fg_d.partition_broadcast(128), writes=[c_t])
    P.dma(hgn[:], hgn_d, writes=[c_t])
    P.dma(sbb[:], sbb_d.partition_broadcast(128), writes=[c_t])
    P.dma(lbl[:], lbl_d.partition_broadcast(128), writes=[c_t])
    P.dma(kbias[:], kb_d, writes=[c_t])
    P.ts(sbbp[:], sbb[:], kbias[:, 0:1], None, ALU.add, reads=[c_t], writes=[c_t])
    P.memset(ONESB[:], 1.0, writes=[c_t], eng="dve")
    P.memset(Sst[:], 0.0, writes=[s_t], eng="dve")
    P.tt(lb[:], lbl[:, 256:512], lbl[:, 0:256], ALU.subtract, reads=[c_t], writes=[c_t])
    P.act(lb[:], lb[:], AF.Exp, reads=[c_t], writes=[c_t])
    P.ts(lb[:], lb[:], 1.0, None, ALU.add, reads=[c_t], writes=[c_t])
    P.recip(lb[:], lb[:], reads=[c_t], writes=[c_t])
    P.ts(oml[:], lb[:], -1.0, 1.0, ALU.mult, ALU.add, reads=[c_t], writes=[c_t])

    if STAGE == 0:
        P.emit()
        return nc
    xst = [sb([128, D], F32) for _ in range(2)]
    xst_t = [T() for _ in range(2)]
    xsb = [sb([128, D], BF16) for _ in range(2)]
    xsb_t = [T() for _ in range(2)]
    st4 = [sb([128, 4], F32) for _ in range(2)]
    st4_t = [T() for _ in range(2)]
    wst = [sb([128, 512], F32) for _ in range(2)]
    wst_t = [T() for _ in range(2)]
    WG = [sb([128, 8, 512], BF16) for _ in range(2)]
    wg_t = [T() for _ in range(2)]
    wsc_t = [T() for _ in range(9)]
    rr = {"x": 0, "w": 0, "wg": 0, "pf": 0, "pb": 0, "npf": 6}

    def rmsnorm_to_bf(src_ap, reads, which):
        st = st4[which]
        stt_ = st4_t[which]
        P.act(xsb[which][:], src_ap, AF.Square, reads=reads, writes=[xsb_t[which], stt_], accum_out=st[:, 0:1])
        P.ts(st[:, 1:2], st[:, 0:1], 1.0 / D, EPS, ALU.mult, ALU.add, reads=[stt_], writes=[stt_])
        P.act(st[:, 2:3], st[:, 1:2], AF.Ln, reads=[stt_], writes=[stt_])
        P.act(st[:, 3:4], st[:, 2:3], AF.Exp, reads=[stt_], writes=[stt_], scale=-0.5)
        P.act(xsb[which][:], src_ap, AF.Copy, reads=list(reads) + [stt_], writes=[xsb_t[which]], scale=st[:, 3:4])

    def transpose_to(dst_ap_fn, which, dst_toks):
        pb = rr["pb"] % 2
        rr["pb"] += 1
        for kb in range(8):
            P.tr(PB[pb][:, kb * 128:(kb + 1) * 128], xsb[which][:, kb * 128:(kb + 1) * 128], ident_b,
                 reads=[xsb_t[which], c_t], writes=[pb_t[pb]])
        P.copy(dst_ap_fn(), PB[pb][:, :].rearrange("p (k t) -> p k t", k=8), reads=[pb_t[pb]], writes=dst_toks)

    for g in range(9):
        wgi = rr["wg"] % 2
        rr["wg"] += 1
        for kb in range(8):
            wi = rr["w"] % 2
            rr["w"] += 1
            if g < 7:
                src = w_in_d[kb * 128:(kb + 1) * 128, g * 512:(g + 1) * 512]
            else:
                src = w_out_d[kb * 128:(kb + 1) * 128, (g - 7) * 512:(g - 6) * 512]
            P.dma(wst[wi][:], src, writes=[wst_t[wi]])
            if g < 7:
                if kb % 2 == 0:
                    P.ts(WG[wgi][:, kb, :], wst[wi][:], gin[:, kb:kb + 1], None, ALU.mult,
                         reads=[wst_t[wi], c_t], writes=[wg_t[wgi]])
                else:
                    P.act(WG[wgi][:, kb, :], wst[wi][:], AF.Copy, reads=[wst_t[wi], c_t], writes=[wg_t[wgi]],
                          scale=gin[:, kb:kb + 1])
            else:
                P.copy(WG[wgi][:, kb, :], wst[wi][:], reads=[wst_t[wi]], writes=[wg_t[wgi]],
                       eng="dve" if kb % 2 == 0 else "pool")
        P.dma(wsc_d[g], WG[wgi][:, :, :].rearrange("p k c -> p (k c)"), reads=[wg_t[wgi]], writes=[wsc_t[g]])

    if STAGE == 1:
        P.emit()
        return nc

    def load_wgroup(g):
        wgi = rr["wg"] % 2
        rr["wg"] += 1
        if not NO_SCRATCH_READ:
            P.dma(WG[wgi][:, :, :].rearrange("p k c -> p (k c)"), wsc_d[g], reads=[wsc_t[g]], writes=[wg_t[wgi]])
        return wgi

    xnT = sb([128, 8, 512], BF16, "xnT")
    xnT_t = [T() for _ in range(4)]
    memT = [xnT[:, :, mb_ * 128:(mb_ + 1) * 128] for mb_ in range(2)]
    memT_t = [xnT_t[0], xnT_t[1]]
    for mb in range(2):
        xi = rr["x"] % 2
        rr["x"] += 1
        P.dma(xst[xi][:], mem_d[mb * 128:(mb + 1) * 128, :], writes=[xst_t[xi]])
        rmsnorm_to_bf(xst[xi][:], [xst_t[xi]], xi)
        transpose_to(lambda mb=mb: memT[mb], xi, [memT_t[mb]])
    wmb = [sb([128, 512], BF16) for _ in range(2)]
    wmb_t = [T() for _ in range(2)]
    for kb in range(8):
        wi = rr["w"] % 2
        rr["w"] += 1
        P.dma(wst[wi][:], w_mem_d[kb * 128:(kb + 1) * 128, :], writes=[wst_t[wi]])
        P.ts(wmb[kb % 2][:], wst[wi][:], gmem[:, kb:kb + 1], None, ALU.mult, reads=[wst_t[wi], c_t],
             writes=[wmb_t[kb % 2]])
        for mb in range(2):
            P.mm(PF[mb][:, :], xnT[:, kb, mb * 128:(mb + 1) * 128], wmb[kb % 2][:], start=(kb == 0), stop=(kb == 7),
                 reads=[memT_t[mb], wmb_t[kb % 2]], writes=[pf_t[mb]])
    kvst = [sb([128, 512], F32) for _ in range(2)]
    kvst_t = [T() for _ in range(2)]
    kbf = [sb([128, 512], BF16) for _ in range(2)]
    kbf_t = [T() for _ in range(2)]
    for mb in range(2):
        P.copy(kvst[mb][:], PF[mb][:, :], reads=[pf_t[mb]], writes=[kvst_t[mb]], eng="act" if mb else "dve")
        P.dma(mk_d[mb * 128:(mb + 1) * 128, :], kvst[mb][:, 0:256], reads=[kvst_t[mb]])
        P.dma(mv_d[mb * 128:(mb + 1) * 128, :], kvst[mb][:, 256:512], reads=[kvst_t[mb]])
        P.copy(kbf[mb][:, 0:256], kvst[mb][:, 0:256], reads=[kvst_t[mb]], writes=[kbf_t[mb]])
        P.copy(MV[:, mb, :], kvst[mb][:, 256:512], reads=[kvst_t[mb]], writes=[mk_t])
        pb = rr["pb"] % 2
        rr["pb"] += 1
        for hp in range(2):
            P.tr(PB[pb][:, hp * 128:(hp + 1) * 128], kbf[mb][:, hp * 128:(hp + 1) * 128], ident_b,
                 reads=[kbf_t[mb], c_t], writes=[pb_t[pb]])
        P.copy(MKT[:, :, mb * 128:(mb + 1) * 128], PB[pb][:, 0:256].rearrange("p (k t) -> p k t", k=2),
               reads=[pb_t[pb]], writes=[mk_t])

    if STAGE == 2:
        P.emit()
        return nc
    QT = sb([128, 4, 512], BF16, "QT")
    qt_t = [T() for _ in range(4)]
    SG = sb([128, 8, 512], BF16, "SG")
    sg_t = [T() for _ in range(8)]
    HQT = sb([128, 2, 512], F32, "HQT")
    hq_t = [T() for _ in range(2)]
    XQT = sb([128, 2, 512], BF16, "XQT")
    xq_t = [T() for _ in range(2)]
    HF = sb([128, 4, 256], F32, "HF")
    HI = sb([128, 4, 256], F32, "HI")
    hf_t = [T() for _ in range(4)]
    hi_t = [T() for _ in range(4)]
    MIX = sb([128, 8, 512], BF16, "MIX")
    mix_t = [T() for _ in range(8)]
    gtmp = [sb([128, 512], F32) for _ in range(2)]
    gtmp_t = [T() for _ in range(2)]
    EB = [sb([128, 512], F32) for _ in range(4)]
    eb_t = [T() for _ in range(4)]
    SPB = wmb + [sb([128, 512], BF16) for _ in range(2)]
    spb_t = wmb_t + [T() for _ in range(2)]
    WB = [sb([128, 512], BF16) for _ in range(2)]
    wb_t = [T() for _ in range(2)]
    AB = [sb([128, 512], BF16) for _ in range(4)]
    ab_t = [T() for _ in range(4)]
    KTf = KT[:, 0, :].bitcast(F32)
    SPACC = [KTf[:, 0:512], KTf[:, 512:1024]]
    spacc_t = [Tok(old=[kt_t[0][kb_] for kb_ in range(NB)]) for _ in range(2)]
    hw = {n_: sb([128, 256], F32, "hw_" + n_) for n_ in ("a", "f", "lf", "k", "kd")}
    hw_t = {n_: T() for n_ in hw}
    hx = {n_: sb([128, 2, 128], F32, "hx_" + n_) for n_ in ("ep", "en", "qe", "ke")}
    hx_t = {n_: T() for n_ in hx}
    hattn = [sb([128, 128], F32) for _ in range(2)]
    hattn_t = [T() for _ in range(2)]
    hdec = sb([128, 4], F32, "hdec")
    hdec_t = T()
    hosb = sb([128, 256], F32, "hosb")
    hosb_t = T()
    hsq = sb([128, 128], F32, "hsq")
    hsq_t = T()
    hrs = sb([128, 128], F32, "hrs")
    hrs_t = T()
    yst = sb([128, D], F32, "yst")
    yst_t = T()

    def pf_next():
        i = rr["pf"] % rr["npf"]
        rr["pf"] += 1
        return i

    def silu_from_psum(pi, dst_ap, dst_tok):
        gi = rr["x"] % 2
        rr["x"] += 1
        P.act(gtmp[gi][:], PF[pi][:, :], AF.Exp, reads=[pf_t[pi]], writes=[gtmp_t[gi]], scale=-1.0)
        P.ts(gtmp[gi][:], gtmp[gi][:], 1.0, None, ALU.add, reads=[gtmp_t[gi]], writes=[gtmp_t[gi]])
        P.recip(gtmp[gi][:], gtmp[gi][:], reads=[gtmp_t[gi]], writes=[gtmp_t[gi]])
        P.tt(dst_ap, PF[pi][:, :], gtmp[gi][:], ALU.mult, reads=[pf_t[pi], gtmp_t[gi]], writes=[dst_tok])

    def feat_proj(wgi, cbs, evac):
        for cbi in cbs:
            pi = pf_next()
            for kb in range(8):
                P.mm(PF[pi][:, :], WG[wgi][:, kb, cbi * 128:(cbi + 1) * 128], xnT[:, kb, :], start=(kb == 0),
                     stop=(kb == 7), reads=[wg_t[wgi]] + xnT_t, writes=[pf_t[pi]])
            evac(cbi, pi)

    def tok_proj(wgi, c0, c1, blk):
        pi = pf_next()
        for kb in range(8):
            P.mm(PF[pi][:, 0:c1 - c0], xnT[:, kb, blk * 128:(blk + 1) * 128], WG[wgi][:, kb, c0:c1],
                 start=(kb == 0), stop=(kb == 7), reads=[wg_t[wgi], xnT_t[blk]], writes=[pf_t[pi]])
        return pi

    def hgrn_block(blk, state_only=False, HP=None, defer=False):
        if HP is None:
            HP = P
        if defer:
            HB = {0: PBf[0], 1: PBf[0], 2: PBf[0], 5: PBf[0], 3: PF[5], 4: PBf[1]}
            hb_t = {0: pb_t[0], 1: pb_t[0], 2: pb_t[0], 5: pb_t[0], 3: pf_t[5], 4: pb_t[1]}
        else:
            HB, hb_t = PF, pf_t

        c0 = blk * 128
        tmpb = [0, 1, 2] if state_only else [0, 1, 2, 5]

        def tmp_next():
            i = tmpb[rr["pf"] % len(tmpb)]
            rr["pf"] += 1
            return i
        a, f, lf, k, kd = hw["a"], hw["f"], hw["lf"], hw["k"], hw["kd"]
        HP.act(a[:], HF[:, blk, :], AF.Exp, reads=[hf_t[blk]], writes=[hw_t["a"]], scale=-1.0)
        HP.ts(a[:], a[:], 1.0, None, ALU.add, reads=[hw_t["a"]], writes=[hw_t["a"]])
        HP.recip(a[:], a[:], reads=[hw_t["a"]], writes=[hw_t["a"]])
        HP.tt(f[:], a[:], oml[:], ALU.mult, reads=[hw_t["a"], c_t], writes=[hw_t["f"]])
        HP.tt(f[:], f[:], lb[:], ALU.add, reads=[hw_t["f"], c_t], writes=[hw_t["f"]])
        HP.act(lf[:], f[:], AF.Ln, reads=[hw_t["f"]], writes=[hw_t["lf"]])
        HP.ts(k[:], f[:], -1.0, 1.0, ALU.mult, ALU.add, reads=[hw_t["f"]], writes=[hw_t["k"]])
        if STAGE == 3.1:
            return
        p_rev = tmp_next()
        HP.mm(HB[p_rev][:, 0:256], SU, lf[:], reads=[c_t, hw_t["lf"]], writes=[hb_t[p_rev]])
        HP.act(kd[:], HB[p_rev][:, 0:256], AF.Exp, reads=[hb_t[p_rev]], writes=[hw_t["kd"]])
        HP.tt(kd[:], kd[:], k[:], ALU.mult, reads=[hw_t["kd"], hw_t["k"]], writes=[hw_t["kd"]])
        if STAGE == 3.2:
            return
        p_bc = tmp_next()
        for hp in range(2):
            HP.mm(HB[p_bc][:, hp * 128:(hp + 1) * 128], lf[:, hp * 128:(hp + 1) * 128], TRI,
                 reads=[hw_t["lf"], c_t], writes=[hb_t[p_bc]])
        HP.act(hx["ep"][:, :, :], HB[p_bc][:, 0:256].rearrange("p (k t) -> p k t", k=2), AF.Exp,
              reads=[hb_t[p_bc]], writes=[hx_t["ep"]])
        if not state_only:
            HP.act(hx["en"][:, :, :], HB[p_bc][:, 0:256].rearrange("p (k t) -> p k t", k=2), AF.Exp,
                  reads=[hb_t[p_bc]], writes=[hx_t["en"]], scale=-1.0)
            HP.tt(hx["qe"][:, :, :], hx["ep"][:, :, :], HQT[:, :, c0:c0 + 128], ALU.mult,
                 reads=[hx_t["ep"]] + hq_t, writes=[hx_t["qe"]])
            if STAGE == 3.3:
                return
            p_kt = tmp_next()
            for hp in range(2):
                HP.tr(HB[p_kt][:, hp * 128:(hp + 1) * 128], k[:, hp * 128:(hp + 1) * 128], ident_f,
                     reads=[hw_t["k"], c_t], writes=[hb_t[p_kt]])
            HP.tt(hx["ke"][:, :, :], HB[p_kt][:, 0:256].rearrange("p (k t) -> p k t", k=2), hx["en"][:, :, :], ALU.mult,
                 reads=[hb_t[p_kt], hx_t["en"]], writes=[hx_t["ke"]])
        for c in range(2):
            for hp in range(2):
                HP.copy(hdec[:, c * 2 + hp: c * 2 + hp + 1], hx["ep"][:, hp, c * 64 + 63: c * 64 + 64],
                       reads=[hx_t["ep"]], writes=[hdec_t])
        if STAGE == 3.4:
            return
        if not state_only:
            p_o = 3
            for h in range(4):
                hp, rb = h // 2, (h % 2) * 64
                p_at = tmp_next()
                HP.mm(HB[p_at][:, 0:128], hx["ke"][rb:rb + 64, hp, :], hx["qe"][rb:rb + 64, hp, :],
                     reads=[hx_t["ke"], hx_t["qe"]], writes=[hb_t[p_at]], tp=(rb, 0))
                ai = h % 2
                HP.tt(hattn[ai][:], HB[p_at][:, 0:128], TRI, ALU.mult, reads=[hb_t[p_at], c_t], writes=[hattn_t[ai]])
                HP.mm(HB[p_o][rb:rb + 64, hp * 128:(hp + 1) * 128], HI[:, blk, h * 64:(h + 1) * 64], hattn[ai][:],
                     start=True, stop=True, reads=[hi_t[blk], hattn_t[ai]], writes=[hb_t[p_o]], tp=(0, rb))
            if STAGE == 3.5:
                return
        p_i = 4
        for c in range(2):
            for h in range(4):
                if STAGE == 3.55 or state_only:
                    break
                hp, rb = h // 2, (h % 2) * 64
                HP.mm(HB[p_i][rb:rb + 64, hp * 128 + c * 64: hp * 128 + c * 64 + 64],
                     Sst[rb:rb + 64, hp * 64:(hp + 1) * 64], hx["qe"][rb:rb + 64, hp, c * 64:(c + 1) * 64],
                     start=True, stop=True, reads=[s_t, hx_t["qe"]], writes=[hb_t[p_i]], tp=(rb, rb))
            if STAGE == 3.57:
                continue
            p_s = tmp_next()
            for h in range(4):
                hp, rb = h // 2, (h % 2) * 64
                HP.mm(HB[p_s][rb:rb + 64, hp * 64:(hp + 1) * 64], kd[c * 64:(c + 1) * 64, h * 64:(h + 1) * 64],
                     HI[c * 64:(c + 1) * 64, blk, h * 64:(h + 1) * 64], reads=[hw_t["kd"], hi_t[blk]],
                     writes=[hb_t[p_s]], tp=(c * 64, rb))
            for hp in range(2):
                HP.stt(Sst[:, hp * 64:(hp + 1) * 64], Sst[:, hp * 64:(hp + 1) * 64],
                      hdec[:, c * 2 + hp: c * 2 + hp + 1], HB[p_s][:, hp * 64:(hp + 1) * 64], ALU.mult, ALU.add,
                      reads=[s_t, hdec_t, hb_t[p_s]], writes=[s_t])
        if STAGE in (3.55, 3.57, 3.6) or state_only:
            return
        HP.copy(hosb[:], HB[p_i][:, 0:256], reads=[hb_t[p_i]], writes=[hosb_t], eng="act")
        HP.tt(hosb[:], hosb[:], HB[p_o][:, 0:256], ALU.add, reads=[hosb_t, hb_t[p_o]], writes=[hosb_t])
        for hp in range(2):
            HP.act(hsq[:], hosb[:, hp * 128:(hp + 1) * 128], AF.Square, reads=[hosb_t], writes=[hsq_t])
            p_n = tmp_next()
            HP.mm(HB[p_n][:, 0:128], OB, hsq[:], reads=[c_t, hsq_t], writes=[hb_t[p_n]])
            HP.ts(hrs[:], HB[p_n][:, 0:128], 1.0 / 64, EPS, ALU.mult, ALU.add, reads=[hb_t[p_n]], writes=[hrs_t])
            HP.act(hrs[:], hrs[:], AF.Ln, reads=[hrs_t], writes=[hrs_t])
            HP.act(hrs[:], hrs[:], AF.Exp, reads=[hrs_t], writes=[hrs_t], scale=-0.5)
            HP.tt(hrs[:], hosb[:, hp * 128:(hp + 1) * 128], hrs[:], ALU.mult, reads=[hosb_t, hrs_t],
                 writes=[hrs_t])
            HP.stt(MIX[:, 4 + hp, c0:c0 + 128], hrs[:], hgn[:, hp:hp + 1], SG[:, 4 + hp, c0:c0 + 128], ALU.mult,
                  ALU.mult, reads=[hrs_t, c_t, sg_t[4 + hp]], writes=[mix_t[4 + hp]])

    def xattn(HP=None, defer=False):
        if HP is None:
            HP = P
        for hp in range(2):
            if defer:
                bk = {"o": (PF[5], pf_t[5]), "d": (PBf[1], pb_t[1])}
                PT, pt_t = kbf, kbf_t
            else:
                i_o, i_d = pf_next(), pf_next()
                bk = {"o": (PF[i_o], pf_t[i_o]), "d": (PF[i_d], pf_t[i_d])}
                PT, pt_t = AB, ab_t
            (PO, po_t), (PD, pd_t) = bk["o"], bk["d"]
            for hh in range(2):
                h, rb = hp * 2 + hh, hh * 64
                for mb in range(2):
                    if defer:
                        PS_, ps_t = PBf[0], pb_t[0]
                    else:
                        i_s = pf_next()
                        PS_, ps_t = PF[i_s], pf_t[i_s]
                    HP.mm(PS_[:, :], MKT[rb:rb + 64, hp, mb * 128:(mb + 1) * 128], XQT[rb:rb + 64, hp, :],
                          reads=[mk_t, xq_t[hp]], writes=[ps_t], tp=(rb, 0))
                    ai = rr["x"] % 2
                    rr["x"] += 1
                    HP.act(PT[ai][:], PS_[:, :], AF.Exp, reads=[ps_t], writes=[pt_t[ai]])
                    HP.mm(PO[rb:rb + 64, :], MV[:, mb, h * 64:(h + 1) * 64], PT[ai][:], start=(mb == 0),
                          stop=(mb == 1), reads=[mk_t, pt_t[ai]], writes=[po_t], tp=(0, rb))
                    HP.mm(PD[rb:rb + 64, :], ONESB[:, 0:64], PT[ai][:], start=(mb == 0), stop=(mb == 1),
                          reads=[c_t, pt_t[ai]], writes=[pd_t], tp=(0, rb))
            gi = rr["x"] % 2
            rr["x"] += 1
            HP.recip(gtmp[gi][:], PD[:, :], reads=[pd_t], writes=[gtmp_t[gi]])
            HP.tt(gtmp[gi][:], PO[:, :], gtmp[gi][:], ALU.mult, reads=[po_t, gtmp_t[gi]],
                  writes=[gtmp_t[gi]])
            HP.tt(MIX[:, 6 + hp, :], gtmp[gi][:], SG[:, 6 + hp, :], ALU.mult, reads=[gtmp_t[gi], sg_t[6 + hp]],
                  writes=[mix_t[6 + hp]])

    def sb_attention(qi, side=None):
        nkb = 4 * qi + 4
        for hp in range(4):
            av = 4
            kbs = list(range(nkb - 1, -1, -1))
            n = len(kbs)

            def stage_a1(i):
                kb = kbs[i]
                r = i % 2
                for hh in range(2):
                    rb = hh * 64
                    P.mm(PF[hh][:, :], KT[rb:rb + 64, hp, kb * 128:(kb + 1) * 128], QT[rb:rb + 64, hp, :],
                         reads=[kt_t[hp][kb], qt_t[hp]], writes=[pf_t[hh]], tp=(rb, 0))
                z = kb - 4 * qi
                bsrc = sbbp if kb < 4 * HALF else sbb
                for hh in range(2):
                    h = hp * 2 + hh
                    e = hh * 2 + r
                    P.act(EB[e][:], PF[hh][:, :], AF.Exp, reads=[pf_t[hh], c_t], writes=[eb_t[e]],
                          bias=bsrc[:, h:h + 1])
                    if z >= 0:
                        P.tt(EB[e][:], EB[e][:], MSK[z], ALU.mult, reads=[eb_t[e], c_t], writes=[eb_t[e]])

            def stage_a2(i):
                r = i % 2
                for hh in range(2):
                    e = hh * 2 + r
                    P.act(SPB[e][:], EB[e][:], AF.Ln, reads=[eb_t[e]], writes=[spb_t[e]], bias=1.0)

            def stage_b(i):
                kb = kbs[i]
                r = i % 2
                for hh in range(2):
                    e = hh * 2 + r
                    P.mm(PF[2 + hh][:, :], UIN, SPB[e][:], start=(kb == nkb - 1), stop=(kb == 0),
                         reads=[c_t, spb_t[e]], writes=[pf_t[2 + hh]])
                for hh in range(2):
                    e = hh * 2 + r
                    P.act(WB[hh][:], PF[2 + hh][:, :], AF.Exp, reads=[pf_t[2 + hh]], writes=[wb_t[hh]], scale=-1.0)
                    P.tt(AB[e][:], EB[e][:], WB[hh][:], ALU.mult, reads=[eb_t[e], wb_t[hh]], writes=[ab_t[e]])

            def stage_b2(i):
                kb = kbs[i]
                r = i % 2
                if kb > 0:
                    for hh in range(2):
                        e = hh * 2 + r
                        P.mm(PF[2 + hh][:, :], LST, SPB[e][:], start=False, stop=False, reads=[c_t, spb_t[e]],
                             writes=[pf_t[2 + hh]])

            def stage_c(i):
                kb = kbs[i]
                r = i % 2
                for hh in range(2):
                    rb = hh * 64
                    e = hh * 2 + r
                    h = hp * 2 + hh
                    P.mm(PF[av][rb:rb + 64, :], VA[:, kb, h * 64:(h + 1) * 64], AB[e][:], start=(kb == nkb - 1),
                         stop=(kb == 0), reads=[va_t[kb], ab_t[e]], writes=[pf_t[av]], tp=(0, rb))

            for i in range(n + 2):
                if i < n:
                    stage_a1(i)
                if 0 <= i - 2 < n:
                    stage_b2(i - 2)
                if 0 <= i - 1 < n:
                    stage_b(i - 1)
                if i < n:
                    stage_a2(i)
                if 0 <= i - 2 < n:
                    stage_c(i - 2)
                if side is not None:
                    side(i)
            P.tt(MIX[:, hp, :], PF[av][:, :], SG[:, hp, :], ALU.mult, reads=[pf_t[av], sg_t[hp]],
                 writes=[mix_t[hp]])

    HALF = NT // 2

    def tile(ti):
        own = ti >= HALF
        src = xo_d if own else x_d
        r0 = (ti - HALF) * 512 if own else ti * 512
        for blk in range(4):
            xi = rr["x"] % 2
            rr["x"] += 1
            P.dma(xst[xi][:], src[r0 + blk * 128: r0 + (blk + 1) * 128, :], writes=[xst_t[xi]])
            rmsnorm_to_bf(xst[xi][:], [xst_t[xi]], xi)
            transpose_to(lambda blk=blk: xnT[:, :, blk * 128:(blk + 1) * 128], xi, [xnT_t[blk]])
        rr["pf"] = 0
        wgi = load_wgroup(1)
        for blk in range(4):
            gb = ti * 4 + blk
            pi = tok_proj(wgi, 0, 512, blk)
            si = gb % 2
            P.copy(kvst[si][:], PF[pi][:, :], reads=[pf_t[pi]], writes=[kvst_t[si]], eng="act")
            if own:
                P.dma(k_d[r0 + blk * 128: r0 + (blk + 1) * 128, :], kvst[si][:], reads=[kvst_t[si]])
            P.copy(kbf[si][:], kvst[si][:], reads=[kvst_t[si]], writes=[kbf_t[si]])
            pb = rr["pb"] % 2
            rr["pb"] += 1
            for hp in range(4):
                P.tr(PB[pb][:, hp * 128:(hp + 1) * 128], kbf[si][:, hp * 128:(hp + 1) * 128], ident_b,
                     reads=[kbf_t[si], c_t], writes=[pb_t[pb]])
            P.copy(KT[:, :, gb * 128:(gb + 1) * 128], PB[pb][:, 0:512].rearrange("p (k t) -> p k t", k=4),
                   reads=[pb_t[pb]], writes=[kt_t[hp_][gb] for hp_ in range(4)])
        wgi = load_wgroup(2)
        for blk in range(4):
            gb = ti * 4 + blk
            pi = tok_proj(wgi, 0, 512, blk)
            si = gb % 2
            P.copy(kvst[si][:], PF[pi][:, :], reads=[pf_t[pi]], writes=[kvst_t[si]], eng="act")
            if own:
                P.dma(v_d[r0 + blk * 128: r0 + (blk + 1) * 128, :], kvst[si][:], reads=[kvst_t[si]])
            P.copy(VA[:, gb, :], kvst[si][:], reads=[kvst_t[si]], writes=[va_t[gb]])
        if own:
            wgi = load_wgroup(0)
            feat_proj(wgi, range(4), lambda cbi, pi: P.act(QT[:, cbi, :], PF[pi][:, :], AF.Copy, reads=[pf_t[pi]],
                                                          writes=[qt_t[cbi]], scale=0.125))
            wgi = load_wgroup(3)
            feat_proj(wgi, range(4), lambda cbi, pi: silu_from_psum(pi, SG[:, cbi, :], sg_t[cbi]))
        wgi = load_wgroup(4)
        if own:
            feat_proj(wgi, range(2), lambda cbi, pi: P.copy(HQT[:, cbi, :], PF[pi][:, :], reads=[pf_t[pi]],
                                                            writes=[hq_t[cbi]], eng="act"))
        for blk in range(4):
            pi = tok_proj(wgi, 256, 512, blk)
            P.copy(HF[:, blk, :], PF[pi][:, 0:256], reads=[pf_t[pi]], writes=[hf_t[blk]])
        wgi = load_wgroup(5)
        for blk in range(4):
            pi = tok_proj(wgi, 0, 256, blk)
            P.copy(HI[:, blk, :], PF[pi][:, 0:256], reads=[pf_t[pi]], writes=[hi_t[blk]], eng="act")
        if own:
            feat_proj(wgi, range(2, 4), lambda cbi, pi: silu_from_psum(pi, SG[:, 2 + cbi, :], sg_t[2 + cbi]))
            wgi = load_wgroup(6)
            feat_proj(wgi, range(2), lambda cbi, pi: P.act(XQT[:, cbi, :], PF[pi][:, :], AF.Copy, reads=[pf_t[pi]],
                                                           writes=[xq_t[cbi]], scale=0.125))
            feat_proj(wgi, range(2, 4), lambda cbi, pi: silu_from_psum(pi, SG[:, 4 + cbi, :], sg_t[4 + cbi]))
        if not own:
            for blk in range(4):
                hgrn_block(blk, state_only=True)
            return
        HD = Deferred()
        for blk in range(4):
            hgrn_block(blk, HP=HD, defer=True)
        xattn(HP=HD, defer=True)
        nq = len(HD.q)
        n_it = 4 * (4 * ti + 6)
        per = (nq + n_it - 1) // n_it + 1
        sb_attention(ti, side=lambda i_: HD.pump(per))
        HD.pump(10 ** 9)
        wg0 = load_wgroup(7)
        wg1 = load_wgroup(8)
        for blk in range(4):
            xi = rr["x"] % 2
            rr["x"] += 1
            P.dma(xst[xi][:], src[r0 + blk * 128: r0 + (blk + 1) * 128, :], writes=[xst_t[xi]])
            for half, wg in ((0, wg0), (1, wg1)):
                pi = pf_next()
                kbs = list(range(8)) if DBG_KBS is None else list(DBG_KBS)
                for kb in kbs:
                    P.mm(PF[pi][:, :], MIX[:, kb, blk * 128:(blk + 1) * 128], WG[wg][:, kb, :], start=(kb == kbs[0]),
                         stop=(kb == kbs[-1]), reads=[mix_t[kb], wg_t[wg]], writes=[pf_t[pi]])
                P.tt(yst[:, half * 512:(half + 1) * 512], PF[pi][:, :], xst[xi][:, half * 512:(half + 1) * 512],
                     ALU.add, reads=[pf_t[pi], xst_t[xi]], writes=[yst_t])
            st, stt_ = st4[xi], st4_t[xi]
            P.act(xsb[xi][:], yst[:], AF.Square, reads=[yst_t], writes=[xsb_t[xi], stt_], accum_out=st[:, 0:1])
            P.ts(st[:, 1:2], st[:, 0:1], 1.0 / D, EPS, ALU.mult, ALU.add, reads=[stt_], writes=[stt_])
            P.act(st[:, 2:3], st[:, 1:2], AF.Ln, reads=[stt_], writes=[stt_])
            P.act(st[:, 3:4], st[:, 2:3], AF.Exp, reads=[stt_], writes=[stt_], scale=-0.5)
            if DBG_PRE:
                P.copy(xst[xi][:], yst[:], reads=[yst_t], writes=[xst_t[xi]])
            else:
                P.stt(xst[xi][:], yst[:], st[:, 3:4], fgb[:], ALU.mult, ALU.mult, reads=[yst_t, stt_, c_t],
                      writes=[xst_t[xi]])
            P.dma(y_d[r0 + blk * 128: r0 + (blk + 1) * 128, :], xst[xi][:], reads=[xst_t[xi]])

    SS = {}

    def sample_front():
        G = [EB[0], EB[1], kvst[0], kvst[1], wst[0], wst[1], gtmp[0]]
        g_t = [eb_t[0], eb_t[1], kvst_t[0], kvst_t[1], wst_t[0], wst_t[1], gtmp_t[0]]
        TMP = [gtmp[1][:, :], HQT[:, 1, :]]
        tmp_t = [gtmp_t[1], Tok(old=hq_t)]
        QB, qb_t = HQT[:, 0, :], Tok(old=hq_t)
        HFv = HF[:, :, :].rearrange("p a b -> p (a b)")
        HIv = HI[:, :, :].rearrange("p a b -> p (a b)")
        HFa, HFb, HIa, HIb = HFv[:, 0:512], HFv[:, 512:1024], HIv[:, 0:512], HIv[:, 512:1024]
        hfa_t, hfb_t, hia_t, hib_t = Tok(old=hf_t), Tok(old=hf_t), Tok(old=hi_t), Tok(old=hi_t)
        ZALL, E_ = SPACC[0], SPACC[1]
        v3 = lambda ap: ap.rearrange("p (g h) -> p g h", h=8)
        otok = sb([128, D], F32, "otok")
        otok_t = T()
        ptb = sb([128, NSAMP * NPAGE], I32, "ptb")
        idxa = sb([128, NSAMP * NPAGE], I32, "idxa")
        idx_t = T()
        hgnb = lbl[:, 0:256]
        FKQ = sb([128, 24], F32, "fkq")
        fkq_t = T()
        SM = sb([128, 32], F32, "sm")
        sm_t = T()
        scr_t = T()

        P.memset(otok[:], 0.0, writes=[otok_t], eng="dve")
        P.dma(hgnb, hgr_d.partition_broadcast(128), writes=[c_t])
        P.dma(ptb[:], pt_d.partition_broadcast(128), writes=[idx_t])
        P.ts(idxa[:], ptb[:], 128.0, IOTA, ALU.mult, ALU.add, reads=[idx_t, c_t], writes=[idx_t])
        P.memset(yst[:], 0.0, writes=[yst_t], eng="dve")
        P.dma(yst[0:NSAMP, :], xs_d, writes=[yst_t])
        rmsnorm_to_bf(yst[:], [yst_t], 0)
        transpose_to(lambda: xnT[:, :, 0:128], 0, [xnT_t[0]])
        for g in range(7):
            wgi = load_wgroup(g)
            pi = tok_proj(wgi, 0, 512, 0)
            P.copy(G[g][:], PF[pi][:, :], reads=[pf_t[pi]], writes=[g_t[g]], eng="act" if g % 2 else "dve")
            P.dma(scr_d[:, g * 512:(g + 1) * 512], G[g][0:NSAMP, :], reads=[g_t[g]], writes=[scr_t])
        P.dma(ks_d, G[1][0:NSAMP, :], reads=[g_t[1]])
        P.dma(vs_d, G[2][0:NSAMP, :], reads=[g_t[2]])
        if STAGE == 10.1:
            return
        a, f, k = hw["a"], hw["f"], hw["k"]
        P.act(a[:], G[4][:, 256:512], AF.Exp, reads=[g_t[4]], writes=[hw_t["a"]], scale=-1.0)
        P.ts(a[:], a[:], 1.0, None, ALU.add, reads=[hw_t["a"]], writes=[hw_t["a"]])
        P.recip(a[:], a[:], reads=[hw_t["a"]], writes=[hw_t["a"]])
        P.tt(f[:], a[:], oml[:], ALU.mult, reads=[hw_t["a"], c_t], writes=[hw_t["f"]])
        P.tt(f[:], f[:], lb[:], ALU.add, reads=[hw_t["f"], c_t], writes=[hw_t["f"]])
        P.ts(k[:], f[:], -1.0, 1.0, ALU.mult, ALU.add, reads=[hw_t["f"]], writes=[hw_t["k"]])
        p_f = pf_next()
        for j, (src, st_) in enumerate(((f, hw_t["f"]), (k, hw_t["k"]), (G[4], g_t[4]))):
            for hp in range(2):
                c = (j * 2 + hp) * 4
                P.tr(PF[p_f][:, c:c + 4], src[0:4, hp * 128:(hp + 1) * 128], ident_f[0:4, 0:4],
                     reads=[st_, c_t], writes=[pf_t[p_f]])
        P.copy(FKQ[:], PF[p_f][:, 0:24], reads=[pf_t[p_f]], writes=[fkq_t])

        SS.update(dict(G=G, g_t=g_t, TMP=TMP, tmp_t=tmp_t, HFa=HFa, HFb=HFb, HIa=HIa, HIb=HIb, hfa_t=hfa_t, hfb_t=hfb_t,
                       hia_t=hia_t, hib_t=hib_t, otok=otok, otok_t=otok_t, idxa=idxa, idx_t=idx_t, hgnb=hgnb, FKQ=FKQ,
                       fkq_t=fkq_t, SM=SM, sm_t=sm_t, scr_t=scr_t))

    def sample_stream():
        otok, otok_t, idxa, idx_t, scr_t = SS["otok"], SS["otok_t"], SS["idxa"], SS["idx_t"], SS["scr_t"]
        PG = [KT[:, hp_, S // 2:S].bitcast(F32) for hp_ in range(4)]
        VAf = VA[:, NB // 2:NB, :].rearrange("p a b -> p (a b)").bitcast(F32)
        PG += [VAf[:, 2048:3072]]
        pg_t = [T() for _ in range(5)]
        f32v = lambda ap: ap.rearrange("p a b -> p (a b)").bitcast(F32)
        for ap_, toks_ in ((f32v(MIX[:, 0:4, :]), mix_t[0:4]), (f32v(MIX[:, 4:8, :]), mix_t[4:8])):
            t_ = T()
            PG.append(ap_)
            pg_t.append(t_)
            for tk in toks_:
                tk.old = (tk.old or []) + [t_]
        NPB = len(PG)
        VBF = [VA[:, NB - 4 + j_, :] for j_ in range(4)]
        vbf_t = [T() for _ in range(4)]
        A8b = sb([128, 4, 8], BF16, "a8b")

        TMPs = [VAf[:, 0:512], VAf[:, 512:1024]]
        tmps_t = [T(), T()]
        QBs, qbs_t = VAf[:, 1024:1536], T()
        RES, res_t = VAf[:, 1536:2048], T()
        for hp_ in range(4):
            for kb_ in range(NB // 2, NB):
                kt_t[hp_][kb_].old = [pg_t[hp_]]
        for kb_ in range(NB // 2, NB):
            va_t[kb_].old = [pg_t[4], tmps_t[0], tmps_t[1], qbs_t, res_t] + vbf_t
        S8 = sb([128, 4, 48], F32, "s8")
        S8b = sb([128, 4, 8], BF16, "s8b")
        s8_t = [T() for _ in range(4)]
        NPG = NPAGE
        CB, AVB = 5, 4
        npg = [0]
        for n in range(NSAMP):
            P.dma(QBs, scr_d[n:n + 1, 0:512].partition_broadcast(128), reads=[scr_t], writes=[qbs_t])
            pages = list(range(NPG - 1, -1, -1))

            def st1(ii):
                p = pages[ii]
                i = npg[0] + ii
                col = n * NPAGE + p
                b8 = i % 4
                P.gather(PG[i % NPB], ckv_d[:, :], idxa[:, col:col + 1], reads=[idx_t], writes=[pg_t[i % NPB]])
                P.tt(TMPs[i % 2], PG[i % NPB][:, 0:512], QBs, ALU.mult, reads=[pg_t[i % NPB], qbs_t], writes=[tmps_t[i % 2]])
                P.copy(VBF[b8], PG[i % NPB][:, 512:1024], reads=[pg_t[i % NPB]], writes=[vbf_t[b8]], eng="act")
                P.reduce(S8[:, b8, 0:8], TMPs[i % 2].rearrange("p (h d) -> p h d", h=8), ALU.add,
                         reads=[tmps_t[i % 2]], writes=[s8_t[b8]])
                P.stt(S8[:, b8, 0:8], S8[:, b8, 0:8], 0.125, sbb[:, 0:8], ALU.mult, ALU.add,
                      reads=[s8_t[b8], c_t], writes=[s8_t[b8]])
                P.act(S8[:, b8, 8:16], S8[:, b8, 0:8], AF.Exp, reads=[s8_t[b8]], writes=[s8_t[b8]])
                P.act(S8b[:, b8, :], S8[:, b8, 8:16], AF.Ln, reads=[s8_t[b8]], writes=[s8_t[b8]], bias=1.0)

            def st2(ii):
                i = npg[0] + ii
                b8 = i % 4
                P.mm(PF[CB][:, 0:8], UIN, S8b[:, b8, :], start=(ii == 0), stop=(ii == NPG - 1),
                     reads=[c_t, s8_t[b8]], writes=[pf_t[CB]])
                P.act(S8[:, b8, 24:32], PF[CB][:, 0:8], AF.Exp, reads=[pf_t[CB]], writes=[s8_t[b8]], scale=-1.0)
                P.tt(A8b[:, b8, :], S8[:, b8, 8:16], S8[:, b8, 24:32], ALU.mult, reads=[s8_t[b8]],
                     writes=[s8_t[b8]])
                if ii < NPG - 1:
                    P.mm(PF[CB][:, 0:8], LST, S8b[:, b8, :], start=False, stop=False, reads=[c_t, s8_t[b8]],
                         writes=[pf_t[CB]])
                P.mm(PF[AVB][0:8, :], A8b[:, b8, :], VBF[b8], start=(ii == 0), stop=(ii == NPG - 1),
                     reads=[s8_t[b8], vbf_t[b8]], writes=[pf_t[AVB]])

            for ii in range(NPG + 1):
                if ii < NPG:
                    st1(ii)
                if ii >= 1:
                    st2(ii - 1)
                yield
            npg[0] += NPG
            P.copy(RES[0:8, :], PF[AVB][0:8, :], reads=[pf_t[AVB]], writes=[res_t])
            for h in range(8):
                P.dma(otok[n:n + 1, h * 64:(h + 1) * 64], RES[h:h + 1, h * 64:(h + 1) * 64], reads=[res_t],
                      writes=[otok_t])
            yield

    def sample_back():
        G, g_t, TMP, tmp_t = SS["G"], SS["g_t"], SS["TMP"], SS["tmp_t"]
        HFa, HFb, HIa, HIb = SS["HFa"], SS["HFb"], SS["HIa"], SS["HIb"]
        hfa_t, hfb_t, hia_t, hib_t = SS["hfa_t"], SS["hfb_t"], SS["hia_t"], SS["hib_t"]
        otok, otok_t, hgnb, FKQ, fkq_t, SM, sm_t, scr_t = (SS["otok"], SS["otok_t"], SS["hgnb"], SS["FKQ"], SS["fkq_t"],
                                                          SS["SM"], SS["sm_t"], SS["scr_t"])
        for g in (3, 5, 6):
            P.dma(G[g][0:NSAMP, :], scr_d[:, g * 512:(g + 1) * 512], reads=[scr_t], writes=[g_t[g]])
        P.memset(yst[:], 0.0, writes=[yst_t], eng="dve")
        P.dma(yst[0:NSAMP, :], xs_d, writes=[yst_t])
        for n in range(NSAMP):
            VB, S0, SN, KV = hw["kd"], hsq, hrs, hattn[0]
            P.dma(VB[:], scr_d[n:n + 1, 5 * 512:5 * 512 + 256].partition_broadcast(128), reads=[scr_t],
                  writes=[hw_t["kd"]])
            P.dma(S0[:], sst_d[n], writes=[hsq_t])
            for half in range(2):
                r0 = half * 64
                for hp in range(2):
                    h = hp * 2 + half
                    ck_, cf_ = (1 * 2 + hp) * 4 + n, (0 * 2 + hp) * 4 + n
                    P.ts(KV[r0:r0 + 64, hp * 64:(hp + 1) * 64], VB[r0:r0 + 64, h * 64:(h + 1) * 64],
                         FKQ[r0:r0 + 64, ck_:ck_ + 1], None, ALU.mult, reads=[hw_t["kd"], fkq_t],
                         writes=[hattn_t[0]])
                    P.stt(SN[r0:r0 + 64, hp * 64:(hp + 1) * 64], S0[r0:r0 + 64, hp * 64:(hp + 1) * 64],
                          FKQ[r0:r0 + 64, cf_:cf_ + 1], KV[r0:r0 + 64, hp * 64:(hp + 1) * 64], ALU.mult, ALU.add,
                          reads=[hsq_t, fkq_t, hattn_t[0]], writes=[hrs_t])
            P.dma(hss_d[n], SN[:], reads=[hrs_t])
            if STAGE == 10.55:
                continue
            p_h = 5
            for hp in range(2):
                cq = (2 * 2 + hp) * 4 + n
                P.ts(KV[:, hp * 64:(hp + 1) * 64], SN[:, hp * 64:(hp + 1) * 64], FKQ[:, cq:cq + 1], None, ALU.mult,
                     reads=[hrs_t, fkq_t], writes=[hattn_t[0]])
            P.mm(PF[p_h][:, 0:128], OB, KV[:, :], reads=[c_t, hattn_t[0]], writes=[pf_t[p_h]])
            P.copy(hosb[:, 0:128], PF[p_h][:, 0:128], reads=[pf_t[p_h]], writes=[hosb_t])
            for half in range(2):
                for hp in range(2):
                    h = hp * 2 + half
                    P.dma(otok[n:n + 1, 512 + h * 64:512 + (h + 1) * 64],
                          hosb[half * 64:half * 64 + 1, hp * 64:(hp + 1) * 64], reads=[hosb_t], writes=[otok_t])
            if STAGE == 10.6:
                continue
            XQB, CMK, CMV, XT = hw["lf"], TMP[0], TMP[1], HFb
            P.dma(XQB[:], scr_d[n:n + 1, 6 * 512:6 * 512 + 256].partition_broadcast(128), reads=[scr_t],
                  writes=[hw_t["lf"]])
            P.dma(CMK.rearrange("p (b c) -> p b c", b=2), cmk_d[n].rearrange("(b m) c -> m b c", b=2),
                  writes=[tmp_t[0]])
            P.dma(CMV.rearrange("p (b c) -> p b c", b=2), cmv_d[n].rearrange("(b m) c -> m b c", b=2),
                  writes=[tmp_t[1]])
            for mb in range(2):
                P.tt(XT[:, mb * 256:(mb + 1) * 256], CMK[:, mb * 256:(mb + 1) * 256], XQB[:], ALU.mult,
                     reads=[tmp_t[0], hw_t["lf"]], writes=[hfb_t])
                P.reduce(SM[:, 8 + mb * 4:12 + mb * 4],
                         XT[:, mb * 256:(mb + 1) * 256].rearrange("p (h d) -> p h d", h=4), ALU.add,
                         reads=[hfb_t], writes=[sm_t])
            P.act(SM[:, 16:24], SM[:, 8:16], AF.Exp, reads=[sm_t], writes=[sm_t], scale=0.125)
            p_x, p_d = 0, 1
            for mb in range(2):
                P.mm(PF[p_x][0:4, 0:256], SM[:, 16 + mb * 4:20 + mb * 4], CMV[:, mb * 256:(mb + 1) * 256],
                     start=(mb == 0), stop=(mb == 1), reads=[sm_t, tmp_t[1]], writes=[pf_t[p_x]])
            for mb in range(2):
                P.mm(PF[p_d][0:4, 0:1], SM[:, 16 + mb * 4:20 + mb * 4], ONEC, start=(mb == 0), stop=(mb == 1),
                     reads=[sm_t, c_t], writes=[pf_t[p_d]])
            P.recip(SM[0:4, 24:25], PF[p_d][0:4, 0:1], reads=[pf_t[p_d]], writes=[sm_t])
            P.ts(hosb[0:4, :], PF[p_x][0:4, 0:256], SM[0:4, 24:25], None, ALU.mult, reads=[pf_t[p_x], sm_t],
                 writes=[hosb_t])
            for h in range(4):
                P.dma(otok[n:n + 1, 768 + h * 64:768 + (h + 1) * 64], hosb[h:h + 1, h * 64:(h + 1) * 64],
                      reads=[hosb_t], writes=[otok_t])
        if STAGE == 10.7:
            return
        HG = otok[:, 512:768]
        P.tt(hosb[:], HG, HG, ALU.mult, reads=[otok_t], writes=[hosb_t])
        P.reduce(SM[:, 0:4], hosb[:].rearrange("p (h d) -> p h d", h=4), ALU.add, reads=[hosb_t], writes=[sm_t])
        P.ts(SM[:, 0:4], SM[:, 0:4], 1.0 / 64, EPS, ALU.mult, ALU.add, reads=[sm_t], writes=[sm_t])
        P.act(SM[:, 4:8], SM[:, 0:4], AF.Ln, reads=[sm_t], writes=[sm_t])
        P.act(SM[:, 8:12], SM[:, 4:8], AF.Exp, reads=[sm_t], writes=[sm_t], scale=-0.5)
        for h in range(4):
            P.ts(otok[:, 512 + h * 64:512 + (h + 1) * 64], otok[:, 512 + h * 64:512 + (h + 1) * 64],
                 SM[:, 8 + h:9 + h], None, ALU.mult, reads=[otok_t, sm_t], writes=[otok_t])
        P.tt(HG, HG, hgnb, ALU.mult, reads=[otok_t, c_t], writes=[otok_t])
        for (gsrc, gtok, c0, c1, o0) in ((G[3], g_t[3], 0, 512, 0), (G[5], g_t[5], 256, 512, 512),
                                         (G[6], g_t[6], 256, 512, 768)):
            w_ = c1 - c0
            t_ = HIb[:, 0:w_]
            P.act(t_, gsrc[:, c0:c1], AF.Exp, reads=[gtok], writes=[hib_t], scale=-1.0)
            P.ts(t_, t_, 1.0, None, ALU.add, reads=[hib_t], writes=[hib_t])
            P.recip(t_, t_, reads=[hib_t], writes=[hib_t])
            P.tt(t_, t_, gsrc[:, c0:c1], ALU.mult, reads=[hib_t, gtok], writes=[hib_t])
            P.tt(otok[:, o0:o0 + w_], otok[:, o0:o0 + w_], t_, ALU.mult, reads=[otok_t, hib_t], writes=[otok_t])
        P.copy(xsb[0][:], otok[:], reads=[otok_t], writes=[xsb_t[0]])
        transpose_to(lambda: MIX[:, :, 0:128], 0, mix_t)
        wg0 = load_wgroup(7)
        wg1 = load_wgroup(8)
        st, stt_ = st4[0], st4_t[0]
        for half, wg in ((0, wg0), (1, wg1)):
            pi = pf_next()
            for kb in range(8):
                P.mm(PF[pi][:, :], MIX[:, kb, 0:128], WG[wg][:, kb, :], start=(kb == 0), stop=(kb == 7),
                     reads=[mix_t[kb], wg_t[wg]], writes=[pf_t[pi]])
            P.tt(G[half][:], PF[pi][:, :], yst[:, half * 512:(half + 1) * 512], ALU.add,
                 reads=[pf_t[pi], yst_t], writes=[g_t[half]])
            P.act(xsb[1][:, half * 512:(half + 1) * 512], G[half][:], AF.Square, reads=[g_t[half]],
                  writes=[xsb_t[1], stt_], accum_out=st[:, half:half + 1])
        P.tt(st[:, 0:1], st[:, 0:1], st[:, 1:2], ALU.add, reads=[stt_], writes=[stt_])
        P.ts(st[:, 1:2], st[:, 0:1], 1.0 / D, EPS, ALU.mult, ALU.add, reads=[stt_], writes=[stt_])
        P.act(st[:, 2:3], st[:, 1:2], AF.Ln, reads=[stt_], writes=[stt_])
        P.act(st[:, 3:4], st[:, 2:3], AF.Exp, reads=[stt_], writes=[stt_], scale=-0.5)
        for half in range(2):
            P.stt(G[half][:], G[half][:], st[:, 3:4], fgb[:, half * 512:(half + 1) * 512], ALU.mult, ALU.mult,
                  reads=[g_t[half], stt_, c_t], writes=[g_t[half]])
            P.dma(ys_d[:, half * 512:(half + 1) * 512], G[half][0:NSAMP, :], reads=[g_t[half]])

    gen = [None]

    def pump(k):
        if gen[0] is None:
            return
        for _ in range(k):
            try:
                next(gen[0])
            except StopIteration:
                gen[0] = None
                return

    if ENABLE_SAMPLE:
        sample_front()
        gen[0] = sample_stream()
        rr["npf"] = 4
        P.hook = lambda: pump(1)
        P.every = 6
    for ti in range(N_TILES):
        if ti == HALF or N_TILES < HALF:
            P.hook = None
            pump(10 ** 9)
            rr["npf"] = 6
        tile(ti)
    P.hook = None
    pump(10 ** 9)
    rr["npf"] = 6
    P.dma(hgs_d, Sst[:], reads=[s_t])


    if ENABLE_SAMPLE:
        sample_back()

    P.emit()
    return nc


_NC_CACHE = {}


def _consts():
    s = np.arange(128)[:, None]
    t = np.arange(128)[None, :]
    same = (s // 64) == (t // 64)
    ident = np.eye(128, dtype=np.float32)
    tri = ((s <= t) & same).astype(np.float32)
    su = ((s > t) & same).astype(np.float32)
    ob = same.astype(np.float32)
    cf = np.concatenate([ident, tri, su, ob, np.arange(128, dtype=np.float32)[:, None], np.ones((128, 1), np.float32)],
                        axis=1).astype(np.float32)
    uin = (s >= t).astype(np.float32)
    tq = np.arange(512)[None, :]
    msk = [((d * 128 + s) < tq).astype(np.float32) for d in range(4)]
    lst = (s < t).astype(np.float32)
    cb = np.concatenate([ident, uin, lst] + msk, axis=1).astype(ml_dtypes.bfloat16)
    return cf, cb


def kernel(x_prompt, x_sample, mem_prompt, cache_k, cache_v, page_table, state_hgrn, cache_mem_k, cache_mem_v,
           norm_gain, w_in, sb_bias, hg_lb_logits, hg_norm_gain, mem_norm_gain, w_mem_kv, w_out, final_norm_gain):
    if "nc" not in _NC_CACHE:
        _NC_CACHE["nc"] = build_program()
    nc = _NC_CACHE["nc"]
    f = lambda a: np.ascontiguousarray(np.asarray(a, dtype=np.float32))
    cf, cb = _consts()
    common = {
        "w_in": f(w_in[0]), "w_out": f(w_out[0]), "w_mem": f(w_mem_kv[0]),
        "gin": f(np.asarray(norm_gain[0]).reshape(8, 128).T),
        "gmem": f(np.asarray(mem_norm_gain[0]).reshape(8, 128).T),
        "fg": f(np.asarray(final_norm_gain).reshape(1, D)),
        "hgn": f(np.asarray(hg_norm_gain[0]).reshape(2, 128).T),
        "sbb": f(np.asarray(sb_bias[0]).reshape(1, 8)),
        "lbl": f(np.asarray(hg_lb_logits).reshape(1, 512)),
        "cf": cf, "cb": cb,
    }
    if ENABLE_SAMPLE:
        ckv = np.concatenate([f(cache_k[0]).reshape(POOL_PAGES * 128, 512),
                              f(cache_v[0]).reshape(POOL_PAGES * 128, 512)], axis=1)
    in_maps = []
    for c in range(8):
        b = c // 2
        m = dict(common)
        r = c % 2
        xb = f(x_prompt[b])
        m["x"] = np.ascontiguousarray(xb[:S // 2]) if r == 1 else np.zeros((S // 2, D), np.float32)
        m["xo"] = np.ascontiguousarray(xb[r * (S // 2):(r + 1) * (S // 2)])
        m["kbias"] = np.full((128, 1), 0.0 if r == 1 else -30000.0, np.float32)
        m["mem"] = f(mem_prompt[b])
        if ENABLE_SAMPLE:
            sl = slice(NSAMP * c, NSAMP * (c + 1))
            m["xs"] = f(x_sample[sl, 0])
            m["pt"] = np.ascontiguousarray(np.asarray(page_table[sl], dtype=np.int32).reshape(1, NSAMP * NPAGE))
            m["ckv"] = ckv
            st = f(state_hgrn[0, sl]).reshape(NSAMP, 2, 2, 64, 64).transpose(0, 2, 3, 1, 4).reshape(NSAMP, 128, 128)
            m["sst"] = np.ascontiguousarray(st)
            m["cmk"] = f(cache_mem_k[0, sl]).reshape(NSAMP, 256, 256)
            m["cmv"] = f(cache_mem_v[0, sl]).reshape(NSAMP, 256, 256)
            m["hgr"] = f(np.asarray(hg_norm_gain[0]).reshape(1, 256))
        in_maps.append(m)
    res = run_bass_kernel_spmd(nc, in_maps[:NCORES], core_ids=list(range(NCORES))).results
    res = list(res) + [res[0], res[1 % NCORES]] * ((8 - NCORES) // 2 + 1)

    def unstate(a):
        return a.reshape(2, 64, 2, 64).transpose(2, 0, 1, 3).reshape(4, 64, 64)

    cat = lambda b, n_: np.concatenate([res[2 * b][n_], res[2 * b + 1][n_]], axis=0)
    y_prompt = np.stack([cat(b, "y") for b in range(4)]).astype(np.float32)
    k_prompt = np.stack([cat(b, "k") for b in range(4)]).reshape(1, 4, S, 8, 64).astype(np.float32)
    v_prompt = np.stack([cat(b, "v") for b in range(4)]).reshape(1, 4, S, 8, 64).astype(np.float32)
    hgrn_prompt = np.stack([unstate(res[2 * b + 1]["hgs"]) for b in range(4)])[None].astype(np.float32)
    mem_k = np.stack([res[2 * b]["mk"] for b in range(4)]).reshape(1, 4, 256, 4, 64).astype(np.float32)
    mem_v = np.stack([res[2 * b]["mv"] for b in range(4)]).reshape(1, 4, 256, 4, 64).astype(np.float32)
    if ENABLE_SAMPLE:
        y_sample = np.concatenate([res[c]["ys"] for c in range(8)]).reshape(32, 1, D).astype(np.float32)
        k_sample = np.concatenate([res[c]["ks"] for c in range(8)]).reshape(1, 32, 1, 8, 64).astype(np.float32)
        v_sample = np.concatenate([res[c]["vs"] for c in range(8)]).reshape(1, 32, 1, 8, 64).astype(np.float32)
        hgrn_sample = np.concatenate([np.stack([unstate(res[c]["hss"][i]) for i in range(NSAMP)])
                                      for c in range(8)])[None].astype(np.float32)
    else:
        y_sample = np.zeros((32, 1, D), np.float32)
        k_sample = np.zeros((1, 32, 1, 8, 64), np.float32)
        v_sample = np.zeros((1, 32, 1, 8, 64), np.float32)
        hgrn_sample = np.zeros((1, 32, 4, 64, 64), np.float32)
    return (y_prompt, y_sample, k_prompt, v_prompt, hgrn_prompt, mem_k, mem_v, k_sample, v_sample, hgrn_sample)
```

```python
import contextlib
import numpy as np
import ml_dtypes
import concourse.bass as bass
import concourse.mybir as mybir
from concourse.bass_utils import run_bass_kernel_spmd

F32 = mybir.dt.float32
BF16 = mybir.dt.bfloat16
I32 = mybir.dt.int32
AF = mybir.ActivationFunctionType
ALU = mybir.AluOpType
AX = mybir.AxisListType

D = 1024
S = 4096
NT = S // 512
NB = S // 128
DIN = 3584
EPS = 1e-6
NSAMP = 4
NPAGE = 64
N_DMA_SEM = 12
RING = 3 * N_DMA_SEM

ENABLE_SAMPLE = True
N_TILES = NT
STAGE = 99
NCORES = 8
POOL_PAGES = 2560
NO_SCRATCH_READ = False
DBG_PRE = False
DBG_KBS = None


class Tok:
    __slots__ = ("w", "r", "ps", "old")

    def __init__(self, ps=False, old=None):
        self.w = None
        self.r = []
        self.ps = ps
        self.old = old


class Prog:
    def __init__(self, nc, es):
        self.nc = nc
        self.es = es
        self.ops = []
        self.hook = None
        self.every = 6
        self._n = 0
        self._in_hook = False
        self.eng = {"pe": nc.tensor, "act": nc.scalar, "dve": nc.vector, "pool": nc.gpsimd, "sp": nc.sync}

    def op(self, eng, fn, reads=(), writes=(), dma=False):
        idx = len(self.ops)
        deps = set()
        for t in reads:
            if t.w is not None:
                deps.add(t.w)
            if t.ps:
                deps.update(r for r in t.r if self.ops[r][0] != eng)
        for t in writes:
            if t.w is not None:
                deps.add(t.w)
            deps.update(t.r)
            if t.old:
                for o in t.old:
                    if o.w is not None:
                        deps.add(o.w)
                    deps.update(o.r)
                t.old = None
        deps.discard(idx)
        self.ops.append([eng, fn, deps, dma])
        for t in reads:
            t.r.append(idx)
        for t in writes:
            t.w = idx
            t.r = []
        if self.hook is not None and not self._in_hook:
            self._n += 1
            if self._n % self.every == 0:
                self._in_hook = True
                try:
                    self.hook()
                finally:
                    self._in_hook = False
        return idx

    def dma(self, out, in_, reads=(), writes=(), q="sp", **kw):
        e = self.eng[q]
        return self.op(q, lambda: e.dma_start(out=out, in_=in_, **kw), reads, writes, dma=True)

    def gather(self, out, in_, idx_ap, reads=(), writes=()):
        g = self.nc.gpsimd
        return self.op("pool", lambda: g.indirect_dma_start(
            out=out, out_offset=None, in_=in_, in_offset=bass.IndirectOffsetOnAxis(ap=idx_ap, axis=0)),
            reads, writes, dma=True)

    def act(self, out, in_, func, reads=(), writes=(), **kw):
        a = self.nc.scalar
        return self.op("act", lambda: a.activation(out=out, in_=in_, func=func, **kw), reads, writes)

    def tt(self, out, in0, in1, op, reads=(), writes=(), eng="dve"):
        e = self.eng[eng]
        return self.op(eng, lambda: e.tensor_tensor(out=out, in0=in0, in1=in1, op=op), reads, writes)

    def ts(self, out, in0, s1, s2, op0, op1=None, reads=(), writes=(), eng="dve"):
        e = self.eng[eng]
        if op1 is None:
            return self.op(eng, lambda: e.tensor_scalar(out=out, in0=in0, scalar1=s1, scalar2=None, op0=op0),
                           reads, writes)
        return self.op(eng, lambda: e.tensor_scalar(out=out, in0=in0, scalar1=s1, scalar2=s2, op0=op0, op1=op1),
                       reads, writes)

    def stt(self, out, in0, scalar, in1, op0, op1, reads=(), writes=()):
        e = self.nc.vector
        return self.op("dve", lambda: e.scalar_tensor_tensor(out=out, in0=in0, scalar=scalar, in1=in1,
                                                             op0=op0, op1=op1), reads, writes)

    def copy(self, out, in_, reads=(), writes=(), eng="dve"):
        e = self.eng[eng]
        if eng == "act":
            return self.op(eng, lambda: e.activation(out=out, in_=in_, func=AF.Copy), reads, writes)
        return self.op(eng, lambda: e.tensor_copy(out=out, in_=in_), reads, writes)

    def recip(self, out, in_, reads=(), writes=()):
        e = self.nc.vector
        return self.op("dve", lambda: e.reciprocal(out=out, in_=in_), reads, writes)

    def reduce(self, out, in_, op, reads=(), writes=()):
        e = self.nc.vector
        return self.op("dve", lambda: e.tensor_reduce(out=out, in_=in_, axis=AX.X, op=op), reads, writes)

    def scan(self, out, d0, d1, reads=(), writes=()):
        e = self.nc.vector
        return self.op("dve", lambda: e.tensor_tensor_scan(out=out, data0=d0, data1=d1, initial=0.0,
                                                           op0=ALU.mult, op1=ALU.add), reads, writes)

    def memset(self, ap, val, writes=(), eng="pool"):
        e = self.eng[eng]
        return self.op(eng, lambda: e.memset(ap, val), (), writes)

    def mm(self, out, lhsT, rhs, start=True, stop=True, reads=(), writes=(), tp=None):
        t = self.nc.tensor
        if tp is None:
            return self.op("pe", lambda: t.matmul(out, lhsT=lhsT, rhs=rhs, start=start, stop=stop), reads, writes)
        return self.op("pe", lambda: t.matmul(out, lhsT=lhsT, rhs=rhs, start=start, stop=stop, tile_position=tp),
                       reads, writes)

    def tr(self, out, in_, ident, reads=(), writes=()):
        t = self.nc.tensor
        return self.op("pe", lambda: t.transpose(out=out, in_=in_, identity=ident), reads, writes)

    def emit(self):
        nc, es = self.nc, self.es
        ops = self.ops
        n = len(ops)
        comp = ("pe", "act", "dve", "pool")
        pos = [0] * n
        cnt = {e: 0 for e in comp}
        qbase = {"sp": 0, "pool": N_DMA_SEM, "act": 2 * N_DMA_SEM}
        dq = {q: [] for q in qbase}
        for i, (eng, fn, deps, dma) in enumerate(ops):
            if dma:
                k = len(dq[eng])
                pos[i] = (k // N_DMA_SEM) * RING + qbase[eng] + (k % N_DMA_SEM)
                dq[eng].append(i)
            else:
                pos[i] = cnt[eng]
                cnt[eng] += 1
        dma_ops = [i for i in range(n) if ops[i][3]]
        for q, lst in dq.items():
            for k, i in enumerate(lst):
                if k >= N_DMA_SEM:
                    ops[i][2].add(lst[k - N_DMA_SEM])
        waited = {}
        marked = [False] * n
        waits = [None] * n
        for i, (eng, fn, deps, dma) in enumerate(ops):
            wl = []
            for d in sorted(deps):
                deng, _, _, ddma = ops[d]
                if ddma:
                    key = (eng, "dma", pos[d] % RING)
                    val = pos[d] // RING
                else:
                    if deng == eng:
                        if eng == "pe":
                            continue
                        if eng != "pool" and pos[i] - pos[d] > 2:
                            continue
                    key = (eng, deng)
                    val = pos[d]
                if waited.get(key, -1) >= val:
                    continue
                waited[key] = val
                wl.append(d)
                marked[d] = True
            waits[i] = wl
        sem = {e: es.enter_context(nc.semaphore("s_" + e)) for e in comp}
        dsem = [es.enter_context(nc.semaphore("s_dma%d" % k)) for k in range(RING)]
        val = [0] * n
        c2 = {e: 0 for e in comp}
        for i, (eng, fn, deps, dma) in enumerate(ops):
            if dma:
                val[i] = 16 * (pos[i] // RING + 1)
            elif marked[i]:
                c2[eng] += 1
                val[i] = c2[eng]
        for i, (eng, fn, deps, dma) in enumerate(ops):
            e = self.eng[eng]
            for d in waits[i]:
                deng, _, _, ddma = ops[d]
                if ddma:
                    e.wait_ge(dsem[pos[d] % RING], val[d])
                else:
                    e.wait_ge(sem[deng], val[d])
            ins = fn()
            if dma:
                ins.then_inc(dsem[pos[i] % RING], 16)
            elif marked[i]:
                ins.then_inc(sem[eng], 1)
        last = {}
        for i in dma_ops:
            last[pos[i] % RING] = val[i]
        for k, v in last.items():
            nc.sync.wait_ge(dsem[k], v)
        for e in comp:
            if c2[e] > 0:
                nc.sync.wait_ge(sem[e], c2[e])


def build_program():
    nc = bass.Bass("TRN2", target_bir_lowering=False)
    es = contextlib.ExitStack()
    P = Prog(nc, es)

    def din(name, shape, dt=F32):
        return nc.dram_tensor(name, list(shape), dt, kind="ExternalInput").ap()

    def dout(name, shape, dt=F32):
        return nc.dram_tensor(name, list(shape), dt, kind="ExternalOutput").ap()

    x_d = din("x", [S // 2, D])
    xo_d = din("xo", [S // 2, D])
    kb_d = din("kbias", [128, 1])
    mem_d = din("mem", [256, D])
    w_in_d = din("w_in", [D, DIN])
    w_out_d = din("w_out", [D, D])
    w_mem_d = din("w_mem", [D, 512])
    gin_d = din("gin", [128, 8])
    gmem_d = din("gmem", [128, 8])
    fg_d = din("fg", [1, D])
    hgn_d = din("hgn", [128, 2])
    sbb_d = din("sbb", [1, 8])
    lbl_d = din("lbl", [1, 512])
    cf_d = din("cf", [128, 514])
    cb_d = din("cb", [128, 384 + 2048], BF16)
    y_d = dout("y", [S // 2, D])
    k_d = dout("k", [S // 2, 512])
    v_d = dout("v", [S // 2, 512])
    hgs_d = dout("hgs", [128, 128])
    mk_d = dout("mk", [256, 256])
    mv_d = dout("mv", [256, 256])
    wsc_d = nc.dram_tensor("wsc", [9, 128, 4096], BF16, kind="Internal").ap()
    if ENABLE_SAMPLE:
        xs_d = din("xs", [NSAMP, D])
        pt_d = din("pt", [1, NSAMP * NPAGE], I32)
        ckv_d = din("ckv", [POOL_PAGES * 128, 1024])
        sst_d = din("sst", [NSAMP, 128, 128])
        cmk_d = din("cmk", [NSAMP, 256, 256])
        cmv_d = din("cmv", [NSAMP, 256, 256])
        ys_d = dout("ys", [NSAMP, D])
        ks_d = dout("ks", [NSAMP, 512])
        vs_d = dout("vs", [NSAMP, 512])
        hss_d = dout("hss", [NSAMP, 128, 128])
        hgr_d = din("hgr", [1, 256])
        scr_d = nc.dram_tensor("scr", [NSAMP, DIN], F32, kind="Internal").ap()

    cnt = [0]

    def sb(shape, dt=F32, name=None):
        cnt[0] += 1
        return es.enter_context(nc.sbuf_tensor("sb_" + (name or ("t%d" % cnt[0])), list(shape), dt))

    def psum(shape, dt=F32):
        cnt[0] += 1
        return es.enter_context(nc.psum_tensor("p%d" % cnt[0], list(shape), dt))

    def T():
        return Tok()

    KT = sb([128, 4, S], BF16, "KT")
    kt_t = [[T() for _ in range(NB)] for _ in range(4)]
    VA = sb([128, NB, 512], BF16, "VA")
    va_t = [T() for _ in range(NB)]
    MKT = sb([128, 2, 256], BF16, "MKT")
    MV = sb([128, 2, 256], BF16, "MV")
    mk_t = T()
    cf = sb([128, 514], F32, "cf")
    cb = sb([128, 384 + 2048], BF16, "cb")
    c_t = T()
    ident_f, TRI, SU, OB = cf[:, 0:128], cf[:, 128:256], cf[:, 256:384], cf[:, 384:512]
    IOTA, ONEC = cf[:, 512:513], cf[:, 513:514]
    ident_b, UIN, LST = cb[:, 0:128], cb[:, 128:256], cb[:, 256:384]
    MSK = [cb[:, 384 + 512 * d: 384 + 512 * (d + 1)] for d in range(4)]
    ONESB = sb([128, 128], BF16, "onesb")
    gin = sb([128, 8], F32, "gin")
    gmem = sb([128, 8], F32, "gmem")
    fgb = sb([128, D], F32, "fgb")
    hgn = sb([128, 2], F32, "hgn")
    sbb = sb([128, 8], F32, "sbb")
    sbbp = sb([128, 8], F32, "sbbp")
    kbias = sb([128, 1], F32, "kbias")
    lbl = sb([128, 512], F32, "lbl")
    lb = sb([128, 256], F32, "lb")
    oml = sb([128, 256], F32, "oml")
    Sst = sb([128, 128], F32, "Sst")
    s_t = T()

    PF = [psum([128, 512], F32) for _ in range(6)]
    pf_t = [Tok(ps=True) for _ in range(6)]
    PB = [psum([128, 1024], BF16) for _ in range(2)]
    pb_t = [Tok(ps=True) for _ in range(2)]
    PBf = [PB[i_][:, :].bitcast(F32) for i_ in range(2)]

    class Deferred:
        def __init__(self):
            self.q = []

        def __getattr__(self, name):
            return lambda *a, **k: self.q.append((name, a, k))

        def pump(self, n_):
            for _ in range(min(n_, len(self.q))):
                name, a, k = self.q.pop(0)
                getattr(P, name)(*a, **k)


    P.dma(cf[:], cf_d, writes=[c_t])
    P.dma(cb[:], cb_d, writes=[c_t])
    P.dma(gin[:], gin_d, writes=[c_t])
    P.dma(gmem[:], gmem_d, writes=[c_t])
    P.dma(fgb[:], fg_d.partition_broadcast(128), writes=[c_t])
    P.dma(hgn[:], hgn_d, writes=[c_t])
    P.dma(sbb[:], sbb_d.partition_broadcast(128), writes=[c_t])
    P.dma(lbl[:], lbl_d.partition_broadcast(128), writes=[c_t])
    P.dma(kbias[:], kb_d, writes=[c_t])
    P.ts(sbbp[:], sbb[:], kbias[:, 0:1], None, ALU.add, reads=[c_t], writes=[c_t])
    P.memset(ONESB[:], 1.0, writes=[c_t], eng="dve")
    P.memset(Sst[:], 0.0, writes=[s_t], eng="dve")
    P.tt(lb[:], lbl[:, 256:512], lbl[:, 0:256], ALU.subtract, reads=[c_t], writes=[c_t])
    P.act(lb[:], lb[:], AF.Exp, reads=[c_t], writes=[c_t])
    P.ts(lb[:], lb[:], 1.0, None, ALU.add, reads=[c_t], writes=[c_t])
    P.recip(lb[:], lb[:], reads=[c_t], writes=[c_t])
    P.ts(oml[:], lb[:], -1.0, 1.0, ALU.mult, ALU.add, reads=[c_t], writes=[c_t])

    if STAGE == 0:
        P.emit()
        return nc
    xst = [sb([128, D], F32) for _ in range(2)]
    xst_t = [T() for _ in range(2)]
    xsb = [sb([128, D], BF16) for _ in range(2)]
    xsb_t = [T() for _ in range(2)]
    st4 = [sb([128, 4], F32) for _ in range(2)]
    st4_t = [T() for _ in range(2)]
    wst = [sb([128, 512], F32) for _ in range(2)]
    wst_t = [T() for _ in range(2)]
    WG = [sb([128, 8, 512], BF16) for _ in range(2)]
    wg_t = [T() for _ in range(2)]
    wsc_t = [T() for _ in range(9)]
    rr = {"x": 0, "w": 0, "wg": 0, "pf": 0, "pb": 0, "npf": 6}

    def rmsnorm_to_bf(src_ap, reads, which):
        st = st4[which]
        stt_ = st4_t[which]
        P.act(xsb[which][:], src_ap, AF.Square, reads=reads, writes=[xsb_t[which], stt_], accum_out=st[:, 0:1])
        P.ts(st[:, 1:2], st[:, 0:1], 1.0 / D, EPS, ALU.mult, ALU.add, reads=[stt_], writes=[stt_])
        P.act(st[:, 2:3], st[:, 1:2], AF.Ln, reads=[stt_], writes=[stt_])
        P.act(st[:, 3:4], st[:, 2:3], AF.Exp, reads=[stt_], writes=[stt_], scale=-0.5)
        P.act(xsb[which][:], src_ap, AF.Copy, reads=list(reads) + [stt_], writes=[xsb_t[which]], scale=st[:, 3:4])

    def transpose_to(dst_ap_fn, which, dst_toks):
        pb = rr["pb"] % 2
        rr["pb"] += 1
        for kb in range(8):
            P.tr(PB[pb][:, kb * 128:(kb + 1) * 128], xsb[which][:, kb * 128:(kb + 1) * 128], ident_b,
                 reads=[xsb_t[which], c_t], writes=[pb_t[pb]])
        P.copy(dst_ap_fn(), PB[pb][:, :].rearrange("p (k t) -> p k t", k=8), reads=[pb_t[pb]], writes=dst_toks)

    for g in range(9):
        wgi = rr["wg"] % 2
        rr["wg"] += 1
        for kb in range(8):
            wi = rr["w"] % 2
            rr["w"] += 1
            if g < 7:
                src = w_in_d[kb * 128:(kb + 1) * 128, g * 512:(g + 1) * 512]
            else:
                src = w_out_d[kb * 128:(kb + 1) * 128, (g - 7) * 512:(g - 6) * 512]
            P.dma(wst[wi][:], src, writes=[wst_t[wi]])
            if g < 7:
                if kb % 2 == 0:
                    P.ts(WG[wgi][:, kb, :], wst[wi][:], gin[:, kb:kb + 1], None, ALU.mult,
                         reads=[wst_t[wi], c_t], writes=[wg_t[wgi]])
                else:
                    P.act(WG[wgi][:, kb, :], wst[wi][:], AF.Copy, reads=[wst_t[wi], c_t], writes=[wg_t[wgi]],
                          scale=gin[:, kb:kb + 1])
            else:
                P.copy(WG[wgi][:, kb, :], wst[wi][:], reads=[wst_t[wi]], writes=[wg_t[wgi]],
                       eng="dve" if kb % 2 == 0 else "pool")
        P.dma(wsc_d[g], WG[wgi][:, :, :].rearrange("p k c -> p (k c)"), reads=[wg_t[wgi]], writes=[wsc_t[g]])

    if STAGE == 1:
        P.emit()
        return nc

    def load_wgroup(g):
        wgi = rr["wg"] % 2
        rr["wg"] += 1
        if not NO_SCRATCH_READ:
            P.dma(WG[wgi][:, :, :].rearrange("p k c -> p (k c)"), wsc_d[g], reads=[wsc_t[g]], writes=[wg_t[wgi]])
        return wgi

    xnT = sb([128, 8, 512], BF16, "xnT")
    xnT_t = [T() for _ in range(4)]
    memT = [xnT[:, :, mb_ * 128:(mb_ + 1) * 128] for mb_ in range(2)]
    memT_t = [xnT_t[0], xnT_t[1]]
    for mb in range(2):
        xi = rr["x"] % 2
        rr["x"] += 1
        P.dma(xst[xi][:], mem_d[mb * 128:(mb + 1) * 128, :], writes=[xst_t[xi]])
        rmsnorm_to_bf(xst[xi][:], [xst_t[xi]], xi)
        transpose_to(lambda mb=mb: memT[mb], xi, [memT_t[mb]])
    wmb = [sb([128, 512], BF16) for _ in range(2)]
    wmb_t = [T() for _ in range(2)]
    for kb in range(8):
        wi = rr["w"] % 2
        rr["w"] += 1
        P.dma(wst[wi][:], w_mem_d[kb * 128:(kb + 1) * 128, :], writes=[wst_t[wi]])
        P.ts(wmb[kb % 2][:], wst[wi][:], gmem[:, kb:kb + 1], None, ALU.mult, reads=[wst_t[wi], c_t],
             writes=[wmb_t[kb % 2]])
        for mb in range(2):
            P.mm(PF[mb][:, :], xnT[:, kb, mb * 128:(mb + 1) * 128], wmb[kb % 2][:], start=(kb == 0), stop=(kb == 7),
                 reads=[memT_t[mb], wmb_t[kb % 2]], writes=[pf_t[mb]])
    kvst = [sb([128, 512], F32) for _ in range(2)]
    kvst_t = [T() for _ in range(2)]
    kbf = [sb([128, 512], BF16) for _ in range(2)]
    kbf_t = [T() for _ in range(2)]
    for mb in range(2):
        P.copy(kvst[mb][:], PF[mb][:, :], reads=[pf_t[mb]], writes=[kvst_t[mb]], eng="act" if mb else "dve")
        P.dma(mk_d[mb * 128:(mb + 1) * 128, :], kvst[mb][:, 0:256], reads=[kvst_t[mb]])
        P.dma(mv_d[mb * 128:(mb + 1) * 128, :], kvst[mb][:, 256:512], reads=[kvst_t[mb]])
        P.copy(kbf[mb][:, 0:256], kvst[mb][:, 0:256], reads=[kvst_t[mb]], writes=[kbf_t[mb]])
        P.copy(MV[:, mb, :], kvst[mb][:, 256:512], reads=[kvst_t[mb]], writes=[mk_t])
        pb = rr["pb"] % 2
        rr["pb"] += 1
        for hp in range(2):
            P.tr(PB[pb][:, hp * 128:(hp + 1) * 128], kbf[mb][:, hp * 128:(hp + 1) * 128], ident_b,
                 reads=[kbf_t[mb], c_t], writes=[pb_t[pb]])
        P.copy(MKT[:, :, mb * 128:(mb + 1) * 128], PB[pb][:, 0:256].rearrange("p (k t) -> p k t", k=2),
               reads=[pb_t[pb]], writes=[mk_t])

    if STAGE == 2:
        P.emit()
        return nc
    QT = sb([128, 4, 512], BF16, "QT")
    qt_t = [T() for _ in range(4)]
    SG = sb([128, 8, 512], BF16, "SG")
    sg_t = [T() for _ in range(8)]
    HQT = sb([128, 2, 512], F32, "HQT")
    hq_t = [T() for _ in range(2)]
    XQT = sb([128, 2, 512], BF16, "XQT")
    xq_t = [T() for _ in range(2)]
    HF = sb([128, 4, 256], F32, "HF")
    HI = sb([128, 4, 256], F32, "HI")
    hf_t = [T() for _ in range(4)]
    hi_t = [T() for _ in range(4)]
    MIX = sb([128, 8, 512], BF16, "MIX")
    mix_t = [T() for _ in range(8)]
    gtmp = [sb([128, 512], F32) for _ in range(2)]
    gtmp_t = [T() for _ in range(2)]
    EB = [sb([128, 512], F32) for _ in range(4)]
    eb_t = [T() for _ in range(4)]
    SPB = wmb + [sb([128, 512], BF16) for _ in range(2)]
    spb_t = wmb_t + [T() for _ in range(2)]
    WB = [sb([128, 512], BF16) for _ in range(2)]
    wb_t = [T() for _ in range(2)]
    AB = [sb([128, 512], BF16) for _ in range(4)]
    ab_t = [T() for _ in range(4)]
    KTf = KT[:, 0, :].bitcast(F32)
    SPACC = [KTf[:, 0:512], KTf[:, 512:1024]]
    spacc_t = [Tok(old=[kt_t[0][kb_] for kb_ in range(NB)]) for _ in range(2)]
    hw = {n_: sb([128, 256], F32, "hw_" + n_) for n_ in ("a", "f", "lf", "k", "kd")}
    hw_t = {n_: T() for n_ in hw}
    hx = {n_: sb([128, 2, 128], F32, "hx_" + n_) for n_ in ("ep", "en", "qe", "ke")}
    hx_t = {n_: T() for n_ in hx}
    hattn = [sb([128, 128], F32) for _ in range(2)]
    hattn_t = [T() for _ in range(2)]
    hdec = sb([128, 4], F32, "hdec")
    hdec_t = T()
    hosb = sb([128, 256], F32, "hosb")
    hosb_t = T()
    hsq = sb([128, 128], F32, "hsq")
    hsq_t = T()
    hrs = sb([128, 128], F32, "hrs")
    hrs_t = T()
    yst = sb([128, D], F32, "yst")
    yst_t = T()

    def pf_next():
        i = rr["pf"] % rr["npf"]
        rr["pf"] += 1
        return i

    def silu_from_psum(pi, dst_ap, dst_tok):
        gi = rr["x"] % 2
        rr["x"] += 1
        P.act(gtmp[gi][:], PF[pi][:, :], AF.Exp, reads=[pf_t[pi]], writes=[gtmp_t[gi]], scale=-1.0)
        P.ts(gtmp[gi][:], gtmp[gi][:], 1.0, None, ALU.add, reads=[gtmp_t[gi]], writes=[gtmp_t[gi]])
        P.recip(gtmp[gi][:], gtmp[gi][:], reads=[gtmp_t[gi]], writes=[gtmp_t[gi]])
        P.tt(dst_ap, PF[pi][:, :], gtmp[gi][:], ALU.mult, reads=[pf_t[pi], gtmp_t[gi]], writes=[dst_tok])

    def feat_proj(wgi, cbs, evac):
        for cbi in cbs:
            pi = pf_next()
            for kb in range(8):
                P.mm(PF[pi][:, :], WG[wgi][:, kb, cbi * 128:(cbi + 1) * 128], xnT[:, kb, :], start=(kb == 0),
                     stop=(kb == 7), reads=[wg_t[wgi]] + xnT_t, writes=[pf_t[pi]])
            evac(cbi, pi)

    def tok_proj(wgi, c0, c1, blk):
        pi = pf_next()
        for kb in range(8):
            P.mm(PF[pi][:, 0:c1 - c0], xnT[:, kb, blk * 128:(blk + 1) * 128], WG[wgi][:, kb, c0:c1],
                 start=(kb == 0), stop=(kb == 7), reads=[wg_t[wgi], xnT_t[blk]], writes=[pf_t[pi]])
        return pi

    def hgrn_block(blk, state_only=False, HP=None, defer=False):
        if HP is None:
            HP = P
        if defer:
            HB = {0: PBf[0], 1: PBf[0], 2: PBf[0], 5: PBf[0], 3: PF[5], 4: PBf[1]}
            hb_t = {0: pb_t[0], 1: pb_t[0], 2: pb_t[0], 5: pb_t[0], 3: pf_t[5], 4: pb_t[1]}
        else:
            HB, hb_t = PF, pf_t

        c0 = blk * 128
        tmpb = [0, 1, 2] if state_only else [0, 1, 2, 5]

        def tmp_next():
            i = tmpb[rr["pf"] % len(tmpb)]
            rr["pf"] += 1
            return i
        a, f, lf, k, kd = hw["a"], hw["f"], hw["lf"], hw["k"], hw["kd"]
        HP.act(a[:], HF[:, blk, :], AF.Exp, reads=[hf_t[blk]], writes=[hw_t["a"]], scale=-1.0)
        HP.ts(a[:], a[:], 1.0, None, ALU.add, reads=[hw_t["a"]], writes=[hw_t["a"]])
        HP.recip(a[:], a[:], reads=[hw_t["a"]], writes=[hw_t["a"]])
        HP.tt(f[:], a[:], oml[:], ALU.mult, reads=[hw_t["a"], c_t], writes=[hw_t["f"]])
        HP.tt(f[:], f[:], lb[:], ALU.add, reads=[hw_t["f"], c_t], writes=[hw_t["f"]])
        HP.act(lf[:], f[:], AF.Ln, reads=[hw_t["f"]], writes=[hw_t["lf"]])
        HP.ts(k[:], f[:], -1.0, 1.0, ALU.mult, ALU.add, reads=[hw_t["f"]], writes=[hw_t["k"]])
        if STAGE == 3.1:
            return
        p_rev = tmp_next()
        HP.mm(HB[p_rev][:, 0:256], SU, lf[:], reads=[c_t, hw_t["lf"]], writes=[hb_t[p_rev]])
        HP.act(kd[:], HB[p_rev][:, 0:256], AF.Exp, reads=[hb_t[p_rev]], writes=[hw_t["kd"]])
        HP.tt(kd[:], kd[:], k[:], ALU.mult, reads=[hw_t["kd"], hw_t["k"]], writes=[hw_t["kd"]])
        if STAGE == 3.2:
            return
        p_bc = tmp_next()
        for hp in range(2):
            HP.mm(HB[p_bc][:, hp * 128:(hp + 1) * 128], lf[:, hp * 128:(hp + 1) * 128], TRI,
                 reads=[hw_t["lf"], c_t], writes=[hb_t[p_bc]])
        HP.act(hx["ep"][:, :, :], HB[p_bc][:, 0:256].rearrange("p (k t) -> p k t", k=2), AF.Exp,
              reads=[hb_t[p_bc]], writes=[hx_t["ep"]])
        if not state_only:
            HP.act(hx["en"][:, :, :], HB[p_bc][:, 0:256].rearrange("p (k t) -> p k t", k=2), AF.Exp,
                  reads=[hb_t[p_bc]], writes=[hx_t["en"]], scale=-1.0)
            HP.tt(hx["qe"][:, :, :], hx["ep"][:, :, :], HQT[:, :, c0:c0 + 128], ALU.mult,
                 reads=[hx_t["ep"]] + hq_t, writes=[hx_t["qe"]])
            if STAGE == 3.3:
                return
            p_kt = tmp_next()
            for hp in range(2):
                HP.tr(HB[p_kt][:, hp * 128:(hp + 1) * 128], k[:, hp * 128:(hp + 1) * 128], ident_f,
                     reads=[hw_t["k"], c_t], writes=[hb_t[p_kt]])
            HP.tt(hx["ke"][:, :, :], HB[p_kt][:, 0:256].rearrange("p (k t) -> p k t", k=2), hx["en"][:, :, :], ALU.mult,
                 reads=[hb_t[p_kt], hx_t["en"]], writes=[hx_t["ke"]])
        for c in range(2):
            for hp in range(2):
                HP.copy(hdec[:, c * 2 + hp: c * 2 + hp + 1], hx["ep"][:, hp, c * 64 + 63: c * 64 + 64],
                       reads=[hx_t["ep"]], writes=[hdec_t])
        if STAGE == 3.4:
            return
        if not state_only:
            p_o = 3
            for h in range(4):
                hp, rb = h // 2, (h % 2) * 64
                p_at = tmp_next()
                HP.mm(HB[p_at][:, 0:128], hx["ke"][rb:rb + 64, hp, :], hx["qe"][rb:rb + 64, hp, :],
                     reads=[hx_t["ke"], hx_t["qe"]], writes=[hb_t[p_at]], tp=(rb, 0))
                ai = h % 2
                HP.tt(hattn[ai][:], HB[p_at][:, 0:128], TRI, ALU.mult, reads=[hb_t[p_at], c_t], writes=[hattn_t[ai]])
                HP.mm(HB[p_o][rb:rb + 64, hp * 128:(hp + 1) * 128], HI[:, blk, h * 64:(h + 1) * 64], hattn[ai][:],
                     start=True, stop=True, reads=[hi_t[blk], hattn_t[ai]], writes=[hb_t[p_o]], tp=(0, rb))
            if STAGE == 3.5:
                return
        p_i = 4
        for c in range(2):
            for h in range(4):
                if STAGE == 3.55 or state_only:
                    break
                hp, rb = h // 2, (h % 2) * 64
                HP.mm(HB[p_i][rb:rb + 64, hp * 128 + c * 64: hp * 128 + c * 64 + 64],
                     Sst[rb:rb + 64, hp * 64:(hp + 1) * 64], hx["qe"][rb:rb + 64, hp, c * 64:(c + 1) * 64],
                     start=True, stop=True, reads=[s_t, hx_t["qe"]], writes=[hb_t[p_i]], tp=(rb, rb))
            if STAGE == 3.57:
                continue
            p_s = tmp_next()
            for h in range(4):
                hp, rb = h // 2, (h % 2) * 64
                HP.mm(HB[p_s][rb:rb + 64, hp * 64:(hp + 1) * 64], kd[c * 64:(c + 1) * 64, h * 64:(h + 1) * 64],
                     HI[c * 64:(c + 1) * 64, blk, h * 64:(h + 1) * 64], reads=[hw_t["kd"], hi_t[blk]],
                     writes=[hb_t[p_s]], tp=(c * 64, rb))
            for hp in range(2):
                HP.stt(Sst[:, hp * 64:(hp + 1) * 64], Sst[:, hp * 64:(hp + 1) * 64],
                      hdec[:, c * 2 + hp: c * 2 + hp + 1], HB[p_s][:, hp * 64:(hp + 1) * 64], ALU.mult, ALU.add,
                      reads=[s_t, hdec_t, hb_t[p_s]], writes=[s_t])
        if STAGE in (3.55, 3.57, 3.6) or state_only:
            return
        HP.copy(hosb[:], HB[p_i][:, 0:256], reads=[hb_t[p_i]], writes=[hosb_t], eng="act")
        HP.tt(hosb[:], hosb[:], HB[p_o][:, 0:256], ALU.add, reads=[hosb_t, hb_t[p_o]], writes=[hosb_t])
        for hp in range(2):
            HP.act(hsq[:], hosb[:, hp * 128:(hp + 1) * 128], AF.Square, reads=[hosb_t], writes=[hsq_t])
            p_n = tmp_next()
            HP.mm(HB[p_n][:, 0:128], OB, hsq[:], reads=[c_t, hsq_t], writes=[hb_t[p_n]])
            HP.ts(hrs[:], HB[p_n][:, 0:128], 1.0 / 64, EPS, ALU.mult, ALU.add, reads=[hb_t[p_n]], writes=[hrs_t])
            HP.act(hrs[:], hrs[:], AF.Ln, reads=[hrs_t], writes=[hrs_t])
            HP.act(hrs[:], hrs[:], AF.Exp, reads=[hrs_t], writes=[hrs_t], scale=-0.5)
            HP.tt(hrs[:], hosb[:, hp * 128:(hp + 1) * 128], hrs[:], ALU.mult, reads=[hosb_t, hrs_t],
                 writes=[hrs_t])
            HP.stt(MIX[:, 4 + hp, c0:c0 + 128], hrs[:], hgn[:, hp:hp + 1], SG[:, 4 + hp, c0:c0 + 128], ALU.mult,
                  ALU.mult, reads=[hrs_t, c_t, sg_t[4 + hp]], writes=[mix_t[4 + hp]])

    def xattn(HP=None, defer=False):
        if HP is None:
            HP = P
        for hp in range(2):
            if defer:
                bk = {"o": (PF[5], pf_t[5]), "d": (PBf[1], pb_t[1])}
                PT, pt_t = kbf, kbf_t
            else:
                i_o, i_d = pf_next(), pf_next()
                bk = {"o": (PF[i_o], pf_t[i_o]), "d": (PF[i_d], pf_t[i_d])}
                PT, pt_t = AB, ab_t
            (PO, po_t), (PD, pd_t) = bk["o"], bk["d"]
            for hh in range(2):
                h, rb = hp * 2 + hh, hh * 64
                for mb in range(2):
                    if defer:
                        PS_, ps_t = PBf[0], pb_t[0]
                    else:
                        i_s = pf_next()
                        PS_, ps_t = PF[i_s], pf_t[i_s]
                    HP.mm(PS_[:, :], MKT[rb:rb + 64, hp, mb * 128:(mb + 1) * 128], XQT[rb:rb + 64, hp, :],
                          reads=[mk_t, xq_t[hp]], writes=[ps_t], tp=(rb, 0))
                    ai = rr["x"] % 2
                    rr["x"] += 1
                    HP.act(PT[ai][:], PS_[:, :], AF.Exp, reads=[ps_t], writes=[pt_t[ai]])
                    HP.mm(PO[rb:rb + 64, :], MV[:, mb, h * 64:(h + 1) * 64], PT[ai][:], start=(mb == 0),
                          stop=(mb == 1), reads=[mk_t, pt_t[ai]], writes=[po_t], tp=(0, rb))
                    HP.mm(PD[rb:rb + 64, :], ONESB[:, 0:64], PT[ai][:], start=(mb == 0), stop=(mb == 1),
                          reads=[c_t, pt_t[ai]], writes=[pd_t], tp=(0, rb))
            gi = rr["x"] % 2
            rr["x"] += 1
            HP.recip(gtmp[gi][:], PD[:, :], reads=[pd_t], writes=[gtmp_t[gi]])
            HP.tt(gtmp[gi][:], PO[:, :], gtmp[gi][:], ALU.mult, reads=[po_t, gtmp_t[gi]],
                  writes=[gtmp_t[gi]])
            HP.tt(MIX[:, 6 + hp, :], gtmp[gi][:], SG[:, 6 + hp, :], ALU.mult, reads=[gtmp_t[gi], sg_t[6 + hp]],
                  writes=[mix_t[6 + hp]])

    def sb_attention(qi, side=None):
        nkb = 4 * qi + 4
        for hp in range(4):
            av = 4
            kbs = list(range(nkb - 1, -1, -1))
            n = len(kbs)

            def stage_a1(i):
                kb = kbs[i]
                r = i % 2
                for hh in range(2):
                    rb = hh * 64
                    P.mm(PF[hh][:, :], KT[rb:rb + 64, hp, kb * 128:(kb + 1) * 128], QT[rb:rb + 64, hp, :],
                         reads=[kt_t[hp][kb], qt_t[hp]], writes=[pf_t[hh]], tp=(rb, 0))
                z = kb - 4 * qi
                bsrc = sbbp if kb < 4 * HALF else sbb
                for hh in range(2):
                    h = hp * 2 + hh
                    e = hh * 2 + r
                    P.act(EB[e][:], PF[hh][:, :], AF.Exp, reads=[pf_t[hh], c_t], writes=[eb_t[e]],
                          bias=bsrc[:, h:h + 1])
                    if z >= 0:
                        P.tt(EB[e][:], EB[e][:], MSK[z], ALU.mult, reads=[eb_t[e], c_t], writes=[eb_t[e]])

            def stage_a2(i):
                r = i % 2
                for hh in range(2):
                    e = hh * 2 + r
                    P.act(SPB[e][:], EB[e][:], AF.Ln, reads=[eb_t[e]], writes=[spb_t[e]], bias=1.0)

            def stage_b(i):
                kb = kbs[i]
                r = i % 2
                for hh in range(2):
                    e = hh * 2 + r
                    P.mm(PF[2 + hh][:, :], UIN, SPB[e][:], start=(kb == nkb - 1), stop=(kb == 0),
                         reads=[c_t, spb_t[e]], writes=[pf_t[2 + hh]])
                for hh in range(2):
                    e = hh * 2 + r
                    P.act(WB[hh][:], PF[2 + hh][:, :], AF.Exp, reads=[pf_t[2 + hh]], writes=[wb_t[hh]], scale=-1.0)
                    P.tt(AB[e][:], EB[e][:], WB[hh][:], ALU.mult, reads=[eb_t[e], wb_t[hh]], writes=[ab_t[e]])

            def stage_b2(i):
                kb = kbs[i]
                r = i % 2
                if kb > 0:
                    for hh in range(2):
                        e = hh * 2 + r
                        P.mm(PF[2 + hh][:, :], LST, SPB[e][:], start=False, stop=False, reads=[c_t, spb_t[e]],
                             writes=[pf_t[2 + hh]])

            def stage_c(i):
                kb = kbs[i]
                r = i % 2
                for hh in range(2):
                    rb = hh * 64
                    e = hh * 2 + r
                    h = hp * 2 + hh
                    P.mm(PF[av][rb:rb + 64, :], VA[:, kb, h * 64:(h + 1) * 64], AB[e][:], start=(kb == nkb - 1),
                         stop=(kb == 0), reads=[va_t[kb], ab_t[e]], writes=[pf_t[av]], tp=(0, rb))

            for i in range(n + 2):
                if i < n:
                    stage_a1(i)
                if 0 <= i - 2 < n:
                    stage_b2(i - 2)
                if 0 <= i - 1 < n:
                    stage_b(i - 1)
                if i < n:
                    stage_a2(i)
                if 0 <= i - 2 < n:
                    stage_c(i - 2)
                if side is not None:
                    side(i)
            P.tt(MIX[:, hp, :], PF[av][:, :], SG[:, hp, :], ALU.mult, reads=[pf_t[av], sg_t[hp]],
                 writes=[mix_t[hp]])

    HALF = NT // 2

    def tile(ti):
        own = ti >= HALF
        src = xo_d if own else x_d
        r0 = (ti - HALF) * 512 if own else ti * 512
        for blk in range(4):
            xi = rr["x"] % 2
            rr["x"] += 1
            P.dma(xst[xi][:], src[r0 + blk * 128: r0 + (blk + 1) * 128, :], writes=[xst_t[xi]])
            rmsnorm_to_bf(xst[xi][:], [xst_t[xi]], xi)
            transpose_to(lambda blk=blk: xnT[:, :, blk * 128:(blk + 1) * 128], xi, [xnT_t[blk]])
        rr["pf"] = 0
        wgi = load_wgroup(1)
        for blk in range(4):
            gb = ti * 4 + blk
            pi = tok_proj(wgi, 0, 512, blk)
            si = gb % 2
            P.copy(kvst[si][:], PF[pi][:, :], reads=[pf_t[pi]], writes=[kvst_t[si]], eng="act")
            if own:
                P.dma(k_d[r0 + blk * 128: r0 + (blk + 1) * 128, :], kvst[si][:], reads=[kvst_t[si]])
            P.copy(kbf[si][:], kvst[si][:], reads=[kvst_t[si]], writes=[kbf_t[si]])
            pb = rr["pb"] % 2
            rr["pb"] += 1
            for hp in range(4):
                P.tr(PB[pb][:, hp * 128:(hp + 1) * 128], kbf[si][:, hp * 128:(hp + 1) * 128], ident_b,
                     reads=[kbf_t[si], c_t], writes=[pb_t[pb]])
            P.copy(KT[:, :, gb * 128:(gb + 1) * 128], PB[pb][:, 0:512].rearrange("p (k t) -> p k t", k=4),
                   reads=[pb_t[pb]], writes=[kt_t[hp_][gb] for hp_ in range(4)])
        wgi = load_wgroup(2)
        for blk in range(4):
            gb = ti * 4 + blk
            pi = tok_proj(wgi, 0, 512, blk)
            si = gb % 2
            P.copy(kvst[si][:], PF[pi][:, :], reads=[pf_t[pi]], writes=[kvst_t[si]], eng="act")
            if own:
                P.dma(v_d[r0 + blk * 128: r0 + (blk + 1) * 128, :], kvst[si][:], reads=[kvst_t[si]])
            P.copy(VA[:, gb, :], kvst[si][:], reads=[kvst_t[si]], writes=[va_t[gb]])
        if own:
            wgi = load_wgroup(0)
            feat_proj(wgi, range(4), lambda cbi, pi: P.act(QT[:, cbi, :], PF[pi][:, :], AF.Copy, reads=[pf_t[pi]],
                                                          writes=[qt_t[cbi]], scale=0.125))
            wgi = load_wgroup(3)
            feat_proj(wgi, range(4), lambda cbi, pi: silu_from_psum(pi, SG[:, cbi, :], sg_t[cbi]))
        wgi = load_wgroup(4)
        if own:
            feat_proj(wgi, range(2), lambda cbi, pi: P.copy(HQT[:, cbi, :], PF[pi][:, :], reads=[pf_t[pi]],
                                                            writes=[hq_t[cbi]], eng="act"))
        for blk in range(4):
            pi = tok_proj(wgi, 256, 512, blk)
            P.copy(HF[:, blk, :], PF[pi][:, 0:256], reads=[pf_t[pi]], writes=[hf_t[blk]])
        wgi = load_wgroup(5)
        for blk in range(4):
            pi = tok_proj(wgi, 0, 256, blk)
            P.copy(HI[:, blk, :], PF[pi][:, 0:256], reads=[pf_t[pi]], writes=[hi_t[blk]], eng="act")
        if own:
            feat_proj(wgi, range(2, 4), lambda cbi, pi: silu_from_psum(pi, SG[:, 2 + cbi, :], sg_t[2 + cbi]))
            wgi = load_wgroup(6)
            feat_proj(wgi, range(2), lambda cbi, pi: P.act(XQT[:, cbi, :], PF[pi][:, :], AF.Copy, reads=[pf_t[pi]],
                                                           writes=[xq_t[cbi]], scale=0.125))
            feat_proj(wgi, range(2, 4), lambda cbi, pi: silu_from_psum(pi, SG[:, 4 + cbi, :], sg_t[4 + cbi]))
        if not own:
            for blk in range(4):
                hgrn_block(blk, state_only=True)
            return
        HD = Deferred()
        for blk in range(4):
            hgrn_block(blk, HP=HD, defer=True)
        xattn(HP=HD, defer=True)
        nq = len(HD.q)
        n_it = 4 * (4 * ti + 6)
        per = (nq + n_it - 1) // n_it + 1
        sb_attention(ti, side=lambda i_: HD.pump(per))
        HD.pump(10 ** 9)
        wg0 = load_wgroup(7)
        wg1 = load_wgroup(8)
        for blk in range(4):
            xi = rr["x"] % 2
            rr["x"] += 1
            P.dma(xst[xi][:], src[r0 + blk * 128: r0 + (blk + 1) * 128, :], writes=[xst_t[xi]])
            for half, wg in ((0, wg0), (1, wg1)):
                pi = pf_next()
                kbs = list(range(8)) if DBG_KBS is None else list(DBG_KBS)
                for kb in kbs:
                    P.mm(PF[pi][:, :], MIX[:, kb, blk * 128:(blk + 1) * 128], WG[wg][:, kb, :], start=(kb == kbs[0]),
                         stop=(kb == kbs[-1]), reads=[mix_t[kb], wg_t[wg]], writes=[pf_t[pi]])
                P.tt(yst[:, half * 512:(half + 1) * 512], PF[pi][:, :], xst[xi][:, half * 512:(half + 1) * 512],
                     ALU.add, reads=[pf_t[pi], xst_t[xi]], writes=[yst_t])
            st, stt_ = st4[xi], st4_t[xi]
            P.act(xsb[xi][:], yst[:], AF.Square, reads=[yst_t], writes=[xsb_t[xi], stt_], accum_out=st[:, 0:1])
            P.ts(st[:, 1:2], st[:, 0:1], 1.0 / D, EPS, ALU.mult, ALU.add, reads=[stt_], writes=[stt_])
            P.act(st[:, 2:3], st[:, 1:2], AF.Ln, reads=[stt_], writes=[stt_])
            P.act(st[:, 3:4], st[:, 2:3], AF.Exp, reads=[stt_], writes=[stt_], scale=-0.5)
            if DBG_PRE:
                P.copy(xst[xi][:], yst[:], reads=[yst_t], writes=[xst_t[xi]])
            else:
                P.stt(xst[xi][:], yst[:], st[:, 3:4], fgb[:], ALU.mult, ALU.mult, reads=[yst_t, stt_, c_t],
                      writes=[xst_t[xi]])
            P.dma(y_d[r0 + blk * 128: r0 + (blk + 1) * 128, :], xst[xi][:], reads=[xst_t[xi]])

    SS = {}

    def sample_front():
        G = [EB[0], EB[1], kvst[0], kvst[1], wst[0], wst[1], gtmp[0]]
        g_t = [eb_t[0], eb_t[1], kvst_t[0], kvst_t[1], wst_t[0], wst_t[1], gtmp_t[0]]
        TMP = [gtmp[1][:, :], HQT[:, 1, :]]
        tmp_t = [gtmp_t[1], Tok(old=hq_t)]
        QB, qb_t = HQT[:, 0, :], Tok(old=hq_t)
        HFv = HF[:, :, :].rearrange("p a b -> p (a b)")
        HIv = HI[:, :, :].rearrange("p a b -> p (a b)")
        HFa, HFb, HIa, HIb = HFv[:, 0:512], HFv[:, 512:1024], HIv[:, 0:512], HIv[:, 512:1024]
        hfa_t, hfb_t, hia_t, hib_t = Tok(old=hf_t), Tok(old=hf_t), Tok(old=hi_t), Tok(old=hi_t)
        ZALL, E_ = SPACC[0], SPACC[1]
        v3 = lambda ap: ap.rearrange("p (g h) -> p g h", h=8)
        otok = sb([128, D], F32, "otok")
        otok_t = T()
        ptb = sb([128, NSAMP * NPAGE], I32, "ptb")
        idxa = sb([128, NSAMP * NPAGE], I32, "idxa")
        idx_t = T()
        hgnb = lbl[:, 0:256]
        FKQ = sb([128, 24], F32, "fkq")
        fkq_t = T()
        SM = sb([128, 32], F32, "sm")
        sm_t = T()
        scr_t = T()

        P.memset(otok[:], 0.0, writes=[otok_t], eng="dve")
        P.dma(hgnb, hgr_d.partition_broadcast(128), writes=[c_t])
        P.dma(ptb[:], pt_d.partition_broadcast(128), writes=[idx_t])
        P.ts(idxa[:], ptb[:], 128.0, IOTA, ALU.mult, ALU.add, reads=[idx_t, c_t], writes=[idx_t])
        P.memset(yst[:], 0.0, writes=[yst_t], eng="dve")
        P.dma(yst[0:NSAMP, :], xs_d, writes=[yst_t])
        rmsnorm_to_bf(yst[:], [yst_t], 0)
        transpose_to(lambda: xnT[:, :, 0:128], 0, [xnT_t[0]])
        for g in range(7):
            wgi = load_wgroup(g)
            pi = tok_proj(wgi, 0, 512, 0)
            P.copy(G[g][:], PF[pi][:, :], reads=[pf_t[pi]], writes=[g_t[g]], eng="act" if g % 2 else "dve")
            P.dma(scr_d[:, g * 512:(g + 1) * 512], G[g][0:NSAMP, :], reads=[g_t[g]], writes=[scr_t])
        P.dma(ks_d, G[1][0:NSAMP, :], reads=[g_t[1]])
        P.dma(vs_d, G[2][0:NSAMP, :], reads=[g_t[2]])
        if STAGE == 10.1:
            return
        a, f, k = hw["a"], hw["f"], hw["k"]
        P.act(a[:], G[4][:, 256:512], AF.Exp, reads=[g_t[4]], writes=[hw_t["a"]], scale=-1.0)
        P.ts(a[:], a[:], 1.0, None, ALU.add, reads=[hw_t["a"]], writes=[hw_t["a"]])
        P.recip(a[:], a[:], reads=[hw_t["a"]], writes=[hw_t["a"]])
        P.tt(f[:], a[:], oml[:], ALU.mult, reads=[hw_t["a"], c_t], writes=[hw_t["f"]])
        P.tt(f[:], f[:], lb[:], ALU.add, reads=[hw_t["f"], c_t], writes=[hw_t["f"]])
        P.ts(k[:], f[:], -1.0, 1.0, ALU.mult, ALU.add, reads=[hw_t["f"]], writes=[hw_t["k"]])
        p_f = pf_next()
        for j, (src, st_) in enumerate(((f, hw_t["f"]), (k, hw_t["k"]), (G[4], g_t[4]))):
            for hp in range(2):
                c = (j * 2 + hp) * 4
                P.tr(PF[p_f][:, c:c + 4], src[0:4, hp * 128:(hp + 1) * 128], ident_f[0:4, 0:4],
                     reads=[st_, c_t], writes=[pf_t[p_f]])
        P.copy(FKQ[:], PF[p_f][:, 0:24], reads=[pf_t[p_f]], writes=[fkq_t])

        SS.update(dict(G=G, g_t=g_t, TMP=TMP, tmp_t=tmp_t, HFa=HFa, HFb=HFb, HIa=HIa, HIb=HIb, hfa_t=hfa_t, hfb_t=hfb_t,
                       hia_t=hia_t, hib_t=hib_t, otok=otok, otok_t=otok_t, idxa=idxa, idx_t=idx_t, hgnb=hgnb, FKQ=FKQ,
                       fkq_t=fkq_t, SM=SM, sm_t=sm_t, scr_t=scr_t))

    def sample_stream():
        otok, otok_t, idxa, idx_t, scr_t = SS["otok"], SS["otok_t"], SS["idxa"], SS["idx_t"], SS["scr_t"]
        PG = [KT[:, hp_, S // 2:S].bitcast(F32) for hp_ in range(4)]
        VAf = VA[:, NB // 2:NB, :].rearrange("p a b -> p (a b)").bitcast(F32)
        pg_t = [T() for _ in range(4)]
        f32v = lambda ap: ap.rearrange("p a b -> p (a b)").bitcast(F32)
        for ap_, toks_ in ((f32v(MIX[:, 0:4, :]), mix_t[0:4]), (f32v(MIX[:, 4:8, :]), mix_t[4:8])):
            t_ = T()
            PG.append(ap_)
            pg_t.append(t_)
            for tk in toks_:
                tk.old = (tk.old or []) + [t_]
        NPB = len(PG)
        NR = 8
        VBF = [VA[:, NB - 8 + j_, :] for j_ in range(NR)]
        vbf_t = [T() for _ in range(NR)]
        A8b = sb([128, NR, 8], BF16, "a8b")

        TMPs = [VAf[:, 0:512], VAf[:, 512:1024]]
        tmps_t = [T(), T()]
        QBs, qbs_t = VAf[:, 1024:1536], T()
        RES, res_t = VAf[:, 1536:2048], T()
        for hp_ in range(4):
            for kb_ in range(NB // 2, NB):
                kt_t[hp_][kb_].old = [pg_t[hp_]]
        for kb_ in range(NB // 2, NB):
            va_t[kb_].old = [tmps_t[0], tmps_t[1], qbs_t, res_t] + vbf_t
        S8 = sb([128, NR, 32], F32, "s8")
        S8b = sb([128, NR, 8], BF16, "s8b")
        s8_t = [T() for _ in range(NR)]
        NPG = NPAGE
        CB, AVB = 5, 4
        npg = [0]
        for n in range(NSAMP):
            P.dma(QBs, scr_d[n:n + 1, 0:512].partition_broadcast(128), reads=[scr_t], writes=[qbs_t])
            pages = list(range(NPG - 1, -1, -1))

            def st1(ii):
                p = pages[ii]
                i = npg[0] + ii
                col = n * NPAGE + p
                b8 = i % NR
                P.gather(PG[i % NPB], ckv_d[:, :], idxa[:, col:col + 1], reads=[idx_t], writes=[pg_t[i % NPB]])
                P.tt(TMPs[i % 2], PG[i % NPB][:, 0:512], QBs, ALU.mult, reads=[pg_t[i % NPB], qbs_t], writes=[tmps_t[i % 2]])
                P.copy(VBF[b8], PG[i % NPB][:, 512:1024], reads=[pg_t[i % NPB]], writes=[vbf_t[b8]], eng="act")
                P.reduce(S8[:, b8, 0:8], TMPs[i % 2].rearrange("p (h d) -> p h d", h=8), ALU.add,
                         reads=[tmps_t[i % 2]], writes=[s8_t[b8]])
                P.stt(S8[:, b8, 0:8], S8[:, b8, 0:8], 0.125, sbb[:, 0:8], ALU.mult, ALU.add,
                      reads=[s8_t[b8], c_t], writes=[s8_t[b8]])
                P.act(S8[:, b8, 8:16], S8[:, b8, 0:8], AF.Exp, reads=[s8_t[b8]], writes=[s8_t[b8]])
                P.act(S8b[:, b8, :], S8[:, b8, 8:16], AF.Ln, reads=[s8_t[b8]], writes=[s8_t[b8]], bias=1.0)

            def st2(ii):
                i = npg[0] + ii
                b8 = i % NR
                P.mm(PF[CB][:, 0:8], UIN, S8b[:, b8, :], start=(ii == 0), stop=(ii == NPG - 1),
                     reads=[c_t, s8_t[b8]], writes=[pf_t[CB]])
                P.act(S8[:, b8, 24:32], PF[CB][:, 0:8], AF.Exp, reads=[pf_t[CB]], writes=[s8_t[b8]], scale=-1.0)
                P.tt(A8b[:, b8, :], S8[:, b8, 8:16], S8[:, b8, 24:32], ALU.mult, reads=[s8_t[b8]],
                     writes=[s8_t[b8]])
                if ii < NPG - 1:
                    P.mm(PF[CB][:, 0:8], LST, S8b[:, b8, :], start=False, stop=False, reads=[c_t, s8_t[b8]],
                         writes=[pf_t[CB]])
                P.mm(PF[AVB][0:8, :], A8b[:, b8, :], VBF[b8], start=(ii == 0), stop=(ii == NPG - 1),
                     reads=[s8_t[b8], vbf_t[b8]], writes=[pf_t[AVB]])

            for ii in range(NPG + 1):
                if ii < NPG:
                    st1(ii)
                if ii >= 1:
                    st2(ii - 1)
                yield
            npg[0] += NPG
            P.copy(RES[0:8, :], PF[AVB][0:8, :], reads=[pf_t[AVB]], writes=[res_t])
            for h in range(8):
                P.dma(otok[n:n + 1, h * 64:(h + 1) * 64], RES[h:h + 1, h * 64:(h + 1) * 64], reads=[res_t],
                      writes=[otok_t])
            yield

    def sample_back():
        G, g_t, TMP, tmp_t = SS["G"], SS["g_t"], SS["TMP"], SS["tmp_t"]
        HFa, HFb, HIa, HIb = SS["HFa"], SS["HFb"], SS["HIa"], SS["HIb"]
        hfa_t, hfb_t, hia_t, hib_t = SS["hfa_t"], SS["hfb_t"], SS["hia_t"], SS["hib_t"]
        otok, otok_t, hgnb, FKQ, fkq_t, SM, sm_t, scr_t = (SS["otok"], SS["otok_t"], SS["hgnb"], SS["FKQ"], SS["fkq_t"],
                                                          SS["SM"], SS["sm_t"], SS["scr_t"])
        for g in (3, 5, 6):
            P.dma(G[g][0:NSAMP, :], scr_d[:, g * 512:(g + 1) * 512], reads=[scr_t], writes=[g_t[g]])
        P.memset(yst[:], 0.0, writes=[yst_t], eng="dve")
        P.dma(yst[0:NSAMP, :], xs_d, writes=[yst_t])
        for n in range(NSAMP):
            VB, S0, SN, KV = hw["kd"], hsq, hrs, hattn[0]
            P.dma(VB[:], scr_d[n:n + 1, 5 * 512:5 * 512 + 256].partition_broadcast(128), reads=[scr_t],
                  writes=[hw_t["kd"]])
            P.dma(S0[:], sst_d[n], writes=[hsq_t])
            for half in range(2):
                r0 = half * 64
                for hp in range(2):
                    h = hp * 2 + half
                    ck_, cf_ = (1 * 2 + hp) * 4 + n, (0 * 2 + hp) * 4 + n
                    P.ts(KV[r0:r0 + 64, hp * 64:(hp + 1) * 64], VB[r0:r0 + 64, h * 64:(h + 1) * 64],
                         FKQ[r0:r0 + 64, ck_:ck_ + 1], None, ALU.mult, reads=[hw_t["kd"], fkq_t],
                         writes=[hattn_t[0]])
                    P.stt(SN[r0:r0 + 64, hp * 64:(hp + 1) * 64], S0[r0:r0 + 64, hp * 64:(hp + 1) * 64],
                          FKQ[r0:r0 + 64, cf_:cf_ + 1], KV[r0:r0 + 64, hp * 64:(hp + 1) * 64], ALU.mult, ALU.add,
                          reads=[hsq_t, fkq_t, hattn_t[0]], writes=[hrs_t])
            P.dma(hss_d[n], SN[:], reads=[hrs_t])
            if STAGE == 10.55:
                continue
            p_h = 5
            for hp in range(2):
                cq = (2 * 2 + hp) * 4 + n
                P.ts(KV[:, hp * 64:(hp + 1) * 64], SN[:, hp * 64:(hp + 1) * 64], FKQ[:, cq:cq + 1], None, ALU.mult,
                     reads=[hrs_t, fkq_t], writes=[hattn_t[0]])
            P.mm(PF[p_h][:, 0:128], OB, KV[:, :], reads=[c_t, hattn_t[0]], writes=[pf_t[p_h]])
            P.copy(hosb[:, 0:128], PF[p_h][:, 0:128], reads=[pf_t[p_h]], writes=[hosb_t])
            for half in range(2):
                for hp in range(2):
                    h = hp * 2 + half
                    P.dma(otok[n:n + 1, 512 + h * 64:512 + (h + 1) * 64],
                          hosb[half * 64:half * 64 + 1, hp * 64:(hp + 1) * 64], reads=[hosb_t], writes=[otok_t])
            if STAGE == 10.6:
                continue
            XQB, CMK, CMV, XT = hw["lf"], TMP[0], TMP[1], HFb
            P.dma(XQB[:], scr_d[n:n + 1, 6 * 512:6 * 512 + 256].partition_broadcast(128), reads=[scr_t],
                  writes=[hw_t["lf"]])
            P.dma(CMK.rearrange("p (b c) -> p b c", b=2), cmk_d[n].rearrange("(b m) c -> m b c", b=2),
                  writes=[tmp_t[0]])
            P.dma(CMV.rearrange("p (b c) -> p b c", b=2), cmv_d[n].rearrange("(b m) c -> m b c", b=2),
                  writes=[tmp_t[1]])
            for mb in range(2):
                P.tt(XT[:, mb * 256:(mb + 1) * 256], CMK[:, mb * 256:(mb + 1) * 256], XQB[:], ALU.mult,
                     reads=[tmp_t[0], hw_t["lf"]], writes=[hfb_t])
                P.reduce(SM[:, 8 + mb * 4:12 + mb * 4],
                         XT[:, mb * 256:(mb + 1) * 256].rearrange("p (h d) -> p h d", h=4), ALU.add,
                         reads=[hfb_t], writes=[sm_t])
            P.act(SM[:, 16:24], SM[:, 8:16], AF.Exp, reads=[sm_t], writes=[sm_t], scale=0.125)
            p_x, p_d = 0, 1
            for mb in range(2):
                P.mm(PF[p_x][0:4, 0:256], SM[:, 16 + mb * 4:20 + mb * 4], CMV[:, mb * 256:(mb + 1) * 256],
                     start=(mb == 0), stop=(mb == 1), reads=[sm_t, tmp_t[1]], writes=[pf_t[p_x]])
            for mb in range(2):
                P.mm(PF[p_d][0:4, 0:1], SM[:, 16 + mb * 4:20 + mb * 4], ONEC, start=(mb == 0), stop=(mb == 1),
                     reads=[sm_t, c_t], writes=[pf_t[p_d]])
            P.recip(SM[0:4, 24:25], PF[p_d][0:4, 0:1], reads=[pf_t[p_d]], writes=[sm_t])
            P.ts(hosb[0:4, :], PF[p_x][0:4, 0:256], SM[0:4, 24:25], None, ALU.mult, reads=[pf_t[p_x], sm_t],
                 writes=[hosb_t])
            for h in range(4):
                P.dma(otok[n:n + 1, 768 + h * 64:768 + (h + 1) * 64], hosb[h:h + 1, h * 64:(h + 1) * 64],
                      reads=[hosb_t], writes=[otok_t])
        if STAGE == 10.7:
            return
        HG = otok[:, 512:768]
        P.tt(hosb[:], HG, HG, ALU.mult, reads=[otok_t], writes=[hosb_t])
        P.reduce(SM[:, 0:4], hosb[:].rearrange("p (h d) -> p h d", h=4), ALU.add, reads=[hosb_t], writes=[sm_t])
        P.ts(SM[:, 0:4], SM[:, 0:4], 1.0 / 64, EPS, ALU.mult, ALU.add, reads=[sm_t], writes=[sm_t])
        P.act(SM[:, 4:8], SM[:, 0:4], AF.Ln, reads=[sm_t], writes=[sm_t])
        P.act(SM[:, 8:12], SM[:, 4:8], AF.Exp, reads=[sm_t], writes=[sm_t], scale=-0.5)
        for h in range(4):
            P.ts(otok[:, 512 + h * 64:512 + (h + 1) * 64], otok[:, 512 + h * 64:512 + (h + 1) * 64],
                 SM[:, 8 + h:9 + h], None, ALU.mult, reads=[otok_t, sm_t], writes=[otok_t])
        P.tt(HG, HG, hgnb, ALU.mult, reads=[otok_t, c_t], writes=[otok_t])
        for (gsrc, gtok, c0, c1, o0) in ((G[3], g_t[3], 0, 512, 0), (G[5], g_t[5], 256, 512, 512),
                                         (G[6], g_t[6], 256, 512, 768)):
            w_ = c1 - c0
            t_ = HIb[:, 0:w_]
            P.act(t_, gsrc[:, c0:c1], AF.Exp, reads=[gtok], writes=[hib_t], scale=-1.0)
            P.ts(t_, t_, 1.0, None, ALU.add, reads=[hib_t], writes=[hib_t])
            P.recip(t_, t_, reads=[hib_t], writes=[hib_t])
            P.tt(t_, t_, gsrc[:, c0:c1], ALU.mult, reads=[hib_t, gtok], writes=[hib_t])
            P.tt(otok[:, o0:o0 + w_], otok[:, o0:o0 + w_], t_, ALU.mult, reads=[otok_t, hib_t], writes=[otok_t])
        P.copy(xsb[0][:], otok[:], reads=[otok_t], writes=[xsb_t[0]])
        transpose_to(lambda: MIX[:, :, 0:128], 0, mix_t)
        wg0 = load_wgroup(7)
        wg1 = load_wgroup(8)
        st, stt_ = st4[0], st4_t[0]
        for half, wg in ((0, wg0), (1, wg1)):
            pi = pf_next()
            for kb in range(8):
                P.mm(PF[pi][:, :], MIX[:, kb, 0:128], WG[wg][:, kb, :], start=(kb == 0), stop=(kb == 7),
                     reads=[mix_t[kb], wg_t[wg]], writes=[pf_t[pi]])
            P.tt(G[half][:], PF[pi][:, :], yst[:, half * 512:(half + 1) * 512], ALU.add,
                 reads=[pf_t[pi], yst_t], writes=[g_t[half]])
            P.act(xsb[1][:, half * 512:(half + 1) * 512], G[half][:], AF.Square, reads=[g_t[half]],
                  writes=[xsb_t[1], stt_], accum_out=st[:, half:half + 1])
        P.tt(st[:, 0:1], st[:, 0:1], st[:, 1:2], ALU.add, reads=[stt_], writes=[stt_])
        P.ts(st[:, 1:2], st[:, 0:1], 1.0 / D, EPS, ALU.mult, ALU.add, reads=[stt_], writes=[stt_])
        P.act(st[:, 2:3], st[:, 1:2], AF.Ln, reads=[stt_], writes=[stt_])
        P.act(st[:, 3:4], st[:, 2:3], AF.Exp, reads=[stt_], writes=[stt_], scale=-0.5)
        for half in range(2):
            P.stt(G[half][:], G[half][:], st[:, 3:4], fgb[:, half * 512:(half + 1) * 512], ALU.mult, ALU.mult,
                  reads=[g_t[half], stt_, c_t], writes=[g_t[half]])
            P.dma(ys_d[:, half * 512:(half + 1) * 512], G[half][0:NSAMP, :], reads=[g_t[half]])

    gen = [None]

    def pump(k):
        if gen[0] is None:
            return
        for _ in range(k):
            try:
                next(gen[0])
            except StopIteration:
                gen[0] = None
                return

    if ENABLE_SAMPLE:
        sample_front()
        gen[0] = sample_stream()
        rr["npf"] = 4
        P.hook = lambda: pump(1)
        P.every = 6
    for ti in range(N_TILES):
        if ti == HALF or N_TILES < HALF:
            P.hook = None
            pump(10 ** 9)
            rr["npf"] = 6
        tile(ti)
    P.hook = None
    pump(10 ** 9)
    rr["npf"] = 6
    P.dma(hgs_d, Sst[:], reads=[s_t])


    if ENABLE_SAMPLE:
        sample_back()

    P.emit()
    return nc


_NC_CACHE = {}


def _consts():
    s = np.arange(128)[:, None]
    t = np.arange(128)[None, :]
    same = (s // 64) == (t // 64)
    ident = np.eye(128, dtype=np.float32)
    tri = ((s <= t) & same).astype(np.float32)
    su = ((s > t) & same).astype(np.float32)
    ob = same.astype(np.float32)
    cf = np.concatenate([ident, tri, su, ob, np.arange(128, dtype=np.float32)[:, None], np.ones((128, 1), np.float32)],
                        axis=1).astype(np.float32)
    uin = (s >= t).astype(np.float32)
    tq = np.arange(512)[None, :]
    msk = [((d * 128 + s) < tq).astype(np.float32) for d in range(4)]
    lst = (s < t).astype(np.float32)
    cb = np.concatenate([ident, uin, lst] + msk, axis=1).astype(ml_dtypes.bfloat16)
    return cf, cb


def kernel(x_prompt, x_sample, mem_prompt, cache_k, cache_v, page_table, state_hgrn, cache_mem_k, cache_mem_v,
           norm_gain, w_in, sb_bias, hg_lb_logits, hg_norm_gain, mem_norm_gain, w_mem_kv, w_out, final_norm_gain):
    if "nc" not in _NC_CACHE:
        _NC_CACHE["nc"] = build_program()
    nc = _NC_CACHE["nc"]
    f = lambda a: np.ascontiguousarray(np.asarray(a, dtype=np.float32))
    cf, cb = _consts()
    common = {
        "w_in": f(w_in[0]), "w_out": f(w_out[0]), "w_mem": f(w_mem_kv[0]),
        "gin": f(np.asarray(norm_gain[0]).reshape(8, 128).T),
        "gmem": f(np.asarray(mem_norm_gain[0]).reshape(8, 128).T),
        "fg": f(np.asarray(final_norm_gain).reshape(1, D)),
        "hgn": f(np.asarray(hg_norm_gain[0]).reshape(2, 128).T),
        "sbb": f(np.asarray(sb_bias[0]).reshape(1, 8)),
        "lbl": f(np.asarray(hg_lb_logits).reshape(1, 512)),
        "cf": cf, "cb": cb,
    }
    if ENABLE_SAMPLE:
        ckv = np.concatenate([f(cache_k[0]).reshape(POOL_PAGES * 128, 512),
                              f(cache_v[0]).reshape(POOL_PAGES * 128, 512)], axis=1)
    in_maps = []
    for c in range(8):
        b = c // 2
        m = dict(common)
        r = c % 2
        xb = f(x_prompt[b])
        m["x"] = np.ascontiguousarray(xb[:S // 2]) if r == 1 else np.zeros((S // 2, D), np.float32)
        m["xo"] = np.ascontiguousarray(xb[r * (S // 2):(r + 1) * (S // 2)])
        m["kbias"] = np.full((128, 1), 0.0 if r == 1 else -30000.0, np.float32)
        m["mem"] = f(mem_prompt[b])
        if ENABLE_SAMPLE:
            sl = slice(NSAMP * c, NSAMP * (c + 1))
            m["xs"] = f(x_sample[sl, 0])
            m["pt"] = np.ascontiguousarray(np.asarray(page_table[sl], dtype=np.int32).reshape(1, NSAMP * NPAGE))
            m["ckv"] = ckv
            st = f(state_hgrn[0, sl]).reshape(NSAMP, 2, 2, 64, 64).transpose(0, 2, 3, 1, 4).reshape(NSAMP, 128, 128)
            m["sst"] = np.ascontiguousarray(st)
            m["cmk"] = f(cache_mem_k[0, sl]).reshape(NSAMP, 256, 256)
            m["cmv"] = f(cache_mem_v[0, sl]).reshape(NSAMP, 256, 256)
            m["hgr"] = f(np.asarray(hg_norm_gain[0]).reshape(1, 256))
        in_maps.append(m)
    res = run_bass_kernel_spmd(nc, in_maps[:NCORES], core_ids=list(range(NCORES))).results
    res = list(res) + [res[0], res[1 % NCORES]] * ((8 - NCORES) // 2 + 1)

    def unstate(a):
        return a.reshape(2, 64, 2, 64).transpose(2, 0, 1, 3).reshape(4, 64, 64)

    cat = lambda b, n_: np.concatenate([res[2 * b][n_], res[2 * b + 1][n_]], axis=0)
    y_prompt = np.stack([cat(b, "y") for b in range(4)]).astype(np.float32)
    k_prompt = np.stack([cat(b, "k") for b in range(4)]).reshape(1, 4, S, 8, 64).astype(np.float32)
    v_prompt = np.stack([cat(b, "v") for b in range(4)]).reshape(1, 4, S, 8, 64).astype(np.float32)
    hgrn_prompt = np.stack([unstate(res[2 * b + 1]["hgs"]) for b in range(4)])[None].astype(np.float32)
    mem_k = np.stack([res[2 * b]["mk"] for b in range(4)]).reshape(1, 4, 256, 4, 64).astype(np.float32)
    mem_v = np.stack([res[2 * b]["mv"] for b in range(4)]).reshape(1, 4, 256, 4, 64).astype(np.float32)
    if ENABLE_SAMPLE:
        y_sample = np.concatenate([res[c]["ys"] for c in range(8)]).reshape(32, 1, D).astype(np.float32)
        k_sample = np.concatenate([res[c]["ks"] for c in range(8)]).reshape(1, 32, 1, 8, 64).astype(np.float32)
        v_sample = np.concatenate([res[c]["vs"] for c in range(8)]).reshape(1, 32, 1, 8, 64).astype(np.float32)
        hgrn_sample = np.concatenate([np.stack([unstate(res[c]["hss"][i]) for i in range(NSAMP)])
                                      for c in range(8)])[None].astype(np.float32)
    else:
        y_sample = np.zeros((32, 1, D), np.float32)
        k_sample = np.zeros((1, 32, 1, 8, 64), np.float32)
        v_sample = np.zeros((1, 32, 1, 8, 64), np.float32)
        hgrn_sample = np.zeros((1, 32, 4, 64, 64), np.float32)
    return (y_prompt, y_sample, k_prompt, v_prompt, hgrn_prompt, mem_k, mem_v, k_sample, v_sample, hgrn_sample)
```

```python
import contextlib
import numpy as np
import ml_dtypes
import concourse.bass as bass
import concourse.mybir as mybir
from concourse.bass_utils import run_bass_kernel_spmd

F32 = mybir.dt.float32
BF16 = mybir.dt.bfloat16
I32 = mybir.dt.int32
AF = mybir.ActivationFunctionType
ALU = mybir.AluOpType
AX = mybir.AxisListType

D = 1024
S = 4096
NT = S // 512
NB = S // 128
DIN = 3584
EPS = 1e-6
NSAMP = 4
NPAGE = 64
N_DMA_SEM = 12
RING = 3 * N_DMA_SEM

ENABLE_SAMPLE = True
N_TILES = NT
STAGE = 99
NCORES = 8
POOL_PAGES = 2560
NO_SCRATCH_READ = False
DBG_PRE = False
DBG_KBS = None


class Tok:
    __slots__ = ("w", "r", "ps", "old")

    def __init__(self, ps=False, old=None):
        self.w = None
        self.r = []
        self.ps = ps
        self.old = old


class Prog:
    def __init__(self, nc, es):
        self.nc = nc
        self.es = es
        self.ops = []
        self.hook = None
        self.every = 6
        self._n = 0
        self._in_hook = False
        self.eng = {"pe": nc.tensor, "act": nc.scalar, "dve": nc.vector, "pool": nc.gpsimd, "sp": nc.sync}

    def op(self, eng, fn, reads=(), writes=(), dma=False):
        idx = len(self.ops)
        deps = set()
        for t in reads:
            if t.w is not None:
                deps.add(t.w)
            if t.ps:
                deps.update(r for r in t.r if self.ops[r][0] != eng)
        for t in writes:
            if t.w is not None:
                deps.add(t.w)
            deps.update(t.r)
            if t.old:
                for o in t.old:
                    if o.w is not None:
                        deps.add(o.w)
                    deps.update(o.r)
                t.old = None
        deps.discard(idx)
        self.ops.append([eng, fn, deps, dma])
        for t in reads:
            t.r.append(idx)
        for t in writes:
            t.w = idx
            t.r = []
        if self.hook is not None and not self._in_hook:
            self._n += 1
            if self._n % self.every == 0:
                self._in_hook = True
                try:
                    self.hook()
                finally:
                    self._in_hook = False
        return idx

    def dma(self, out, in_, reads=(), writes=(), q="sp", **kw):
        e = self.eng[q]
        return self.op(q, lambda: e.dma_start(out=out, in_=in_, **kw), reads, writes, dma=True)

    def gather(self, out, in_, idx_ap, reads=(), writes=()):
        g = self.nc.gpsimd
        return self.op("pool", lambda: g.indirect_dma_start(
            out=out, out_offset=None, in_=in_, in_offset=bass.IndirectOffsetOnAxis(ap=idx_ap, axis=0)),
            reads, writes, dma=True)

    def act(self, out, in_, func, reads=(), writes=(), **kw):
        a = self.nc.scalar
        return self.op("act", lambda: a.activation(out=out, in_=in_, func=func, **kw), reads, writes)

    def tt(self, out, in0, in1, op, reads=(), writes=(), eng="dve"):
        e = self.eng[eng]
        return self.op(eng, lambda: e.tensor_tensor(out=out, in0=in0, in1=in1, op=op), reads, writes)

    def ts(self, out, in0, s1, s2, op0, op1=None, reads=(), writes=(), eng="dve"):
        e = self.eng[eng]
        if op1 is None:
            return self.op(eng, lambda: e.tensor_scalar(out=out, in0=in0, scalar1=s1, scalar2=None, op0=op0),
                           reads, writes)
        return self.op(eng, lambda: e.tensor_scalar(out=out, in0=in0, scalar1=s1, scalar2=s2, op0=op0, op1=op1),
                       reads, writes)

    def stt(self, out, in0, scalar, in1, op0, op1, reads=(), writes=()):
        e = self.nc.vector
        return self.op("dve", lambda: e.scalar_tensor_tensor(out=out, in0=in0, scalar=scalar, in1=in1,
                                                             op0=op0, op1=op1), reads, writes)

    def copy(self, out, in_, reads=(), writes=(), eng="dve"):
        e = self.eng[eng]
        if eng == "act":
            return self.op(eng, lambda: e.activation(out=out, in_=in_, func=AF.Copy), reads, writes)
        return self.op(eng, lambda: e.tensor_copy(out=out, in_=in_), reads, writes)

    def recip(self, out, in_, reads=(), writes=()):
        e = self.nc.vector
        return self.op("dve", lambda: e.reciprocal(out=out, in_=in_), reads, writes)

    def reduce(self, out, in_, op, reads=(), writes=()):
        e = self.nc.vector
        return self.op("dve", lambda: e.tensor_reduce(out=out, in_=in_, axis=AX.X, op=op), reads, writes)

    def scan(self, out, d0, d1, reads=(), writes=()):
        e = self.nc.vector
        return self.op("dve", lambda: e.tensor_tensor_scan(out=out, data0=d0, data1=d1, initial=0.0,
                                                           op0=ALU.mult, op1=ALU.add), reads, writes)

    def memset(self, ap, val, writes=(), eng="pool"):
        e = self.eng[eng]
        return self.op(eng, lambda: e.memset(ap, val), (), writes)

    def mm(self, out, lhsT, rhs, start=True, stop=True, reads=(), writes=(), tp=None):
        t = self.nc.tensor
        if tp is None:
            return self.op("pe", lambda: t.matmul(out, lhsT=lhsT, rhs=rhs, start=start, stop=stop), reads, writes)
        return self.op("pe", lambda: t.matmul(out, lhsT=lhsT, rhs=rhs, start=start, stop=stop, tile_position=tp),
                       reads, writes)

    def tr(self, out, in_, ident, reads=(), writes=()):
        t = self.nc.tensor
        return self.op("pe", lambda: t.transpose(out=out, in_=in_, identity=ident), reads, writes)

    def emit(self):
        nc, es = self.nc, self.es
        ops = self.ops
        n = len(ops)
        comp = ("pe", "act", "dve", "pool")
        pos = [0] * n
        cnt = {e: 0 for e in comp}
        qbase = {"sp": 0, "pool": N_DMA_SEM, "act": 2 * N_DMA_SEM}
        dq = {q: [] for q in qbase}
        for i, (eng, fn, deps, dma) in enumerate(ops):
            if dma:
                k = len(dq[eng])
                pos[i] = (k // N_DMA_SEM) * RING + qbase[eng] + (k % N_DMA_SEM)
                dq[eng].append(i)
            else:
                pos[i] = cnt[eng]
                cnt[eng] += 1
        dma_ops = [i for i in range(n) if ops[i][3]]
        for q, lst in dq.items():
            for k, i in enumerate(lst):
                if k >= N_DMA_SEM:
                    ops[i][2].add(lst[k - N_DMA_SEM])
        waited = {}
        marked = [False] * n
        waits = [None] * n
        for i, (eng, fn, deps, dma) in enumerate(ops):
            wl = []
            for d in sorted(deps):
                deng, _, _, ddma = ops[d]
                if ddma:
                    key = (eng, "dma", pos[d] % RING)
                    val = pos[d] // RING
                else:
                    if deng == eng:
                        if eng == "pe":
                            continue
                        if eng != "pool" and pos[i] - pos[d] > 2:
                            continue
                    key = (eng, deng)
                    val = pos[d]
                if waited.get(key, -1) >= val:
                    continue
                waited[key] = val
                wl.append(d)
                marked[d] = True
            waits[i] = wl
        sem = {e: es.enter_context(nc.semaphore("s_" + e)) for e in comp}
        dsem = [es.enter_context(nc.semaphore("s_dma%d" % k)) for k in range(RING)]
        val = [0] * n
        c2 = {e: 0 for e in comp}
        for i, (eng, fn, deps, dma) in enumerate(ops):
            if dma:
                val[i] = 16 * (pos[i] // RING + 1)
            elif marked[i]:
                c2[eng] += 1
                val[i] = c2[eng]
        for i, (eng, fn, deps, dma) in enumerate(ops):
            e = self.eng[eng]
            for d in waits[i]:
                deng, _, _, ddma = ops[d]
                if ddma:
                    e.wait_ge(dsem[pos[d] % RING], val[d])
                else:
                    e.wait_ge(sem[deng], val[d])
            ins = fn()
            if dma:
                ins.then_inc(dsem[pos[i] % RING], 16)
            elif marked[i]:
                ins.then_inc(sem[eng], 1)
        last = {}
        for i in dma_ops:
            last[pos[i] % RING] = val[i]
        for k, v in last.items():
            nc.sync.wait_ge(dsem[k], v)
        for e in comp:
            if c2[e] > 0:
                nc.sync.wait_ge(sem[e], c2[e])


def build_program():
    nc = bass.Bass("TRN2", target_bir_lowering=False)
    es = contextlib.ExitStack()
    P = Prog(nc, es)

    def din(name, shape, dt=F32):
        return nc.dram_tensor(name, list(shape), dt, kind="ExternalInput").ap()

    def dout(name, shape, dt=F32):
        return nc.dram_tensor(name, list(shape), dt, kind="ExternalOutput").ap()

    x_d = din("x", [S // 2, D])
    xo_d = din("xo", [S // 2, D])
    kb_d = din("kbias", [128, 1])
    mem_d = din("mem", [256, D])
    w_in_d = din("w_in", [D, DIN])
    w_out_d = din("w_out", [D, D])
    w_mem_d = din("w_mem", [D, 512])
    gin_d = din("gin", [128, 8])
    gmem_d = din("gmem", [128, 8])
    fg_d = din("fg", [1, D])
    hgn_d = din("hgn", [128, 2])
    sbb_d = din("sbb", [1, 8])
    lbl_d = din("lbl", [1, 512])
    cf_d = din("cf", [128, 514])
    cb_d = din("cb", [128, 384 + 2048], BF16)
    y_d = dout("y", [S // 2, D])
    k_d = dout("k", [S // 2, 512])
    v_d = dout("v", [S // 2, 512])
    hgs_d = dout("hgs", [128, 128])
    mk_d = dout("mk", [256, 256])
    mv_d = dout("mv", [256, 256])
    wsc_d = nc.dram_tensor("wsc", [9, 128, 4096], BF16, kind="Internal").ap()
    if ENABLE_SAMPLE:
        xs_d = din("xs", [NSAMP, D])
        pt_d = din("pt", [1, NSAMP * NPAGE], I32)
        ckv_d = din("ckv", [POOL_PAGES * 128, 1024])
        sst_d = din("sst", [NSAMP, 128, 128])
        cmk_d = din("cmk", [NSAMP, 256, 256])
        cmv_d = din("cmv", [NSAMP, 256, 256])
        ys_d = dout("ys", [NSAMP, D])
        ks_d = dout("ks", [NSAMP, 512])
        vs_d = dout("vs", [NSAMP, 512])
        hss_d = dout("hss", [NSAMP, 128, 128])
        hgr_d = din("hgr", [1, 256])
        scr_d = nc.dram_tensor("scr", [NSAMP, DIN], F32, kind="Internal").ap()

    cnt = [0]

    def sb(shape, dt=F32, name=None):
        cnt[0] += 1
        return es.enter_context(nc.sbuf_tensor("sb_" + (name or ("t%d" % cnt[0])), list(shape), dt))

    def psum(shape, dt=F32):
        cnt[0] += 1
        return es.enter_context(nc.psum_tensor("p%d" % cnt[0], list(shape), dt))

    def T():
        return Tok()

    KT = sb([128, 4, S], BF16, "KT")
    kt_t = [[T() for _ in range(NB)] for _ in range(4)]
    VA = sb([128, NB, 512], BF16, "VA")
    va_t = [T() for _ in range(NB)]
    MKT = sb([128, 2, 256], BF16, "MKT")
    MV = sb([128, 2, 256], BF16, "MV")
    mk_t = T()
    cf = sb([128, 514], F32, "cf")
    cb = sb([128, 384 + 2048], BF16, "cb")
    c_t = T()
    ident_f, TRI, SU, OB = cf[:, 0:128], cf[:, 128:256], cf[:, 256:384], cf[:, 384:512]
    IOTA, ONEC = cf[:, 512:513], cf[:, 513:514]
    ident_b, UIN, LST = cb[:, 0:128], cb[:, 128:256], cb[:, 256:384]
    MSK = [cb[:, 384 + 512 * d: 384 + 512 * (d + 1)] for d in range(4)]
    ONESB = sb([128, 128], BF16, "onesb")
    gin = sb([128, 8], F32, "gin")
    gmem = sb([128, 8], F32, "gmem")
    fgb = sb([128, D], F32, "fgb")
    hgn = sb([128, 2], F32, "hgn")
    sbb = sb([128, 8], F32, "sbb")
    sbbp = sb([128, 8], F32, "sbbp")
    kbias = sb([128, 1], F32, "kbias")
    lbl = sb([128, 512], F32, "lbl")
    lb = sb([128, 256], F32, "lb")
    oml = sb([128, 256], F32, "oml")
    Sst = sb([128, 128], F32, "Sst")
    s_t = T()

    PF = [psum([128, 512], F32) for _ in range(6)]
    pf_t = [Tok(ps=True) for _ in range(6)]
    PB = [psum([128, 1024], BF16) for _ in range(2)]
    pb_t = [Tok(ps=True) for _ in range(2)]
    PBf = [PB[i_][:, :].bitcast(F32) for i_ in range(2)]

    class Deferred:
        def __init__(self):
            self.q = []

        def __getattr__(self, name):
            return lambda *a, **k: self.q.append((name, a, k))

        def pump(self, n_):
            for _ in range(min(n_, len(self.q))):
                name, a, k = self.q.pop(0)
                getattr(P, name)(*a, **k)


    P.dma(cf[:], cf_d, writes=[c_t])
    P.dma(cb[:], cb_d, writes=[c_t])
    P.dma(gin[:], gin_d, writes=[c_t])
    P.dma(gmem[:], gmem_d, writes=[c_t])
    P.dma(fgb[:], fg_d.partition_broadcast(128), writes=[c_t])
    P.dma(hgn[:], hgn_d, writes=[c_t])
    P.dma(sbb[:], sbb_d.partition_broadcast(128), writes=[c_t])
    P.dma(lbl[:], lbl_d.partition_broadcast(128), writes=[c_t])
    P.dma(kbias[:], kb_d, writes=[c_t])
    P.ts(sbbp[:], sbb[:], kbias[:, 0:1], None, ALU.add, reads=[c_t], writes=[c_t])
    P.memset(ONESB[:], 1.0, writes=[c_t], eng="dve")
    P.memset(Sst[:], 0.0, writes=[s_t], eng="dve")
    P.tt(lb[:], lbl[:, 256:512], lbl[:, 0:256], ALU.subtract, reads=[c_t], writes=[c_t])
    P.act(lb[:], lb[:], AF.Exp, reads=[c_t], writes=[c_t])
    P.ts(lb[:], lb[:], 1.0, None, ALU.add, reads=[c_t], writes=[c_t])
    P.recip(lb[:], lb[:], reads=[c_t], writes=[c_t])
    P.ts(oml[:], lb[:], -1.0, 1.0, ALU.mult, ALU.add, reads=[c_t], writes=[c_t])

    if STAGE == 0:
        P.emit()
        return nc
    xst = [sb([128, D], F32) for _ in range(2)]
    xst_t = [T() for _ in range(2)]
    xsb = [sb([128, D], BF16) for _ in range(2)]
    xsb_t = [T() for _ in range(2)]
    st4 = [sb([128, 4], F32) for _ in range(2)]
    st4_t = [T() for _ in range(2)]
    wst = [sb([128, 512], F32) for _ in range(2)]
    wst_t = [T() for _ in range(2)]
    WG = [sb([128, 8, 512], BF16) for _ in range(2)]
    wg_t = [T() for _ in range(2)]
    wsc_t = [T() for _ in range(9)]
    rr = {"x": 0, "w": 0, "wg": 0, "pf": 0, "pb": 0, "npf": 6}

    def rmsnorm_to_bf(src_ap, reads, which):
        st = st4[which]
        stt_ = st4_t[which]
        P.act(xsb[which][:], src_ap, AF.Square, reads=reads, writes=[xsb_t[which], stt_], accum_out=st[:, 0:1])
        P.ts(st[:, 1:2], st[:, 0:1], 1.0 / D, EPS, ALU.mult, ALU.add, reads=[stt_], writes=[stt_])
        P.act(st[:, 2:3], st[:, 1:2], AF.Ln, reads=[stt_], writes=[stt_])
        P.act(st[:, 3:4], st[:, 2:3], AF.Exp, reads=[stt_], writes=[stt_], scale=-0.5)
        P.act(xsb[which][:], src_ap, AF.Copy, reads=list(reads) + [stt_], writes=[xsb_t[which]], scale=st[:, 3:4])

    def transpose_to(dst_ap_fn, which, dst_toks):
        pb = rr["pb"] % 2
        rr["pb"] += 1
        for kb in range(8):
            P.tr(PB[pb][:, kb * 128:(kb + 1) * 128], xsb[which][:, kb * 128:(kb + 1) * 128], ident_b,
                 reads=[xsb_t[which], c_t], writes=[pb_t[pb]])
        P.copy(dst_ap_fn(), PB[pb][:, :].rearrange("p (k t) -> p k t", k=8), reads=[pb_t[pb]], writes=dst_toks)

    for g in range(9):
        wgi = rr["wg"] % 2
        rr["wg"] += 1
        for kb in range(8):
            wi = rr["w"] % 2
            rr["w"] += 1
            if g < 7:
                src = w_in_d[kb * 128:(kb + 1) * 128, g * 512:(g + 1) * 512]
            else:
                src = w_out_d[kb * 128:(kb + 1) * 128, (g - 7) * 512:(g - 6) * 512]
            P.dma(wst[wi][:], src, writes=[wst_t[wi]])
            if g < 7:
                if kb % 2 == 0:
                    P.ts(WG[wgi][:, kb, :], wst[wi][:], gin[:, kb:kb + 1], None, ALU.mult,
                         reads=[wst_t[wi], c_t], writes=[wg_t[wgi]])
                else:
                    P.act(WG[wgi][:, kb, :], wst[wi][:], AF.Copy, reads=[wst_t[wi], c_t], writes=[wg_t[wgi]],
                          scale=gin[:, kb:kb + 1])
            else:
                P.copy(WG[wgi][:, kb, :], wst[wi][:], reads=[wst_t[wi]], writes=[wg_t[wgi]],
                       eng="dve" if kb % 2 == 0 else "pool")
        P.dma(wsc_d[g], WG[wgi][:, :, :].rearrange("p k c -> p (k c)"), reads=[wg_t[wgi]], writes=[wsc_t[g]])

    if STAGE == 1:
        P.emit()
        return nc

    def load_wgroup(g):
        wgi = rr["wg"] % 2
        rr["wg"] += 1
        if not NO_SCRATCH_READ:
            P.dma(WG[wgi][:, :, :].rearrange("p k c -> p (k c)"), wsc_d[g], reads=[wsc_t[g]], writes=[wg_t[wgi]])
        return wgi

    xnT = sb([128, 8, 512], BF16, "xnT")
    xnT_t = [T() for _ in range(4)]
    memT = [xnT[:, :, mb_ * 128:(mb_ + 1) * 128] for mb_ in range(2)]
    memT_t = [xnT_t[0], xnT_t[1]]
    for mb in range(2):
        xi = rr["x"] % 2
        rr["x"] += 1
        P.dma(xst[xi][:], mem_d[mb * 128:(mb + 1) * 128, :], writes=[xst_t[xi]])
        rmsnorm_to_bf(xst[xi][:], [xst_t[xi]], xi)
        transpose_to(lambda mb=mb: memT[mb], xi, [memT_t[mb]])
    wmb = [sb([128, 512], BF16) for _ in range(2)]
    wmb_t = [T() for _ in range(2)]
    for kb in range(8):
        wi = rr["w"] % 2
        rr["w"] += 1
        P.dma(wst[wi][:], w_mem_d[kb * 128:(kb + 1) * 128, :], writes=[wst_t[wi]])
        P.ts(wmb[kb % 2][:], wst[wi][:], gmem[:, kb:kb + 1], None, ALU.mult, reads=[wst_t[wi], c_t],
             writes=[wmb_t[kb % 2]])
        for mb in range(2):
            P.mm(PF[mb][:, :], xnT[:, kb, mb * 128:(mb + 1) * 128], wmb[kb % 2][:], start=(kb == 0), stop=(kb == 7),
                 reads=[memT_t[mb], wmb_t[kb % 2]], writes=[pf_t[mb]])
    kvst = [sb([128, 512], F32) for _ in range(2)]
    kvst_t = [T() for _ in range(2)]
    kbf = [sb([128, 512], BF16) for _ in range(2)]
    kbf_t = [T() for _ in range(2)]
    for mb in range(2):
        P.copy(kvst[mb][:], PF[mb][:, :], reads=[pf_t[mb]], writes=[kvst_t[mb]], eng="act" if mb else "dve")
        P.dma(mk_d[mb * 128:(mb + 1) * 128, :], kvst[mb][:, 0:256], reads=[kvst_t[mb]])
        P.dma(mv_d[mb * 128:(mb + 1) * 128, :], kvst[mb][:, 256:512], reads=[kvst_t[mb]])
        P.copy(kbf[mb][:, 0:256], kvst[mb][:, 0:256], reads=[kvst_t[mb]], writes=[kbf_t[mb]])
        P.copy(MV[:, mb, :], kvst[mb][:, 256:512], reads=[kvst_t[mb]], writes=[mk_t])
        pb = rr["pb"] % 2
        rr["pb"] += 1
        for hp in range(2):
            P.tr(PB[pb][:, hp * 128:(hp + 1) * 128], kbf[mb][:, hp * 128:(hp + 1) * 128], ident_b,
                 reads=[kbf_t[mb], c_t], writes=[pb_t[pb]])
        P.copy(MKT[:, :, mb * 128:(mb + 1) * 128], PB[pb][:, 0:256].rearrange("p (k t) -> p k t", k=2),
               reads=[pb_t[pb]], writes=[mk_t])

    if STAGE == 2:
        P.emit()
        return nc
    QT = sb([128, 4, 512], BF16, "QT")
    qt_t = [T() for _ in range(4)]
    SG = sb([128, 8, 512], BF16, "SG")
    sg_t = [T() for _ in range(8)]
    HQT = sb([128, 2, 512], F32, "HQT")
    hq_t = [T() for _ in range(2)]
    XQT = sb([128, 2, 512], BF16, "XQT")
    xq_t = [T() for _ in range(2)]
    HF = sb([128, 4, 256], F32, "HF")
    HI = sb([128, 4, 256], F32, "HI")
    hf_t = [T() for _ in range(4)]
    hi_t = [T() for _ in range(4)]
    MIX = sb([128, 8, 512], BF16, "MIX")
    mix_t = [T() for _ in range(8)]
    gtmp = [sb([128, 512], F32) for _ in range(2)]
    gtmp_t = [T() for _ in range(2)]
    EB = [sb([128, 512], F32) for _ in range(4)]
    eb_t = [T() for _ in range(4)]
    SPB = wmb + [sb([128, 512], BF16) for _ in range(2)]
    spb_t = wmb_t + [T() for _ in range(2)]
    WB = [sb([128, 512], BF16) for _ in range(2)]
    wb_t = [T() for _ in range(2)]
    AB = [sb([128, 512], BF16) for _ in range(4)]
    ab_t = [T() for _ in range(4)]
    KTf = KT[:, 0, :].bitcast(F32)
    SPACC = [KTf[:, 0:512], KTf[:, 512:1024]]
    spacc_t = [Tok(old=[kt_t[0][kb_] for kb_ in range(NB)]) for _ in range(2)]
    hw = {n_: sb([128, 256], F32, "hw_" + n_) for n_ in ("a", "f", "lf", "k", "kd")}
    hw_t = {n_: T() for n_ in hw}
    hx = {n_: sb([128, 2, 128], F32, "hx_" + n_) for n_ in ("ep", "en", "qe", "ke")}
    hx_t = {n_: T() for n_ in hx}
    hattn = [sb([128, 128], F32) for _ in range(2)]
    hattn_t = [T() for _ in range(2)]
    hdec = sb([128, 4], F32, "hdec")
    hdec_t = T()
    hosb = sb([128, 256], F32, "hosb")
    hosb_t = T()
    hsq = sb([128, 128], F32, "hsq")
    hsq_t = T()
    hrs = sb([128, 128], F32, "hrs")
    hrs_t = T()
    yst = sb([128, D], F32, "yst")
    yst_t = T()

    def pf_next():
        i = rr["pf"] % rr["npf"]
        rr["pf"] += 1
        return i

    def silu_from_psum(pi, dst_ap, dst_tok):
        gi = rr["x"] % 2
        rr["x"] += 1
        P.act(gtmp[gi][:], PF[pi][:, :], AF.Exp, reads=[pf_t[pi]], writes=[gtmp_t[gi]], scale=-1.0)
        P.ts(gtmp[gi][:], gtmp[gi][:], 1.0, None, ALU.add, reads=[gtmp_t[gi]], writes=[gtmp_t[gi]])
        P.recip(gtmp[gi][:], gtmp[gi][:], reads=[gtmp_t[gi]], writes=[gtmp_t[gi]])
        P.tt(dst_ap, PF[pi][:, :], gtmp[gi][:], ALU.mult, reads=[pf_t[pi], gtmp_t[gi]], writes=[dst_tok])

    def feat_proj(wgi, cbs, evac):
        for cbi in cbs:
            pi = pf_next()
            for kb in range(8):
                P.mm(PF[pi][:, :], WG[wgi][:, kb, cbi * 128:(cbi + 1) * 128], xnT[:, kb, :], start=(kb == 0),
                     stop=(kb == 7), reads=[wg_t[wgi]] + xnT_t, writes=[pf_t[pi]])
            evac(cbi, pi)

    def tok_proj(wgi, c0, c1, blk):
        pi = pf_next()
        for kb in range(8):
            P.mm(PF[pi][:, 0:c1 - c0], xnT[:, kb, blk * 128:(blk + 1) * 128], WG[wgi][:, kb, c0:c1],
                 start=(kb == 0), stop=(kb == 7), reads=[wg_t[wgi], xnT_t[blk]], writes=[pf_t[pi]])
        return pi

    def hgrn_block(blk, state_only=False, HP=None, defer=False):
        if HP is None:
            HP = P
        if defer:
            HB = {0: PBf[0], 1: PBf[0], 2: PBf[0], 5: PBf[0], 3: PF[5], 4: PBf[1]}
            hb_t = {0: pb_t[0], 1: pb_t[0], 2: pb_t[0], 5: pb_t[0], 3: pf_t[5], 4: pb_t[1]}
        else:
            HB, hb_t = PF, pf_t

        c0 = blk * 128
        tmpb = [0, 1, 2] if state_only else [0, 1, 2, 5]

        def tmp_next():
            i = tmpb[rr["pf"] % len(tmpb)]
            rr["pf"] += 1
            return i
        a, f, lf, k, kd = hw["a"], hw["f"], hw["lf"], hw["k"], hw["kd"]
        HP.act(a[:], HF[:, blk, :], AF.Exp, reads=[hf_t[blk]], writes=[hw_t["a"]], scale=-1.0)
        HP.ts(a[:], a[:], 1.0, None, ALU.add, reads=[hw_t["a"]], writes=[hw_t["a"]])
        HP.recip(a[:], a[:], reads=[hw_t["a"]], writes=[hw_t["a"]])
        HP.tt(f[:], a[:], oml[:], ALU.mult, reads=[hw_t["a"], c_t], writes=[hw_t["f"]])
        HP.tt(f[:], f[:], lb[:], ALU.add, reads=[hw_t["f"], c_t], writes=[hw_t["f"]])
        HP.act(lf[:], f[:], AF.Ln, reads=[hw_t["f"]], writes=[hw_t["lf"]])
        HP.ts(k[:], f[:], -1.0, 1.0, ALU.mult, ALU.add, reads=[hw_t["f"]], writes=[hw_t["k"]])
        if STAGE == 3.1:
            return
        p_rev = tmp_next()
        HP.mm(HB[p_rev][:, 0:256], SU, lf[:], reads=[c_t, hw_t["lf"]], writes=[hb_t[p_rev]])
        HP.act(kd[:], HB[p_rev][:, 0:256], AF.Exp, reads=[hb_t[p_rev]], writes=[hw_t["kd"]])
        HP.tt(kd[:], kd[:], k[:], ALU.mult, reads=[hw_t["kd"], hw_t["k"]], writes=[hw_t["kd"]])
        if STAGE == 3.2:
            return
        p_bc = tmp_next()
        for hp in range(2):
            HP.mm(HB[p_bc][:, hp * 128:(hp + 1) * 128], lf[:, hp * 128:(hp + 1) * 128], TRI,
                 reads=[hw_t["lf"], c_t], writes=[hb_t[p_bc]])
        HP.act(hx["ep"][:, :, :], HB[p_bc][:, 0:256].rearrange("p (k t) -> p k t", k=2), AF.Exp,
              reads=[hb_t[p_bc]], writes=[hx_t["ep"]])
        if not state_only:
            HP.act(hx["en"][:, :, :], HB[p_bc][:, 0:256].rearrange("p (k t) -> p k t", k=2), AF.Exp,
                  reads=[hb_t[p_bc]], writes=[hx_t["en"]], scale=-1.0)
            HP.tt(hx["qe"][:, :, :], hx["ep"][:, :, :], HQT[:, :, c0:c0 + 128], ALU.mult,
                 reads=[hx_t["ep"]] + hq_t, writes=[hx_t["qe"]])
            if STAGE == 3.3:
                return
            p_kt = tmp_next()
            for hp in range(2):
                HP.tr(HB[p_kt][:, hp * 128:(hp + 1) * 128], k[:, hp * 128:(hp + 1) * 128], ident_f,
                     reads=[hw_t["k"], c_t], writes=[hb_t[p_kt]])
            HP.tt(hx["ke"][:, :, :], HB[p_kt][:, 0:256].rearrange("p (k t) -> p k t", k=2), hx["en"][:, :, :], ALU.mult,
                 reads=[hb_t[p_kt], hx_t["en"]], writes=[hx_t["ke"]])
        for c in range(2):
            for hp in range(2):
                HP.copy(hdec[:, c * 2 + hp: c * 2 + hp + 1], hx["ep"][:, hp, c * 64 + 63: c * 64 + 64],
                       reads=[hx_t["ep"]], writes=[hdec_t])
        if STAGE == 3.4:
            return
        if not state_only:
            p_o = 3
            for h in range(4):
                hp, rb = h // 2, (h % 2) * 64
                p_at = tmp_next()
                HP.mm(HB[p_at][:, 0:128], hx["ke"][rb:rb + 64, hp, :], hx["qe"][rb:rb + 64, hp, :],
                     reads=[hx_t["ke"], hx_t["qe"]], writes=[hb_t[p_at]], tp=(rb, 0))
                ai = h % 2
                HP.tt(hattn[ai][:], HB[p_at][:, 0:128], TRI, ALU.mult, reads=[hb_t[p_at], c_t], writes=[hattn_t[ai]])
                HP.mm(HB[p_o][rb:rb + 64, hp * 128:(hp + 1) * 128], HI[:, blk, h * 64:(h + 1) * 64], hattn[ai][:],
                     start=True, stop=True, reads=[hi_t[blk], hattn_t[ai]], writes=[hb_t[p_o]], tp=(0, rb))
            if STAGE == 3.5:
                return
        p_i = 4
        for c in range(2):
            for h in range(4):
                if STAGE == 3.55 or state_only:
                    break
                hp, rb = h // 2, (h % 2) * 64
                HP.mm(HB[p_i][rb:rb + 64, hp * 128 + c * 64: hp * 128 + c * 64 + 64],
                     Sst[rb:rb + 64, hp * 64:(hp + 1) * 64], hx["qe"][rb:rb + 64, hp, c * 64:(c + 1) * 64],
                     start=True, stop=True, reads=[s_t, hx_t["qe"]], writes=[hb_t[p_i]], tp=(rb, rb))
            if STAGE == 3.57:
                continue
            p_s = tmp_next()
            for h in range(4):
                hp, rb = h // 2, (h % 2) * 64
                HP.mm(HB[p_s][rb:rb + 64, hp * 64:(hp + 1) * 64], kd[c * 64:(c + 1) * 64, h * 64:(h + 1) * 64],
                     HI[c * 64:(c + 1) * 64, blk, h * 64:(h + 1) * 64], reads=[hw_t["kd"], hi_t[blk]],
                     writes=[hb_t[p_s]], tp=(c * 64, rb))
            for hp in range(2):
                HP.stt(Sst[:, hp * 64:(hp + 1) * 64], Sst[:, hp * 64:(hp + 1) * 64],
                      hdec[:, c * 2 + hp: c * 2 + hp + 1], HB[p_s][:, hp * 64:(hp + 1) * 64], ALU.mult, ALU.add,
                      reads=[s_t, hdec_t, hb_t[p_s]], writes=[s_t])
        if STAGE in (3.55, 3.57, 3.6) or state_only:
            return
        HP.copy(hosb[:], HB[p_i][:, 0:256], reads=[hb_t[p_i]], writes=[hosb_t], eng="act")
        HP.tt(hosb[:], hosb[:], HB[p_o][:, 0:256], ALU.add, reads=[hosb_t, hb_t[p_o]], writes=[hosb_t])
        for hp in range(2):
            HP.act(hsq[:], hosb[:, hp * 128:(hp + 1) * 128], AF.Square, reads=[hosb_t], writes=[hsq_t])
            p_n = tmp_next()
            HP.mm(HB[p_n][:, 0:128], OB, hsq[:], reads=[c_t, hsq_t], writes=[hb_t[p_n]])
            HP.ts(hrs[:], HB[p_n][:, 0:128], 1.0 / 64, EPS, ALU.mult, ALU.add, reads=[hb_t[p_n]], writes=[hrs_t])
            HP.act(hrs[:], hrs[:], AF.Ln, reads=[hrs_t], writes=[hrs_t])
            HP.act(hrs[:], hrs[:], AF.Exp, reads=[hrs_t], writes=[hrs_t], scale=-0.5)
            HP.tt(hrs[:], hosb[:, hp * 128:(hp + 1) * 128], hrs[:], ALU.mult, reads=[hosb_t, hrs_t],
                 writes=[hrs_t])
            HP.stt(MIX[:, 4 + hp, c0:c0 + 128], hrs[:], hgn[:, hp:hp + 1], SG[:, 4 + hp, c0:c0 + 128], ALU.mult,
                  ALU.mult, reads=[hrs_t, c_t, sg_t[4 + hp]], writes=[mix_t[4 + hp]])

    def xattn(HP=None, defer=False):
        if HP is None:
            HP = P
        for hp in range(2):
            if defer:
                bk = {"o": (PF[5], pf_t[5]), "d": (PBf[1], pb_t[1])}
                PT, pt_t = kbf, kbf_t
            else:
                i_o, i_d = pf_next(), pf_next()
                bk = {"o": (PF[i_o], pf_t[i_o]), "d": (PF[i_d], pf_t[i_d])}
                PT, pt_t = AB, ab_t
            (PO, po_t), (PD, pd_t) = bk["o"], bk["d"]
            for hh in range(2):
                h, rb = hp * 2 + hh, hh * 64
                for mb in range(2):
                    if defer:
                        PS_, ps_t = PBf[0], pb_t[0]
                    else:
                        i_s = pf_next()
                        PS_, ps_t = PF[i_s], pf_t[i_s]
                    HP.mm(PS_[:, :], MKT[rb:rb + 64, hp, mb * 128:(mb + 1) * 128], XQT[rb:rb + 64, hp, :],
                          reads=[mk_t, xq_t[hp]], writes=[ps_t], tp=(rb, 0))
                    ai = rr["x"] % 2
                    rr["x"] += 1
                    HP.act(PT[ai][:], PS_[:, :], AF.Exp, reads=[ps_t], writes=[pt_t[ai]])
                    HP.mm(PO[rb:rb + 64, :], MV[:, mb, h * 64:(h + 1) * 64], PT[ai][:], start=(mb == 0),
                          stop=(mb == 1), reads=[mk_t, pt_t[ai]], writes=[po_t], tp=(0, rb))
                    HP.mm(PD[rb:rb + 64, :], ONESB[:, 0:64], PT[ai][:], start=(mb == 0), stop=(mb == 1),
                          reads=[c_t, pt_t[ai]], writes=[pd_t], tp=(0, rb))
            gi = rr["x"] % 2
            rr["x"] += 1
            HP.recip(gtmp[gi][:], PD[:, :], reads=[pd_t], writes=[gtmp_t[gi]])
            HP.tt(gtmp[gi][:], PO[:, :], gtmp[gi][:], ALU.mult, reads=[po_t, gtmp_t[gi]],
                  writes=[gtmp_t[gi]])
            HP.tt(MIX[:, 6 + hp, :], gtmp[gi][:], SG[:, 6 + hp, :], ALU.mult, reads=[gtmp_t[gi], sg_t[6 + hp]],
                  writes=[mix_t[6 + hp]])

    def sb_attention(qi, side=None):
        nkb = 4 * qi + 4
        for hp in range(4):
            av = 4
            kbs = list(range(nkb - 1, -1, -1))
            n = len(kbs)

            def stage_a1(i):
                kb = kbs[i]
                r = i % 2
                for hh in range(2):
                    rb = hh * 64
                    P.mm(PF[hh][:, :], KT[rb:rb + 64, hp, kb * 128:(kb + 1) * 128], QT[rb:rb + 64, hp, :],
                         reads=[kt_t[hp][kb], qt_t[hp]], writes=[pf_t[hh]], tp=(rb, 0))
                z = kb - 4 * qi
                bsrc = sbbp if kb < 4 * HALF else sbb
                for hh in range(2):
                    h = hp * 2 + hh
                    e = hh * 2 + r
                    P.act(EB[e][:], PF[hh][:, :], AF.Exp, reads=[pf_t[hh], c_t], writes=[eb_t[e]],
                          bias=bsrc[:, h:h + 1])
                    if z >= 0:
                        P.tt(EB[e][:], EB[e][:], MSK[z], ALU.mult, reads=[eb_t[e], c_t], writes=[eb_t[e]])

            def stage_a2(i):
                r = i % 2
                for hh in range(2):
                    e = hh * 2 + r
                    P.act(SPB[e][:], EB[e][:], AF.Ln, reads=[eb_t[e]], writes=[spb_t[e]], bias=1.0)

            def stage_b(i):
                kb = kbs[i]
                r = i % 2
                for hh in range(2):
                    e = hh * 2 + r
                    P.mm(PF[2 + hh][:, :], UIN, SPB[e][:], start=(kb == nkb - 1), stop=(kb == 0),
                         reads=[c_t, spb_t[e]], writes=[pf_t[2 + hh]])
                for hh in range(2):
                    e = hh * 2 + r
                    P.act(WB[hh][:], PF[2 + hh][:, :], AF.Exp, reads=[pf_t[2 + hh]], writes=[wb_t[hh]], scale=-1.0)
                    P.tt(AB[e][:], EB[e][:], WB[hh][:], ALU.mult, reads=[eb_t[e], wb_t[hh]], writes=[ab_t[e]])

            def stage_b2(i):
                kb = kbs[i]
                r = i % 2
                if kb > 0:
                    for hh in range(2):
                        e = hh * 2 + r
                        P.mm(PF[2 + hh][:, :], LST, SPB[e][:], start=False, stop=False, reads=[c_t, spb_t[e]],
                             writes=[pf_t[2 + hh]])

            def stage_c(i):
                kb = kbs[i]
                r = i % 2
                for hh in range(2):
                    rb = hh * 64
                    e = hh * 2 + r
                    h = hp * 2 + hh
                    P.mm(PF[av][rb:rb + 64, :], VA[:, kb, h * 64:(h + 1) * 64], AB[e][:], start=(kb == nkb - 1),
                         stop=(kb == 0), reads=[va_t[kb], ab_t[e]], writes=[pf_t[av]], tp=(0, rb))

            sd = side if side is not None else (lambda i_: None)
            for i in range(n + 2):
                if i < n:
                    stage_a1(i)
                sd(i)
                if 0 <= i - 2 < n:
                    stage_b2(i - 2)
                if 0 <= i - 1 < n:
                    stage_b(i - 1)
                sd(i)
                if i < n:
                    stage_a2(i)
                sd(i)
                if 0 <= i - 2 < n:
                    stage_c(i - 2)
                sd(i)
            P.tt(MIX[:, hp, :], PF[av][:, :], SG[:, hp, :], ALU.mult, reads=[pf_t[av], sg_t[hp]],
                 writes=[mix_t[hp]])

    HALF = NT // 2

    def tile(ti):
        own = ti >= HALF
        src = xo_d if own else x_d
        r0 = (ti - HALF) * 512 if own else ti * 512
        for blk in range(4):
            xi = rr["x"] % 2
            rr["x"] += 1
            P.dma(xst[xi][:], src[r0 + blk * 128: r0 + (blk + 1) * 128, :], writes=[xst_t[xi]])
            rmsnorm_to_bf(xst[xi][:], [xst_t[xi]], xi)
            transpose_to(lambda blk=blk: xnT[:, :, blk * 128:(blk + 1) * 128], xi, [xnT_t[blk]])
        rr["pf"] = 0
        wgi = load_wgroup(1)
        for blk in range(4):
            gb = ti * 4 + blk
            pi = tok_proj(wgi, 0, 512, blk)
            si = gb % 2
            P.copy(kvst[si][:], PF[pi][:, :], reads=[pf_t[pi]], writes=[kvst_t[si]], eng="act")
            if own:
                P.dma(k_d[r0 + blk * 128: r0 + (blk + 1) * 128, :], kvst[si][:], reads=[kvst_t[si]])
            P.copy(kbf[si][:], kvst[si][:], reads=[kvst_t[si]], writes=[kbf_t[si]])
            pb = rr["pb"] % 2
            rr["pb"] += 1
            for hp in range(4):
                P.tr(PB[pb][:, hp * 128:(hp + 1) * 128], kbf[si][:, hp * 128:(hp + 1) * 128], ident_b,
                     reads=[kbf_t[si], c_t], writes=[pb_t[pb]])
            P.copy(KT[:, :, gb * 128:(gb + 1) * 128], PB[pb][:, 0:512].rearrange("p (k t) -> p k t", k=4),
                   reads=[pb_t[pb]], writes=[kt_t[hp_][gb] for hp_ in range(4)])
        wgi = load_wgroup(2)
        for blk in range(4):
            gb = ti * 4 + blk
            pi = tok_proj(wgi, 0, 512, blk)
            si = gb % 2
            P.copy(kvst[si][:], PF[pi][:, :], reads=[pf_t[pi]], writes=[kvst_t[si]], eng="act")
            if own:
                P.dma(v_d[r0 + blk * 128: r0 + (blk + 1) * 128, :], kvst[si][:], reads=[kvst_t[si]])
            P.copy(VA[:, gb, :], kvst[si][:], reads=[kvst_t[si]], writes=[va_t[gb]])
        if own:
            wgi = load_wgroup(0)
            feat_proj(wgi, range(4), lambda cbi, pi: P.act(QT[:, cbi, :], PF[pi][:, :], AF.Copy, reads=[pf_t[pi]],
                                                          writes=[qt_t[cbi]], scale=0.125))
            wgi = load_wgroup(3)
            feat_proj(wgi, range(4), lambda cbi, pi: silu_from_psum(pi, SG[:, cbi, :], sg_t[cbi]))
        wgi = load_wgroup(4)
        if own:
            feat_proj(wgi, range(2), lambda cbi, pi: P.copy(HQT[:, cbi, :], PF[pi][:, :], reads=[pf_t[pi]],
                                                            writes=[hq_t[cbi]], eng="act"))
        for blk in range(4):
            pi = tok_proj(wgi, 256, 512, blk)
            P.copy(HF[:, blk, :], PF[pi][:, 0:256], reads=[pf_t[pi]], writes=[hf_t[blk]])
        wgi = load_wgroup(5)
        for blk in range(4):
            pi = tok_proj(wgi, 0, 256, blk)
            P.copy(HI[:, blk, :], PF[pi][:, 0:256], reads=[pf_t[pi]], writes=[hi_t[blk]], eng="act")
        if own:
            feat_proj(wgi, range(2, 4), lambda cbi, pi: silu_from_psum(pi, SG[:, 2 + cbi, :], sg_t[2 + cbi]))
            wgi = load_wgroup(6)
            feat_proj(wgi, range(2), lambda cbi, pi: P.act(XQT[:, cbi, :], PF[pi][:, :], AF.Copy, reads=[pf_t[pi]],
                                                           writes=[xq_t[cbi]], scale=0.125))
            feat_proj(wgi, range(2, 4), lambda cbi, pi: silu_from_psum(pi, SG[:, 4 + cbi, :], sg_t[4 + cbi]))
        if not own:
            for blk in range(4):
                hgrn_block(blk, state_only=True)
            return
        HD = Deferred()
        for blk in range(4):
            hgrn_block(blk, HP=HD, defer=True)
        xattn(HP=HD, defer=True)
        nq = len(HD.q)
        n_it = 4 * (4 * ti + 6)
        per = max(1, (nq + 4 * n_it - 1) // (4 * n_it))
        sb_attention(ti, side=lambda i_: HD.pump(per))
        HD.pump(10 ** 9)
        wg0 = load_wgroup(7)
        wg1 = load_wgroup(8)
        for blk in range(4):
            xi = rr["x"] % 2
            rr["x"] += 1
            P.dma(xst[xi][:], src[r0 + blk * 128: r0 + (blk + 1) * 128, :], writes=[xst_t[xi]])
            for half, wg in ((0, wg0), (1, wg1)):
                pi = pf_next()
                kbs = list(range(8)) if DBG_KBS is None else list(DBG_KBS)
                for kb in kbs:
                    P.mm(PF[pi][:, :], MIX[:, kb, blk * 128:(blk + 1) * 128], WG[wg][:, kb, :], start=(kb == kbs[0]),
                         stop=(kb == kbs[-1]), reads=[mix_t[kb], wg_t[wg]], writes=[pf_t[pi]])
                P.tt(yst[:, half * 512:(half + 1) * 512], PF[pi][:, :], xst[xi][:, half * 512:(half + 1) * 512],
                     ALU.add, reads=[pf_t[pi], xst_t[xi]], writes=[yst_t])
            st, stt_ = st4[xi], st4_t[xi]
            P.act(xsb[xi][:], yst[:], AF.Square, reads=[yst_t], writes=[xsb_t[xi], stt_], accum_out=st[:, 0:1])
            P.ts(st[:, 1:2], st[:, 0:1], 1.0 / D, EPS, ALU.mult, ALU.add, reads=[stt_], writes=[stt_])
            P.act(st[:, 2:3], st[:, 1:2], AF.Ln, reads=[stt_], writes=[stt_])
            P.act(st[:, 3:4], st[:, 2:3], AF.Exp, reads=[stt_], writes=[stt_], scale=-0.5)
            if DBG_PRE:
                P.copy(xst[xi][:], yst[:], reads=[yst_t], writes=[xst_t[xi]])
            else:
                P.stt(xst[xi][:], yst[:], st[:, 3:4], fgb[:], ALU.mult, ALU.mult, reads=[yst_t, stt_, c_t],
                      writes=[xst_t[xi]])
            P.dma(y_d[r0 + blk * 128: r0 + (blk + 1) * 128, :], xst[xi][:], reads=[xst_t[xi]])

    SS = {}

    def sample_front():
        G = [EB[0], EB[1], kvst[0], kvst[1], wst[0], wst[1], gtmp[0]]
        g_t = [eb_t[0], eb_t[1], kvst_t[0], kvst_t[1], wst_t[0], wst_t[1], gtmp_t[0]]
        TMP = [gtmp[1][:, :], HQT[:, 1, :]]
        tmp_t = [gtmp_t[1], Tok(old=hq_t)]
        QB, qb_t = HQT[:, 0, :], Tok(old=hq_t)
        HFv = HF[:, :, :].rearrange("p a b -> p (a b)")
        HIv = HI[:, :, :].rearrange("p a b -> p (a b)")
        HFa, HFb, HIa, HIb = HFv[:, 0:512], HFv[:, 512:1024], HIv[:, 0:512], HIv[:, 512:1024]
        hfa_t, hfb_t, hia_t, hib_t = Tok(old=hf_t), Tok(old=hf_t), Tok(old=hi_t), Tok(old=hi_t)
        ZALL, E_ = SPACC[0], SPACC[1]
        v3 = lambda ap: ap.rearrange("p (g h) -> p g h", h=8)
        otok = sb([128, D], F32, "otok")
        otok_t = T()
        ptb = sb([128, NSAMP * NPAGE], I32, "ptb")
        idxa = sb([128, NSAMP * NPAGE], I32, "idxa")
        idx_t = T()
        hgnb = lbl[:, 0:256]
        FKQ = sb([128, 24], F32, "fkq")
        fkq_t = T()
        SM = sb([128, 32], F32, "sm")
        sm_t = T()
        scr_t = T()

        P.memset(otok[:], 0.0, writes=[otok_t], eng="dve")
        P.dma(hgnb, hgr_d.partition_broadcast(128), writes=[c_t])
        P.dma(ptb[:], pt_d.partition_broadcast(128), writes=[idx_t])
        P.ts(idxa[:], ptb[:], 128.0, IOTA, ALU.mult, ALU.add, reads=[idx_t, c_t], writes=[idx_t])
        P.memset(yst[:], 0.0, writes=[yst_t], eng="dve")
        P.dma(yst[0:NSAMP, :], xs_d, writes=[yst_t])
        rmsnorm_to_bf(yst[:], [yst_t], 0)
        transpose_to(lambda: xnT[:, :, 0:128], 0, [xnT_t[0]])
        for g in range(7):
            wgi = load_wgroup(g)
            pi = tok_proj(wgi, 0, 512, 0)
            P.copy(G[g][:], PF[pi][:, :], reads=[pf_t[pi]], writes=[g_t[g]], eng="act" if g % 2 else "dve")
            P.dma(scr_d[:, g * 512:(g + 1) * 512], G[g][0:NSAMP, :], reads=[g_t[g]], writes=[scr_t])
        P.dma(ks_d, G[1][0:NSAMP, :], reads=[g_t[1]])
        P.dma(vs_d, G[2][0:NSAMP, :], reads=[g_t[2]])
        if STAGE == 10.1:
            return
        a, f, k = hw["a"], hw["f"], hw["k"]
        P.act(a[:], G[4][:, 256:512], AF.Exp, reads=[g_t[4]], writes=[hw_t["a"]], scale=-1.0)
        P.ts(a[:], a[:], 1.0, None, ALU.add, reads=[hw_t["a"]], writes=[hw_t["a"]])
        P.recip(a[:], a[:], reads=[hw_t["a"]], writes=[hw_t["a"]])
        P.tt(f[:], a[:], oml[:], ALU.mult, reads=[hw_t["a"], c_t], writes=[hw_t["f"]])
        P.tt(f[:], f[:], lb[:], ALU.add, reads=[hw_t["f"], c_t], writes=[hw_t["f"]])
        P.ts(k[:], f[:], -1.0, 1.0, ALU.mult, ALU.add, reads=[hw_t["f"]], writes=[hw_t["k"]])
        p_f = pf_next()
        for j, (src, st_) in enumerate(((f, hw_t["f"]), (k, hw_t["k"]), (G[4], g_t[4]))):
            for hp in range(2):
                c = (j * 2 + hp) * 4
                P.tr(PF[p_f][:, c:c + 4], src[0:4, hp * 128:(hp + 1) * 128], ident_f[0:4, 0:4],
                     reads=[st_, c_t], writes=[pf_t[p_f]])
        P.copy(FKQ[:], PF[p_f][:, 0:24], reads=[pf_t[p_f]], writes=[fkq_t])

        SS.update(dict(G=G, g_t=g_t, TMP=TMP, tmp_t=tmp_t, HFa=HFa, HFb=HFb, HIa=HIa, HIb=HIb, hfa_t=hfa_t, hfb_t=hfb_t,
                       hia_t=hia_t, hib_t=hib_t, otok=otok, otok_t=otok_t, idxa=idxa, idx_t=idx_t, hgnb=hgnb, FKQ=FKQ,
                       fkq_t=fkq_t, SM=SM, sm_t=sm_t, scr_t=scr_t))

    def sample_stream():
        otok, otok_t, idxa, idx_t, scr_t = SS["otok"], SS["otok_t"], SS["idxa"], SS["idx_t"], SS["scr_t"]
        PG = [KT[:, hp_, S // 2:S].bitcast(F32) for hp_ in range(4)]
        VAf = VA[:, NB // 2:NB, :].rearrange("p a b -> p (a b)").bitcast(F32)
        PG += [VAf[:, 2048:3072]]
        pg_t = [T() for _ in range(5)]
        NPB = 5
        VBF = [VA[:, NB - 4 + j_, :] for j_ in range(4)]
        vbf_t = [T() for _ in range(4)]
        A8b = sb([128, 4, 8], BF16, "a8b")

        TMPs = [VAf[:, 0:512], VAf[:, 512:1024]]
        tmps_t = [T(), T()]
        QBs, qbs_t = VAf[:, 1024:1536], T()
        RES, res_t = VAf[:, 1536:2048], T()
        for hp_ in range(4):
            for kb_ in range(NB // 2, NB):
                kt_t[hp_][kb_].old = [pg_t[hp_]]
        for kb_ in range(NB // 2, NB):
            va_t[kb_].old = [pg_t[4], tmps_t[0], tmps_t[1], qbs_t, res_t] + vbf_t
        S8 = sb([128, 4, 48], F32, "s8")
        S8b = sb([128, 4, 8], BF16, "s8b")
        s8_t = [T() for _ in range(4)]
        NPG = NPAGE
        CB, AVB = 5, 4
        npg = [0]
        for n in range(NSAMP):
            P.dma(QBs, scr_d[n:n + 1, 0:512].partition_broadcast(128), reads=[scr_t], writes=[qbs_t])
            pages = list(range(NPG - 1, -1, -1))

            def st1(ii):
                p = pages[ii]
                i = npg[0] + ii
                col = n * NPAGE + p
                b8 = i % 4
                P.gather(PG[i % NPB], ckv_d[:, :], idxa[:, col:col + 1], reads=[idx_t], writes=[pg_t[i % NPB]])
                P.tt(TMPs[i % 2], PG[i % NPB][:, 0:512], QBs, ALU.mult, reads=[pg_t[i % NPB], qbs_t], writes=[tmps_t[i % 2]])
                P.copy(VBF[b8], PG[i % NPB][:, 512:1024], reads=[pg_t[i % NPB]], writes=[vbf_t[b8]], eng="act")
                P.reduce(S8[:, b8, 0:8], TMPs[i % 2].rearrange("p (h d) -> p h d", h=8), ALU.add,
                         reads=[tmps_t[i % 2]], writes=[s8_t[b8]])
                P.stt(S8[:, b8, 0:8], S8[:, b8, 0:8], 0.125, sbb[:, 0:8], ALU.mult, ALU.add,
                      reads=[s8_t[b8], c_t], writes=[s8_t[b8]])
                P.act(S8[:, b8, 8:16], S8[:, b8, 0:8], AF.Exp, reads=[s8_t[b8]], writes=[s8_t[b8]])
                P.act(S8b[:, b8, :], S8[:, b8, 8:16], AF.Ln, reads=[s8_t[b8]], writes=[s8_t[b8]], bias=1.0)

            def st2(ii):
                i = npg[0] + ii
                b8 = i % 4
                P.mm(PF[CB][:, 0:8], UIN, S8b[:, b8, :], start=(ii == 0), stop=(ii == NPG - 1),
                     reads=[c_t, s8_t[b8]], writes=[pf_t[CB]])
                P.act(S8[:, b8, 24:32], PF[CB][:, 0:8], AF.Exp, reads=[pf_t[CB]], writes=[s8_t[b8]], scale=-1.0)
                P.tt(A8b[:, b8, :], S8[:, b8, 8:16], S8[:, b8, 24:32], ALU.mult, reads=[s8_t[b8]],
                     writes=[s8_t[b8]])
                if ii < NPG - 1:
                    P.mm(PF[CB][:, 0:8], LST, S8b[:, b8, :], start=False, stop=False, reads=[c_t, s8_t[b8]],
                         writes=[pf_t[CB]])
                P.mm(PF[AVB][0:8, :], A8b[:, b8, :], VBF[b8], start=(ii == 0), stop=(ii == NPG - 1),
                     reads=[s8_t[b8], vbf_t[b8]], writes=[pf_t[AVB]])

            for ii in range(NPG + 1):
                if ii < NPG:
                    st1(ii)
                if ii >= 1:
                    st2(ii - 1)
                yield
            npg[0] += NPG
            P.copy(RES[0:8, :], PF[AVB][0:8, :], reads=[pf_t[AVB]], writes=[res_t])
            for h in range(8):
                P.dma(otok[n:n + 1, h * 64:(h + 1) * 64], RES[h:h + 1, h * 64:(h + 1) * 64], reads=[res_t],
                      writes=[otok_t])
            yield

    def sample_back():
        G, g_t, TMP, tmp_t = SS["G"], SS["g_t"], SS["TMP"], SS["tmp_t"]
        HFa, HFb, HIa, HIb = SS["HFa"], SS["HFb"], SS["HIa"], SS["HIb"]
        hfa_t, hfb_t, hia_t, hib_t = SS["hfa_t"], SS["hfb_t"], SS["hia_t"], SS["hib_t"]
        otok, otok_t, hgnb, FKQ, fkq_t, SM, sm_t, scr_t = (SS["otok"], SS["otok_t"], SS["hgnb"], SS["FKQ"], SS["fkq_t"],
                                                          SS["SM"], SS["sm_t"], SS["scr_t"])
        for g in (3, 5, 6):
            P.dma(G[g][0:NSAMP, :], scr_d[:, g * 512:(g + 1) * 512], reads=[scr_t], writes=[g_t[g]])
        P.memset(yst[:], 0.0, writes=[yst_t], eng="dve")
        P.dma(yst[0:NSAMP, :], xs_d, writes=[yst_t])
        for n in range(NSAMP):
            VB, S0, SN, KV = hw["kd"], hsq, hrs, hattn[0]
            P.dma(VB[:], scr_d[n:n + 1, 5 * 512:5 * 512 + 256].partition_broadcast(128), reads=[scr_t],
                  writes=[hw_t["kd"]])
            P.dma(S0[:], sst_d[n], writes=[hsq_t])
            for half in range(2):
                r0 = half * 64
                for hp in range(2):
                    h = hp * 2 + half
                    ck_, cf_ = (1 * 2 + hp) * 4 + n, (0 * 2 + hp) * 4 + n
                    P.ts(KV[r0:r0 + 64, hp * 64:(hp + 1) * 64], VB[r0:r0 + 64, h * 64:(h + 1) * 64],
                         FKQ[r0:r0 + 64, ck_:ck_ + 1], None, ALU.mult, reads=[hw_t["kd"], fkq_t],
                         writes=[hattn_t[0]])
                    P.stt(SN[r0:r0 + 64, hp * 64:(hp + 1) * 64], S0[r0:r0 + 64, hp * 64:(hp + 1) * 64],
                          FKQ[r0:r0 + 64, cf_:cf_ + 1], KV[r0:r0 + 64, hp * 64:(hp + 1) * 64], ALU.mult, ALU.add,
                          reads=[hsq_t, fkq_t, hattn_t[0]], writes=[hrs_t])
            P.dma(hss_d[n], SN[:], reads=[hrs_t])
            if STAGE == 10.55:
                continue
            p_h = 5
            for hp in range(2):
                cq = (2 * 2 + hp) * 4 + n
                P.ts(KV[:, hp * 64:(hp + 1) * 64], SN[:, hp * 64:(hp + 1) * 64], FKQ[:, cq:cq + 1], None, ALU.mult,
                     reads=[hrs_t, fkq_t], writes=[hattn_t[0]])
            P.mm(PF[p_h][:, 0:128], OB, KV[:, :], reads=[c_t, hattn_t[0]], writes=[pf_t[p_h]])
            P.copy(hosb[:, 0:128], PF[p_h][:, 0:128], reads=[pf_t[p_h]], writes=[hosb_t])
            for half in range(2):
                for hp in range(2):
                    h = hp * 2 + half
                    P.dma(otok[n:n + 1, 512 + h * 64:512 + (h + 1) * 64],
                          hosb[half * 64:half * 64 + 1, hp * 64:(hp + 1) * 64], reads=[hosb_t], writes=[otok_t])
            if STAGE == 10.6:
                continue
            XQB, CMK, CMV, XT = hw["lf"], TMP[0], TMP[1], HFb
            P.dma(XQB[:], scr_d[n:n + 1, 6 * 512:6 * 512 + 256].partition_broadcast(128), reads=[scr_t],
                  writes=[hw_t["lf"]])
            P.dma(CMK.rearrange("p (b c) -> p b c", b=2), cmk_d[n].rearrange("(b m) c -> m b c", b=2),
                  writes=[tmp_t[0]])
            P.dma(CMV.rearrange("p (b c) -> p b c", b=2), cmv_d[n].rearrange("(b m) c -> m b c", b=2),
                  writes=[tmp_t[1]])
            for mb in range(2):
                P.tt(XT[:, mb * 256:(mb + 1) * 256], CMK[:, mb * 256:(mb + 1) * 256], XQB[:], ALU.mult,
                     reads=[tmp_t[0], hw_t["lf"]], writes=[hfb_t])
                P.reduce(SM[:, 8 + mb * 4:12 + mb * 4],
                         XT[:, mb * 256:(mb + 1) * 256].rearrange("p (h d) -> p h d", h=4), ALU.add,
                         reads=[hfb_t], writes=[sm_t])
            P.act(SM[:, 16:24], SM[:, 8:16], AF.Exp, reads=[sm_t], writes=[sm_t], scale=0.125)
            p_x, p_d = 0, 1
            for mb in range(2):
                P.mm(PF[p_x][0:4, 0:256], SM[:, 16 + mb * 4:20 + mb * 4], CMV[:, mb * 256:(mb + 1) * 256],
                     start=(mb == 0), stop=(mb == 1), reads=[sm_t, tmp_t[1]], writes=[pf_t[p_x]])
            for mb in range(2):
                P.mm(PF[p_d][0:4, 0:1], SM[:, 16 + mb * 4:20 + mb * 4], ONEC, start=(mb == 0), stop=(mb == 1),
                     reads=[sm_t, c_t], writes=[pf_t[p_d]])
            P.recip(SM[0:4, 24:25], PF[p_d][0:4, 0:1], reads=[pf_t[p_d]], writes=[sm_t])
            P.ts(hosb[0:4, :], PF[p_x][0:4, 0:256], SM[0:4, 24:25], None, ALU.mult, reads=[pf_t[p_x], sm_t],
                 writes=[hosb_t])
            for h in range(4):
                P.dma(otok[n:n + 1, 768 + h * 64:768 + (h + 1) * 64], hosb[h:h + 1, h * 64:(h + 1) * 64],
                      reads=[hosb_t], writes=[otok_t])
        if STAGE == 10.7:
            return
        HG = otok[:, 512:768]
        P.tt(hosb[:], HG, HG, ALU.mult, reads=[otok_t], writes=[hosb_t])
        P.reduce(SM[:, 0:4], hosb[:].rearrange("p (h d) -> p h d", h=4), ALU.add, reads=[hosb_t], writes=[sm_t])
        P.ts(SM[:, 0:4], SM[:, 0:4], 1.0 / 64, EPS, ALU.mult, ALU.add, reads=[sm_t], writes=[sm_t])
        P.act(SM[:, 4:8], SM[:, 0:4], AF.Ln, reads=[sm_t], writes=[sm_t])
        P.act(SM[:, 8:12], SM[:, 4:8], AF.Exp, reads=[sm_t], writes=[sm_t], scale=-0.5)
        for h in range(4):
            P.ts(otok[:, 512 + h * 64:512 + (h + 1) * 64], otok[:, 512 + h * 64:512 + (h + 1) * 64],
                 SM[:, 8 + h:9 + h], None, ALU.mult, reads=[otok_t, sm_t], writes=[otok_t])
        P.tt(HG, HG, hgnb, ALU.mult, reads=[otok_t, c_t], writes=[otok_t])
        for (gsrc, gtok, c0, c1, o0) in ((G[3], g_t[3], 0, 512, 0), (G[5], g_t[5], 256, 512, 512),
                                         (G[6], g_t[6], 256, 512, 768)):
            w_ = c1 - c0
            t_ = HIb[:, 0:w_]
            P.act(t_, gsrc[:, c0:c1], AF.Exp, reads=[gtok], writes=[hib_t], scale=-1.0)
            P.ts(t_, t_, 1.0, None, ALU.add, reads=[hib_t], writes=[hib_t])
            P.recip(t_, t_, reads=[hib_t], writes=[hib_t])
            P.tt(t_, t_, gsrc[:, c0:c1], ALU.mult, reads=[hib_t, gtok], writes=[hib_t])
            P.tt(otok[:, o0:o0 + w_], otok[:, o0:o0 + w_], t_, ALU.mult, reads=[otok_t, hib_t], writes=[otok_t])
        P.copy(xsb[0][:], otok[:], reads=[otok_t], writes=[xsb_t[0]])
        transpose_to(lambda: MIX[:, :, 0:128], 0, mix_t)
        wg0 = load_wgroup(7)
        wg1 = load_wgroup(8)
        st, stt_ = st4[0], st4_t[0]
        for half, wg in ((0, wg0), (1, wg1)):
            pi = pf_next()
            for kb in range(8):
                P.mm(PF[pi][:, :], MIX[:, kb, 0:128], WG[wg][:, kb, :], start=(kb == 0), stop=(kb == 7),
                     reads=[mix_t[kb], wg_t[wg]], writes=[pf_t[pi]])
            P.tt(G[half][:], PF[pi][:, :], yst[:, half * 512:(half + 1) * 512], ALU.add,
                 reads=[pf_t[pi], yst_t], writes=[g_t[half]])
            P.act(xsb[1][:, half * 512:(half + 1) * 512], G[half][:], AF.Square, reads=[g_t[half]],
                  writes=[xsb_t[1], stt_], accum_out=st[:, half:half + 1])
        P.tt(st[:, 0:1], st[:, 0:1], st[:, 1:2], ALU.add, reads=[stt_], writes=[stt_])
        P.ts(st[:, 1:2], st[:, 0:1], 1.0 / D, EPS, ALU.mult, ALU.add, reads=[stt_], writes=[stt_])
        P.act(st[:, 2:3], st[:, 1:2], AF.Ln, reads=[stt_], writes=[stt_])
        P.act(st[:, 3:4], st[:, 2:3], AF.Exp, reads=[stt_], writes=[stt_], scale=-0.5)
        for half in range(2):
            P.stt(G[half][:], G[half][:], st[:, 3:4], fgb[:, half * 512:(half + 1) * 512], ALU.mult, ALU.mult,
                  reads=[g_t[half], stt_, c_t], writes=[g_t[half]])
            P.dma(ys_d[:, half * 512:(half + 1) * 512], G[half][0:NSAMP, :], reads=[g_t[half]])

    gen = [None]

    def pump(k):
        if gen[0] is None:
            return
        for _ in range(k):
            try:
                next(gen[0])
            except StopIteration:
                gen[0] = None
                return

    if ENABLE_SAMPLE:
        sample_front()
        gen[0] = sample_stream()
        rr["npf"] = 4
        P.hook = lambda: pump(1)
        P.every = 6
    for ti in range(N_TILES):
        if ti == HALF or N_TILES < HALF:
            P.hook = None
            pump(10 ** 9)
            rr["npf"] = 6
        tile(ti)
    P.hook = None
    pump(10 ** 9)
    rr["npf"] = 6
    P.dma(hgs_d, Sst[:], reads=[s_t])


    if ENABLE_SAMPLE:
        sample_back()

    P.emit()
    return nc


_NC_CACHE = {}


def _consts():
    s = np.arange(128)[:, None]
    t = np.arange(128)[None, :]
    same = (s // 64) == (t // 64)
    ident = np.eye(128, dtype=np.float32)
    tri = ((s <= t) & same).astype(np.float32)
    su = ((s > t) & same).astype(np.float32)
    ob = same.astype(np.float32)
    cf = np.concatenate([ident, tri, su, ob, np.arange(128, dtype=np.float32)[:, None], np.ones((128, 1), np.float32)],
                        axis=1).astype(np.float32)
    uin = (s >= t).astype(np.float32)
    tq = np.arange(512)[None, :]
    msk = [((d * 128 + s) < tq).astype(np.float32) for d in range(4)]
    lst = (s < t).astype(np.float32)
    cb = np.concatenate([ident, uin, lst] + msk, axis=1).astype(ml_dtypes.bfloat16)
    return cf, cb


def kernel(x_prompt, x_sample, mem_prompt, cache_k, cache_v, page_table, state_hgrn, cache_mem_k, cache_mem_v,
           norm_gain, w_in, sb_bias, hg_lb_logits, hg_norm_gain, mem_norm_gain, w_mem_kv, w_out, final_norm_gain):
    if "nc" not in _NC_CACHE:
        _NC_CACHE["nc"] = build_program()
    nc = _NC_CACHE["nc"]
    f = lambda a: np.ascontiguousarray(np.asarray(a, dtype=np.float32))
    cf, cb = _consts()
    common = {
        "w_in": f(w_in[0]), "w_out": f(w_out[0]), "w_mem": f(w_mem_kv[0]),
        "gin": f(np.asarray(norm_gain[0]).reshape(8, 128).T),
        "gmem": f(np.asarray(mem_norm_gain[0]).reshape(8, 128).T),
        "fg": f(np.asarray(final_norm_gain).reshape(1, D)),
        "hgn": f(np.asarray(hg_norm_gain[0]).reshape(2, 128).T),
        "sbb": f(np.asarray(sb_bias[0]).reshape(1, 8)),
        "lbl": f(np.asarray(hg_lb_logits).reshape(1, 512)),
        "cf": cf, "cb": cb,
    }
    if ENABLE_SAMPLE:
        ckv = np.concatenate([f(cache_k[0]).reshape(POOL_PAGES * 128, 512),
                              f(cache_v[0]).reshape(POOL_PAGES * 128, 512)], axis=1)
    in_maps = []
    for c in range(8):
        b = c // 2
        m = dict(common)
        r = c % 2
        xb = f(x_prompt[b])
        m["x"] = np.ascontiguousarray(xb[:S // 2]) if r == 1 else np.zeros((S // 2, D), np.float32)
        m["xo"] = np.ascontiguousarray(xb[r * (S // 2):(r + 1) * (S // 2)])
        m["kbias"] = np.full((128, 1), 0.0 if r == 1 else -30000.0, np.float32)
        m["mem"] = f(mem_prompt[b])
        if ENABLE_SAMPLE:
            sl = slice(NSAMP * c, NSAMP * (c + 1))
            m["xs"] = f(x_sample[sl, 0])
            m["pt"] = np.ascontiguousarray(np.asarray(page_table[sl], dtype=np.int32).reshape(1, NSAMP * NPAGE))
            m["ckv"] = ckv
            st = f(state_hgrn[0, sl]).reshape(NSAMP, 2, 2, 64, 64).transpose(0, 2, 3, 1, 4).reshape(NSAMP, 128, 128)
            m["sst"] = np.ascontiguousarray(st)
            m["cmk"] = f(cache_mem_k[0, sl]).reshape(NSAMP, 256, 256)
            m["cmv"] = f(cache_mem_v[0, sl]).reshape(NSAMP, 256, 256)
            m["hgr"] = f(np.asarray(hg_norm_gain[0]).reshape(1, 256))
        in_maps.append(m)
    res = run_bass_kernel_spmd(nc, in_maps[:NCORES], core_ids=list(range(NCORES))).results
    res = list(res) + [res[0], res[1 % NCORES]] * ((8 - NCORES) // 2 + 1)

    def unstate(a):
        return a.reshape(2, 64, 2, 64).transpose(2, 0, 1, 3).reshape(4, 64, 64)

    cat = lambda b, n_: np.concatenate([res[2 * b][n_], res[2 * b + 1][n_]], axis=0)
    y_prompt = np.stack([cat(b, "y") for b in range(4)]).astype(np.float32)
    k_prompt = np.stack([cat(b, "k") for b in range(4)]).reshape(1, 4, S, 8, 64).astype(np.float32)
    v_prompt = np.stack([cat(b, "v") for b in range(4)]).reshape(1, 4, S, 8, 64).astype(np.float32)
    hgrn_prompt = np.stack([unstate(res[2 * b + 1]["hgs"]) for b in range(4)])[None].astype(np.float32)
    mem_k = np.stack([res[2 * b]["mk"] for b in range(4)]).reshape(1, 4, 256, 4, 64).astype(np.float32)
    mem_v = np.stack([res[2 * b]["mv"] for b in range(4)]).reshape(1, 4, 256, 4, 64).astype(np.float32)
    if ENABLE_SAMPLE:
        y_sample = np.concatenate([res[c]["ys"] for c in range(8)]).reshape(32, 1, D).astype(np.float32)
        k_sample = np.concatenate([res[c]["ks"] for c in range(8)]).reshape(1, 32, 1, 8, 64).astype(np.float32)
        v_sample = np.concatenate([res[c]["vs"] for c in range(8)]).reshape(1, 32, 1, 8, 64).astype(np.float32)
        hgrn_sample = np.concatenate([np.stack([unstate(res[c]["hss"][i]) for i in range(NSAMP)])
                                      for c in range(8)])[None].astype(np.float32)
    else:
        y_sample = np.zeros((32, 1, D), np.float32)
        k_sample = np.zeros((1, 32, 1, 8, 64), np.float32)
        v_sample = np.zeros((1, 32, 1, 8, 64), np.float32)
        hgrn_sample = np.zeros((1, 32, 4, 64, 64), np.float32)
    return (y_prompt, y_sample, k_prompt, v_prompt, hgrn_prompt, mem_k, mem_v, k_sample, v_sample, hgrn_sample)
```
